# Optimizing a Trainium2 kernel written in Bass

```python
import jax
import jax.numpy as jnp
from jax import lax
import numpy as np

D_MODEL = 1024
BATCH = 8
SEQ = 2048
DEPTH = 2

GRID_W = 64
CTX_LEN = 256
HEAD_DIM = 64
NORM_EPS = 1e-6
NEG_INF = -1e30
N_MOD = 6

A_HEADS = 4
A_KV_HEADS = 2
A_WINDOW = 128
A_BLOCK = 128
ROPE_BASE = 10000.0
ROPE_AXIS_DIM = HEAD_DIM // 2

B_HEADS = 4
NA_ROWS = 8
NA_COLS = 16

C_HEADS = 8
C_DECAY_LORA = 64
C_ICLR_LORA = 64
C_GATE_LORA = 128
C_CONV = 3
C_GN_EPS = 64e-5
C_DECAY_SCALE = 0.6065306597126334

A_Q = A_HEADS * HEAD_DIM
A_KV = A_KV_HEADS * HEAD_DIM
B_W = B_HEADS * HEAD_DIM
C_W = C_HEADS * HEAD_DIM
MIX_W = A_Q + B_W + C_W
C_IN = 3 * C_W + 2 * C_DECAY_LORA + 2 * C_ICLR_LORA + C_GATE_LORA
IN_W = A_Q + 2 * A_KV + 3 * B_W + C_IN

PEER_HEADS = 8
PEER_NKEYS = 128
PEER_EXPERTS = PEER_NKEYS * PEER_NKEYS
PEER_QDIM = 256
PEER_TOPK = 16
PEER_CHUNK = 128

kernel_name = 'hybrid_dit_swa_natten_rwkv7_peer'


def rms_norm(x, g):
    xf = x.astype(jnp.float32)
    y = xf * lax.rsqrt(jnp.mean(xf * xf, axis=-1, keepdims=True) + NORM_EPS)
    return (y * g.astype(jnp.float32)).astype(x.dtype)


def split_heads(t, n_heads):
    return t.reshape(t.shape[0], t.shape[1], n_heads, HEAD_DIM)


def axial_rope_tables(n_tokens):
    t = np.arange(n_tokens)
    inv_freq = ROPE_BASE ** (-np.arange(0, ROPE_AXIS_DIM, 2) / ROPE_AXIS_DIM)
    ang = np.stack([(t // GRID_W)[:, None] * inv_freq[None], (t % GRID_W)[:, None] * inv_freq[None]], axis=1)
    return jnp.asarray(np.cos(ang), jnp.float32), jnp.asarray(np.sin(ang), jnp.float32)


def apply_axial_rope(x, cos, sin):
    b, s, h, d = x.shape
    xa = x.reshape(b, s, h, 2, 2, ROPE_AXIS_DIM // 2)
    x1, x2 = xa[..., 0, :], xa[..., 1, :]
    cs = cos[None, :, None].astype(x.dtype)
    sn = sin[None, :, None].astype(x.dtype)
    return jnp.stack([x1 * cs - x2 * sn, x2 * cs + x1 * sn], axis=-2).reshape(b, s, h, d)


def dense_context_attention(q, k, v, sink):
    b, l, hq, d = q.shape
    hkv = k.shape[2]
    qg = q.reshape(b, l, hkv, hq // hkv, d)
    s = jnp.einsum('bqhgd,bkhd->bhgqk', qg, k).astype(jnp.float32) * d ** -0.5
    if sink is not None:
        s_sink = jnp.broadcast_to(sink.astype(jnp.float32).reshape(hkv, hq // hkv, 1, 1), s.shape[:-1] + (1,))
        p = jax.nn.softmax(jnp.concatenate([s, s_sink], axis=-1), axis=-1)[..., :-1]
    else:
        p = jax.nn.softmax(s, axis=-1)
    o = jnp.einsum('bhgqk,bkhd->bqhgd', p.astype(v.dtype), v)
    return o.reshape(b, l, hq * d)


def banded_window_attention(q, k, v, kc, vc, sink):
    b, s, hq, d = q.shape
    hkv = k.shape[2]
    g = hq // hkv
    nb = s // A_BLOCK
    qb = q.reshape(b, nb, A_BLOCK, hkv, g, d)

    def band(t):
        tp = jnp.pad(t, ((0, 0), (A_BLOCK, A_BLOCK), (0, 0), (0, 0))).reshape(b, nb + 2, A_BLOCK, hkv, d)
        return jnp.concatenate([tp[:, :-2], tp[:, 1:-1], tp[:, 2:]], axis=2)

    kb, vb = band(k), band(v)
    qpos = np.arange(s).reshape(nb, A_BLOCK)
    kpos = np.arange(nb)[:, None] * A_BLOCK - A_BLOCK + np.arange(3 * A_BLOCK)[None, :]
    valid = ((kpos[:, None, :] >= 0) & (kpos[:, None, :] < s)
             & (np.abs(kpos[:, None, :] - qpos[:, :, None]) <= A_WINDOW))
    scale = d ** -0.5
    s_loc = jnp.einsum('bnqhgd,bnkhd->bnhgqk', qb, kb).astype(jnp.float32) * scale
    s_loc = jnp.where(valid[None, :, None, None], s_loc, NEG_INF)
    s_ctx = jnp.einsum('bnqhgd,bchd->bnhgqc', qb, kc).astype(jnp.float32) * scale
    s_sink = jnp.broadcast_to(sink.astype(jnp.float32).reshape(1, 1, hkv, g, 1, 1), s_ctx.shape[:-1] + (1,))
    p = jax.nn.softmax(jnp.concatenate([s_loc, s_ctx, s_sink], axis=-1), axis=-1).astype(v.dtype)
    nk = 3 * A_BLOCK
    o = (jnp.einsum('bnhgqk,bnkhd->bnqhgd', p[..., :nk], vb)
         + jnp.einsum('bnhgqc,bchd->bnqhgd', p[..., nk:nk + kc.shape[1]], vc))
    return o.reshape(b, s, hq * d)


def neighbourhood_attention(q, k, v, kc, vc, rpb):
    b, s, h, d = q.shape
    rows = s // GRID_W
    kr = min(NA_ROWS, rows)
    ncb = GRID_W // NA_COLS
    reg_w = 2 * NA_COLS
    n_reg = kr * reg_w
    row = np.arange(rows)
    row_start = np.clip(row - kr // 2, 0, rows - kr)
    blk = np.arange(ncb)
    reg_start = np.clip(blk * NA_COLS - NA_COLS // 2, 0, GRID_W - reg_w)
    key_row = row_start[:, None] + np.arange(kr)[None, :]
    key_col = reg_start[:, None] + np.arange(reg_w)[None, :]
    idx = (key_row[:, None, :, None] * GRID_W + key_col[None, :, None, :]).reshape(-1)
    kg = jnp.take(k, jnp.asarray(idx), axis=1).reshape(b, rows, ncb, n_reg, h, d)
    vg = jnp.take(v, jnp.asarray(idx), axis=1).reshape(b, rows, ncb, n_reg, h, d)
    q_col = blk[:, None] * NA_COLS + np.arange(NA_COLS)[None, :]
    win_start = np.clip(q_col - NA_COLS // 2, 0, GRID_W - NA_COLS)
    col_ok = ((key_col[:, None, :] >= win_start[:, :, None])
              & (key_col[:, None, :] < win_start[:, :, None] + NA_COLS))
    valid = np.broadcast_to(col_ok[:, :, None, :], (ncb, NA_COLS, kr, reg_w)).reshape(ncb, NA_COLS, n_reg)
    dr = key_row - row[:, None] + NA_ROWS - 1
    dc = np.clip(key_col[:, None, :] - q_col[:, :, None] + NA_COLS - 1, 0, 2 * NA_COLS - 2)
    bias = rpb[:, dr[:, None, None, :, None], dc[None, :, :, None, :]].astype(jnp.float32)
    bias = bias.reshape(h, rows, ncb, NA_COLS, n_reg).transpose(1, 2, 0, 3, 4)
    qb = q.reshape(b, rows, ncb, NA_COLS, h, d)
    scale = d ** -0.5
    s_loc = jnp.einsum('brjqhd,brjkhd->brjhqk', qb, kg).astype(jnp.float32) * scale + bias[None]
    s_loc = jnp.where(valid[None, None, :, None], s_loc, NEG_INF)
    s_ctx = jnp.einsum('brjqhd,bchd->brjhqc', qb, kc).astype(jnp.float32) * scale
    p = jax.nn.softmax(jnp.concatenate([s_loc, s_ctx], axis=-1), axis=-1).astype(v.dtype)
    o = (jnp.einsum('brjhqk,brjkhd->brjqhd', p[..., :n_reg], vg)
         + jnp.einsum('brjhqc,bchd->brjqhd', p[..., n_reg:], vc))
    return o.reshape(b, s, h * d)


def centred_conv(p, w):
    pp = jnp.pad(p, ((0, 0), (1, 1), (0, 0)))
    return pp[:, :-2] * w[0] + pp[:, 1:-1] * w[1] + pp[:, 2:] * w[2]


def rwkv7_prepare(p, lp):
    b, t, _ = p.shape
    p = centred_conv(p, lp['r7_conv'])
    cuts = [C_W, 2 * C_W, 3 * C_W, 3 * C_W + 2 * C_DECAY_LORA, 3 * C_W + 2 * C_DECAY_LORA + 2 * C_ICLR_LORA]
    r, k, v, wd, ad, gd = jnp.split(p, cuts, axis=-1)
    g = jax.nn.sigmoid(gd) @ lp['r7_g2']
    kkh = split_heads(k * lp['r7_kk'], C_HEADS).astype(jnp.float32)
    kk = kkh / jnp.maximum(jnp.sqrt(jnp.sum(kkh * kkh, axis=-1, keepdims=True)), 1e-12)
    z_w = lp['r7_w0'] + jnp.einsum('btdr,drc->btdc', jnp.tanh(wd.reshape(b, t, 2, C_DECAY_LORA)), lp['r7_w2'])
    decay = jnp.exp(-C_DECAY_SCALE * jax.nn.sigmoid(z_w.astype(jnp.float32)))
    z_a = lp['r7_a0'] + jnp.einsum('btdr,drc->btdc', ad.reshape(b, t, 2, C_ICLR_LORA), lp['r7_a2'])
    a = jax.nn.sigmoid(z_a.astype(jnp.float32))
    k_dir = k[:, :, None].astype(jnp.float32) * (1.0 + (a - 1.0) * lp['r7_ka'])
    return {'r': r, 'v': v, 'g': g, 'kk': kk, 'decay': decay, 'a': a, 'k': k_dir}


def wkv_scan(state0, r, w, k, v, kk, a, reverse):
    def step(S, inp):
        r_t, w_t, k_t, v_t, kk_t, a_t = inp
        sa = jnp.einsum('bhvk,bhk->bhv', S, -kk_t)
        S = S * w_t[:, :, None, :] + sa[..., None] * (kk_t * a_t)[:, :, None, :] + v_t[..., None] * k_t[:, :, None, :]
        return S, jnp.einsum('bhvk,bhk->bhv', S, r_t)

    xs = tuple(jnp.moveaxis(t, 1, 0) for t in (r, w, k, v, kk, a))
    s_final, ys = lax.scan(step, state0, xs, reverse=reverse)
    return s_final, jnp.moveaxis(ys, 0, 1)


def rwkv7_output(prep, y, lp):
    b, t = y.shape[:2]
    mu = jnp.mean(y, axis=-1, keepdims=True)
    var = jnp.mean(jnp.square(y - mu), axis=-1, keepdims=True)
    yn = ((y - mu) * lax.rsqrt(var + C_GN_EPS)).reshape(b, t, C_W) * lp['r7_lnw'] + lp['r7_lnb']
    rh = split_heads(prep['r'], C_HEADS).astype(jnp.float32)[:, :, None]
    kh = prep['k'].reshape(b, t, 2, C_HEADS, HEAD_DIM)
    bonus = jnp.sum(jnp.sum(rh * kh * lp['r7_rk'], axis=-1), axis=2)
    bonus = (bonus[..., None] * split_heads(prep['v'], C_HEADS).astype(jnp.float32)).reshape(b, t, C_W)
    return ((yn + bonus) * prep['g'].astype(jnp.float32)).astype(prep['r'].dtype)


def rwkv7_mixer(p_lat, p_ctx, lp, with_ctx_out):
    lat, ctx = rwkv7_prepare(p_lat, lp), rwkv7_prepare(p_ctx, lp)

    def scan_inputs(prep, d):
        f = lambda t: split_heads(t, C_HEADS).astype(jnp.float32)
        return (f(prep['r']), f(prep['decay'][:, :, d]), f(prep['k'][:, :, d]), f(prep['v']),
                prep['kk'], f(prep['a'][:, :, d]))

    state0 = jnp.zeros((p_lat.shape[0], C_HEADS, HEAD_DIM, HEAD_DIM), jnp.float32)
    s_f, yc_f = wkv_scan(state0, *scan_inputs(ctx, 0), reverse=False)
    _, yl_f = wkv_scan(s_f, *scan_inputs(lat, 0), reverse=False)
    s_b, yc_b = wkv_scan(state0, *scan_inputs(ctx, 1), reverse=True)
    _, yl_b = wkv_scan(s_b, *scan_inputs(lat, 1), reverse=True)
    out_lat = rwkv7_output(lat, yl_f + yl_b, lp)
    out_ctx = rwkv7_output(ctx, yc_f + yc_b, lp) if with_ctx_out else None
    return out_lat, out_ctx


def mixer_block(h_lat, h_ctx, lp, rope, with_ctx_out):
    cos, sin = rope
    cuts = [A_Q, A_Q + A_KV, A_Q + 2 * A_KV, A_Q + 2 * A_KV + B_W, A_Q + 2 * A_KV + 2 * B_W, A_Q + 2 * A_KV + 3 * B_W]
    aq, ak, av, bq, bk, bv, cx = jnp.split(h_lat @ lp['w_in'], cuts, axis=-1)
    aqc, akc, avc, bqc, bkc, bvc, cxc = jnp.split(h_ctx @ lp['w_in'], cuts, axis=-1)
    qa = apply_axial_rope(rms_norm(split_heads(aq, A_HEADS), lp['a_qnorm']), cos, sin)
    ka = apply_axial_rope(rms_norm(split_heads(ak, A_KV_HEADS), lp['a_knorm']), cos, sin)
    kac = rms_norm(split_heads(akc, A_KV_HEADS), lp['a_knorm'])
    vac = split_heads(avc, A_KV_HEADS)
    o_a = banded_window_attention(qa, ka, split_heads(av, A_KV_HEADS), kac, vac, lp['a_sink'])
    qb = rms_norm(split_heads(bq, B_HEADS), lp['b_qnorm'])
    kb = rms_norm(split_heads(bk, B_HEADS), lp['b_knorm'])
    kbc = rms_norm(split_heads(bkc, B_HEADS), lp['b_knorm'])
    vbc = split_heads(bvc, B_HEADS)
    o_b = neighbourhood_attention(qb, kb, split_heads(bv, B_HEADS), kbc, vbc, lp['b_rpb'])
    o_c, o_c_ctx = rwkv7_mixer(cx, cxc, lp, with_ctx_out)
    y_lat = jnp.concatenate([o_a, o_b, o_c], axis=-1) @ lp['w_out']
    if not with_ctx_out:
        return y_lat, None
    o_ac = dense_context_attention(rms_norm(split_heads(aqc, A_HEADS), lp['a_qnorm']), kac, vac, lp['a_sink'])
    o_bc = dense_context_attention(rms_norm(split_heads(bqc, B_HEADS), lp['b_qnorm']), kbc, vbc, None)
    y_ctx = jnp.concatenate([o_ac, o_bc, o_c_ctx], axis=-1) @ lp['w_out']
    return y_lat, y_ctx


def peer_ffn(h, wq, sub_keys, u, v):
    b, t, d = h.shape
    hc = h.reshape(b * t // PEER_CHUNK, PEER_CHUNK, d)

    def chunk(hb):
        q = (hb @ wq).reshape(PEER_CHUNK, PEER_HEADS, 2, PEER_QDIM // 2)
        s = jnp.einsum('thpd,hpnd->thpn', q, sub_keys).astype(jnp.float32)
        sv, si = lax.top_k(s, PEER_TOPK)
        cand = (sv[:, :, 0, :, None] + sv[:, :, 1, None, :]).reshape(PEER_CHUNK, PEER_HEADS, PEER_TOPK * PEER_TOPK)
        best, ci = lax.top_k(cand, PEER_TOPK)
        i1 = jnp.take_along_axis(si[:, :, 0], ci // PEER_TOPK, axis=-1)
        i2 = jnp.take_along_axis(si[:, :, 1], ci % PEER_TOPK, axis=-1)
        expert = i1 * PEER_NKEYS + i2
        gate = jax.nn.softmax(best, axis=-1)
        ue = jnp.take(u, expert, axis=0)
        ve = jnp.take(v, expert, axis=0)
        act = jax.nn.gelu(jnp.einsum('td,thkd->thk', hb, ue).astype(jnp.float32), approximate=False)
        return jnp.einsum('thk,thkd->td', (gate * act).astype(hb.dtype), ve)

    return lax.map(chunk, hc).reshape(b, t, d)


def setup_inputs(seed: int = 0) -> dict:
    key = jax.random.key(seed)
    ks = iter(jax.random.split(key, 40))

    def nrm(shape, scale):
        return scale * jax.random.normal(next(ks), shape, jnp.float32)

    L, D = DEPTH, D_MODEL
    decay_base = -6.0 + 5.5 * jnp.linspace(0.0, 1.0, C_W, dtype=jnp.float32) ** 0.9
    conv_base = jnp.array([0.25, 1.0, 0.25], jnp.float32)[None, :, None]
    return {
        'x': nrm((BATCH, SEQ, D), 1.0),
        'c': nrm((BATCH, D), 1.0),
        'ctx': nrm((BATCH, CTX_LEN, D), 1.0),
        'c_ctx': nrm((D,), 1.0),
        'norm_mix': 1.0 + nrm((L, D), 0.05),
        'norm_ffn': 1.0 + nrm((L, D), 0.05),
        'w_mod': nrm((L, D, N_MOD * D), 0.5 * D ** -0.5),
        'b_mod': nrm((L, N_MOD * D), 0.02),
        'w_in': nrm((L, D, IN_W), D ** -0.5),
        'w_out': nrm((L, MIX_W, D), MIX_W ** -0.5),
        'a_qnorm': 1.0 + nrm((L, HEAD_DIM), 0.05),
        'a_knorm': 1.0 + nrm((L, HEAD_DIM), 0.05),
        'a_sink': nrm((L, A_HEADS), 0.5),
        'b_qnorm': 1.0 + nrm((L, HEAD_DIM), 0.05),
        'b_knorm': 1.0 + nrm((L, HEAD_DIM), 0.05),
        'b_rpb': nrm((L, B_HEADS, 2 * NA_ROWS - 1, 2 * NA_COLS - 1), 0.1),
        'r7_conv': conv_base + nrm((L, C_CONV, C_IN), 0.05),
        'r7_w0': decay_base + nrm((L, 2, C_W), 0.3),
        'r7_w2': nrm((L, 2, C_DECAY_LORA, C_W), 0.5 * C_DECAY_LORA ** -0.5),
        'r7_a0': nrm((L, 2, C_W), 0.3),
        'r7_a2': nrm((L, 2, C_ICLR_LORA, C_W), 0.5 * C_ICLR_LORA ** -0.5),
        'r7_g2': nrm((L, C_GATE_LORA, C_W), C_GATE_LORA ** -0.5),
        'r7_kk': 0.85 + nrm((L, C_W), 0.05),
        'r7_ka': 1.0 + nrm((L, C_W), 0.05),
        'r7_rk': nrm((L, C_HEADS, HEAD_DIM), 0.1),
        'r7_lnw': 1.0 + nrm((L, C_W), 0.05),
        'r7_lnb': nrm((L, C_W), 0.02),
        'peer_wq': nrm((L, D, PEER_HEADS * PEER_QDIM), D ** -0.5),
        'peer_keys': nrm((L, PEER_HEADS, 2, PEER_NKEYS, PEER_QDIM // 2), (PEER_QDIM // 2) ** -0.5),
        'peer_u': nrm((L, PEER_EXPERTS, D), D ** -0.5),
        'peer_v': nrm((L, PEER_EXPERTS, D), 0.3),
    }


def reference(x, c, ctx, c_ctx, norm_mix, norm_ffn, w_mod, b_mod, w_in, w_out,
              a_qnorm, a_knorm, a_sink, b_qnorm, b_knorm, b_rpb,
              r7_conv, r7_w0, r7_w2, r7_a0, r7_a2, r7_g2, r7_kk, r7_ka, r7_rk, r7_lnw, r7_lnb,
              peer_wq, peer_keys, peer_u, peer_v):
    rope = axial_rope_tables(x.shape[1])
    x_lat, x_ctx = x, ctx
    silu_c = jax.nn.silu(c)
    silu_cc = jax.nn.silu(c_ctx)
    for l in range(DEPTH):
        with_ctx = l < DEPTH - 1
        mod = (silu_c @ w_mod[l] + b_mod[l])[:, None, :]
        mod_c = silu_cc @ w_mod[l] + b_mod[l]
        sh1, sc1, g1, sh2, sc2, g2 = jnp.split(mod, N_MOD, axis=-1)
        csh1, csc1, cg1, csh2, csc2, cg2 = jnp.split(mod_c, N_MOD, axis=-1)
        lp = {
            'w_in': w_in[l], 'w_out': w_out[l],
            'a_qnorm': a_qnorm[l], 'a_knorm': a_knorm[l], 'a_sink': a_sink[l],
            'b_qnorm': b_qnorm[l], 'b_knorm': b_knorm[l], 'b_rpb': b_rpb[l],
            'r7_conv': r7_conv[l], 'r7_w0': r7_w0[l], 'r7_w2': r7_w2[l], 'r7_a0': r7_a0[l],
            'r7_a2': r7_a2[l], 'r7_g2': r7_g2[l], 'r7_kk': r7_kk[l], 'r7_ka': r7_ka[l],
            'r7_rk': r7_rk[l], 'r7_lnw': r7_lnw[l], 'r7_lnb': r7_lnb[l],
        }
        h_lat = rms_norm(x_lat, norm_mix[l]) * (1.0 + sc1) + sh1
        h_ctx = rms_norm(x_ctx, norm_mix[l]) * (1.0 + csc1) + csh1
        y_lat, y_ctx = mixer_block(h_lat, h_ctx, lp, rope, with_ctx)
        x_lat = x_lat + g1 * y_lat
        h2 = rms_norm(x_lat, norm_ffn[l]) * (1.0 + sc2) + sh2
        x_lat = x_lat + g2 * peer_ffn(h2, peer_wq[l], peer_keys[l], peer_u[l], peer_v[l])
        if with_ctx:
            x_ctx = x_ctx + cg1 * y_ctx
            h2c = rms_norm(x_ctx, norm_ffn[l]) * (1.0 + csc2) + csh2
            x_ctx = x_ctx + cg2 * peer_ffn(h2c, peer_wq[l], peer_keys[l], peer_u[l], peer_v[l])
    return x_lat
```

```python
import numpy as np
import ml_dtypes
import concourse.bass as bass
import concourse.mybir as mybir
from concourse.bass_utils import run_bass_kernel_spmd

F32 = mybir.dt.float32
BF16 = mybir.dt.bfloat16
U32 = mybir.dt.uint32
I32 = mybir.dt.int32
AF = mybir.ActivationFunctionType
ALU = mybir.AluOpType
AX = mybir.AxisListType

ENGS = ["pe", "act", "dve", "pool", "sp"]
DT_SIZE = {F32: 4, BF16: 2, U32: 4, I32: 4}

D = 1024
NCTX = 256
NLAT = 2048
NTOK = NCTX + NLAT
NT = NTOK // 128
DEPTH = 2
EPS = 1e-6


class Prog:
    def __init__(self, nc, n_dma_sems=32):
        self.nc = nc
        self.ops = {e: [] for e in ENGS}
        self.cnt = {e: 0 for e in ENGS}
        self.waited = {e: {} for e in ENGS}
        self.res = {}
        self.n_dma_sems = n_dma_sems
        self.dma_use = [0] * n_dma_sems
        self.dma_last = [None] * n_dma_sems
        self.dma_rr = 0
        self.sb_off = 16 * 1024
        self.sb_id = 0
        self.SB_CAP = 216 * 1024

    def sb_mark(self):
        return self.sb_off

    def sb_reset(self, off=0):
        self.sb_off = off

    def sb(self, shape, dtype, name=""):
        nbytes = int(np.prod(shape[1:])) * DT_SIZE[dtype]
        off = (self.sb_off + 63) // 64 * 64
        assert off + nbytes <= self.SB_CAP, f"SBUF overflow {off}+{nbytes} ({name})"
        self.sb_off = off + nbytes
        self.sb_id += 1
        return self.nc.alloc_sbuf_tensor_at(f"sb{self.sb_id}_{name}", list(shape), dtype, offset=off)

    def _deps(self, r, w):
        deps = []
        for k in r:
            st = self.res.get(k)
            if st and st[0] is not None:
                deps.append(st[0])
        for k in w:
            st = self.res.get(k)
            if st:
                if st[0] is not None:
                    deps.append(st[0])
                deps.extend(st[1])
        return deps

    def _commit(self, tok, r, w):
        for k in r:
            st = self.res.setdefault(k, [None, []])
            st[1].append(tok)
        for k in w:
            self.res[k] = [tok, []]

    def _waits_for(self, eng, deps):
        wd = self.waited[eng]
        best = {}
        for t in deps:
            if t[0] == 'c':
                if t[1] == eng and eng == 'pe':
                    continue
                key = ('c', t[1])
            else:
                key = ('d', t[1])
            if wd.get(key, 0) >= t[2]:
                continue
            best[key] = max(best.get(key, 0), t[2])
        for k, v in best.items():
            wd[k] = v
        return list(best.items())

    def op(self, eng, fn, r=(), w=()):
        deps = self._deps(r, w)
        waits = self._waits_for(eng, deps)
        self.cnt[eng] += 1
        tok = ('c', eng, self.cnt[eng])
        self.ops[eng].append((waits, fn, ('c', eng), 1))
        self._commit(tok, r, w)
        return tok

    def dma(self, fn, r=(), w=(), eng="sp"):
        deps = list(self._deps(r, w))
        i = self.dma_rr
        self.dma_rr = (self.dma_rr + 1) % self.n_dma_sems
        if self.dma_last[i] is not None:
            deps.append(self.dma_last[i])
        waits = self._waits_for(eng, deps)
        self.dma_use[i] += 1
        tok = ('d', i, 16 * self.dma_use[i])
        self.dma_last[i] = tok
        self.ops[eng].append((waits, fn, ('d', i), 16))
        self._commit(tok, r, w)
        return tok

    def barrier(self):
        toks = [('c', e, self.cnt[e]) for e in ENGS if self.cnt[e] > 0]
        toks += [t for t in self.dma_last if t is not None]
        for e in ENGS:
            waits = self._waits_for(e, toks)
            if waits:
                self.ops[e].append((waits, None, None, 0))
        self.res = {}

    def emit(self):
        nc = self.nc
        from contextlib import ExitStack
        with ExitStack() as es:
            csem = {e: es.enter_context(nc.semaphore(f"c_{e}")) for e in ENGS}
            dsem = [es.enter_context(nc.semaphore(f"d_{i}")) for i in range(self.n_dma_sems)]
            block = es.enter_context(nc.Block())

            def sem_of(key):
                return csem[key[1]] if key[0] == 'c' else dsem[key[1]]

            def run(engname, e):
                for waits, fn, inc_key, inc in self.ops[engname]:
                    for k, v in waits:
                        e.wait_ge(sem_of(k), v)
                    if fn is None:
                        continue
                    ins = fn(e)
                    ins.then_inc(sem_of(inc_key), inc)

            @block.tensor
            def _(e):
                run("pe", e)

            @block.scalar
            def _(e):
                run("act", e)

            @block.vector
            def _(e):
                run("dve", e)

            @block.gpsimd
            def _(e):
                run("pool", e)

            @block.sync
            def _(e):
                run("sp", e)


def na_tables():
    cases = {}
    tabs = []
    keys = {}
    ar = np.arange(128)
    for p in range(16):
        for kb in range(16):
            krow = 2 * kb + ar // 64
            kcol = ar % 64
            qrow = 2 * p + ar // 64
            qcol = ar % 64
            rs = np.clip(qrow - 4, 0, 24)
            vr = (krow[:, None] >= rs[None, :]) & (krow[:, None] < rs[None, :] + 8)
            ws = np.clip(qcol - 8, 0, 48)
            vc = (kcol[:, None] >= ws[None, :]) & (kcol[:, None] < ws[None, :] + 16)
            valid = vr & vc
            if not valid.any():
                continue
            dr = krow[:, None] - qrow[None, :] + 7
            dc = np.clip(kcol[:, None] - qcol[None, :] + 15, 0, 30)
            dr = np.where(valid, dr, 0)
            dc = np.where(valid, dc, 0)
            key = (dr.tobytes(), dc.tobytes(), valid.tobytes())
            if key not in keys:
                keys[key] = len(tabs)
                tabs.append((dr, dc, valid))
            cases[(p, kb)] = keys[key]
    return cases, tabs


NA_CASES, NA_TABS = na_tables()
NTAB = len(NA_TABS)


def rope_tables():
    t = np.arange(NLAT)
    inv_freq = 10000.0 ** (-np.arange(0, 32, 2) / 32)
    ang = np.stack([(t // 64)[:, None] * inv_freq[None], (t % 64)[:, None] * inv_freq[None]], axis=1)
    return np.cos(ang).astype(np.float32).reshape(NLAT, 32), np.sin(ang).astype(np.float32).reshape(NLAT, 32)


def build(cfg=None):
    cfg = cfg or {}
    dbg = cfg.get("dbg", [])
    nc = bass.Bass("TRN2", target_bir_lowering=False)
    p = Prog(nc)

    def din(name, shape, dt=F32):
        return nc.dram_tensor(name, list(shape), dt, kind="ExternalInput").ap()

    def dscr(name, shape, dt=F32):
        kind = "Internal"
        if name in cfg.get("dump", []):
            kind = "ExternalOutput"
        if name in cfg.get("feed", []):
            kind = "ExternalInput"
        return nc.dram_tensor(name, list(shape), dt, kind=kind).ap()

    I = {}
    I['x'] = din('x', [NTOK, D])
    I['cc'] = din('cc', [128, 8, 2])
    I['norm_mix'] = din('norm_mix', [DEPTH, D])
    I['norm_ffn'] = din('norm_ffn', [DEPTH, D])
    I['w_mod'] = din('w_mod', [DEPTH, D, 6 * D])
    I['b_mod'] = din('b_mod', [DEPTH, 6 * D])
    I['w_in'] = din('w_in', [DEPTH, D, 3200])
    I['w_out'] = din('w_out', [DEPTH, D, D])
    for n in ['a_qnorm', 'a_knorm', 'b_qnorm', 'b_knorm']:
        I[n] = din(n, [DEPTH, 64])
    I['a_sink'] = din('a_sink', [DEPTH, 4])
    I['btab'] = din('btab', [DEPTH, 128, NTAB, 4, 128])
    I['bmask'] = din('bmask', [128, NTAB, 128])
    I['amask'] = din('amask', [128, 2, 128])
    I['ident'] = din('ident', [128, 128])
    I['cos'] = din('cos', [NLAT, 32])
    I['sin'] = din('sin', [NLAT, 32])
    I['r7_conv'] = din('r7_conv', [DEPTH, 128, 15, 3])
    I['r7_w0'] = din('r7_w0', [DEPTH, 2, 512])
    I['r7_a0'] = din('r7_a0', [DEPTH, 2, 512])
    I['r7_w2'] = din('r7_w2', [DEPTH, 2, 64, 512])
    I['r7_a2'] = din('r7_a2', [DEPTH, 2, 64, 512])
    I['r7_g2'] = din('r7_g2', [DEPTH, 128, 512])
    for n in ['r7_kk', 'r7_ka', 'r7_lnw', 'r7_lnb']:
        I[n] = din(n, [DEPTH, 512])
    I['r7_rk'] = din('r7_rk', [DEPTH, 8, 64])
    I['peer_wq'] = din('peer_wq', [DEPTH, D, 2048])
    I['peer_keys'] = din('peer_keys', [DEPTH, 8, 2, 128, 128])
    I['peer_u'] = [din(f'peer_u{l}', [16384, D]) for l in range(DEPTH)]
    I['peer_v'] = [din(f'peer_v{l}', [16384, D]) for l in range(DEPTH)]
    I['iota'] = din('iota', [128, 256])
    I['tri'] = din('tri', [64, 2, 64])
    I['mg'] = din('mg', [64, 2, 128])
    I['mn'] = din('mn', [64, 2, 64])
    out_d = nc.dram_tensor('out', [NLAT, D], F32, kind="ExternalOutput").ap()

    S = {}
    S['mod'] = dscr('s_mod', [DEPTH, 2, 6 * D])
    S['xs'] = dscr('s_xs', [NTOK, D])
    S['o'] = dscr('s_o', [NTOK, D])
    S['pcT'] = dscr('s_pcT', [1920, NTOK])
    S['tm'] = dscr('s_tm', [NTOK, 10, 512])
    S['bon'] = dscr('s_bon', [NTOK, 8])
    S['y'] = dscr('s_y', [2, NTOK, 512])
    S['h2'] = dscr('s_h2', [NTOK, D])
    S['T'] = [dscr(f's_T{l}', [16384, 2 * D], BF16) for l in range(DEPTH)]
    DBG = {}
    for name, shape in cfg.get("dbg_out", {}).items():
        DBG[name] = nc.dram_tensor(name, list(shape), F32, kind="ExternalOutput").ap()

    ps = [nc.alloc_psum_tensor(f"ps{i}", [128, 512], F32) for i in range(8)]

    def PS(i):
        return ('ps', i)

    ident_f = p.sb([128, 128], F32, "identf")
    ident_b = p.sb([128, 128], BF16, "identb")
    eps_col = p.sb([128, 1], F32, "eps")
    p.dma(lambda e: e.dma_start(out=ident_f[:], in_=I['ident']), w=['identf'])
    p.op('dve', lambda e: e.tensor_copy(out=ident_b[:], in_=ident_f[:]), r=['identf'], w=['identb'])
    p.op('dve', lambda e: e.memset(eps_col[:], EPS), w=['eps'])
    p.barrier()
    base_mark = p.sb_mark()

    def phase_mod():
        p.sb_reset(base_mark)
        cc = p.sb([128, 8, 2], F32, "cc")
        scc = p.sb([128, 8, 2], F32, "scc")
        p.dma(lambda e: e.dma_start(out=cc[:], in_=I['cc']), w=['cc'])
        p.op('act', lambda e: e.activation(out=scc[:], in_=cc[:], func=AF.Silu), r=['cc'], w=['scc'])
        wt = [p.sb([128, 8, 512], F32, f"wmod{i}") for i in range(2)]
        bm = p.sb([2, 6 * D], F32, "bm")
        mo = p.sb([2, 6 * D], F32, "mo")
        k = 0
        for l in range(DEPTH):
            p.dma(lambda e, l=l: e.dma_start(out=bm[:], in_=I['b_mod'][l].partition_broadcast(2)),
                  w=['bm'])
            for cch in range(12):
                b = k % 2
                k += 1
                src = I['w_mod'][l, :, cch * 512:(cch + 1) * 512].rearrange("(j p) n -> p j n", p=128)
                p.dma(lambda e, b=b, src=src: e.dma_start(out=wt[b][:], in_=src), w=[('wmod', b)])
                pb = cch % 2
                for j in range(8):
                    p.op('pe', lambda e, b=b, j=j, pb=pb: e.matmul(ps[pb][0:2, :], scc[:, j, :], wt[b][:, j, :],
                                                                    start=(j == 0), stop=(j == 7)),
                         r=['scc', ('wmod', b)], w=[PS(pb)])
                p.op('dve', lambda e, pb=pb, cch=cch: e.tensor_tensor(
                    out=mo[:, cch * 512:(cch + 1) * 512], in0=ps[pb][0:2, :], in1=bm[:, cch * 512:(cch + 1) * 512],
                    op=ALU.add), r=[PS(pb), 'bm'], w=['mo'])
            p.dma(lambda e, l=l: e.dma_start(out=S['mod'][l], in_=mo[:]), r=['mo'], w=['S_mod'])
        p.barrier()

    def load_bc(dst, src_1d, key):
        P = dst.shape[0]
        p.dma(lambda e: e.dma_start(out=dst, in_=src_1d.partition_broadcast(P)), w=[key])

    def norm_tiles(l, which, src, hT, hT_off, tm_dram=None):
        nv = I['norm_mix'] if which == 0 else I['norm_ffn']
        so = 0 if which == 0 else 3
        G = [p.sb([128, D], F32, f"G{s}") for s in range(2)]
        SH = [p.sb([128, D], F32, f"SH{s}") for s in range(2)]
        tmp = p.sb([128, D], F32, "gtmp")
        for s in range(2):
            load_bc(tmp[:], nv[l], 'gtmp')
            load_bc(G[s][:], S['mod'][l, s, (so + 1) * D:(so + 2) * D], ('G', s))
            load_bc(SH[s][:], S['mod'][l, s, so * D:(so + 1) * D], ('SH', s))
            p.op('dve', lambda e, s=s: e.scalar_tensor_tensor(out=G[s][:], in0=G[s][:], scalar=1.0, in1=tmp[:],
                                                             op0=ALU.add, op1=ALU.mult),
                 r=['gtmp', ('G', s)], w=[('G', s)])
        NBUF = 4
        xt = [p.sb([128, D], F32, f"xt{i}") for i in range(NBUF)]
        junk = p.sb([128, D], F32, "junk")
        hb = [p.sb([128, D], BF16, f"hb{i}") for i in range(NBUF)]
        ss = [p.sb([128, 1], F32, f"ss{i}") for i in range(NBUF)]
        for t in range(NT):
            if t < 2 and l == DEPTH - 1 and which == 1:
                continue
            b = t % NBUF
            s = 1 if t < 2 else 0
            p.dma(lambda e, b=b, t=t: e.dma_start(out=xt[b][:], in_=src[t * 128:(t + 1) * 128, :]), w=[('xt', b)])
            p.op('act', lambda e, b=b: e.activation(out=junk[:], in_=xt[b][:], func=AF.Square, accum_out=ss[b][:]),
                 r=[('xt', b)], w=[('ss', b)])
            p.op('act', lambda e, b=b: e.activation(out=ss[b][:], in_=ss[b][:], func=AF.Sqrt, bias=eps_col[:],
                                                    scale=1.0 / D), r=[('ss', b)], w=[('ss', b)])
            p.op('dve', lambda e, b=b: e.reciprocal(out=ss[b][:], in_=ss[b][:]), r=[('ss', b)], w=[('ss', b)])
            p.op('dve', lambda e, b=b, s=s: e.scalar_tensor_tensor(out=xt[b][:], in0=xt[b][:], scalar=ss[b][:, 0:1],
                                                                 in1=G[s][:], op0=ALU.mult, op1=ALU.mult),
                 r=[('xt', b), ('ss', b), ('G', s)], w=[('xt', b)])
            if tm_dram is not None:
                p.op('dve', lambda e, b=b, s=s: e.tensor_tensor(out=xt[b][:], in0=xt[b][:], in1=SH[s][:], op=ALU.add),
                     r=[('xt', b), ('SH', s)], w=[('xt', b)])
                p.dma(lambda e, b=b, t=t: e.dma_start(out=tm_dram[t * 128:(t + 1) * 128, :], in_=xt[b][:]),
                      r=[('xt', b)], w=[('tmd', t)])
                p.op('act', lambda e, b=b: e.activation(out=hb[b][:], in_=xt[b][:], func=AF.Copy),
                     r=[('xt', b)], w=[('hb', b)])
            else:
                p.op('dve', lambda e, b=b, s=s: e.tensor_tensor(out=hb[b][:], in0=xt[b][:], in1=SH[s][:], op=ALU.add),
                     r=[('xt', b), ('SH', s)], w=[('hb', b)])
            pbank = 4 + b
            pv = ps[pbank][:, 0:512].bitcast(BF16)
            for j in range(8):
                p.op('pe', lambda e, b=b, j=j, pv=pv: e.transpose(out=pv[:, j * 128:(j + 1) * 128],
                                                                 in_=hb[b][:, j * 128:(j + 1) * 128],
                                                                 identity=ident_b[:]),
                     r=[('hb', b), 'identb'], w=[PS(pbank)])
            o = hT_off(t)
            p.op('act', lambda e, pv=pv, o=o: e.activation(
                out=hT[:, :, o:o + 128], in_=pv.rearrange("p (j t) -> p j t", j=8), func=AF.Copy),
                r=[PS(pbank)], w=[('hT', t)])

    def phase_proj(l):
        p.sb_reset(base_mark)
        qkT = p.sb([64, 14, NTOK], BF16, "qkT")
        Vaug = p.sb([128, NT, 6, 65], BF16, "Vaug")
        mark_persist = p.sb_mark()
        hT = p.sb([128, 8, NTOK], BF16, "hT")
        m_afterh = p.sb_mark()
        wAB = p.sb([128, 8, 1280], BF16, "wAB")
        for j in range(8):
            p.dma(lambda e, j=j: e.dma_start(out=wAB[:, j, :], in_=I['w_in'][l, j * 128:(j + 1) * 128, 0:1280]),
                  w=[('wAB', j)], eng="pool")
        p.op('pool', lambda e: e.memset(Vaug[:, :, :, 64:65], 1.0), w=['Vones'])
        m0 = p.sb_mark()
        norm_tiles(l, 0, I['x'] if l == 0 else S['xs'], hT, lambda t: t * 128)
        p.barrier()
        p.sb_reset(m0)
        wC = p.sb([128, 8, 1920], BF16, "wC")
        for j in range(8):
            p.dma(lambda e, j=j: e.dma_start(out=wC[:, j, :], in_=I['w_in'][l, j * 128:(j + 1) * 128, 1280:3200]),
                  w=[('wC', j)], eng="pool")
        m_afterwc = p.sb_mark()
        GA = p.sb([128, 6, 64], F32, "GA")
        GB = p.sb([128, 8, 64], F32, "GB")
        for h in range(6):
            load_bc(GA[:, h, :], I['a_qnorm'][l] if h < 4 else I['a_knorm'][l], 'GA')
        for h in range(8):
            load_bc(GB[:, h, :], I['b_qnorm'][l] if h < 4 else I['b_knorm'][l], 'GB')
        p.op('act', lambda e: e.mul(out=GA[:, 0:4, :], in_=GA[:, 0:4, :], mul=0.125), r=['GA'], w=['GA'])
        p.op('act', lambda e: e.mul(out=GB[:, 0:4, :], in_=GB[:, 0:4, :], mul=0.125), r=['GB'], w=['GB'])
        cs = [p.sb([128, 2, 32], F32, f"cs{i}") for i in range(2)]
        xn = [p.sb([128, 14, 64], F32, f"xn{i}") for i in range(2)]
        sq = p.sb([128, 14, 64], F32, "sq")
        ssq = [p.sb([128, 14], F32, f"ssq{i}") for i in range(2)]
        xr = [p.sb([128, 14, 64], BF16, f"xr{i}") for i in range(2)]
        RT = [p.sb([128, 6, 2, 16], F32, f"ropeT{i}") for i in range(4)]
        for t in range(NT):
            b = t % 2
            lat = t >= 2
            bA, bB, bV = (0, 1, 2) if t % 2 == 0 else (3, 6, 7)
            for bank, c0, c1 in ((bA, 0, 512), (bB, 512, 1024), (bV, 1024, 1280)):
                for j in range(8):
                    p.op('pe', lambda e, bank=bank, c0=c0, c1=c1, j=j, t=t: e.matmul(
                        ps[bank][:, 0:c1 - c0], hT[:, j, t * 128:(t + 1) * 128], wAB[:, j, c0:c1],
                        start=(j == 0), stop=(j == 7)),
                        r=[('hT', t), ('wAB', j)], w=[PS(bank)])
            if lat:
                tl = t - 2
                p.dma(lambda e, b=b, tl=tl: e.dma_start(out=cs[b][:, 0, :], in_=I['cos'][tl * 128:(tl + 1) * 128, :]),
                      w=[('cs', b)])
                p.dma(lambda e, b=b, tl=tl: e.dma_start(out=cs[b][:, 1, :], in_=I['sin'][tl * 128:(tl + 1) * 128, :]),
                      w=[('cs', b)])
            p.op('act', lambda e, t=t, bA=bA: e.activation(out=Vaug[:, t, 0:2, 0:64],
                                                    in_=ps[bA][:, 384:512].rearrange("p (h d) -> p h d", h=2),
                                                    func=AF.Copy), r=[PS(bA)], w=[('V', t)])
            p.op('act', lambda e, t=t, bV=bV: e.activation(out=Vaug[:, t, 2:6, 0:64],
                                                    in_=ps[bV][:, 0:256].rearrange("p (h d) -> p h d", h=4),
                                                    func=AF.Copy), r=[PS(bV)], w=[('V', t)])
            p.op('act', lambda e, b=b, bA=bA: e.activation(out=xn[b][:, 0:6, :],
                                                    in_=ps[bA][:, 0:384].rearrange("p (h d) -> p h d", h=6),
                                                    func=AF.Copy), r=[PS(bA)], w=[('xn', b)])
            p.op('act', lambda e, b=b, bB=bB: e.activation(out=xn[b][:, 6:14, :],
                                                    in_=ps[bB][:, 0:512].rearrange("p (h d) -> p h d", h=8),
                                                    func=AF.Copy), r=[PS(bB)], w=[('xn', b)])
            p.op('dve', lambda e, b=b: e.tensor_tensor(out=sq[:], in0=xn[b][:], in1=xn[b][:], op=ALU.mult),
                 r=[('xn', b)], w=['sq'])
            p.op('dve', lambda e, b=b: e.tensor_reduce(out=ssq[b][:], in_=sq[:], axis=AX.X, op=ALU.add),
                 r=['sq'], w=[('ssq', b)])
            p.op('act', lambda e, b=b: e.activation(out=ssq[b][:], in_=ssq[b][:], func=AF.Sqrt, bias=eps_col[:],
                                                    scale=1.0 / 64), r=[('ssq', b)], w=[('ssq', b)])
            p.op('dve', lambda e, b=b: e.reciprocal(out=ssq[b][:], in_=ssq[b][:]), r=[('ssq', b)], w=[('ssq', b)])
            p.op('dve', lambda e, b=b: e.tensor_tensor(out=xn[b][:], in0=xn[b][:],
                                                       in1=ssq[b][:].unsqueeze(2).to_broadcast([128, 14, 64]),
                                                       op=ALU.mult), r=[('xn', b), ('ssq', b)], w=[('xn', b)])
            p.op('dve', lambda e, b=b: e.tensor_tensor(out=xr[b][:, 6:14, :], in0=xn[b][:, 6:14, :], in1=GB[:],
                                                       op=ALU.mult), r=[('xn', b), 'GB'], w=[('xr', b)])
            if lat:
                p.op('dve', lambda e, b=b: e.tensor_tensor(out=xn[b][:, 0:6, :], in0=xn[b][:, 0:6, :], in1=GA[:],
                                                           op=ALU.mult), r=[('xn', b), 'GA'], w=[('xn', b)])
                xv = xn[b][:, 0:6, :].rearrange("p h (a g f) -> p h a g f", a=2, g=2)
                x1 = xv[:, :, :, 0, :]
                x2 = xv[:, :, :, 1, :]
                ov = xr[b][:, 0:6, :].rearrange("p h (a g f) -> p h a g f", a=2, g=2)
                cosb = cs[b][:, 0, :].rearrange("p (a f) -> p a f", a=2).unsqueeze(1).to_broadcast([128, 6, 2, 16])
                sinb = cs[b][:, 1, :].rearrange("p (a f) -> p a f", a=2).unsqueeze(1).to_broadcast([128, 6, 2, 16])
                rk = [('xn', b), ('cs', b)]
                for i, (xa, tb) in enumerate(((x1, cosb), (x2, sinb), (x2, cosb), (x1, sinb))):
                    p.op('dve', lambda e, i=i, xa=xa, tb=tb: e.tensor_tensor(out=RT[i][:], in0=xa, in1=tb, op=ALU.mult),
                         r=rk, w=[('RT', i)])
                p.op('dve', lambda e, ov=ov: e.tensor_tensor(out=ov[:, :, :, 0, :], in0=RT[0][:], in1=RT[1][:],
                                                             op=ALU.subtract), r=[('RT', 0), ('RT', 1)], w=[('xr', b)])
                p.op('dve', lambda e, ov=ov: e.tensor_tensor(out=ov[:, :, :, 1, :], in0=RT[2][:], in1=RT[3][:],
                                                             op=ALU.add), r=[('RT', 2), ('RT', 3)], w=[('xr', b)])
            else:
                p.op('dve', lambda e, b=b: e.tensor_tensor(out=xr[b][:, 0:6, :], in0=xn[b][:, 0:6, :], in1=GA[:],
                                                           op=ALU.mult), r=[('xn', b), 'GA'], w=[('xr', b)])
            for half in range(2):
                bank = 4 + half
                pv = ps[bank][0:64, 0:448].bitcast(BF16)
                for hh in range(7):
                    h = half * 7 + hh
                    p.op('pe', lambda e, b=b, h=h, hh=hh, pv=pv: e.transpose(
                        out=pv[:, hh * 128:(hh + 1) * 128], in_=xr[b][:, h, :], identity=ident_b[:]),
                        r=[('xr', b), 'identb'], w=[PS(bank)])
                p.op('act', lambda e, half=half, pv=pv, t=t: e.activation(
                    out=qkT[:, half * 7:(half + 1) * 7, t * 128:(t + 1) * 128],
                    in_=pv.rearrange("p (h t) -> p h t", h=7), func=AF.Copy), r=[PS(bank)], w=[('qkT', t)])
        p.barrier()
        p.sb_reset(m_afterwc)
        cw = p.sb([128, 15, 3], F32, "cw")
        p.dma(lambda e: e.dma_start(out=cw[:], in_=I['r7_conv'][l]), w=['cw'])
        rawc = [p.sb([128, NCTX + 2], F32, f"rawc{i}") for i in range(2)]
        rawl = [p.sb([128, NLAT + 2], F32, f"rawl{i}") for i in range(2)]
        cvo = [p.sb([128, NTOK], F32, f"cvo{i}") for i in range(2)]
        for i in range(2):
            p.op('pool', lambda e, i=i: e.memset(rawc[i][:], 0.0), w=[('rawc', i)])
            p.op('pool', lambda e, i=i: e.memset(rawl[i][:], 0.0), w=[('rawl', i)])
        bk = 0
        for ch in range(15):
            b = ch % 2
            groups = [(rawc[b], ('rawc', b), 1, 0, 256)] + [(rawl[b], ('rawl', b), 1 + 512 * g, 256 + 512 * g, 512)
                                                             for g in range(4)]
            for (raw, rkey, ro, tok0, n) in groups:
                bank = bk % 4
                bk += 1
                for j in range(8):
                    p.op('pe', lambda e, bank=bank, j=j, ch=ch, tok0=tok0, n=n: e.matmul(
                        ps[bank][:, 0:n], wC[:, j, ch * 128:(ch + 1) * 128], hT[:, j, tok0:tok0 + n],
                        start=(j == 0), stop=(j == 7)), r=[('wC', j)], w=[PS(bank)])
                p.op('act', lambda e, raw=raw, ro=ro, n=n, bank=bank: e.activation(
                    out=raw[:, ro:ro + n], in_=ps[bank][:, 0:n], func=AF.Copy), r=[PS(bank)], w=[rkey])
            for (raw, rkey, n, o0) in ((rawc[b], ('rawc', b), NCTX, 0), (rawl[b], ('rawl', b), NLAT, NCTX)):
                dst = cvo[b][:, o0:o0 + n]
                p.op('dve', lambda e, raw=raw, n=n, dst=dst, ch=ch: e.tensor_scalar(
                    out=dst, in0=raw[:, 1:1 + n], scalar1=cw[:, ch, 1:2], scalar2=None, op0=ALU.mult),
                    r=[rkey, 'cw'], w=[('cvo', b)])
                p.op('dve', lambda e, raw=raw, n=n, dst=dst, ch=ch: e.scalar_tensor_tensor(
                    out=dst, in0=raw[:, 0:n], scalar=cw[:, ch, 0:1], in1=dst, op0=ALU.mult, op1=ALU.add),
                    r=[rkey, 'cw'], w=[('cvo', b)])
                p.op('dve', lambda e, raw=raw, n=n, dst=dst, ch=ch: e.scalar_tensor_tensor(
                    out=dst, in0=raw[:, 2:2 + n], scalar=cw[:, ch, 2:3], in1=dst, op0=ALU.mult, op1=ALU.add),
                    r=[rkey, 'cw'], w=[('cvo', b)])
            if ch == 12:
                p.op('act', lambda e, b=b: e.activation(out=cvo[b][:], in_=cvo[b][:], func=AF.Tanh),
                     r=[('cvo', b)], w=[('cvo', b)])
            if ch == 14:
                p.op('act', lambda e, b=b: e.activation(out=cvo[b][:], in_=cvo[b][:], func=AF.Sigmoid),
                     r=[('cvo', b)], w=[('cvo', b)])
            p.dma(lambda e, b=b, ch=ch: e.dma_start(out=S['pcT'][ch * 128:(ch + 1) * 128, :], in_=cvo[b][:]),
                  r=[('cvo', b)], w=[('pcT', ch)])
        p.barrier()
        return qkT, Vaug, mark_persist

    def phase_attn(l, qkT, Vaug, mark_persist):
        with_ctx = l < DEPTH - 1
        p.sb_reset(mark_persist)
        btab = p.sb([128, NTAB, 4, 128], F32, "btab")
        bmask = p.sb([128, NTAB, 128], F32, "bmask")
        amask = p.sb([128, 2, 128], F32, "amask")
        esink = p.sb([128, 4], F32, "esink")
        o_all = [p.sb([128, 512], F32, f"oall{i}") for i in range(2)]
        ex = [p.sb([128, 8, 128], F32, f"ex{i}") for i in range(2)]
        pT = [p.sb([128, 8, 128], BF16, f"pT{i}") for i in range(2)]
        den = [p.sb([128, 1], F32, f"den{i}") for i in range(2)]
        for tb in range(NTAB):
            p.dma(lambda e, tb=tb: e.dma_start(out=btab[:, tb], in_=I['btab'][l, :, tb]), w=['btab'])
        p.dma(lambda e: e.dma_start(out=bmask[:], in_=I['bmask']), w=['bmask'])
        p.dma(lambda e: e.dma_start(out=amask[:], in_=I['amask']), w=['amask'])
        load_bc(esink[:], I['a_sink'][l], 'esink')
        p.op('act', lambda e: e.activation(out=esink[:], in_=esink[:], func=AF.Exp), r=['esink'], w=['esink'])
        p.op('act', lambda e: e.activation(out=btab[:], in_=btab[:], func=AF.Exp), r=['btab'], w=['btab'])
        for h in range(4):
            p.op('dve', lambda e, h=h: e.tensor_tensor(out=btab[:, :, h, :], in0=btab[:, :, h, :], in1=bmask[:],
                                                       op=ALU.mult), r=['btab', 'bmask'], w=['btab'])
        it = 0
        for t in range(NT):
            if t < 2 and not with_ctx:
                continue
            ob = t % 2
            for grp in range(2):
                for h in range(4):
                    b = it % 2
                    it += 1
                    if grp == 0:
                        qs, ks, vs = h, 4 + h // 2, h // 2
                    else:
                        qs, ks, vs = 6 + h, 10 + h, 2 + h
                    if t < 2:
                        blocks = [(0, None), (1, None)]
                    elif grp == 0:
                        n = t - 2
                        blocks = [(t, None), (0, None), (1, None)]
                        if n > 0:
                            blocks.append((t - 1, amask[:, 0, :]))
                        if n < 15:
                            blocks.append((t + 1, amask[:, 1, :]))
                    else:
                        pq = t - 2
                        blocks = [(0, None), (1, None)]
                        for kb in range(16):
                            if (pq, kb) in NA_CASES:
                                blocks.append((kb + 2, btab[:, NA_CASES[(pq, kb)], h, :]))
                    nb = len(blocks)
                    nn = sum(1 for _, tb in blocks if tb is None)
                    sb0, sb1 = (0, 1) if b == 0 else (2, 3)
                    ob_ps = 4 + b
                    for i, (kt, tb) in enumerate(blocks):
                        bank = sb0 if i < 4 else sb1
                        p.op('pe', lambda e, bank=bank, i=i, kt=kt, ks=ks, qs=qs, t=t: e.matmul(
                            ps[bank][:, (i % 4) * 128:(i % 4 + 1) * 128], qkT[:, ks, kt * 128:(kt + 1) * 128],
                            qkT[:, qs, t * 128:(t + 1) * 128], start=True, stop=True), w=[PS(bank)])
                    n0 = min(nb, 4)
                    p.op('act', lambda e, b=b, n0=n0, sb0=sb0: e.activation(
                        out=ex[b][:, 0:n0, :], in_=ps[sb0][:, 0:n0 * 128].rearrange("p (n k) -> p n k", n=n0),
                        func=AF.Exp), r=[PS(sb0)], w=[('ex', b)])
                    if nb > 4:
                        n1 = nb - 4
                        p.op('act', lambda e, b=b, n1=n1, sb1=sb1: e.activation(
                            out=ex[b][:, 4:4 + n1, :], in_=ps[sb1][:, 0:n1 * 128].rearrange("p (n k) -> p n k", n=n1),
                            func=AF.Exp), r=[PS(sb1)], w=[('ex', b)])
                    p.op('pool', lambda e, b=b, nn=nn: e.tensor_copy(out=pT[b][:, 0:nn, :], in_=ex[b][:, 0:nn, :]),
                         r=[('ex', b)], w=[('pT', b)])
                    for i, (kt, tb) in enumerate(blocks):
                        if tb is None:
                            continue
                        p.op('dve', lambda e, b=b, i=i, tb=tb: e.tensor_tensor(out=pT[b][:, i, :], in0=ex[b][:, i, :],
                                                                              in1=tb, op=ALU.mult),
                             r=[('ex', b), 'btab', 'amask'], w=[('pT', b)])
                    for i, (kt, tb) in enumerate(blocks):
                        p.op('pe', lambda e, b=b, i=i, kt=kt, vs=vs, ob_ps=ob_ps, nb=nb: e.matmul(
                            ps[ob_ps][:, 0:65], pT[b][:, i, :], Vaug[:, kt, vs, :], start=(i == 0), stop=(i == nb - 1)),
                            r=[('pT', b)], w=[PS(ob_ps)])
                    if grp == 0:
                        p.op('dve', lambda e, b=b, h=h, ob_ps=ob_ps: e.tensor_scalar(
                            out=den[b][:], in0=ps[ob_ps][:, 64:65], scalar1=esink[:, h:h + 1], scalar2=None,
                            op0=ALU.add), r=[PS(ob_ps), 'esink'], w=[('den', b)])
                        p.op('dve', lambda e, b=b: e.reciprocal(out=den[b][:], in_=den[b][:]),
                             r=[('den', b)], w=[('den', b)])
                    else:
                        p.op('dve', lambda e, b=b, ob_ps=ob_ps: e.reciprocal(out=den[b][:], in_=ps[ob_ps][:, 64:65]),
                             r=[PS(ob_ps)], w=[('den', b)])
                    col = grp * 256 + h * 64
                    p.op('dve', lambda e, b=b, ob=ob, col=col, ob_ps=ob_ps: e.tensor_scalar(
                        out=o_all[ob][:, col:col + 64], in0=ps[ob_ps][:, 0:64], scalar1=den[b][:, 0:1], scalar2=None,
                        op0=ALU.mult), r=[PS(ob_ps), ('den', b)], w=[('oall', ob)])
            p.dma(lambda e, ob=ob, t=t: e.dma_start(out=S['o'][t * 128:(t + 1) * 128, 0:512], in_=o_all[ob][:]),
                  r=[('oall', ob)], w=[('So', t)])
        p.barrier()


    def phase_rprep(l):
        p.sb_reset(base_mark)
        w2 = p.sb([128, 512], F32, "w2")
        a2 = p.sb([128, 512], F32, "a2")
        g2 = p.sb([128, 512], F32, "g2")
        w0 = p.sb([1, 2, 512], F32, "w0")
        a0 = p.sb([1, 2, 512], F32, "a0")
        ones = p.sb([1, 128], F32, "ones")
        KKW = p.sb([128, 512], F32, "KKW")
        KA = p.sb([128, 512], F32, "KA")
        RK = p.sb([128, 512], F32, "RK")
        p.dma(lambda e: e.dma_start(out=w2[:], in_=I['r7_w2'][l].rearrange("d r c -> (d r) c")), w=['w2'])
        p.dma(lambda e: e.dma_start(out=a2[:], in_=I['r7_a2'][l].rearrange("d r c -> (d r) c")), w=['a2'])
        p.dma(lambda e: e.dma_start(out=g2[:], in_=I['r7_g2'][l]), w=['g2'])
        p.dma(lambda e: e.dma_start(out=w0[:], in_=I['r7_w0'][l:l + 1]), w=['w0'])
        p.dma(lambda e: e.dma_start(out=a0[:], in_=I['r7_a0'][l:l + 1]), w=['a0'])
        p.op('dve', lambda e: e.memset(ones[:], 1.0), w=['ones'])
        load_bc(KKW[:], I['r7_kk'][l], 'KKW')
        load_bc(KA[:], I['r7_ka'][l], 'KA')
        load_bc(RK[:], I['r7_rk'][l].rearrange("h d -> (h d)"), 'RK')
        fm = [p.sb([128, 15, 128], F32, f"fm{i}") for i in range(2)]
        TM = [p.sb([128, 10, 512], F32, f"TM{i}") for i in range(2)]
        kt = p.sb([128, 512], F32, "kt")
        av = [p.sb([128, 512], F32, f"av{i}") for i in range(2)]
        tmp = p.sb([128, 512], F32, "tmp")
        tmp2 = p.sb([128, 512], F32, "tmp2")
        s8 = p.sb([128, 8], F32, "s8")
        bs = [p.sb([128, 8], F32, f"bs{i}") for i in range(2)]
        for t in range(NT):
            b = t % 2
            p.dma(lambda e, b=b, t=t: e.dma_start(
                out=fm[b][:], in_=S['pcT'][:, t * 128:(t + 1) * 128].rearrange("(c p) t -> p c t", p=128)),
                w=[('fm', b)])
            for q in range(3):
                for c4 in range(4):
                    p.op('pe', lambda e, b=b, q=q, c4=c4: e.transpose(
                        out=ps[q][:, c4 * 128:(c4 + 1) * 128], in_=fm[b][:, q * 4 + c4, :], identity=ident_f[:]),
                        r=[('fm', b), 'identf'], w=[PS(q)])
            for d in range(2):
                pr = slice(d * 64, d * 64 + 64)
                p.op('pe', lambda e, b=b, d=d, pr=pr: e.matmul(ps[3 + d][:, :], fm[b][pr, 12, :], w2[pr, :],
                                                              start=True, stop=False), r=[('fm', b), 'w2'], w=[PS(3 + d)])
                p.op('pe', lambda e, d=d: e.matmul(ps[3 + d][:, :], ones[0:1, :], w0[0:1, d, :], start=False, stop=True),
                     r=['ones', 'w0'], w=[PS(3 + d)])
                p.op('pe', lambda e, b=b, d=d, pr=pr: e.matmul(ps[5 + d][:, :], fm[b][pr, 13, :], a2[pr, :],
                                                              start=True, stop=False), r=[('fm', b), 'a2'], w=[PS(5 + d)])
                p.op('pe', lambda e, d=d: e.matmul(ps[5 + d][:, :], ones[0:1, :], a0[0:1, d, :], start=False, stop=True),
                     r=['ones', 'a0'], w=[PS(5 + d)])
            p.op('pe', lambda e, b=b: e.matmul(ps[7][:, :], fm[b][:, 14, :], g2[:], start=True, stop=True),
                 r=[('fm', b), 'g2'], w=[PS(7)])
            T = TM[b]
            wk = [('TM', b)]
            p.op('act', lambda e, T=T: e.activation(out=T[:, 0, :], in_=ps[0][:, :], func=AF.Copy), r=[PS(0)], w=wk)
            p.op('act', lambda e: e.activation(out=kt[:], in_=ps[1][:, :], func=AF.Copy), r=[PS(1)], w=['kt'])
            p.op('act', lambda e, T=T: e.activation(out=T[:, 1, :], in_=ps[2][:, :], func=AF.Copy), r=[PS(2)], w=wk)
            p.op('act', lambda e, T=T: e.activation(out=T[:, 2, :], in_=ps[7][:, :], func=AF.Copy), r=[PS(7)], w=wk)
            for d in range(2):
                p.op('act', lambda e, T=T, d=d: e.activation(out=T[:, 8 + d, :], in_=ps[3 + d][:, :], func=AF.Sigmoid),
                     r=[PS(3 + d)], w=wk)
                p.op('act', lambda e, d=d: e.activation(out=av[d][:], in_=ps[5 + d][:, :], func=AF.Sigmoid),
                     r=[PS(5 + d)], w=[('av', d)])
                p.op('dve', lambda e, T=T, d=d: e.tensor_scalar(out=T[:, 8 + d, :], in0=T[:, 8 + d, :],
                                                                scalar1=-0.6065306597126334, scalar2=None, op0=ALU.mult),
                     r=wk, w=wk)
            p.op('dve', lambda e: e.tensor_tensor(out=tmp[:], in0=kt[:], in1=KKW[:], op=ALU.mult), r=['kt', 'KKW'], w=['tmp'])
            p.op('dve', lambda e: e.tensor_tensor(out=tmp2[:], in0=tmp[:], in1=tmp[:], op=ALU.mult), r=['tmp'], w=['tmp2'])
            p.op('dve', lambda e: e.tensor_reduce(out=s8[:], in_=tmp2[:].rearrange("p (h d) -> p h d", h=8), axis=AX.X,
                                                  op=ALU.add), r=['tmp2'], w=['s8'])
            p.op('act', lambda e: e.activation(out=s8[:], in_=s8[:], func=AF.Sqrt), r=['s8'], w=['s8'])
            p.op('dve', lambda e: e.tensor_scalar(out=s8[:], in0=s8[:], scalar1=1e-12, scalar2=None, op0=ALU.max),
                 r=['s8'], w=['s8'])
            p.op('dve', lambda e: e.reciprocal(out=s8[:], in_=s8[:]), r=['s8'], w=['s8'])
            p.op('dve', lambda e, T=T: e.tensor_tensor(out=T[:, 3, :].rearrange("p (h d) -> p h d", h=8),
                                                       in0=tmp[:].rearrange("p (h d) -> p h d", h=8),
                                                       in1=s8[:].unsqueeze(2).to_broadcast([128, 8, 64]), op=ALU.mult),
                 r=['tmp', 's8'], w=wk)
            for d in range(2):
                p.op('dve', lambda e, d=d: e.scalar_tensor_tensor(out=tmp2[:], in0=av[d][:], scalar=-1.0, in1=KA[:],
                                                                  op0=ALU.add, op1=ALU.mult),
                     r=[('av', d), 'KA'], w=['tmp2'])
                p.op('dve', lambda e, T=T, d=d: e.scalar_tensor_tensor(out=T[:, 4 + d, :], in0=tmp2[:], scalar=1.0,
                                                                       in1=kt[:], op0=ALU.add, op1=ALU.mult),
                     r=['tmp2', 'kt'], w=wk)
                p.op('dve', lambda e, T=T, d=d: e.tensor_tensor(out=T[:, 6 + d, :], in0=T[:, 3, :], in1=av[d][:],
                                                                op=ALU.mult), r=wk + [('av', d)], w=wk)
            p.op('dve', lambda e, T=T: e.tensor_tensor(out=tmp[:], in0=T[:, 4, :], in1=T[:, 5, :], op=ALU.add),
                 r=wk, w=['tmp'])
            p.op('dve', lambda e: e.tensor_tensor(out=tmp[:], in0=tmp[:], in1=RK[:], op=ALU.mult), r=['tmp', 'RK'], w=['tmp'])
            p.op('dve', lambda e, T=T: e.tensor_tensor(out=tmp[:], in0=tmp[:], in1=T[:, 0, :], op=ALU.mult),
                 r=['tmp'] + wk, w=['tmp'])
            p.op('dve', lambda e, b=b: e.tensor_reduce(out=bs[b][:], in_=tmp[:].rearrange("p (h d) -> p h d", h=8),
                                                       axis=AX.X, op=ALU.add), r=['tmp'], w=[('bs', b)])
            p.dma(lambda e, T=T, t=t: e.dma_start(out=S['tm'][t * 128:(t + 1) * 128], in_=T[:]), r=wk, w=[('Stm', t)])
            p.dma(lambda e, b=b, t=t: e.dma_start(out=S['bon'][t * 128:(t + 1) * 128], in_=bs[b][:]),
                  r=[('bs', b)], w=[('Sbon', t)])
        p.barrier()

    def phase_scan(l):
        p.sb_reset(base_mark)
        PSB = ps
        C = 64
        NCH = NTOK // C
        tri = p.sb([64, 2, 64], F32, "tri")
        mg = p.sb([64, 2, 128], F32, "mg")
        mn = p.sb([64, 2, 64], F32, "mn")
        ones = p.sb([64, 1], F32, "ones1")
        p.dma(lambda e: e.dma_start(out=tri[:], in_=I['tri']), w=['tri'])
        p.dma(lambda e: e.dma_start(out=mg[:], in_=I['mg']), w=['mg'])
        p.dma(lambda e: e.dma_start(out=mn[:], in_=I['mn']), w=['mn'])
        p.op('dve', lambda e: e.memset(ones[:], 1.0), w=['ones1'])
        M = [p.sb([64, 8, 64], F32, f"M{d}") for d in range(2)]
        for d in range(2):
            M0_PLACEHOLDER = None
        X = [[p.sb([64, 6, 512], F32, f"X{d}{i}") for i in range(2)] for d in range(2)]
        def mk(shape, name):
            return [p.sb(shape, F32, f"{name}{d}") for d in range(2)]
        E0s, E1s, E2s = mk([64, 512], "E0"), mk([64, 512], "E1"), mk([64, 512], "E2")
        Ats, Rts, Bts, Kts = mk([64, 512], "At"), mk([64, 512], "Rt"), mk([64, 512], "Bt"), mk([64, 512], "Kt")
        FARs, FBs, FKs = mk([64, 8, 128], "FAR"), mk([64, 8, 64], "FB"), mk([64, 8, 64], "FK")
        G1s, G2s = mk([64, 8, 128], "G1"), mk([64, 8, 128], "G2")
        Tms = [mk([64, 8, 64], f"Tm{i}_") for i in range(2)]
        Nms = [mk([64, 8, 64], f"Nm{i}_") for i in range(2)]
        Zs, Wss, Uss, PCs = mk([64, 8, 64], "Z"), mk([64, 512], "Ws"), mk([64, 512], "Us"), mk([64, 8], "PC")
        Ys = [p.sb([64, 512], F32, f"Ys{d}") for d in range(2)]
        order = {0: list(range(0, 4)) + list(range(4, NCH)), 1: list(range(3, -1, -1)) + list(range(NCH - 1, 3, -1))}
        v3 = lambda ap: ap.rearrange("p (h d) -> p h d", h=8)
        F32R = mybir.dt.float32r
        use_r = cfg.get("fp32r", True)

        def RR(ap):
            return ap.bitcast(F32R) if use_r else ap

        Vrs = mk([64, 512], "Vr")
        Mts = mk([64, 8, 64], "Mt")
        for d in range(2):
            p.op('dve', lambda e, d=d: e.memset(Mts[d][:], 0.0), w=[('Mt', d)])
            p.op('dve', lambda e, d=d: e.tensor_copy(out=RR(M[d][:]), in_=Mts[d][:]), r=[('Mt', d)], w=[('M', d)])

        def mmr(e, out, lhsT, rhs, **kw):
            if use_r:
                return e.matmul(out, lhsT.bitcast(F32R), rhs.bitcast(F32R), **kw)
            return e.matmul(out, lhsT, rhs, **kw)

        def scan_unit(d, c):
            if True:
                tok0 = c * C
                Xd = X[d][c % 2]
                E0, E1, E2, At, Rt, Bt, Kt = E0s[d], E1s[d], E2s[d], Ats[d], Rts[d], Bts[d], Kts[d]
                FAR, FB, FK, G1, G2 = FARs[d], FBs[d], FKs[d], G1s[d], G2s[d]
                Tm = [Tms[0][d], Tms[1][d]]
                Nm = [Nms[0][d], Nms[1][d]]
                Z, Ws, Us, PC = Zs[d], Wss[d], Uss[d], PCs[d]
                ps = [PSB[4 * d + (i % 4)] for i in range(8)]
                PS = lambda i: ('ps', 4 * d + (i % 4))
                xk = [('X', d, c % 2)]
                srcs = [0, 1, 3, 4 + d, 6 + d, 8 + d]
                for i, s in enumerate(srcs):
                    p.dma(lambda e, Xd=Xd, i=i, s=s, tok0=tok0: e.dma_start(out=Xd[:, i, :],
                                                                           in_=S['tm'][tok0:tok0 + C, s, :]), w=xk)
                r_, v_, kk_, k_, b_, lw_ = [Xd[:, i, :] for i in range(6)]
                Vr = Vrs[d]
                p.op('act', lambda e, v_=v_: e.activation(out=RR(Vr[:]), in_=v_, func=AF.Copy), r=xk, w=[('Vr', d)])
                v_ = Vr[:]
                vk = [('Vr', d)]
                p.op('pe', lambda e, d=d, lw_=lw_: e.matmul(ps[0][0:64, :], tri[:, d, :], lw_, start=True, stop=True),
                     r=xk + ['tri'], w=[PS(0)])
                for h in range(8):
                    p.op('pe', lambda e, h=h, lw_=lw_: e.matmul(ps[1][0:64, h:h + 1], lw_[:, h * 64:(h + 1) * 64],
                                                               ones[:, 0:1], start=True, stop=True),
                         r=xk + ['ones1'], w=[PS(1)])
                p.op('act', lambda e: e.activation(out=PC[:], in_=ps[1][0:64, 0:8], func=AF.Exp), r=[PS(1)], w=[('PC', d)])
                p.op('act', lambda e: e.activation(out=E1[:], in_=ps[0][0:64, :], func=AF.Exp), r=[PS(0)], w=[('E1', d)])
                p.op('act', lambda e: e.activation(out=E2[:], in_=ps[0][0:64, :], func=AF.Exp, scale=-1.0),
                     r=[PS(0)], w=[('E2', d)])
                p.op('dve', lambda e, lw_=lw_: e.tensor_tensor(out=E0[:], in0=ps[0][0:64, :], in1=lw_, op=ALU.subtract),
                     r=[PS(0)] + xk, w=[('E0', d)])
                p.op('act', lambda e: e.activation(out=E0[:], in_=E0[:], func=AF.Exp), r=[('E0', d)], w=[('E0', d)])
                p.op('dve', lambda e, kk_=kk_: e.scalar_tensor_tensor(out=At[:], in0=kk_, scalar=-1.0, in1=E0[:],
                                                                      op0=ALU.mult, op1=ALU.mult),
                     r=xk + [('E0', d)], w=[('At', d)])
                p.op('dve', lambda e, r_=r_: e.tensor_tensor(out=Rt[:], in0=r_, in1=E1[:], op=ALU.mult),
                     r=xk + [('E1', d)], w=[('Rt', d)])
                p.op('dve', lambda e, b_=b_: e.tensor_tensor(out=RR(Bt[:]), in0=b_, in1=E2[:], op=ALU.mult),
                     r=xk + [('E2', d)], w=[('Bt', d)])
                p.op('dve', lambda e, k_=k_: e.tensor_tensor(out=RR(Kt[:]), in0=k_, in1=E2[:], op=ALU.mult),
                     r=xk + [('E2', d)], w=[('Kt', d)])
                for bank, src, key in ((2, At, ('At', d)), (3, Rt, ('Rt', d)), (4, Bt, ('Bt', d)), (5, Kt, ('Kt', d))):
                    for h in range(8):
                        p.op('pe', lambda e, bank=bank, src=src, h=h: e.transpose(
                            out=ps[bank][0:64, h * 64:(h + 1) * 64], in_=src[:, h * 64:(h + 1) * 64],
                            identity=ident_f[0:64, 0:64]), r=[key, 'identf'], w=[PS(bank)])
                p.op('act', lambda e: e.activation(out=RR(FAR[:, :, 0:64]), in_=v3(ps[2][0:64, :]), func=AF.Copy),
                     r=[PS(2)], w=[('FAR', d)])
                p.op('act', lambda e: e.activation(out=RR(FAR[:, :, 64:128]), in_=v3(ps[3][0:64, :]), func=AF.Copy),
                     r=[PS(3)], w=[('FAR', d)])
                p.op('dve', lambda e: e.tensor_copy(out=RR(FB[:]), in_=v3(ps[4][0:64, :])), r=[PS(4)], w=[('FB', d)])
                p.op('dve', lambda e: e.tensor_copy(out=RR(FK[:]), in_=v3(ps[5][0:64, :])), r=[PS(5)], w=[('FK', d)])
                for h in range(8):
                    bank = 6 + (h // 4)
                    p.op('pe', lambda e, h=h, bank=bank: mmr(e, ps[bank][0:64, (h % 4) * 128:(h % 4 + 1) * 128],
                                                                   FB[:, h, :], FAR[:, h, :], start=True, stop=True),
                         r=[('FB', d), ('FAR', d)], w=[PS(bank)])
                for hb in range(2):
                    p.op('dve', lambda e, hb=hb, d=d: e.tensor_tensor(
                        out=RR(G1[:, hb * 4:(hb + 1) * 4, :]), in0=ps[6 + hb][0:64, :].rearrange("p (h t) -> p h t", h=4),
                        in1=mg[:, d, :].unsqueeze(1).to_broadcast([64, 4, 128]), op=ALU.mult),
                        r=[PS(6 + hb), 'mg'], w=[('G1', d)])
                for h in range(8):
                    bank = 2 + (h // 4)
                    p.op('pe', lambda e, h=h, bank=bank: mmr(e, ps[bank][0:64, (h % 4) * 128:(h % 4 + 1) * 128],
                                                                   FK[:, h, :], FAR[:, h, :], start=True, stop=True),
                         r=[('FK', d), ('FAR', d)], w=[PS(bank)])
                for hb in range(2):
                    p.op('dve', lambda e, hb=hb, d=d: e.tensor_tensor(
                        out=RR(G2[:, hb * 4:(hb + 1) * 4, :]), in0=ps[2 + hb][0:64, :].rearrange("p (h t) -> p h t", h=4),
                        in1=mg[:, d, :].unsqueeze(1).to_broadcast([64, 4, 128]), op=ALU.mult),
                        r=[PS(2 + hb), 'mg'], w=[('G2', d)])
                for h in range(8):
                    p.op('pe', lambda e, h=h: mmr(e, ps[4][0:64, h * 64:(h + 1) * 64], FAR[:, h, 0:64], FB[:, h, :],
                                                       start=True, stop=True), r=[('FAR', d), ('FB', d)], w=[PS(4)])
                p.op('dve', lambda e, d=d: e.tensor_tensor(out=RR(Nm[0][:]), in0=v3(ps[4][0:64, :]),
                                                           in1=mn[:, d, :].unsqueeze(1).to_broadcast([64, 8, 64]),
                                                           op=ALU.mult), r=[PS(4), 'mn'], w=[('Nm', d, 0)])
                p.op('dve', lambda e: e.tensor_copy(out=RR(Tm[0][:]), in_=G1[:, :, 0:64]), r=[('G1', d)], w=[('Tm', d, 0)])
                p.op('dve', lambda e: e.tensor_tensor(out=RR(Z[:]), in0=G1[:, :, 0:64],
                                                      in1=ident_f[0:64, 0:64].unsqueeze(1).to_broadcast([64, 8, 64]),
                                                      op=ALU.add), r=[('G1', d), 'identf'], w=[('Z', d)])
                cur = 0
                for lev in range(5):
                    nxt = 1 - cur
                    last = lev == 4
                    for h in range(8):
                        p.op('pe', lambda e, h=h, cur=cur: mmr(e, ps[5][0:64, h * 64:(h + 1) * 64], Tm[cur][:, h, :],
                                                                    Nm[cur][:, h, :], start=True, stop=True),
                             r=[('Tm', d, cur), ('Nm', d, cur)], w=[PS(5)])
                    p.op('act', lambda e, nxt=nxt: e.activation(out=RR(Nm[nxt][:]), in_=v3(ps[5][0:64, :]), func=AF.Copy),
                         r=[PS(5)], w=[('Nm', d, nxt)])
                    if not last:
                        for h in range(8):
                            p.op('pe', lambda e, h=h, cur=cur: mmr(e, ps[6][0:64, h * 64:(h + 1) * 64],
                                                                        Nm[cur][:, h, :], Tm[cur][:, h, :],
                                                                        start=True, stop=True),
                                 r=[('Tm', d, cur), ('Nm', d, cur)], w=[PS(6)])
                        p.op('dve', lambda e, nxt=nxt: e.tensor_copy(out=RR(Tm[nxt][:]), in_=v3(ps[6][0:64, :])),
                             r=[PS(6)], w=[('Tm', d, nxt)])
                    for h in range(8):
                        p.op('pe', lambda e, h=h, nxt=nxt: mmr(e, ps[7][0:64, h * 64:(h + 1) * 64], Nm[nxt][:, h, :],
                                                                    Z[:, h, :], start=True, stop=True),
                             r=[('Nm', d, nxt), ('Z', d)], w=[PS(7)])
                    p.op('dve', lambda e: e.tensor_tensor(out=RR(Z[:]), in0=Z[:], in1=v3(ps[7][0:64, :]), op=ALU.add),
                         r=[PS(7), ('Z', d)], w=[('Z', d)])
                    cur = nxt
                Md = M[d]
                for h in range(8):
                    o = ps[0][0:64, h * 64:(h + 1) * 64]
                    p.op('pe', lambda e, h=h, o=o, Md=Md: mmr(e, o, FAR[:, h, 0:64], Md[:, h, :], start=True, stop=False),
                         r=[('FAR', d), ('M', d)], w=[PS(0)])
                    p.op('pe', lambda e, h=h, o=o, v_=v_: mmr(e, o, G2[:, h, 0:64], v_[:, h * 64:(h + 1) * 64],
                                                                   start=False, stop=True), r=[('G2', d)] + vk, w=[PS(0)])
                p.op('act', lambda e: e.activation(out=RR(Ws[:]), in_=ps[0][0:64, :], func=AF.Copy), r=[PS(0)], w=[('Ws', d)])
                for h in range(8):
                    p.op('pe', lambda e, h=h: mmr(e, ps[1][0:64, h * 64:(h + 1) * 64], Z[:, h, :],
                                                       Ws[:, h * 64:(h + 1) * 64], start=True, stop=True),
                         r=[('Z', d), ('Ws', d)], w=[PS(1)])
                p.op('act', lambda e: e.activation(out=RR(Us[:]), in_=ps[1][0:64, :], func=AF.Copy), r=[PS(1)], w=[('Us', d)])
                for h in range(8):
                    o = ps[2][0:64, h * 64:(h + 1) * 64]
                    hs = slice(h * 64, (h + 1) * 64)
                    p.op('pe', lambda e, h=h, o=o, Md=Md: mmr(e, o, FAR[:, h, 64:128], Md[:, h, :], start=True, stop=False),
                         r=[('FAR', d), ('M', d)], w=[PS(2)])
                    p.op('pe', lambda e, h=h, o=o, hs=hs: mmr(e, o, G1[:, h, 64:128], Us[:, hs], start=False, stop=False),
                         r=[('G1', d), ('Us', d)], w=[PS(2)])
                    p.op('pe', lambda e, h=h, o=o, hs=hs, v_=v_: mmr(e, o, G2[:, h, 64:128], v_[:, hs], start=False, stop=True),
                         r=[('G2', d)] + vk, w=[PS(2)])
                p.op('act', lambda e, d=d: e.activation(out=Ys[d][:], in_=ps[2][0:64, :], func=AF.Copy),
                     r=[PS(2)], w=[('Ys', d)])
                p.dma(lambda e, d=d, tok0=tok0: e.dma_start(out=S['y'][d, tok0:tok0 + C, :], in_=Ys[d][:]),
                      r=[('Ys', d)], w=[('Sy', d, c)])
                for h in range(8):
                    o = ps[3][0:64, h * 64:(h + 1) * 64]
                    hs = slice(h * 64, (h + 1) * 64)
                    p.op('pe', lambda e, o=o, hs=hs: mmr(e, o, Bt[:, hs], Us[:, hs], start=True, stop=False),
                         r=[('Bt', d), ('Us', d)], w=[PS(3)])
                    p.op('pe', lambda e, o=o, hs=hs, v_=v_: mmr(e, o, Kt[:, hs], v_[:, hs], start=False, stop=True),
                         r=[('Kt', d)] + vk, w=[PS(3)])
                Mt = Mts[d]
                p.op('dve', lambda e, Md=Md, Mt=Mt: e.tensor_tensor(out=Mt[:], in0=Md[:], in1=v3(ps[3][0:64, :]), op=ALU.add),
                     r=[PS(3), ('M', d)], w=[('Mt', d)])
                p.op('dve', lambda e, Md=Md, Mt=Mt: e.tensor_tensor(out=RR(Md[:]), in0=Mt[:],
                                                             in1=PC[:].unsqueeze(2).to_broadcast([64, 8, 64]),
                                                             op=ALU.mult), r=[('PC', d), ('Mt', d)], w=[('M', d)])
        cin = [[p.sb([128, 1, D], F32, f"cin{i}{q}") for q in range(2)] for i in range(2)]
        cout = [p.sb([128, 1, 2 * D], BF16, f"cout{i}") for i in range(2)]

        def conv_block(blk):
            b = blk % 2
            rows = slice(blk * 128, (blk + 1) * 128)
            for q, tabn in enumerate(('peer_u', 'peer_v')):
                p.dma(lambda e, b=b, q=q, tabn=tabn, rows=rows: e.dma_start(
                    out=cin[b][q][:], in_=I[tabn][l][rows, :].rearrange("(j p) d -> p j d", p=128)), w=[('cin', b, q)],
                    eng="pool")
                p.op('pool', lambda e, b=b, q=q: e.tensor_copy(out=cout[b][:, :, q * D:(q + 1) * D], in_=cin[b][q][:]),
                     r=[('cin', b, q)], w=[('cout', b, q)])
            p.dma(lambda e, b=b, rows=rows: e.dma_start(
                out=S['T'][l][rows, :].rearrange("(j p) d -> p j d", p=128), in_=cout[b][:]),
                r=[('cout', b, 0), ('cout', b, 1)], w=[('cout', b, 0), ('cout', b, 1)], eng="pool")

        nblk = 0
        for step in range(NCH):
            for d in range(2):
                scan_unit(d, order[d][step])
            for _ in range(4):
                if nblk < 128:
                    conv_block(nblk)
                    nblk += 1
        while nblk < 128:
            conv_block(nblk)
            nblk += 1
        p.barrier()


    def phase_rout(l):
        p.sb_reset(base_mark)
        with_ctx = l < DEPTH - 1
        wo = p.sb([128, 8, D], BF16, "wo")
        for j in range(8):
            p.dma(lambda e, j=j: e.dma_start(out=wo[:, j, :], in_=I['w_out'][l, j * 128:(j + 1) * 128, :]),
                  w=[('wo', j)], eng="pool")
        LNW = p.sb([128, 512], F32, "LNW")
        LNB = p.sb([128, 512], F32, "LNB")
        G1b = [p.sb([128, D], F32, f"G1b{s}") for s in range(2)]
        load_bc(LNW[:], I['r7_lnw'][l], 'LNW')
        load_bc(LNB[:], I['r7_lnb'][l], 'LNB')
        gn_eps = p.sb([128, 1], F32, "gneps")
        p.op('dve', lambda e: e.memset(gn_eps[:], 64e-5), w=['gneps'])
        for s in range(2):
            load_bc(G1b[s][:], S['mod'][l, s, 2 * D:3 * D], ('G1b', s))
        yb = [[p.sb([128, 512], F32, f"y{d}{i}") for d in range(2)] for i in range(2)]
        vg = [p.sb([128, 2, 512], F32, f"vg{i}") for i in range(2)]
        bon = [p.sb([128, 8], F32, f"bon{i}") for i in range(2)]
        O = [p.sb([128, D], F32, f"O{i}") for i in range(2)]
        Ob = [p.sb([128, D], BF16, f"Ob{i}") for i in range(2)]
        oT = [p.sb([128, 8, 128], BF16, f"oT{i}") for i in range(2)]
        xt = [p.sb([128, D], F32, f"xr{i}") for i in range(2)]
        yc = p.sb([128, 512], F32, "yc")
        sq = p.sb([128, 512], F32, "sq2")
        m8 = p.sb([128, 8], F32, "m8")
        v8 = p.sb([128, 8], F32, "v8")
        src = I['x'] if l == 0 else S['xs']
        v3 = lambda ap: ap.rearrange("p (h d) -> p h d", h=8)
        bc8 = lambda ap: ap.unsqueeze(2).to_broadcast([128, 8, 64])
        for t in range(NT):
            if t < 2 and not with_ctx:
                continue
            b = t % 2
            s = 1 if t < 2 else 0
            rows = slice(t * 128, (t + 1) * 128)
            for d in range(2):
                p.dma(lambda e, b=b, d=d, rows=rows: e.dma_start(out=yb[b][d][:], in_=S['y'][d, rows, :]), w=[('y', b, d)])
            p.dma(lambda e, b=b, rows=rows: e.dma_start(out=vg[b][:], in_=S['tm'][rows, 1:3, :]), w=[('vg', b)])
            p.dma(lambda e, b=b, rows=rows: e.dma_start(out=bon[b][:], in_=S['bon'][rows, :]), w=[('bon', b)])
            p.dma(lambda e, b=b, rows=rows: e.dma_start(out=O[b][:, 0:512], in_=S['o'][rows, 0:512]), w=[('O', b)])
            p.dma(lambda e, b=b, rows=rows: e.dma_start(out=xt[b][:], in_=src[rows, :]), w=[('xr', b)])
            p.op('dve', lambda e, b=b: e.tensor_tensor(out=yc[:], in0=yb[b][0][:], in1=yb[b][1][:], op=ALU.add),
                 r=[('y', b, 0), ('y', b, 1)], w=['yc'])
            p.op('dve', lambda e: e.tensor_reduce(out=m8[:], in_=v3(yc[:]), axis=AX.X, op=ALU.add), r=['yc'], w=['m8'])
            p.op('dve', lambda e: e.tensor_scalar(out=m8[:], in0=m8[:], scalar1=1.0 / 64, scalar2=None, op0=ALU.mult),
                 r=['m8'], w=['m8'])
            p.op('dve', lambda e: e.tensor_tensor(out=v3(yc[:]), in0=v3(yc[:]), in1=bc8(m8[:]), op=ALU.subtract),
                 r=['yc', 'm8'], w=['yc'])
            p.op('dve', lambda e: e.tensor_tensor(out=sq[:], in0=yc[:], in1=yc[:], op=ALU.mult), r=['yc'], w=['sq2'])
            p.op('dve', lambda e: e.tensor_reduce(out=v8[:], in_=v3(sq[:]), axis=AX.X, op=ALU.add), r=['sq2'], w=['v8'])
            p.op('act', lambda e: e.activation(out=v8[:], in_=v8[:], func=AF.Sqrt, bias=gn_eps[:], scale=1.0 / 64),
                 r=['v8', 'gneps'], w=['v8'])
            p.op('dve', lambda e: e.reciprocal(out=v8[:], in_=v8[:]), r=['v8'], w=['v8'])
            p.op('dve', lambda e: e.tensor_tensor(out=v3(yc[:]), in0=v3(yc[:]), in1=bc8(v8[:]), op=ALU.mult),
                 r=['yc', 'v8'], w=['yc'])
            p.op('dve', lambda e: e.tensor_tensor(out=yc[:], in0=yc[:], in1=LNW[:], op=ALU.mult), r=['yc', 'LNW'], w=['yc'])
            p.op('dve', lambda e: e.tensor_tensor(out=yc[:], in0=yc[:], in1=LNB[:], op=ALU.add), r=['yc', 'LNB'], w=['yc'])
            p.op('dve', lambda e, b=b: e.tensor_tensor(out=v3(sq[:]), in0=v3(vg[b][:, 0, :]), in1=bc8(bon[b][:]),
                                                       op=ALU.mult), r=[('vg', b), ('bon', b)], w=['sq2'])
            p.op('dve', lambda e: e.tensor_tensor(out=yc[:], in0=yc[:], in1=sq[:], op=ALU.add), r=['yc', 'sq2'], w=['yc'])
            p.op('dve', lambda e, b=b: e.tensor_tensor(out=O[b][:, 512:1024], in0=yc[:], in1=vg[b][:, 1, :], op=ALU.mult),
                 r=['yc', ('vg', b)], w=[('O2', b)])
            p.dma(lambda e, b=b, rows=rows: e.dma_start(out=S['o'][rows, 512:1024], in_=O[b][:, 512:1024]),
                  r=[('O2', b)], w=[('So2', t)])
            p.op('act', lambda e, b=b: e.activation(out=Ob[b][:], in_=O[b][:], func=AF.Copy),
                 r=[('O', b), ('O2', b)], w=[('Ob', b)])
            bank = 6 + b
            pv = ps[bank][:, 0:512].bitcast(BF16)
            for j in range(8):
                p.op('pe', lambda e, b=b, j=j, pv=pv: e.transpose(out=pv[:, j * 128:(j + 1) * 128],
                                                                 in_=Ob[b][:, j * 128:(j + 1) * 128], identity=ident_b[:]),
                     r=[('Ob', b), 'identb'], w=[PS(bank)])
            p.op('act', lambda e, b=b, pv=pv: e.activation(out=oT[b][:], in_=pv.rearrange("p (j t) -> p j t", j=8),
                                                            func=AF.Copy), r=[PS(bank)], w=[('oT', b)])
            for half in range(2):
                ybank = 2 * b + half
                for j in range(8):
                    p.op('pe', lambda e, b=b, j=j, half=half, ybank=ybank: e.matmul(
                        ps[ybank][:, :], oT[b][:, j, :], wo[:, j, half * 512:(half + 1) * 512],
                        start=(j == 0), stop=(j == 7)), r=[('oT', b), ('wo', j)], w=[PS(ybank)])
                cs_ = slice(half * 512, (half + 1) * 512)
                p.op('dve', lambda e, b=b, s=s, cs_=cs_, ybank=ybank: e.tensor_tensor(
                    out=O[b][:, cs_], in0=ps[ybank][:, :], in1=G1b[s][:, cs_], op=ALU.mult),
                    r=[PS(ybank), ('G1b', s), ('Ob', b), ('So2', t)], w=[('O', b), ('O2', b)])
                p.op('dve', lambda e, b=b, cs_=cs_: e.tensor_tensor(out=xt[b][:, cs_], in0=xt[b][:, cs_], in1=O[b][:, cs_],
                                                                   op=ALU.add), r=[('O', b), ('xr', b)], w=[('xr', b)])
            p.dma(lambda e, b=b, rows=rows: e.dma_start(out=S['xs'][rows, :], in_=xt[b][:]), r=[('xr', b)], w=[('Sxs', t)])
        p.barrier()

    def phase_peer(l):
        p.sb_reset(base_mark)
        last = l == DEPTH - 1
        eu_all = p.sb([128, NT, 128], U32, "eu_all")
        gate_all = p.sb([128, NT, 128], F32, "gate_all")
        G2b = [p.sb([128, D], F32, f"G2b{s}") for s in range(2)]
        m1 = p.sb_mark()
        hT = p.sb([128, 8, NTOK], BF16, "hT2")
        wq = p.sb([128, 8, 2048], BF16, "wq")
        for j in range(8):
            p.dma(lambda e, j=j: e.dma_start(out=wq[:, j, :], in_=I['peer_wq'][l, j * 128:(j + 1) * 128, :]),
                  w=[('wq', j)], eng="pool")
        keysT = p.sb([128, 16, 128], F32, "keysT")
        m0 = p.sb_mark()
        kraw = p.sb([128, 16, 128], F32, "kraw")
        p.dma(lambda e: e.dma_start(out=kraw[:], in_=I['peer_keys'][l].rearrange("h q n d -> n (h q) d")), w=['kraw'])
        for g in range(4):
            for i in range(4):
                hp = g * 4 + i
                p.op('pe', lambda e, g=g, i=i, hp=hp: e.transpose(out=ps[g][:, i * 128:(i + 1) * 128], in_=kraw[:, hp, :],
                                                                 identity=ident_f[:]), r=['kraw', 'identf'], w=[PS(g)])
            p.op('act', lambda e, g=g: e.activation(out=keysT[:, g * 4:(g + 1) * 4, :],
                                                    in_=ps[g][:, :].rearrange("p (i n) -> p i n", i=4), func=AF.Copy),
                 r=[PS(g)], w=['keysT'])
        p.barrier()
        p.sb_reset(m0)
        norm_tiles(l, 1, S['xs'], hT, lambda t: t * 128, tm_dram=S['h2'])
        p.barrier()
        p.sb_reset(m0)
        for s in range(2):
            load_bc(G2b[s][:], S['mod'][l, s, 5 * D:6 * D], ('G2b', s))
        qT = p.sb([128, 16, 128], F32, "qT")
        sc = p.sb([128, 16, 128], F32, "sc")
        sc2 = p.sb([128, 16, 128], F32, "sc2")
        sv = p.sb([128, 16, 16], F32, "sv")
        si = p.sb([128, 16, 16], U32, "si")
        sif = p.sb([128, 16, 16], F32, "sif")
        cand = p.sb([128, 8, 16, 16], F32, "cand")
        cand2 = p.sb([128, 8, 16, 16], F32, "cand2")
        eidx = p.sb([128, 8, 16, 16], F32, "eidx")
        best = p.sb([128, 8, 16], F32, "best")
        ci = p.sb([128, 8, 16], U32, "ci")
        cif = p.sb([128, 8, 16], F32, "cif")
        iota = p.sb([128, 256], F32, "iota")
        p.dma(lambda e: e.dma_start(out=iota[:], in_=I['iota']), w=['iota'])
        eq4 = p.sb([128, 8, 16, 16], F32, "eq4")
        cu = p.sb([128, 2, 8, 16], U32, "cu")
        cf = p.sb([128, 2, 8, 16], F32, "cf")
        e12 = p.sb([128, 2, 8, 16], F32, "e12")
        esel = p.sb([128, 128], F32, "esel")
        g8 = p.sb([128, 8], F32, "g8")
        tiles = [t for t in range(NT) if not (t < 2 and last)]
        for t in tiles:
            for g in range(4):
                for i in range(4):
                    hp = g * 4 + i
                    for j in range(8):
                        p.op('pe', lambda e, g=g, i=i, hp=hp, j=j, t=t: e.matmul(
                            ps[g][:, i * 128:(i + 1) * 128], wq[:, j, hp * 128:(hp + 1) * 128],
                            hT[:, j, t * 128:(t + 1) * 128], start=(j == 0), stop=(j == 7)),
                            r=[('hT', t), ('wq', j)], w=[PS(g)])
                p.op('act', lambda e, g=g: e.activation(out=qT[:, g * 4:(g + 1) * 4, :],
                                                        in_=ps[g][:, :].rearrange("p (i n) -> p i n", i=4), func=AF.Copy),
                     r=[PS(g)], w=['qT'])
            for g in range(4):
                for i in range(4):
                    hp = g * 4 + i
                    p.op('pe', lambda e, g=g, i=i, hp=hp: e.matmul(ps[4 + g][:, i * 128:(i + 1) * 128], qT[:, hp, :],
                                                                   keysT[:, hp, :], start=True, stop=True),
                         r=['qT', 'keysT'], w=[PS(4 + g)])
                p.op('act', lambda e, g=g: e.activation(out=sc[:, g * 4:(g + 1) * 4, :],
                                                        in_=ps[4 + g][:, :].rearrange("p (i n) -> p i n", i=4), func=AF.Copy),
                     r=[PS(4 + g)], w=['sc'])
            for hp in range(16):
                p.op('dve', lambda e, hp=hp: e.max(out=sv[:, hp, 0:8], in_=sc[:, hp, :]), r=['sc'], w=['sv'])
                p.op('dve', lambda e, hp=hp: e.max_index(out=si[:, hp, 0:8], in_max=sv[:, hp, 0:8], in_values=sc[:, hp, :]),
                     r=['sc', 'sv'], w=['si'])
                p.op('dve', lambda e, hp=hp: e.match_replace(out=sc2[:, hp, :], in_to_replace=sv[:, hp, 0:8],
                                                             in_values=sc[:, hp, :], imm_value=-1e30),
                     r=['sc', 'sv'], w=['sc2'])
                p.op('dve', lambda e, hp=hp: e.max(out=sv[:, hp, 8:16], in_=sc2[:, hp, :]), r=['sc2'], w=['sv'])
                p.op('dve', lambda e, hp=hp: e.max_index(out=si[:, hp, 8:16], in_max=sv[:, hp, 8:16], in_values=sc2[:, hp, :]),
                     r=['sc2', 'sv'], w=['si'])
            p.op('dve', lambda e: e.tensor_copy(out=sif[:], in_=si[:]), r=['si'], w=['sif'])
            svv = sv[:].rearrange("p (h q) k -> p h q k", q=2)
            sfv = sif[:].rearrange("p (h q) k -> p h q k", q=2)
            p.op('dve', lambda e, svv=svv: e.tensor_tensor(
                out=cand[:], in0=svv[:, :, 0, :].unsqueeze(3).to_broadcast([128, 8, 16, 16]),
                in1=svv[:, :, 1, :].unsqueeze(2).to_broadcast([128, 8, 16, 16]), op=ALU.add), r=['sv'], w=['cand'])
            p.op('dve', lambda e, sfv=sfv: e.tensor_scalar(out=sfv[:, :, 0, :], in0=sfv[:, :, 0, :], scalar1=128.0,
                                                           scalar2=None, op0=ALU.mult), r=['sif'], w=['sif'])
            for h in range(8):
                ch = cand[:, h].rearrange("p a b -> p (a b)")
                ch2 = cand2[:, h].rearrange("p a b -> p (a b)")
                p.op('dve', lambda e, h=h, ch=ch: e.max(out=best[:, h, 0:8], in_=ch), r=['cand'], w=['best'])
                p.op('dve', lambda e, h=h, ch=ch: e.max_index(out=ci[:, h, 0:8], in_max=best[:, h, 0:8], in_values=ch),
                     r=['cand', 'best'], w=['ci'])
                p.op('dve', lambda e, h=h, ch=ch, ch2=ch2: e.match_replace(out=ch2, in_to_replace=best[:, h, 0:8],
                                                                           in_values=ch, imm_value=-1e30),
                     r=['cand', 'best'], w=['cand2'])
                p.op('dve', lambda e, h=h, ch2=ch2: e.max(out=best[:, h, 8:16], in_=ch2), r=['cand2'], w=['best'])
                p.op('dve', lambda e, h=h, ch2=ch2: e.max_index(out=ci[:, h, 8:16], in_max=best[:, h, 8:16], in_values=ch2),
                     r=['cand2', 'best'], w=['ci'])
            p.op('dve', lambda e: e.tensor_scalar(out=cu[:, 0], in0=ci[:], scalar1=4, scalar2=None,
                                                  op0=ALU.logical_shift_right), r=['ci'], w=['cu'])
            p.op('dve', lambda e: e.tensor_scalar(out=cu[:, 1], in0=ci[:], scalar1=15, scalar2=None,
                                                  op0=ALU.bitwise_and), r=['ci'], w=['cu'])
            p.op('dve', lambda e: e.tensor_copy(out=cf[:], in_=cu[:]), r=['cu'], w=['cf'])
            io16 = iota[:, 0:16].unsqueeze(1).unsqueeze(1).to_broadcast([128, 8, 16, 16])
            for q in range(2):
                p.op('dve', lambda e, q=q, io16=io16: e.tensor_tensor(
                    out=eq4[:], in0=io16, in1=cf[:, q].unsqueeze(3).to_broadcast([128, 8, 16, 16]), op=ALU.is_equal),
                    r=['iota', 'cf'], w=['eq4'])
                p.op('dve', lambda e, q=q, sfv=sfv: e.tensor_tensor(
                    out=eq4[:], in0=eq4[:], in1=sfv[:, :, q, :].unsqueeze(2).to_broadcast([128, 8, 16, 16]), op=ALU.mult),
                    r=['eq4', 'sif'], w=['eq4'])
                p.op('dve', lambda e, q=q: e.tensor_reduce(out=e12[:, q], in_=eq4[:], axis=AX.X, op=ALU.add),
                     r=['eq4'], w=['e12'])
            p.op('dve', lambda e: e.tensor_tensor(out=esel[:], in0=e12[:, 0].rearrange("p h k -> p (h k)"),
                                                  in1=e12[:, 1].rearrange("p h k -> p (h k)"), op=ALU.add),
                 r=['e12'], w=['esel'])
            p.op('dve', lambda e, t=t: e.tensor_copy(out=eu_all[:, t, :], in_=esel[:]), r=['esel'], w=[('eu', t)])
            gv = gate_all[:, t, :].rearrange("p (h k) -> p h k", h=8)
            p.op('dve', lambda e, gv=gv: e.tensor_tensor(out=gv, in0=best[:],
                                                         in1=best[:, :, 0:1].to_broadcast([128, 8, 16]), op=ALU.subtract),
                 r=['best'], w=[('gate', t)])
            p.op('act', lambda e, t=t: e.activation(out=gate_all[:, t, :], in_=gate_all[:, t, :], func=AF.Exp),
                 r=[('gate', t)], w=[('gate', t)])
            p.op('dve', lambda e, gv=gv: e.tensor_reduce(out=g8[:], in_=gv, axis=AX.X, op=ALU.add), r=[('gate', t)], w=['g8'])
            p.op('dve', lambda e: e.reciprocal(out=g8[:], in_=g8[:]), r=['g8'], w=['g8'])
            p.op('dve', lambda e, gv=gv: e.tensor_tensor(out=gv, in0=gv, in1=g8[:].unsqueeze(2).to_broadcast([128, 8, 16]),
                                                         op=ALU.mult), r=[('gate', t), 'g8'], w=[('gate', t)])
        p.barrier()
        p.sb_reset(m1)
        h2 = [p.sb([128, D], F32, f"h2{i}") for i in range(2)]
        xt = [p.sb([128, D], F32, f"xp{i}") for i in range(2)]
        act = [p.sb([128, 128], F32, f"actv{i}") for i in range(2)]
        wg = [p.sb([128, 128], F32, f"wg{i}") for i in range(2)]
        NACC = 1
        acc = [[p.sb([128, D], F32, f"acc{i}{k}") for k in range(NACC)] for i in range(2)]
        junk = p.sb([128, D], BF16, "pjunk")
        NG = 32
        GS = 8
        gbuf = [p.sb([128, 2 * D], BF16, f"gb{i}") for i in range(NG)]
        NDG = 8
        dg = [p.sb([128, 128], BF16, f"dg{i}") for i in range(NDG)]
        gi = 0
        di = 0
        for t in tiles:
            b = t % 2
            s = 1 if t < 2 else 0
            rows = slice(t * 128, (t + 1) * 128)
            p.dma(lambda e, b=b, rows=rows: e.dma_start(out=h2[b][:], in_=S['h2'][rows, :]), w=[('h2', b)])
            p.dma(lambda e, b=b, rows=rows: e.dma_start(out=xt[b][:], in_=S['xs'][rows, :]), w=[('xp', b)])
            p.op('dve', lambda e, b=b: e.memset(act[b][:], 0.0), w=[('actv', b)])
            for g in range(128 // GS):
                ks = []
                for sidx in range(g * GS, (g + 1) * GS):
                    k = gi % NG
                    gi += 1
                    ks.append(k)
                    p.dma(lambda e, k=k, t=t, sidx=sidx: e.indirect_dma_start(
                        out=gbuf[k][:], out_offset=None, in_=S['T'][l],
                        in_offset=bass.IndirectOffsetOnAxis(ap=eu_all[:, t, sidx:sidx + 1], axis=0)),
                        r=[], w=[('gb', k)], eng="pool")
                    p.op('dve', lambda e, k=k, b=b, sidx=sidx: e.scalar_tensor_tensor(
                        out=junk[:], in0=gbuf[k][:, 0:D], scalar=1.0, in1=h2[b][:], op0=ALU.mult, op1=ALU.mult,
                        accum_out=act[b][:, sidx:sidx + 1]), r=[('gb', k), ('h2', b), ('actv', b)], w=[('actc', b, sidx)])
                gs = slice(g * GS, (g + 1) * GS)
                p.op('act', lambda e, b=b, gs=gs: e.activation(out=wg[b][:, gs], in_=act[b][:, gs], func=AF.Gelu),
                     r=[('actc', b, sidx) for sidx in range(g * GS, (g + 1) * GS)], w=[('wg', b, g)])
                p.op('dve', lambda e, b=b, gs=gs, t=t: e.tensor_tensor(out=wg[b][:, gs], in0=wg[b][:, gs],
                                                                      in1=gate_all[:, t, gs], op=ALU.mult),
                     r=[('wg', b, g)], w=[('wg', b, g)])
                for j, sidx in enumerate(range(g * GS, (g + 1) * GS)):
                    k = ks[j]
                    dj = di % NDG
                    di += 1
                    p.op('act', lambda e, dj=dj, b=b, sidx=sidx: e.activation(
                        out=dg[dj][:], in_=ident_f[:], func=AF.Copy, scale=wg[b][:, sidx:sidx + 1]),
                        r=[('wg', b, g), 'identf'], w=[('dg', dj)])
                    for half in range(2):
                        bank = 2 * b + half
                        p.op('pe', lambda e, dj=dj, k=k, half=half, bank=bank, sidx=sidx: e.matmul(
                            ps[bank][:, :], dg[dj][:], gbuf[k][:, D + half * 512:D + (half + 1) * 512],
                            start=(sidx == 0), stop=(sidx == 127)), r=[('dg', dj), ('gb', k)], w=[PS(bank), ('gbr', k, half)])
            a0 = acc[b][0]
            for half in range(2):
                hs_ = slice(half * 512, (half + 1) * 512)
                p.op('dve', lambda e, a0=a0, s=s, b=b, half=half, hs_=hs_: e.tensor_tensor(
                    out=a0[:, hs_], in0=ps[2 * b + half][:, :], in1=G2b[s][:, hs_], op=ALU.mult),
                    r=[PS(2 * b + half), ('G2b', s)], w=[('acc', b, 0)])
            p.op('dve', lambda e, a0=a0, b=b: e.tensor_tensor(out=xt[b][:], in0=xt[b][:], in1=a0[:], op=ALU.add),
                 r=[('acc', b, 0), ('xp', b)], w=[('xp', b)])
            if last:
                p.dma(lambda e, b=b, t=t: e.dma_start(out=out_d[(t - 2) * 128:(t - 1) * 128, :], in_=xt[b][:]),
                      r=[('xp', b)], w=[('outd', t)])
            else:
                p.dma(lambda e, b=b, rows=rows: e.dma_start(out=S['xs'][rows, :], in_=xt[b][:]),
                      r=[('xp', b)], w=[('Sxs', t)])
        p.barrier()

    PHASES = cfg.get("phases", ["proj", "rprep", "scan", "rout", "peer"])

    phase_mod()
    for l in range(cfg.get("layers", DEPTH)):
        if 'proj' in PHASES:
            qkT, Vaug, mp = phase_proj(l)
            phase_attn(l, qkT, Vaug, mp)
        if 'rprep' in PHASES:
            phase_rprep(l)
        if 'scan' in PHASES:
            phase_scan(l)
        if 'rout' in PHASES:
            phase_rout(l)
        if 'peer' in PHASES:
            phase_peer(l)
    p.barrier()
    p.emit()
    return nc


def prep_inputs(inputs):
    f = lambda a: np.ascontiguousarray(np.asarray(a, dtype=np.float32))
    x, c, ctx, c_ctx = f(inputs['x']), f(inputs['c']), f(inputs['ctx']), f(inputs['c_ctx'])
    shared = {}
    for n in ['norm_mix', 'norm_ffn', 'w_mod', 'b_mod', 'w_in', 'w_out', 'a_qnorm', 'a_knorm', 'b_qnorm', 'b_knorm',
              'a_sink']:
        shared[n] = f(inputs[n])
    rpb = f(inputs['b_rpb'])
    btab = np.zeros((DEPTH, 128, NTAB, 4, 128), np.float32)
    bmask = np.zeros((128, NTAB, 128), np.float32)
    for i, (dr, dc, valid) in enumerate(NA_TABS):
        g = rpb[:, :, dr, dc]
        btab[:, :, i, :, :] = np.where(valid[None, None], g, 0.0).transpose(0, 2, 1, 3)
        bmask[:, i, :] = valid
    shared['btab'] = btab
    shared['bmask'] = bmask
    ar = np.arange(128)
    am = np.zeros((128, 2, 128), np.float32)
    am[:, 0, :] = (ar[:, None] >= ar[None, :])
    am[:, 1, :] = (ar[:, None] <= ar[None, :])
    shared['amask'] = am
    shared['ident'] = np.eye(128, dtype=np.float32)
    cos, sin = rope_tables()
    shared['cos'], shared['sin'] = cos, sin
    rc = f(inputs['r7_conv'])
    for n in ['r7_w0', 'r7_a0', 'r7_w2', 'r7_a2', 'r7_g2', 'r7_kk', 'r7_ka', 'r7_lnw', 'r7_lnb', 'r7_rk', 'peer_wq', 'peer_keys']:
        shared[n] = f(inputs[n])
    for l in range(DEPTH):
        shared[f'peer_u{l}'] = f(inputs['peer_u'][l])
        shared[f'peer_v{l}'] = f(inputs['peer_v'][l])
    a64 = np.arange(64)
    tri = np.zeros((64, 2, 64), np.float32)
    tri[:, 0, :] = a64[:, None] <= a64[None, :]
    tri[:, 1, :] = a64[:, None] >= a64[None, :]
    mg = np.zeros((64, 2, 128), np.float32)
    mg[:, 0, 0:64] = a64[:, None] < a64[None, :]
    mg[:, 0, 64:128] = a64[:, None] <= a64[None, :]
    mg[:, 1, 0:64] = a64[:, None] > a64[None, :]
    mg[:, 1, 64:128] = a64[:, None] >= a64[None, :]
    mn = np.zeros((64, 2, 64), np.float32)
    mn[:, 0, :] = a64[None, :] < a64[:, None]
    mn[:, 1, :] = a64[None, :] > a64[:, None]
    shared['tri'], shared['mg'], shared['mn'] = tri, mg, mn
    shared['iota'] = np.ascontiguousarray(np.broadcast_to(np.arange(256, dtype=np.float32), (128, 256)))
    shared['r7_conv'] = np.ascontiguousarray(rc.reshape(DEPTH, 3, 15, 128).transpose(0, 3, 2, 1))
    maps = []
    for b in range(8):
        m = dict(shared)
        m['x'] = np.ascontiguousarray(np.concatenate([ctx[b], x[b]], axis=0))
        cc = np.stack([c[b], c_ctx], axis=-1)
        m['cc'] = np.ascontiguousarray(cc.reshape(8, 128, 2).transpose(1, 0, 2))
        maps.append(m)
    return maps


_NC_CACHE = {}


def kernel(**inputs):
    if 'nc' not in _NC_CACHE:
        _NC_CACHE['nc'] = build({})
    nc = _NC_CACHE['nc']
    maps = prep_inputs(inputs)
    res = run_bass_kernel_spmd(nc, maps, core_ids=list(range(8)))
    return np.stack([np.asarray(r['out'], dtype=np.float32) for r in res.results], axis=0)
```

```python
import numpy as np
import ml_dtypes
import concourse.bass as bass
import concourse.mybir as mybir
from concourse.bass_utils import run_bass_kernel_spmd

F32 = mybir.dt.float32
BF16 = mybir.dt.bfloat16
U32 = mybir.dt.uint32
I32 = mybir.dt.int32
AF = mybir.ActivationFunctionType
ALU = mybir.AluOpType
AX = mybir.AxisListType

ENGS = ["pe", "act", "dve", "pool", "sp"]
DT_SIZE = {F32: 4, BF16: 2, U32: 4, I32: 4}

D = 1024
NCTX = 256
NLAT = 2048
NTOK = NCTX + NLAT
NT = NTOK // 128
DEPTH = 2
EPS = 1e-6


class Prog:
    def __init__(self, nc, n_dma_sems=32):
        self.nc = nc
        self.ops = {e: [] for e in ENGS}
        self.cnt = {e: 0 for e in ENGS}
        self.waited = {e: {} for e in ENGS}
        self.res = {}
        self.n_dma_sems = n_dma_sems
        self.dma_use = [0] * n_dma_sems
        self.dma_last = [None] * n_dma_sems
        self.dma_rr = 0
        self.sb_off = 16 * 1024
        self.sb_id = 0
        self.SB_CAP = 216 * 1024

    def sb_mark(self):
        return self.sb_off

    def sb_reset(self, off=0):
        self.sb_off = off

    def sb(self, shape, dtype, name=""):
        nbytes = int(np.prod(shape[1:])) * DT_SIZE[dtype]
        off = (self.sb_off + 63) // 64 * 64
        assert off + nbytes <= self.SB_CAP, f"SBUF overflow {off}+{nbytes} ({name})"
        self.sb_off = off + nbytes
        self.sb_id += 1
        return self.nc.alloc_sbuf_tensor_at(f"sb{self.sb_id}_{name}", list(shape), dtype, offset=off)

    def _deps(self, r, w):
        deps = []
        for k in r:
            st = self.res.get(k)
            if st and st[0] is not None:
                deps.append(st[0])
        for k in w:
            st = self.res.get(k)
            if st:
                if st[0] is not None:
                    deps.append(st[0])
                deps.extend(st[1])
        return deps

    def _commit(self, tok, r, w):
        for k in r:
            st = self.res.setdefault(k, [None, []])
            st[1].append(tok)
        for k in w:
            self.res[k] = [tok, []]

    def _waits_for(self, eng, deps):
        wd = self.waited[eng]
        best = {}
        for t in deps:
            if t[0] == 'c':
                if t[1] == eng and eng == 'pe':
                    continue
                key = ('c', t[1])
            else:
                key = ('d', t[1])
            if wd.get(key, 0) >= t[2]:
                continue
            best[key] = max(best.get(key, 0), t[2])
        for k, v in best.items():
            wd[k] = v
        return list(best.items())

    def op(self, eng, fn, r=(), w=()):
        deps = self._deps(r, w)
        waits = self._waits_for(eng, deps)
        self.cnt[eng] += 1
        tok = ('c', eng, self.cnt[eng])
        self.ops[eng].append((waits, fn, ('c', eng), 1))
        self._commit(tok, r, w)
        return tok

    def dma(self, fn, r=(), w=(), eng="sp"):
        deps = list(self._deps(r, w))
        i = self.dma_rr
        self.dma_rr = (self.dma_rr + 1) % self.n_dma_sems
        if self.dma_last[i] is not None:
            deps.append(self.dma_last[i])
        waits = self._waits_for(eng, deps)
        self.dma_use[i] += 1
        tok = ('d', i, 16 * self.dma_use[i])
        self.dma_last[i] = tok
        self.ops[eng].append((waits, fn, ('d', i), 16))
        self._commit(tok, r, w)
        return tok

    def barrier(self):
        toks = [('c', e, self.cnt[e]) for e in ENGS if self.cnt[e] > 0]
        toks += [t for t in self.dma_last if t is not None]
        for e in ENGS:
            waits = self._waits_for(e, toks)
            if waits:
                self.ops[e].append((waits, None, None, 0))
        self.res = {}

    def emit(self):
        nc = self.nc
        from contextlib import ExitStack
        with ExitStack() as es:
            csem = {e: es.enter_context(nc.semaphore(f"c_{e}")) for e in ENGS}
            dsem = [es.enter_context(nc.semaphore(f"d_{i}")) for i in range(self.n_dma_sems)]
            block = es.enter_context(nc.Block())

            def sem_of(key):
                return csem[key[1]] if key[0] == 'c' else dsem[key[1]]

            def run(engname, e):
                for waits, fn, inc_key, inc in self.ops[engname]:
                    for k, v in waits:
                        e.wait_ge(sem_of(k), v)
                    if fn is None:
                        continue
                    ins = fn(e)
                    ins.then_inc(sem_of(inc_key), inc)

            @block.tensor
            def _(e):
                run("pe", e)

            @block.scalar
            def _(e):
                run("act", e)

            @block.vector
            def _(e):
                run("dve", e)

            @block.gpsimd
            def _(e):
                run("pool", e)

            @block.sync
            def _(e):
                run("sp", e)


def na_tables():
    cases = {}
    tabs = []
    keys = {}
    ar = np.arange(128)
    for p in range(16):
        for kb in range(16):
            krow = 2 * kb + ar // 64
            kcol = ar % 64
            qrow = 2 * p + ar // 64
            qcol = ar % 64
            rs = np.clip(qrow - 4, 0, 24)
            vr = (krow[:, None] >= rs[None, :]) & (krow[:, None] < rs[None, :] + 8)
            ws = np.clip(qcol - 8, 0, 48)
            vc = (kcol[:, None] >= ws[None, :]) & (kcol[:, None] < ws[None, :] + 16)
            valid = vr & vc
            if not valid.any():
                continue
            dr = krow[:, None] - qrow[None, :] + 7
            dc = np.clip(kcol[:, None] - qcol[None, :] + 15, 0, 30)
            dr = np.where(valid, dr, 0)
            dc = np.where(valid, dc, 0)
            key = (dr.tobytes(), dc.tobytes(), valid.tobytes())
            if key not in keys:
                keys[key] = len(tabs)
                tabs.append((dr, dc, valid))
            cases[(p, kb)] = keys[key]
    return cases, tabs


NA_CASES, NA_TABS = na_tables()
NTAB = len(NA_TABS)


def rope_tables():
    t = np.arange(NLAT)
    inv_freq = 10000.0 ** (-np.arange(0, 32, 2) / 32)
    ang = np.stack([(t // 64)[:, None] * inv_freq[None], (t % 64)[:, None] * inv_freq[None]], axis=1)
    return np.cos(ang).astype(np.float32).reshape(NLAT, 32), np.sin(ang).astype(np.float32).reshape(NLAT, 32)


def build(cfg=None):
    cfg = cfg or {}
    dbg = cfg.get("dbg", [])
    nc = bass.Bass("TRN2", target_bir_lowering=False)
    p = Prog(nc)

    def din(name, shape, dt=F32):
        return nc.dram_tensor(name, list(shape), dt, kind="ExternalInput").ap()

    def dscr(name, shape, dt=F32):
        kind = "Internal"
        if name in cfg.get("dump", []):
            kind = "ExternalOutput"
        if name in cfg.get("feed", []):
            kind = "ExternalInput"
        return nc.dram_tensor(name, list(shape), dt, kind=kind).ap()

    I = {}
    I['x'] = din('x', [NTOK, D])
    I['cc'] = din('cc', [128, 8, 2])
    I['norm_mix'] = din('norm_mix', [DEPTH, D])
    I['norm_ffn'] = din('norm_ffn', [DEPTH, D])
    I['w_mod'] = din('w_mod', [DEPTH, D, 6 * D])
    I['b_mod'] = din('b_mod', [DEPTH, 6 * D])
    I['w_in'] = din('w_in', [DEPTH, D, 3200])
    I['w_out'] = din('w_out', [DEPTH, D, D])
    for n in ['a_qnorm', 'a_knorm', 'b_qnorm', 'b_knorm']:
        I[n] = din(n, [DEPTH, 64])
    I['a_sink'] = din('a_sink', [DEPTH, 4])
    I['btab'] = din('btab', [DEPTH, 128, NTAB, 4, 128])
    I['bmask'] = din('bmask', [128, NTAB, 128])
    I['amask'] = din('amask', [128, 2, 128])
    I['ident'] = din('ident', [128, 128])
    I['cos'] = din('cos', [NLAT, 32])
    I['sin'] = din('sin', [NLAT, 32])
    I['r7_conv'] = din('r7_conv', [DEPTH, 128, 15, 3])
    I['r7_w0'] = din('r7_w0', [DEPTH, 2, 512])
    I['r7_a0'] = din('r7_a0', [DEPTH, 2, 512])
    I['r7_w2'] = din('r7_w2', [DEPTH, 2, 64, 512])
    I['r7_a2'] = din('r7_a2', [DEPTH, 2, 64, 512])
    I['r7_g2'] = din('r7_g2', [DEPTH, 128, 512])
    for n in ['r7_kk', 'r7_ka', 'r7_lnw', 'r7_lnb']:
        I[n] = din(n, [DEPTH, 512])
    I['r7_rk'] = din('r7_rk', [DEPTH, 8, 64])
    I['peer_wq'] = din('peer_wq', [DEPTH, D, 2048])
    I['peer_keys'] = din('peer_keys', [DEPTH, 8, 2, 128, 128])
    I['peer_u'] = [din(f'peer_u{l}', [16384, D]) for l in range(DEPTH)]
    I['peer_v'] = [din(f'peer_v{l}', [16384, D]) for l in range(DEPTH)]
    I['iota'] = din('iota', [128, 256])
    I['tri'] = din('tri', [64, 2, 64])
    I['mg'] = din('mg', [64, 2, 128])
    I['mn'] = din('mn', [64, 2, 64])
    out_d = nc.dram_tensor('out', [NLAT, D], F32, kind="ExternalOutput").ap()

    S = {}
    S['mod'] = dscr('s_mod', [DEPTH, 2, 6 * D])
    S['xs'] = dscr('s_xs', [NTOK, D])
    S['o'] = dscr('s_o', [NTOK, D])
    S['pcT'] = dscr('s_pcT', [1920, NTOK])
    S['tm'] = dscr('s_tm', [NTOK, 10, 512])
    S['bon'] = dscr('s_bon', [NTOK, 8])
    S['y'] = dscr('s_y', [2, NTOK, 512])
    S['h2'] = dscr('s_h2', [NTOK, D])
    S['T'] = [dscr(f's_T{l}', [16384, 2 * D], BF16) for l in range(DEPTH)]
    DBG = {}
    for name, shape in cfg.get("dbg_out", {}).items():
        DBG[name] = nc.dram_tensor(name, list(shape), F32, kind="ExternalOutput").ap()

    ps = [nc.alloc_psum_tensor(f"ps{i}", [128, 512], F32) for i in range(8)]

    def PS(i):
        return ('ps', i)

    ident_f = p.sb([128, 128], F32, "identf")
    ident_b = p.sb([128, 128], BF16, "identb")
    eps_col = p.sb([128, 1], F32, "eps")
    p.dma(lambda e: e.dma_start(out=ident_f[:], in_=I['ident']), w=['identf'])
    p.op('dve', lambda e: e.tensor_copy(out=ident_b[:], in_=ident_f[:]), r=['identf'], w=['identb'])
    p.op('dve', lambda e: e.memset(eps_col[:], EPS), w=['eps'])
    p.barrier()
    base_mark = p.sb_mark()

    def phase_mod():
        p.sb_reset(base_mark)
        cc = p.sb([128, 8, 2], F32, "cc")
        scc = p.sb([128, 8, 2], F32, "scc")
        p.dma(lambda e: e.dma_start(out=cc[:], in_=I['cc']), w=['cc'])
        p.op('act', lambda e: e.activation(out=scc[:], in_=cc[:], func=AF.Silu), r=['cc'], w=['scc'])
        wt = [p.sb([128, 8, 512], F32, f"wmod{i}") for i in range(2)]
        bm = p.sb([2, 6 * D], F32, "bm")
        mo = p.sb([2, 6 * D], F32, "mo")
        k = 0
        for l in range(DEPTH):
            p.dma(lambda e, l=l: e.dma_start(out=bm[:], in_=I['b_mod'][l].partition_broadcast(2)),
                  w=['bm'])
            for cch in range(12):
                b = k % 2
                k += 1
                src = I['w_mod'][l, :, cch * 512:(cch + 1) * 512].rearrange("(j p) n -> p j n", p=128)
                p.dma(lambda e, b=b, src=src: e.dma_start(out=wt[b][:], in_=src), w=[('wmod', b)])
                pb = cch % 2
                for j in range(8):
                    p.op('pe', lambda e, b=b, j=j, pb=pb: e.matmul(ps[pb][0:2, :], scc[:, j, :], wt[b][:, j, :],
                                                                    start=(j == 0), stop=(j == 7)),
                         r=['scc', ('wmod', b)], w=[PS(pb)])
                p.op('dve', lambda e, pb=pb, cch=cch: e.tensor_tensor(
                    out=mo[:, cch * 512:(cch + 1) * 512], in0=ps[pb][0:2, :], in1=bm[:, cch * 512:(cch + 1) * 512],
                    op=ALU.add), r=[PS(pb), 'bm'], w=['mo'])
            p.dma(lambda e, l=l: e.dma_start(out=S['mod'][l], in_=mo[:]), r=['mo'], w=['S_mod'])
        p.barrier()

    def load_bc(dst, src_1d, key):
        P = dst.shape[0]
        p.dma(lambda e: e.dma_start(out=dst, in_=src_1d.partition_broadcast(P)), w=[key])

    def norm_tiles(l, which, src, hT, hT_off, tm_dram=None):
        nv = I['norm_mix'] if which == 0 else I['norm_ffn']
        so = 0 if which == 0 else 3
        G = [p.sb([128, D], F32, f"G{s}") for s in range(2)]
        SH = [p.sb([128, D], F32, f"SH{s}") for s in range(2)]
        tmp = p.sb([128, D], F32, "gtmp")
        for s in range(2):
            load_bc(tmp[:], nv[l], 'gtmp')
            load_bc(G[s][:], S['mod'][l, s, (so + 1) * D:(so + 2) * D], ('G', s))
            load_bc(SH[s][:], S['mod'][l, s, so * D:(so + 1) * D], ('SH', s))
            p.op('dve', lambda e, s=s: e.scalar_tensor_tensor(out=G[s][:], in0=G[s][:], scalar=1.0, in1=tmp[:],
                                                             op0=ALU.add, op1=ALU.mult),
                 r=['gtmp', ('G', s)], w=[('G', s)])
        NBUF = 4
        xt = [p.sb([128, D], F32, f"xt{i}") for i in range(NBUF)]
        junk = p.sb([128, D], F32, "junk")
        hb = [p.sb([128, D], BF16, f"hb{i}") for i in range(NBUF)]
        ss = [p.sb([128, 1], F32, f"ss{i}") for i in range(NBUF)]
        def stageA(t):
            b = t % NBUF
            s = 1 if t < 2 else 0
            p.dma(lambda e, b=b, t=t: e.dma_start(out=xt[b][:], in_=src[t * 128:(t + 1) * 128, :]), w=[('xt', b)])
            p.op('act', lambda e, b=b: e.activation(out=junk[:], in_=xt[b][:], func=AF.Square, accum_out=ss[b][:]),
                 r=[('xt', b)], w=[('ss', b)])
            p.op('act', lambda e, b=b: e.activation(out=ss[b][:], in_=ss[b][:], func=AF.Sqrt, bias=eps_col[:],
                                                    scale=1.0 / D), r=[('ss', b)], w=[('ss', b)])
            p.op('dve', lambda e, b=b: e.reciprocal(out=ss[b][:], in_=ss[b][:]), r=[('ss', b)], w=[('ss', b)])
            p.op('dve', lambda e, b=b, s=s: e.scalar_tensor_tensor(out=xt[b][:], in0=xt[b][:], scalar=ss[b][:, 0:1],
                                                                 in1=G[s][:], op0=ALU.mult, op1=ALU.mult),
                 r=[('xt', b), ('ss', b), ('G', s)], w=[('xt', b)])
            if tm_dram is not None:
                p.op('dve', lambda e, b=b, s=s: e.tensor_tensor(out=xt[b][:], in0=xt[b][:], in1=SH[s][:], op=ALU.add),
                     r=[('xt', b), ('SH', s)], w=[('xt', b)])
                p.dma(lambda e, b=b, t=t: e.dma_start(out=tm_dram[t * 128:(t + 1) * 128, :], in_=xt[b][:]),
                      r=[('xt', b)], w=[('tmd', t)])
                p.op('act', lambda e, b=b: e.activation(out=hb[b][:], in_=xt[b][:], func=AF.Copy),
                     r=[('xt', b)], w=[('hb', b)])
            else:
                p.op('dve', lambda e, b=b, s=s: e.tensor_tensor(out=hb[b][:], in0=xt[b][:], in1=SH[s][:], op=ALU.add),
                     r=[('xt', b), ('SH', s)], w=[('hb', b)])

        def stageB(t):
            b = t % NBUF
            pbank = 4 + b
            pv = ps[pbank][:, 0:512].bitcast(BF16)
            for j in range(8):
                p.op('pe', lambda e, b=b, j=j, pv=pv: e.transpose(out=pv[:, j * 128:(j + 1) * 128],
                                                                 in_=hb[b][:, j * 128:(j + 1) * 128],
                                                                 identity=ident_b[:]),
                     r=[('hb', b), 'identb'], w=[PS(pbank)])
            o = hT_off(t)
            p.op('act', lambda e, pv=pv, o=o: e.activation(
                out=hT[:, :, o:o + 128], in_=pv.rearrange("p (j t) -> p j t", j=8), func=AF.Copy),
                r=[PS(pbank)], w=[('hT', t)])


        tl_ = [t for t in range(NT) if not (t < 2 and l == DEPTH - 1 and which == 1)]
        SKEW = 2
        for i in range(len(tl_) + SKEW):
            if i < len(tl_):
                stageA(tl_[i])
            if i >= SKEW:
                stageB(tl_[i - SKEW])
    def phase_proj(l):
        p.sb_reset(base_mark)
        qkT = p.sb([64, 14, NTOK], BF16, "qkT")
        Vaug = p.sb([128, NT, 6, 65], BF16, "Vaug")
        mark_persist = p.sb_mark()
        hT = p.sb([128, 8, NTOK], BF16, "hT")
        m_afterh = p.sb_mark()
        wAB = p.sb([128, 8, 1280], BF16, "wAB")
        for j in range(8):
            p.dma(lambda e, j=j: e.dma_start(out=wAB[:, j, :], in_=I['w_in'][l, j * 128:(j + 1) * 128, 0:1280]),
                  w=[('wAB', j)], eng="pool")
        p.op('pool', lambda e: e.memset(Vaug[:, :, :, 64:65], 1.0), w=['Vones'])
        m0 = p.sb_mark()
        norm_tiles(l, 0, I['x'] if l == 0 else S['xs'], hT, lambda t: t * 128)
        p.barrier()
        p.sb_reset(m0)
        wC = p.sb([128, 8, 1920], BF16, "wC")
        for j in range(8):
            p.dma(lambda e, j=j: e.dma_start(out=wC[:, j, :], in_=I['w_in'][l, j * 128:(j + 1) * 128, 1280:3200]),
                  w=[('wC', j)], eng="pool")
        m_afterwc = p.sb_mark()
        GA = p.sb([128, 6, 64], F32, "GA")
        GB = p.sb([128, 8, 64], F32, "GB")
        for h in range(6):
            load_bc(GA[:, h, :], I['a_qnorm'][l] if h < 4 else I['a_knorm'][l], 'GA')
        for h in range(8):
            load_bc(GB[:, h, :], I['b_qnorm'][l] if h < 4 else I['b_knorm'][l], 'GB')
        p.op('act', lambda e: e.mul(out=GA[:, 0:4, :], in_=GA[:, 0:4, :], mul=0.125), r=['GA'], w=['GA'])
        p.op('act', lambda e: e.mul(out=GB[:, 0:4, :], in_=GB[:, 0:4, :], mul=0.125), r=['GB'], w=['GB'])
        cs = [p.sb([128, 2, 32], F32, f"cs{i}") for i in range(2)]
        xn = [p.sb([128, 14, 64], F32, f"xn{i}") for i in range(2)]
        sq = p.sb([128, 14, 64], F32, "sq")
        ssq = [p.sb([128, 14], F32, f"ssq{i}") for i in range(2)]
        xr = [p.sb([128, 14, 64], BF16, f"xr{i}") for i in range(2)]
        RT = [p.sb([128, 6, 2, 16], F32, f"ropeT{i}") for i in range(4)]
        for t in range(NT):
            b = t % 2
            lat = t >= 2
            bA, bB, bV = (0, 1, 2) if t % 2 == 0 else (3, 6, 7)
            for bank, c0, c1 in ((bA, 0, 512), (bB, 512, 1024), (bV, 1024, 1280)):
                for j in range(8):
                    p.op('pe', lambda e, bank=bank, c0=c0, c1=c1, j=j, t=t: e.matmul(
                        ps[bank][:, 0:c1 - c0], hT[:, j, t * 128:(t + 1) * 128], wAB[:, j, c0:c1],
                        start=(j == 0), stop=(j == 7)),
                        r=[('hT', t), ('wAB', j)], w=[PS(bank)])
            if lat:
                tl = t - 2
                p.dma(lambda e, b=b, tl=tl: e.dma_start(out=cs[b][:, 0, :], in_=I['cos'][tl * 128:(tl + 1) * 128, :]),
                      w=[('cs', b)])
                p.dma(lambda e, b=b, tl=tl: e.dma_start(out=cs[b][:, 1, :], in_=I['sin'][tl * 128:(tl + 1) * 128, :]),
                      w=[('cs', b)])
            p.op('act', lambda e, t=t, bA=bA: e.activation(out=Vaug[:, t, 0:2, 0:64],
                                                    in_=ps[bA][:, 384:512].rearrange("p (h d) -> p h d", h=2),
                                                    func=AF.Copy), r=[PS(bA)], w=[('V', t)])
            p.op('act', lambda e, t=t, bV=bV: e.activation(out=Vaug[:, t, 2:6, 0:64],
                                                    in_=ps[bV][:, 0:256].rearrange("p (h d) -> p h d", h=4),
                                                    func=AF.Copy), r=[PS(bV)], w=[('V', t)])
            p.op('act', lambda e, b=b, bA=bA: e.activation(out=xn[b][:, 0:6, :],
                                                    in_=ps[bA][:, 0:384].rearrange("p (h d) -> p h d", h=6),
                                                    func=AF.Copy), r=[PS(bA)], w=[('xn', b)])
            p.op('act', lambda e, b=b, bB=bB: e.activation(out=xn[b][:, 6:14, :],
                                                    in_=ps[bB][:, 0:512].rearrange("p (h d) -> p h d", h=8),
                                                    func=AF.Copy), r=[PS(bB)], w=[('xn', b)])
            p.op('dve', lambda e, b=b: e.tensor_tensor(out=sq[:], in0=xn[b][:], in1=xn[b][:], op=ALU.mult),
                 r=[('xn', b)], w=['sq'])
            p.op('dve', lambda e, b=b: e.tensor_reduce(out=ssq[b][:], in_=sq[:], axis=AX.X, op=ALU.add),
                 r=['sq'], w=[('ssq', b)])
            p.op('act', lambda e, b=b: e.activation(out=ssq[b][:], in_=ssq[b][:], func=AF.Sqrt, bias=eps_col[:],
                                                    scale=1.0 / 64), r=[('ssq', b)], w=[('ssq', b)])
            p.op('dve', lambda e, b=b: e.reciprocal(out=ssq[b][:], in_=ssq[b][:]), r=[('ssq', b)], w=[('ssq', b)])
            p.op('dve', lambda e, b=b: e.tensor_tensor(out=xn[b][:], in0=xn[b][:],
                                                       in1=ssq[b][:].unsqueeze(2).to_broadcast([128, 14, 64]),
                                                       op=ALU.mult), r=[('xn', b), ('ssq', b)], w=[('xn', b)])
            p.op('dve', lambda e, b=b: e.tensor_tensor(out=xr[b][:, 6:14, :], in0=xn[b][:, 6:14, :], in1=GB[:],
                                                       op=ALU.mult), r=[('xn', b), 'GB'], w=[('xr', b)])
            if lat:
                p.op('dve', lambda e, b=b: e.tensor_tensor(out=xn[b][:, 0:6, :], in0=xn[b][:, 0:6, :], in1=GA[:],
                                                           op=ALU.mult), r=[('xn', b), 'GA'], w=[('xn', b)])
                xv = xn[b][:, 0:6, :].rearrange("p h (a g f) -> p h a g f", a=2, g=2)
                x1 = xv[:, :, :, 0, :]
                x2 = xv[:, :, :, 1, :]
                ov = xr[b][:, 0:6, :].rearrange("p h (a g f) -> p h a g f", a=2, g=2)
                cosb = cs[b][:, 0, :].rearrange("p (a f) -> p a f", a=2).unsqueeze(1).to_broadcast([128, 6, 2, 16])
                sinb = cs[b][:, 1, :].rearrange("p (a f) -> p a f", a=2).unsqueeze(1).to_broadcast([128, 6, 2, 16])
                rk = [('xn', b), ('cs', b)]
                for i, (xa, tb) in enumerate(((x1, cosb), (x2, sinb), (x2, cosb), (x1, sinb))):
                    p.op('dve', lambda e, i=i, xa=xa, tb=tb: e.tensor_tensor(out=RT[i][:], in0=xa, in1=tb, op=ALU.mult),
                         r=rk, w=[('RT', i)])
                p.op('dve', lambda e, ov=ov: e.tensor_tensor(out=ov[:, :, :, 0, :], in0=RT[0][:], in1=RT[1][:],
                                                             op=ALU.subtract), r=[('RT', 0), ('RT', 1)], w=[('xr', b)])
                p.op('dve', lambda e, ov=ov: e.tensor_tensor(out=ov[:, :, :, 1, :], in0=RT[2][:], in1=RT[3][:],
                                                             op=ALU.add), r=[('RT', 2), ('RT', 3)], w=[('xr', b)])
            else:
                p.op('dve', lambda e, b=b: e.tensor_tensor(out=xr[b][:, 0:6, :], in0=xn[b][:, 0:6, :], in1=GA[:],
                                                           op=ALU.mult), r=[('xn', b), 'GA'], w=[('xr', b)])
            for half in range(2):
                bank = 4 + half
                pv = ps[bank][0:64, 0:448].bitcast(BF16)
                for hh in range(7):
                    h = half * 7 + hh
                    p.op('pe', lambda e, b=b, h=h, hh=hh, pv=pv: e.transpose(
                        out=pv[:, hh * 128:(hh + 1) * 128], in_=xr[b][:, h, :], identity=ident_b[:]),
                        r=[('xr', b), 'identb'], w=[PS(bank)])
                p.op('act', lambda e, half=half, pv=pv, t=t: e.activation(
                    out=qkT[:, half * 7:(half + 1) * 7, t * 128:(t + 1) * 128],
                    in_=pv.rearrange("p (h t) -> p h t", h=7), func=AF.Copy), r=[PS(bank)], w=[('qkT', t)])
        p.barrier()
        p.sb_reset(m_afterwc)
        cw = p.sb([128, 15, 3], F32, "cw")
        p.dma(lambda e: e.dma_start(out=cw[:], in_=I['r7_conv'][l]), w=['cw'])
        rawc = [p.sb([128, NCTX + 2], F32, f"rawc{i}") for i in range(2)]
        rawl = [p.sb([128, NLAT + 2], F32, f"rawl{i}") for i in range(2)]
        cvo = [p.sb([128, NTOK], F32, f"cvo{i}") for i in range(2)]
        for i in range(2):
            p.op('pool', lambda e, i=i: e.memset(rawc[i][:], 0.0), w=[('rawc', i)])
            p.op('pool', lambda e, i=i: e.memset(rawl[i][:], 0.0), w=[('rawl', i)])
        bk = 0
        for ch in range(15):
            b = ch % 2
            groups = [(rawc[b], ('rawc', b), 1, 0, 256)] + [(rawl[b], ('rawl', b), 1 + 512 * g, 256 + 512 * g, 512)
                                                             for g in range(4)]
            for (raw, rkey, ro, tok0, n) in groups:
                bank = bk % 4
                bk += 1
                for j in range(8):
                    p.op('pe', lambda e, bank=bank, j=j, ch=ch, tok0=tok0, n=n: e.matmul(
                        ps[bank][:, 0:n], wC[:, j, ch * 128:(ch + 1) * 128], hT[:, j, tok0:tok0 + n],
                        start=(j == 0), stop=(j == 7)), r=[('wC', j)], w=[PS(bank)])
                p.op('act', lambda e, raw=raw, ro=ro, n=n, bank=bank: e.activation(
                    out=raw[:, ro:ro + n], in_=ps[bank][:, 0:n], func=AF.Copy), r=[PS(bank)], w=[rkey])
            for (raw, rkey, n, o0) in ((rawc[b], ('rawc', b), NCTX, 0), (rawl[b], ('rawl', b), NLAT, NCTX)):
                dst = cvo[b][:, o0:o0 + n]
                p.op('dve', lambda e, raw=raw, n=n, dst=dst, ch=ch: e.tensor_scalar(
                    out=dst, in0=raw[:, 1:1 + n], scalar1=cw[:, ch, 1:2], scalar2=None, op0=ALU.mult),
                    r=[rkey, 'cw'], w=[('cvo', b)])
                p.op('dve', lambda e, raw=raw, n=n, dst=dst, ch=ch: e.scalar_tensor_tensor(
                    out=dst, in0=raw[:, 0:n], scalar=cw[:, ch, 0:1], in1=dst, op0=ALU.mult, op1=ALU.add),
                    r=[rkey, 'cw'], w=[('cvo', b)])
                p.op('dve', lambda e, raw=raw, n=n, dst=dst, ch=ch: e.scalar_tensor_tensor(
                    out=dst, in0=raw[:, 2:2 + n], scalar=cw[:, ch, 2:3], in1=dst, op0=ALU.mult, op1=ALU.add),
                    r=[rkey, 'cw'], w=[('cvo', b)])
            if ch == 12:
                p.op('act', lambda e, b=b: e.activation(out=cvo[b][:], in_=cvo[b][:], func=AF.Tanh),
                     r=[('cvo', b)], w=[('cvo', b)])
            if ch == 14:
                p.op('act', lambda e, b=b: e.activation(out=cvo[b][:], in_=cvo[b][:], func=AF.Sigmoid),
                     r=[('cvo', b)], w=[('cvo', b)])
            p.dma(lambda e, b=b, ch=ch: e.dma_start(out=S['pcT'][ch * 128:(ch + 1) * 128, :], in_=cvo[b][:]),
                  r=[('cvo', b)], w=[('pcT', ch)])
        p.barrier()
        return qkT, Vaug, mark_persist

    def phase_attn(l, qkT, Vaug, mark_persist):
        with_ctx = l < DEPTH - 1
        p.sb_reset(mark_persist)
        btab = p.sb([128, NTAB, 4, 128], F32, "btab")
        bmask = p.sb([128, NTAB, 128], F32, "bmask")
        amask = p.sb([128, 2, 128], F32, "amask")
        esink = p.sb([128, 4], F32, "esink")
        o_all = [p.sb([128, 512], F32, f"oall{i}") for i in range(2)]
        ex = [p.sb([128, 8, 128], F32, f"ex{i}") for i in range(2)]
        pT = [p.sb([128, 8, 128], BF16, f"pT{i}") for i in range(2)]
        den = [p.sb([128, 1], F32, f"den{i}") for i in range(2)]
        for tb in range(NTAB):
            p.dma(lambda e, tb=tb: e.dma_start(out=btab[:, tb], in_=I['btab'][l, :, tb]), w=['btab'])
        p.dma(lambda e: e.dma_start(out=bmask[:], in_=I['bmask']), w=['bmask'])
        p.dma(lambda e: e.dma_start(out=amask[:], in_=I['amask']), w=['amask'])
        load_bc(esink[:], I['a_sink'][l], 'esink')
        p.op('act', lambda e: e.activation(out=esink[:], in_=esink[:], func=AF.Exp), r=['esink'], w=['esink'])
        p.op('act', lambda e: e.activation(out=btab[:], in_=btab[:], func=AF.Exp), r=['btab'], w=['btab'])
        for h in range(4):
            p.op('dve', lambda e, h=h: e.tensor_tensor(out=btab[:, :, h, :], in0=btab[:, :, h, :], in1=bmask[:],
                                                       op=ALU.mult), r=['btab', 'bmask'], w=['btab'])
        it = 0
        for t in range(NT):
            if t < 2 and not with_ctx:
                continue
            ob = t % 2
            for grp in range(2):
                for h in range(4):
                    b = it % 2
                    it += 1
                    if grp == 0:
                        qs, ks, vs = h, 4 + h // 2, h // 2
                    else:
                        qs, ks, vs = 6 + h, 10 + h, 2 + h
                    if t < 2:
                        blocks = [(0, None), (1, None)]
                    elif grp == 0:
                        n = t - 2
                        blocks = [(t, None), (0, None), (1, None)]
                        if n > 0:
                            blocks.append((t - 1, amask[:, 0, :]))
                        if n < 15:
                            blocks.append((t + 1, amask[:, 1, :]))
                    else:
                        pq = t - 2
                        blocks = [(0, None), (1, None)]
                        for kb in range(16):
                            if (pq, kb) in NA_CASES:
                                blocks.append((kb + 2, btab[:, NA_CASES[(pq, kb)], h, :]))
                    nb = len(blocks)
                    nn = sum(1 for _, tb in blocks if tb is None)
                    sb0, sb1 = (0, 1) if b == 0 else (2, 3)
                    ob_ps = 4 + b
                    for i, (kt, tb) in enumerate(blocks):
                        bank = sb0 if i < 4 else sb1
                        p.op('pe', lambda e, bank=bank, i=i, kt=kt, ks=ks, qs=qs, t=t: e.matmul(
                            ps[bank][:, (i % 4) * 128:(i % 4 + 1) * 128], qkT[:, ks, kt * 128:(kt + 1) * 128],
                            qkT[:, qs, t * 128:(t + 1) * 128], start=True, stop=True), w=[PS(bank)])
                    n0 = min(nb, 4)
                    p.op('act', lambda e, b=b, n0=n0, sb0=sb0: e.activation(
                        out=ex[b][:, 0:n0, :], in_=ps[sb0][:, 0:n0 * 128].rearrange("p (n k) -> p n k", n=n0),
                        func=AF.Exp), r=[PS(sb0)], w=[('ex', b)])
                    if nb > 4:
                        n1 = nb - 4
                        p.op('act', lambda e, b=b, n1=n1, sb1=sb1: e.activation(
                            out=ex[b][:, 4:4 + n1, :], in_=ps[sb1][:, 0:n1 * 128].rearrange("p (n k) -> p n k", n=n1),
                            func=AF.Exp), r=[PS(sb1)], w=[('ex', b)])
                    p.op('pool', lambda e, b=b, nn=nn: e.tensor_copy(out=pT[b][:, 0:nn, :], in_=ex[b][:, 0:nn, :]),
                         r=[('ex', b)], w=[('pT', b)])
                    for i, (kt, tb) in enumerate(blocks):
                        if tb is None:
                            continue
                        p.op('dve', lambda e, b=b, i=i, tb=tb: e.tensor_tensor(out=pT[b][:, i, :], in0=ex[b][:, i, :],
                                                                              in1=tb, op=ALU.mult),
                             r=[('ex', b), 'btab', 'amask'], w=[('pT', b)])
                    for i, (kt, tb) in enumerate(blocks):
                        p.op('pe', lambda e, b=b, i=i, kt=kt, vs=vs, ob_ps=ob_ps, nb=nb: e.matmul(
                            ps[ob_ps][:, 0:65], pT[b][:, i, :], Vaug[:, kt, vs, :], start=(i == 0), stop=(i == nb - 1)),
                            r=[('pT', b)], w=[PS(ob_ps)])
                    if grp == 0:
                        p.op('dve', lambda e, b=b, h=h, ob_ps=ob_ps: e.tensor_scalar(
                            out=den[b][:], in0=ps[ob_ps][:, 64:65], scalar1=esink[:, h:h + 1], scalar2=None,
                            op0=ALU.add), r=[PS(ob_ps), 'esink'], w=[('den', b)])
                        p.op('dve', lambda e, b=b: e.reciprocal(out=den[b][:], in_=den[b][:]),
                             r=[('den', b)], w=[('den', b)])
                    else:
                        p.op('dve', lambda e, b=b, ob_ps=ob_ps: e.reciprocal(out=den[b][:], in_=ps[ob_ps][:, 64:65]),
                             r=[PS(ob_ps)], w=[('den', b)])
                    col = grp * 256 + h * 64
                    p.op('dve', lambda e, b=b, ob=ob, col=col, ob_ps=ob_ps: e.tensor_scalar(
                        out=o_all[ob][:, col:col + 64], in0=ps[ob_ps][:, 0:64], scalar1=den[b][:, 0:1], scalar2=None,
                        op0=ALU.mult), r=[PS(ob_ps), ('den', b)], w=[('oall', ob)])
            p.dma(lambda e, ob=ob, t=t: e.dma_start(out=S['o'][t * 128:(t + 1) * 128, 0:512], in_=o_all[ob][:]),
                  r=[('oall', ob)], w=[('So', t)])
        p.barrier()


    def phase_rprep(l):
        p.sb_reset(base_mark)
        w2 = p.sb([128, 512], F32, "w2")
        a2 = p.sb([128, 512], F32, "a2")
        g2 = p.sb([128, 512], F32, "g2")
        w0 = p.sb([1, 2, 512], F32, "w0")
        a0 = p.sb([1, 2, 512], F32, "a0")
        ones = p.sb([1, 128], F32, "ones")
        KKW = p.sb([128, 512], F32, "KKW")
        KA = p.sb([128, 512], F32, "KA")
        RK = p.sb([128, 512], F32, "RK")
        p.dma(lambda e: e.dma_start(out=w2[:], in_=I['r7_w2'][l].rearrange("d r c -> (d r) c")), w=['w2'])
        p.dma(lambda e: e.dma_start(out=a2[:], in_=I['r7_a2'][l].rearrange("d r c -> (d r) c")), w=['a2'])
        p.dma(lambda e: e.dma_start(out=g2[:], in_=I['r7_g2'][l]), w=['g2'])
        p.dma(lambda e: e.dma_start(out=w0[:], in_=I['r7_w0'][l:l + 1]), w=['w0'])
        p.dma(lambda e: e.dma_start(out=a0[:], in_=I['r7_a0'][l:l + 1]), w=['a0'])
        p.op('dve', lambda e: e.memset(ones[:], 1.0), w=['ones'])
        load_bc(KKW[:], I['r7_kk'][l], 'KKW')
        load_bc(KA[:], I['r7_ka'][l], 'KA')
        load_bc(RK[:], I['r7_rk'][l].rearrange("h d -> (h d)"), 'RK')
        fm = [p.sb([128, 15, 128], F32, f"fm{i}") for i in range(2)]
        TM = [p.sb([128, 10, 512], F32, f"TM{i}") for i in range(2)]
        kt = p.sb([128, 512], F32, "kt")
        av = [p.sb([128, 512], F32, f"av{i}") for i in range(2)]
        tmp = p.sb([128, 512], F32, "tmp")
        tmp2 = p.sb([128, 512], F32, "tmp2")
        s8 = p.sb([128, 8], F32, "s8")
        bs = [p.sb([128, 8], F32, f"bs{i}") for i in range(2)]
        for t in range(NT):
            b = t % 2
            p.dma(lambda e, b=b, t=t: e.dma_start(
                out=fm[b][:], in_=S['pcT'][:, t * 128:(t + 1) * 128].rearrange("(c p) t -> p c t", p=128)),
                w=[('fm', b)])
            for q in range(3):
                for c4 in range(4):
                    p.op('pe', lambda e, b=b, q=q, c4=c4: e.transpose(
                        out=ps[q][:, c4 * 128:(c4 + 1) * 128], in_=fm[b][:, q * 4 + c4, :], identity=ident_f[:]),
                        r=[('fm', b), 'identf'], w=[PS(q)])
            for d in range(2):
                pr = slice(d * 64, d * 64 + 64)
                p.op('pe', lambda e, b=b, d=d, pr=pr: e.matmul(ps[3 + d][:, :], fm[b][pr, 12, :], w2[pr, :],
                                                              start=True, stop=False), r=[('fm', b), 'w2'], w=[PS(3 + d)])
                p.op('pe', lambda e, d=d: e.matmul(ps[3 + d][:, :], ones[0:1, :], w0[0:1, d, :], start=False, stop=True),
                     r=['ones', 'w0'], w=[PS(3 + d)])
                p.op('pe', lambda e, b=b, d=d, pr=pr: e.matmul(ps[5 + d][:, :], fm[b][pr, 13, :], a2[pr, :],
                                                              start=True, stop=False), r=[('fm', b), 'a2'], w=[PS(5 + d)])
                p.op('pe', lambda e, d=d: e.matmul(ps[5 + d][:, :], ones[0:1, :], a0[0:1, d, :], start=False, stop=True),
                     r=['ones', 'a0'], w=[PS(5 + d)])
            p.op('pe', lambda e, b=b: e.matmul(ps[7][:, :], fm[b][:, 14, :], g2[:], start=True, stop=True),
                 r=[('fm', b), 'g2'], w=[PS(7)])
            T = TM[b]
            wk = [('TM', b)]
            p.op('act', lambda e, T=T: e.activation(out=T[:, 0, :], in_=ps[0][:, :], func=AF.Copy), r=[PS(0)], w=wk)
            p.op('act', lambda e: e.activation(out=kt[:], in_=ps[1][:, :], func=AF.Copy), r=[PS(1)], w=['kt'])
            p.op('act', lambda e, T=T: e.activation(out=T[:, 1, :], in_=ps[2][:, :], func=AF.Copy), r=[PS(2)], w=wk)
            p.op('act', lambda e, T=T: e.activation(out=T[:, 2, :], in_=ps[7][:, :], func=AF.Copy), r=[PS(7)], w=wk)
            for d in range(2):
                p.op('act', lambda e, T=T, d=d: e.activation(out=T[:, 8 + d, :], in_=ps[3 + d][:, :], func=AF.Sigmoid),
                     r=[PS(3 + d)], w=wk)
                p.op('act', lambda e, d=d: e.activation(out=av[d][:], in_=ps[5 + d][:, :], func=AF.Sigmoid),
                     r=[PS(5 + d)], w=[('av', d)])
                p.op('dve', lambda e, T=T, d=d: e.tensor_scalar(out=T[:, 8 + d, :], in0=T[:, 8 + d, :],
                                                                scalar1=-0.6065306597126334, scalar2=None, op0=ALU.mult),
                     r=wk, w=wk)
            p.op('dve', lambda e: e.tensor_tensor(out=tmp[:], in0=kt[:], in1=KKW[:], op=ALU.mult), r=['kt', 'KKW'], w=['tmp'])
            p.op('dve', lambda e: e.tensor_tensor(out=tmp2[:], in0=tmp[:], in1=tmp[:], op=ALU.mult), r=['tmp'], w=['tmp2'])
            p.op('dve', lambda e: e.tensor_reduce(out=s8[:], in_=tmp2[:].rearrange("p (h d) -> p h d", h=8), axis=AX.X,
                                                  op=ALU.add), r=['tmp2'], w=['s8'])
            p.op('act', lambda e: e.activation(out=s8[:], in_=s8[:], func=AF.Sqrt), r=['s8'], w=['s8'])
            p.op('dve', lambda e: e.tensor_scalar(out=s8[:], in0=s8[:], scalar1=1e-12, scalar2=None, op0=ALU.max),
                 r=['s8'], w=['s8'])
            p.op('dve', lambda e: e.reciprocal(out=s8[:], in_=s8[:]), r=['s8'], w=['s8'])
            p.op('dve', lambda e, T=T: e.tensor_tensor(out=T[:, 3, :].rearrange("p (h d) -> p h d", h=8),
                                                       in0=tmp[:].rearrange("p (h d) -> p h d", h=8),
                                                       in1=s8[:].unsqueeze(2).to_broadcast([128, 8, 64]), op=ALU.mult),
                 r=['tmp', 's8'], w=wk)
            for d in range(2):
                p.op('dve', lambda e, d=d: e.scalar_tensor_tensor(out=tmp2[:], in0=av[d][:], scalar=-1.0, in1=KA[:],
                                                                  op0=ALU.add, op1=ALU.mult),
                     r=[('av', d), 'KA'], w=['tmp2'])
                p.op('dve', lambda e, T=T, d=d: e.scalar_tensor_tensor(out=T[:, 4 + d, :], in0=tmp2[:], scalar=1.0,
                                                                       in1=kt[:], op0=ALU.add, op1=ALU.mult),
                     r=['tmp2', 'kt'], w=wk)
                p.op('dve', lambda e, T=T, d=d: e.tensor_tensor(out=T[:, 6 + d, :], in0=T[:, 3, :], in1=av[d][:],
                                                                op=ALU.mult), r=wk + [('av', d)], w=wk)
            p.op('dve', lambda e, T=T: e.tensor_tensor(out=tmp[:], in0=T[:, 4, :], in1=T[:, 5, :], op=ALU.add),
                 r=wk, w=['tmp'])
            p.op('dve', lambda e: e.tensor_tensor(out=tmp[:], in0=tmp[:], in1=RK[:], op=ALU.mult), r=['tmp', 'RK'], w=['tmp'])
            p.op('dve', lambda e, T=T: e.tensor_tensor(out=tmp[:], in0=tmp[:], in1=T[:, 0, :], op=ALU.mult),
                 r=['tmp'] + wk, w=['tmp'])
            p.op('dve', lambda e, b=b: e.tensor_reduce(out=bs[b][:], in_=tmp[:].rearrange("p (h d) -> p h d", h=8),
                                                       axis=AX.X, op=ALU.add), r=['tmp'], w=[('bs', b)])
            p.dma(lambda e, T=T, t=t: e.dma_start(out=S['tm'][t * 128:(t + 1) * 128], in_=T[:]), r=wk, w=[('Stm', t)])
            p.dma(lambda e, b=b, t=t: e.dma_start(out=S['bon'][t * 128:(t + 1) * 128], in_=bs[b][:]),
                  r=[('bs', b)], w=[('Sbon', t)])
        p.barrier()

    def phase_scan(l):
        p.sb_reset(base_mark)
        PSB = ps
        C = 64
        NCH = NTOK // C
        tri = p.sb([64, 2, 64], F32, "tri")
        mg = p.sb([64, 2, 128], F32, "mg")
        mn = p.sb([64, 2, 64], F32, "mn")
        ones = p.sb([64, 1], F32, "ones1")
        p.dma(lambda e: e.dma_start(out=tri[:], in_=I['tri']), w=['tri'])
        p.dma(lambda e: e.dma_start(out=mg[:], in_=I['mg']), w=['mg'])
        p.dma(lambda e: e.dma_start(out=mn[:], in_=I['mn']), w=['mn'])
        p.op('dve', lambda e: e.memset(ones[:], 1.0), w=['ones1'])
        M = [p.sb([64, 8, 64], F32, f"M{d}") for d in range(2)]
        for d in range(2):
            M0_PLACEHOLDER = None
        X = [[p.sb([64, 6, 512], F32, f"X{d}{i}") for i in range(2)] for d in range(2)]
        def mk(shape, name):
            return [p.sb(shape, F32, f"{name}{d}") for d in range(2)]
        E0s, E1s, E2s = mk([64, 512], "E0"), mk([64, 512], "E1"), mk([64, 512], "E2")
        Ats, Rts, Bts, Kts = mk([64, 512], "At"), mk([64, 512], "Rt"), mk([64, 512], "Bt"), mk([64, 512], "Kt")
        FARs, FBs, FKs = mk([64, 8, 128], "FAR"), mk([64, 8, 64], "FB"), mk([64, 8, 64], "FK")
        G1s, G2s = mk([64, 8, 128], "G1"), mk([64, 8, 128], "G2")
        Tms = [mk([64, 8, 64], f"Tm{i}_") for i in range(2)]
        Nms = [mk([64, 8, 64], f"Nm{i}_") for i in range(2)]
        Zs, Wss, Uss, PCs = mk([64, 8, 64], "Z"), mk([64, 512], "Ws"), mk([64, 512], "Us"), mk([64, 8], "PC")
        Ys = [p.sb([64, 512], F32, f"Ys{d}") for d in range(2)]
        order = {0: list(range(0, 4)) + list(range(4, NCH)), 1: list(range(3, -1, -1)) + list(range(NCH - 1, 3, -1))}
        v3 = lambda ap: ap.rearrange("p (h d) -> p h d", h=8)
        F32R = mybir.dt.float32r
        use_r = cfg.get("fp32r", True)

        def RR(ap):
            return ap.bitcast(F32R) if use_r else ap

        Vrs = mk([64, 512], "Vr")
        Mts = mk([64, 8, 64], "Mt")
        for d in range(2):
            p.op('dve', lambda e, d=d: e.memset(Mts[d][:], 0.0), w=[('Mt', d)])
            p.op('dve', lambda e, d=d: e.tensor_copy(out=RR(M[d][:]), in_=Mts[d][:]), r=[('Mt', d)], w=[('M', d)])

        def mmr(e, out, lhsT, rhs, **kw):
            if use_r:
                return e.matmul(out, lhsT.bitcast(F32R), rhs.bitcast(F32R), **kw)
            return e.matmul(out, lhsT, rhs, **kw)

        def scan_unit(d, c):
            if True:
                tok0 = c * C
                Xd = X[d][c % 2]
                E0, E1, E2, At, Rt, Bt, Kt = E0s[d], E1s[d], E2s[d], Ats[d], Rts[d], Bts[d], Kts[d]
                FAR, FB, FK, G1, G2 = FARs[d], FBs[d], FKs[d], G1s[d], G2s[d]
                Tm = [Tms[0][d], Tms[1][d]]
                Nm = [Nms[0][d], Nms[1][d]]
                Z, Ws, Us, PC = Zs[d], Wss[d], Uss[d], PCs[d]
                ps = [PSB[4 * d + (i % 4)] for i in range(8)]
                PS = lambda i: ('ps', 4 * d + (i % 4))
                xk = [('X', d, c % 2)]
                srcs = [0, 1, 3, 4 + d, 6 + d, 8 + d]
                for i, s in enumerate(srcs):
                    p.dma(lambda e, Xd=Xd, i=i, s=s, tok0=tok0: e.dma_start(out=Xd[:, i, :],
                                                                           in_=S['tm'][tok0:tok0 + C, s, :]), w=xk)
                r_, v_, kk_, k_, b_, lw_ = [Xd[:, i, :] for i in range(6)]
                Vr = Vrs[d]
                p.op('act', lambda e, v_=v_: e.activation(out=RR(Vr[:]), in_=v_, func=AF.Copy), r=xk, w=[('Vr', d)])
                v_ = Vr[:]
                vk = [('Vr', d)]
                p.op('pe', lambda e, d=d, lw_=lw_: e.matmul(ps[0][0:64, :], tri[:, d, :], lw_, start=True, stop=True),
                     r=xk + ['tri'], w=[PS(0)])
                for h in range(8):
                    p.op('pe', lambda e, h=h, lw_=lw_: e.matmul(ps[1][0:64, h:h + 1], lw_[:, h * 64:(h + 1) * 64],
                                                               ones[:, 0:1], start=True, stop=True),
                         r=xk + ['ones1'], w=[PS(1)])
                p.op('act', lambda e: e.activation(out=PC[:], in_=ps[1][0:64, 0:8], func=AF.Exp), r=[PS(1)], w=[('PC', d)])
                p.op('act', lambda e: e.activation(out=E1[:], in_=ps[0][0:64, :], func=AF.Exp), r=[PS(0)], w=[('E1', d)])
                p.op('act', lambda e: e.activation(out=E2[:], in_=ps[0][0:64, :], func=AF.Exp, scale=-1.0),
                     r=[PS(0)], w=[('E2', d)])
                p.op('dve', lambda e, lw_=lw_: e.tensor_tensor(out=E0[:], in0=ps[0][0:64, :], in1=lw_, op=ALU.subtract),
                     r=[PS(0)] + xk, w=[('E0', d)])
                p.op('act', lambda e: e.activation(out=E0[:], in_=E0[:], func=AF.Exp), r=[('E0', d)], w=[('E0', d)])
                p.op('dve', lambda e, kk_=kk_: e.scalar_tensor_tensor(out=At[:], in0=kk_, scalar=-1.0, in1=E0[:],
                                                                      op0=ALU.mult, op1=ALU.mult),
                     r=xk + [('E0', d)], w=[('At', d)])
                p.op('dve', lambda e, r_=r_: e.tensor_tensor(out=Rt[:], in0=r_, in1=E1[:], op=ALU.mult),
                     r=xk + [('E1', d)], w=[('Rt', d)])
                p.op('dve', lambda e, b_=b_: e.tensor_tensor(out=RR(Bt[:]), in0=b_, in1=E2[:], op=ALU.mult),
                     r=xk + [('E2', d)], w=[('Bt', d)])
                p.op('dve', lambda e, k_=k_: e.tensor_tensor(out=RR(Kt[:]), in0=k_, in1=E2[:], op=ALU.mult),
                     r=xk + [('E2', d)], w=[('Kt', d)])
                for bank, src, key in ((2, At, ('At', d)), (3, Rt, ('Rt', d)), (4, Bt, ('Bt', d)), (5, Kt, ('Kt', d))):
                    for h in range(8):
                        p.op('pe', lambda e, bank=bank, src=src, h=h: e.transpose(
                            out=ps[bank][0:64, h * 64:(h + 1) * 64], in_=src[:, h * 64:(h + 1) * 64],
                            identity=ident_f[0:64, 0:64]), r=[key, 'identf'], w=[PS(bank)])
                p.op('act', lambda e: e.activation(out=RR(FAR[:, :, 0:64]), in_=v3(ps[2][0:64, :]), func=AF.Copy),
                     r=[PS(2)], w=[('FAR', d)])
                p.op('act', lambda e: e.activation(out=RR(FAR[:, :, 64:128]), in_=v3(ps[3][0:64, :]), func=AF.Copy),
                     r=[PS(3)], w=[('FAR', d)])
                p.op('dve', lambda e: e.tensor_copy(out=RR(FB[:]), in_=v3(ps[4][0:64, :])), r=[PS(4)], w=[('FB', d)])
                p.op('dve', lambda e: e.tensor_copy(out=RR(FK[:]), in_=v3(ps[5][0:64, :])), r=[PS(5)], w=[('FK', d)])
                for h in range(8):
                    bank = 6 + (h // 4)
                    p.op('pe', lambda e, h=h, bank=bank: mmr(e, ps[bank][0:64, (h % 4) * 128:(h % 4 + 1) * 128],
                                                                   FB[:, h, :], FAR[:, h, :], start=True, stop=True),
                         r=[('FB', d), ('FAR', d)], w=[PS(bank)])
                for hb in range(2):
                    p.op('dve', lambda e, hb=hb, d=d: e.tensor_tensor(
                        out=RR(G1[:, hb * 4:(hb + 1) * 4, :]), in0=ps[6 + hb][0:64, :].rearrange("p (h t) -> p h t", h=4),
                        in1=mg[:, d, :].unsqueeze(1).to_broadcast([64, 4, 128]), op=ALU.mult),
                        r=[PS(6 + hb), 'mg'], w=[('G1', d)])
                for h in range(8):
                    bank = 2 + (h // 4)
                    p.op('pe', lambda e, h=h, bank=bank: mmr(e, ps[bank][0:64, (h % 4) * 128:(h % 4 + 1) * 128],
                                                                   FK[:, h, :], FAR[:, h, :], start=True, stop=True),
                         r=[('FK', d), ('FAR', d)], w=[PS(bank)])
                for hb in range(2):
                    p.op('dve', lambda e, hb=hb, d=d: e.tensor_tensor(
                        out=RR(G2[:, hb * 4:(hb + 1) * 4, :]), in0=ps[2 + hb][0:64, :].rearrange("p (h t) -> p h t", h=4),
                        in1=mg[:, d, :].unsqueeze(1).to_broadcast([64, 4, 128]), op=ALU.mult),
                        r=[PS(2 + hb), 'mg'], w=[('G2', d)])
                for h in range(8):
                    p.op('pe', lambda e, h=h: mmr(e, ps[4][0:64, h * 64:(h + 1) * 64], FAR[:, h, 0:64], FB[:, h, :],
                                                       start=True, stop=True), r=[('FAR', d), ('FB', d)], w=[PS(4)])
                p.op('dve', lambda e, d=d: e.tensor_tensor(out=RR(Nm[0][:]), in0=v3(ps[4][0:64, :]),
                                                           in1=mn[:, d, :].unsqueeze(1).to_broadcast([64, 8, 64]),
                                                           op=ALU.mult), r=[PS(4), 'mn'], w=[('Nm', d, 0)])
                p.op('dve', lambda e: e.tensor_copy(out=RR(Tm[0][:]), in_=G1[:, :, 0:64]), r=[('G1', d)], w=[('Tm', d, 0)])
                p.op('dve', lambda e: e.tensor_tensor(out=RR(Z[:]), in0=G1[:, :, 0:64],
                                                      in1=ident_f[0:64, 0:64].unsqueeze(1).to_broadcast([64, 8, 64]),
                                                      op=ALU.add), r=[('G1', d), 'identf'], w=[('Z', d)])
                cur = 0
                for lev in range(5):
                    nxt = 1 - cur
                    last = lev == 4
                    for h in range(8):
                        p.op('pe', lambda e, h=h, cur=cur: mmr(e, ps[5][0:64, h * 64:(h + 1) * 64], Tm[cur][:, h, :],
                                                                    Nm[cur][:, h, :], start=True, stop=True),
                             r=[('Tm', d, cur), ('Nm', d, cur)], w=[PS(5)])
                    p.op('act', lambda e, nxt=nxt: e.activation(out=RR(Nm[nxt][:]), in_=v3(ps[5][0:64, :]), func=AF.Copy),
                         r=[PS(5)], w=[('Nm', d, nxt)])
                    if not last:
                        for h in range(8):
                            p.op('pe', lambda e, h=h, cur=cur: mmr(e, ps[6][0:64, h * 64:(h + 1) * 64],
                                                                        Nm[cur][:, h, :], Tm[cur][:, h, :],
                                                                        start=True, stop=True),
                                 r=[('Tm', d, cur), ('Nm', d, cur)], w=[PS(6)])
                        p.op('dve', lambda e, nxt=nxt: e.tensor_copy(out=RR(Tm[nxt][:]), in_=v3(ps[6][0:64, :])),
                             r=[PS(6)], w=[('Tm', d, nxt)])
                    for h in range(8):
                        p.op('pe', lambda e, h=h, nxt=nxt: mmr(e, ps[7][0:64, h * 64:(h + 1) * 64], Nm[nxt][:, h, :],
                                                                    Z[:, h, :], start=True, stop=True),
                             r=[('Nm', d, nxt), ('Z', d)], w=[PS(7)])
                    p.op('dve', lambda e: e.tensor_tensor(out=RR(Z[:]), in0=Z[:], in1=v3(ps[7][0:64, :]), op=ALU.add),
                         r=[PS(7), ('Z', d)], w=[('Z', d)])
                    cur = nxt
                Md = M[d]
                for h in range(8):
                    o = ps[0][0:64, h * 64:(h + 1) * 64]
                    p.op('pe', lambda e, h=h, o=o, Md=Md: mmr(e, o, FAR[:, h, 0:64], Md[:, h, :], start=True, stop=False),
                         r=[('FAR', d), ('M', d)], w=[PS(0)])
                    p.op('pe', lambda e, h=h, o=o, v_=v_: mmr(e, o, G2[:, h, 0:64], v_[:, h * 64:(h + 1) * 64],
                                                                   start=False, stop=True), r=[('G2', d)] + vk, w=[PS(0)])
                p.op('act', lambda e: e.activation(out=RR(Ws[:]), in_=ps[0][0:64, :], func=AF.Copy), r=[PS(0)], w=[('Ws', d)])
                for h in range(8):
                    p.op('pe', lambda e, h=h: mmr(e, ps[1][0:64, h * 64:(h + 1) * 64], Z[:, h, :],
                                                       Ws[:, h * 64:(h + 1) * 64], start=True, stop=True),
                         r=[('Z', d), ('Ws', d)], w=[PS(1)])
                p.op('act', lambda e: e.activation(out=RR(Us[:]), in_=ps[1][0:64, :], func=AF.Copy), r=[PS(1)], w=[('Us', d)])
                for h in range(8):
                    o = ps[2][0:64, h * 64:(h + 1) * 64]
                    hs = slice(h * 64, (h + 1) * 64)
                    p.op('pe', lambda e, h=h, o=o, Md=Md: mmr(e, o, FAR[:, h, 64:128], Md[:, h, :], start=True, stop=False),
                         r=[('FAR', d), ('M', d)], w=[PS(2)])
                    p.op('pe', lambda e, h=h, o=o, hs=hs: mmr(e, o, G1[:, h, 64:128], Us[:, hs], start=False, stop=False),
                         r=[('G1', d), ('Us', d)], w=[PS(2)])
                    p.op('pe', lambda e, h=h, o=o, hs=hs, v_=v_: mmr(e, o, G2[:, h, 64:128], v_[:, hs], start=False, stop=True),
                         r=[('G2', d)] + vk, w=[PS(2)])
                p.op('act', lambda e, d=d: e.activation(out=Ys[d][:], in_=ps[2][0:64, :], func=AF.Copy),
                     r=[PS(2)], w=[('Ys', d)])
                p.dma(lambda e, d=d, tok0=tok0: e.dma_start(out=S['y'][d, tok0:tok0 + C, :], in_=Ys[d][:]),
                      r=[('Ys', d)], w=[('Sy', d, c)])
                for h in range(8):
                    o = ps[3][0:64, h * 64:(h + 1) * 64]
                    hs = slice(h * 64, (h + 1) * 64)
                    p.op('pe', lambda e, o=o, hs=hs: mmr(e, o, Bt[:, hs], Us[:, hs], start=True, stop=False),
                         r=[('Bt', d), ('Us', d)], w=[PS(3)])
                    p.op('pe', lambda e, o=o, hs=hs, v_=v_: mmr(e, o, Kt[:, hs], v_[:, hs], start=False, stop=True),
                         r=[('Kt', d)] + vk, w=[PS(3)])
                Mt = Mts[d]
                p.op('dve', lambda e, Md=Md, Mt=Mt: e.tensor_tensor(out=Mt[:], in0=Md[:], in1=v3(ps[3][0:64, :]), op=ALU.add),
                     r=[PS(3), ('M', d)], w=[('Mt', d)])
                p.op('dve', lambda e, Md=Md, Mt=Mt: e.tensor_tensor(out=RR(Md[:]), in0=Mt[:],
                                                             in1=PC[:].unsqueeze(2).to_broadcast([64, 8, 64]),
                                                             op=ALU.mult), r=[('PC', d), ('Mt', d)], w=[('M', d)])
        cin = [[p.sb([128, 1, D], F32, f"cin{i}{q}") for q in range(2)] for i in range(2)]
        cout = [p.sb([128, 1, 2 * D], BF16, f"cout{i}") for i in range(2)]

        def conv_block(blk):
            b = blk % 2
            rows = slice(blk * 128, (blk + 1) * 128)
            for q, tabn in enumerate(('peer_u', 'peer_v')):
                p.dma(lambda e, b=b, q=q, tabn=tabn, rows=rows: e.dma_start(
                    out=cin[b][q][:], in_=I[tabn][l][rows, :].rearrange("(j p) d -> p j d", p=128)), w=[('cin', b, q)],
                    eng="pool")
                p.op('pool', lambda e, b=b, q=q: e.tensor_copy(out=cout[b][:, :, q * D:(q + 1) * D], in_=cin[b][q][:]),
                     r=[('cin', b, q)], w=[('cout', b, q)])
            p.dma(lambda e, b=b, rows=rows: e.dma_start(
                out=S['T'][l][rows, :].rearrange("(j p) d -> p j d", p=128), in_=cout[b][:]),
                r=[('cout', b, 0), ('cout', b, 1)], w=[('cout', b, 0), ('cout', b, 1)], eng="pool")

        nblk = 0
        for step in range(NCH):
            for d in range(2):
                scan_unit(d, order[d][step])
            for _ in range(4):
                if nblk < 128:
                    conv_block(nblk)
                    nblk += 1
        while nblk < 128:
            conv_block(nblk)
            nblk += 1
        p.barrier()


    def phase_rout(l):
        p.sb_reset(base_mark)
        with_ctx = l < DEPTH - 1
        wo = p.sb([128, 8, D], BF16, "wo")
        for j in range(8):
            p.dma(lambda e, j=j: e.dma_start(out=wo[:, j, :], in_=I['w_out'][l, j * 128:(j + 1) * 128, :]),
                  w=[('wo', j)], eng="pool")
        LNW = p.sb([128, 512], F32, "LNW")
        LNB = p.sb([128, 512], F32, "LNB")
        G1b = [p.sb([128, D], F32, f"G1b{s}") for s in range(2)]
        load_bc(LNW[:], I['r7_lnw'][l], 'LNW')
        load_bc(LNB[:], I['r7_lnb'][l], 'LNB')
        gn_eps = p.sb([128, 1], F32, "gneps")
        p.op('dve', lambda e: e.memset(gn_eps[:], 64e-5), w=['gneps'])
        for s in range(2):
            load_bc(G1b[s][:], S['mod'][l, s, 2 * D:3 * D], ('G1b', s))
        yb = [[p.sb([128, 512], F32, f"y{d}{i}") for d in range(2)] for i in range(2)]
        vg = [p.sb([128, 2, 512], F32, f"vg{i}") for i in range(2)]
        bon = [p.sb([128, 8], F32, f"bon{i}") for i in range(2)]
        O = [p.sb([128, D], F32, f"O{i}") for i in range(2)]
        Ob = [p.sb([128, D], BF16, f"Ob{i}") for i in range(2)]
        oT = [p.sb([128, 8, 128], BF16, f"oT{i}") for i in range(2)]
        xt = [p.sb([128, D], F32, f"xr{i}") for i in range(2)]
        yc = p.sb([128, 512], F32, "yc")
        sq = p.sb([128, 512], F32, "sq2")
        m8 = p.sb([128, 8], F32, "m8")
        v8 = p.sb([128, 8], F32, "v8")
        src = I['x'] if l == 0 else S['xs']
        v3 = lambda ap: ap.rearrange("p (h d) -> p h d", h=8)
        bc8 = lambda ap: ap.unsqueeze(2).to_broadcast([128, 8, 64])
        for t in range(NT):
            if t < 2 and not with_ctx:
                continue
            b = t % 2
            s = 1 if t < 2 else 0
            rows = slice(t * 128, (t + 1) * 128)
            for d in range(2):
                p.dma(lambda e, b=b, d=d, rows=rows: e.dma_start(out=yb[b][d][:], in_=S['y'][d, rows, :]), w=[('y', b, d)])
            p.dma(lambda e, b=b, rows=rows: e.dma_start(out=vg[b][:], in_=S['tm'][rows, 1:3, :]), w=[('vg', b)])
            p.dma(lambda e, b=b, rows=rows: e.dma_start(out=bon[b][:], in_=S['bon'][rows, :]), w=[('bon', b)])
            p.dma(lambda e, b=b, rows=rows: e.dma_start(out=O[b][:, 0:512], in_=S['o'][rows, 0:512]), w=[('O', b)])
            p.dma(lambda e, b=b, rows=rows: e.dma_start(out=xt[b][:], in_=src[rows, :]), w=[('xr', b)])
            p.op('dve', lambda e, b=b: e.tensor_tensor(out=yc[:], in0=yb[b][0][:], in1=yb[b][1][:], op=ALU.add),
                 r=[('y', b, 0), ('y', b, 1)], w=['yc'])
            p.op('dve', lambda e: e.tensor_reduce(out=m8[:], in_=v3(yc[:]), axis=AX.X, op=ALU.add), r=['yc'], w=['m8'])
            p.op('dve', lambda e: e.tensor_scalar(out=m8[:], in0=m8[:], scalar1=1.0 / 64, scalar2=None, op0=ALU.mult),
                 r=['m8'], w=['m8'])
            p.op('dve', lambda e: e.tensor_tensor(out=v3(yc[:]), in0=v3(yc[:]), in1=bc8(m8[:]), op=ALU.subtract),
                 r=['yc', 'm8'], w=['yc'])
            p.op('dve', lambda e: e.tensor_tensor(out=sq[:], in0=yc[:], in1=yc[:], op=ALU.mult), r=['yc'], w=['sq2'])
            p.op('dve', lambda e: e.tensor_reduce(out=v8[:], in_=v3(sq[:]), axis=AX.X, op=ALU.add), r=['sq2'], w=['v8'])
            p.op('act', lambda e: e.activation(out=v8[:], in_=v8[:], func=AF.Sqrt, bias=gn_eps[:], scale=1.0 / 64),
                 r=['v8', 'gneps'], w=['v8'])
            p.op('dve', lambda e: e.reciprocal(out=v8[:], in_=v8[:]), r=['v8'], w=['v8'])
            p.op('dve', lambda e: e.tensor_tensor(out=v3(yc[:]), in0=v3(yc[:]), in1=bc8(v8[:]), op=ALU.mult),
                 r=['yc', 'v8'], w=['yc'])
            p.op('dve', lambda e: e.tensor_tensor(out=yc[:], in0=yc[:], in1=LNW[:], op=ALU.mult), r=['yc', 'LNW'], w=['yc'])
            p.op('dve', lambda e: e.tensor_tensor(out=yc[:], in0=yc[:], in1=LNB[:], op=ALU.add), r=['yc', 'LNB'], w=['yc'])
            p.op('dve', lambda e, b=b: e.tensor_tensor(out=v3(sq[:]), in0=v3(vg[b][:, 0, :]), in1=bc8(bon[b][:]),
                                                       op=ALU.mult), r=[('vg', b), ('bon', b)], w=['sq2'])
            p.op('dve', lambda e: e.tensor_tensor(out=yc[:], in0=yc[:], in1=sq[:], op=ALU.add), r=['yc', 'sq2'], w=['yc'])
            p.op('dve', lambda e, b=b: e.tensor_tensor(out=O[b][:, 512:1024], in0=yc[:], in1=vg[b][:, 1, :], op=ALU.mult),
                 r=['yc', ('vg', b)], w=[('O2', b)])
            p.dma(lambda e, b=b, rows=rows: e.dma_start(out=S['o'][rows, 512:1024], in_=O[b][:, 512:1024]),
                  r=[('O2', b)], w=[('So2', t)])
            p.op('act', lambda e, b=b: e.activation(out=Ob[b][:], in_=O[b][:], func=AF.Copy),
                 r=[('O', b), ('O2', b)], w=[('Ob', b)])
            bank = 6 + b
            pv = ps[bank][:, 0:512].bitcast(BF16)
            for j in range(8):
                p.op('pe', lambda e, b=b, j=j, pv=pv: e.transpose(out=pv[:, j * 128:(j + 1) * 128],
                                                                 in_=Ob[b][:, j * 128:(j + 1) * 128], identity=ident_b[:]),
                     r=[('Ob', b), 'identb'], w=[PS(bank)])
            p.op('act', lambda e, b=b, pv=pv: e.activation(out=oT[b][:], in_=pv.rearrange("p (j t) -> p j t", j=8),
                                                            func=AF.Copy), r=[PS(bank)], w=[('oT', b)])
            for half in range(2):
                ybank = 2 * b + half
                for j in range(8):
                    p.op('pe', lambda e, b=b, j=j, half=half, ybank=ybank: e.matmul(
                        ps[ybank][:, :], oT[b][:, j, :], wo[:, j, half * 512:(half + 1) * 512],
                        start=(j == 0), stop=(j == 7)), r=[('oT', b), ('wo', j)], w=[PS(ybank)])
                cs_ = slice(half * 512, (half + 1) * 512)
                p.op('dve', lambda e, b=b, s=s, cs_=cs_, ybank=ybank: e.tensor_tensor(
                    out=O[b][:, cs_], in0=ps[ybank][:, :], in1=G1b[s][:, cs_], op=ALU.mult),
                    r=[PS(ybank), ('G1b', s), ('Ob', b), ('So2', t)], w=[('O', b), ('O2', b)])
                p.op('dve', lambda e, b=b, cs_=cs_: e.tensor_tensor(out=xt[b][:, cs_], in0=xt[b][:, cs_], in1=O[b][:, cs_],
                                                                   op=ALU.add), r=[('O', b), ('xr', b)], w=[('xr', b)])
            p.dma(lambda e, b=b, rows=rows: e.dma_start(out=S['xs'][rows, :], in_=xt[b][:]), r=[('xr', b)], w=[('Sxs', t)])
        p.barrier()

    def phase_peer(l):
        p.sb_reset(base_mark)
        last = l == DEPTH - 1
        eu_all = p.sb([128, NT, 128], U32, "eu_all")
        gate_all = p.sb([128, NT, 128], F32, "gate_all")
        G2b = [p.sb([128, D], F32, f"G2b{s}") for s in range(2)]
        m1 = p.sb_mark()
        hT = p.sb([128, 8, NTOK], BF16, "hT2")
        wq = p.sb([128, 8, 2048], BF16, "wq")
        for j in range(8):
            p.dma(lambda e, j=j: e.dma_start(out=wq[:, j, :], in_=I['peer_wq'][l, j * 128:(j + 1) * 128, :]),
                  w=[('wq', j)], eng="pool")
        keysT = p.sb([128, 16, 128], F32, "keysT")
        m0 = p.sb_mark()
        kraw = p.sb([128, 16, 128], F32, "kraw")
        p.dma(lambda e: e.dma_start(out=kraw[:], in_=I['peer_keys'][l].rearrange("h q n d -> n (h q) d")), w=['kraw'])
        for g in range(4):
            for i in range(4):
                hp = g * 4 + i
                p.op('pe', lambda e, g=g, i=i, hp=hp: e.transpose(out=ps[g][:, i * 128:(i + 1) * 128], in_=kraw[:, hp, :],
                                                                 identity=ident_f[:]), r=['kraw', 'identf'], w=[PS(g)])
            p.op('act', lambda e, g=g: e.activation(out=keysT[:, g * 4:(g + 1) * 4, :],
                                                    in_=ps[g][:, :].rearrange("p (i n) -> p i n", i=4), func=AF.Copy),
                 r=[PS(g)], w=['keysT'])
        p.barrier()
        p.sb_reset(m0)
        norm_tiles(l, 1, S['xs'], hT, lambda t: t * 128, tm_dram=S['h2'])
        p.barrier()
        p.sb_reset(m0)
        for s in range(2):
            load_bc(G2b[s][:], S['mod'][l, s, 5 * D:6 * D], ('G2b', s))
        qT = p.sb([128, 16, 128], F32, "qT")
        sc = p.sb([128, 16, 128], F32, "sc")
        sc2 = p.sb([128, 16, 128], F32, "sc2")
        sv = p.sb([128, 16, 16], F32, "sv")
        si = p.sb([128, 16, 16], U32, "si")
        sif = p.sb([128, 16, 16], F32, "sif")
        cand = p.sb([128, 8, 16, 16], F32, "cand")
        cand2 = p.sb([128, 8, 16, 16], F32, "cand2")
        eidx = p.sb([128, 8, 16, 16], F32, "eidx")
        best = p.sb([128, 8, 16], F32, "best")
        ci = p.sb([128, 8, 16], U32, "ci")
        cif = p.sb([128, 8, 16], F32, "cif")
        iota = p.sb([128, 256], F32, "iota")
        p.dma(lambda e: e.dma_start(out=iota[:], in_=I['iota']), w=['iota'])
        eq4 = p.sb([128, 8, 16, 16], F32, "eq4")
        cu = p.sb([128, 2, 8, 16], U32, "cu")
        cf = p.sb([128, 2, 8, 16], F32, "cf")
        e12 = p.sb([128, 2, 8, 16], F32, "e12")
        esel = p.sb([128, 128], F32, "esel")
        g8 = p.sb([128, 8], F32, "g8")
        tiles = [t for t in range(NT) if not (t < 2 and last)]
        for t in tiles:
            for g in range(4):
                for i in range(4):
                    hp = g * 4 + i
                    for j in range(8):
                        p.op('pe', lambda e, g=g, i=i, hp=hp, j=j, t=t: e.matmul(
                            ps[g][:, i * 128:(i + 1) * 128], wq[:, j, hp * 128:(hp + 1) * 128],
                            hT[:, j, t * 128:(t + 1) * 128], start=(j == 0), stop=(j == 7)),
                            r=[('hT', t), ('wq', j)], w=[PS(g)])
                p.op('act', lambda e, g=g: e.activation(out=qT[:, g * 4:(g + 1) * 4, :],
                                                        in_=ps[g][:, :].rearrange("p (i n) -> p i n", i=4), func=AF.Copy),
                     r=[PS(g)], w=['qT'])
            for g in range(4):
                for i in range(4):
                    hp = g * 4 + i
                    p.op('pe', lambda e, g=g, i=i, hp=hp: e.matmul(ps[4 + g][:, i * 128:(i + 1) * 128], qT[:, hp, :],
                                                                   keysT[:, hp, :], start=True, stop=True),
                         r=['qT', 'keysT'], w=[PS(4 + g)])
                p.op('act', lambda e, g=g: e.activation(out=sc[:, g * 4:(g + 1) * 4, :],
                                                        in_=ps[4 + g][:, :].rearrange("p (i n) -> p i n", i=4), func=AF.Copy),
                     r=[PS(4 + g)], w=['sc'])
            for hp in range(16):
                p.op('dve', lambda e, hp=hp: e.max(out=sv[:, hp, 0:8], in_=sc[:, hp, :]), r=['sc'], w=['sv'])
                p.op('dve', lambda e, hp=hp: e.max_index(out=si[:, hp, 0:8], in_max=sv[:, hp, 0:8], in_values=sc[:, hp, :]),
                     r=['sc', 'sv'], w=['si'])
                p.op('dve', lambda e, hp=hp: e.match_replace(out=sc2[:, hp, :], in_to_replace=sv[:, hp, 0:8],
                                                             in_values=sc[:, hp, :], imm_value=-1e30),
                     r=['sc', 'sv'], w=['sc2'])
                p.op('dve', lambda e, hp=hp: e.max(out=sv[:, hp, 8:16], in_=sc2[:, hp, :]), r=['sc2'], w=['sv'])
                p.op('dve', lambda e, hp=hp: e.max_index(out=si[:, hp, 8:16], in_max=sv[:, hp, 8:16], in_values=sc2[:, hp, :]),
                     r=['sc2', 'sv'], w=['si'])
            p.op('dve', lambda e: e.tensor_copy(out=sif[:], in_=si[:]), r=['si'], w=['sif'])
            svv = sv[:].rearrange("p (h q) k -> p h q k", q=2)
            sfv = sif[:].rearrange("p (h q) k -> p h q k", q=2)
            p.op('dve', lambda e, svv=svv: e.tensor_tensor(
                out=cand[:], in0=svv[:, :, 0, :].unsqueeze(3).to_broadcast([128, 8, 16, 16]),
                in1=svv[:, :, 1, :].unsqueeze(2).to_broadcast([128, 8, 16, 16]), op=ALU.add), r=['sv'], w=['cand'])
            p.op('dve', lambda e, sfv=sfv: e.tensor_scalar(out=sfv[:, :, 0, :], in0=sfv[:, :, 0, :], scalar1=128.0,
                                                           scalar2=None, op0=ALU.mult), r=['sif'], w=['sif'])
            for h in range(8):
                ch = cand[:, h].rearrange("p a b -> p (a b)")
                ch2 = cand2[:, h].rearrange("p a b -> p (a b)")
                p.op('dve', lambda e, h=h, ch=ch: e.max(out=best[:, h, 0:8], in_=ch), r=['cand'], w=['best'])
                p.op('dve', lambda e, h=h, ch=ch: e.max_index(out=ci[:, h, 0:8], in_max=best[:, h, 0:8], in_values=ch),
                     r=['cand', 'best'], w=['ci'])
                p.op('dve', lambda e, h=h, ch=ch, ch2=ch2: e.match_replace(out=ch2, in_to_replace=best[:, h, 0:8],
                                                                           in_values=ch, imm_value=-1e30),
                     r=['cand', 'best'], w=['cand2'])
                p.op('dve', lambda e, h=h, ch2=ch2: e.max(out=best[:, h, 8:16], in_=ch2), r=['cand2'], w=['best'])
                p.op('dve', lambda e, h=h, ch2=ch2: e.max_index(out=ci[:, h, 8:16], in_max=best[:, h, 8:16], in_values=ch2),
                     r=['cand2', 'best'], w=['ci'])
            p.op('dve', lambda e: e.tensor_scalar(out=cu[:, 0], in0=ci[:], scalar1=4, scalar2=None,
                                                  op0=ALU.logical_shift_right), r=['ci'], w=['cu'])
            p.op('dve', lambda e: e.tensor_scalar(out=cu[:, 1], in0=ci[:], scalar1=15, scalar2=None,
                                                  op0=ALU.bitwise_and), r=['ci'], w=['cu'])
            p.op('dve', lambda e: e.tensor_copy(out=cf[:], in_=cu[:]), r=['cu'], w=['cf'])
            io16 = iota[:, 0:16].unsqueeze(1).unsqueeze(1).to_broadcast([128, 8, 16, 16])
            for q in range(2):
                p.op('dve', lambda e, q=q, io16=io16: e.tensor_tensor(
                    out=eq4[:], in0=io16, in1=cf[:, q].unsqueeze(3).to_broadcast([128, 8, 16, 16]), op=ALU.is_equal),
                    r=['iota', 'cf'], w=['eq4'])
                p.op('dve', lambda e, q=q, sfv=sfv: e.tensor_tensor(
                    out=eq4[:], in0=eq4[:], in1=sfv[:, :, q, :].unsqueeze(2).to_broadcast([128, 8, 16, 16]), op=ALU.mult),
                    r=['eq4', 'sif'], w=['eq4'])
                p.op('dve', lambda e, q=q: e.tensor_reduce(out=e12[:, q], in_=eq4[:], axis=AX.X, op=ALU.add),
                     r=['eq4'], w=['e12'])
            p.op('dve', lambda e: e.tensor_tensor(out=esel[:], in0=e12[:, 0].rearrange("p h k -> p (h k)"),
                                                  in1=e12[:, 1].rearrange("p h k -> p (h k)"), op=ALU.add),
                 r=['e12'], w=['esel'])
            p.op('dve', lambda e, t=t: e.tensor_copy(out=eu_all[:, t, :], in_=esel[:]), r=['esel'], w=[('eu', t)])
            gv = gate_all[:, t, :].rearrange("p (h k) -> p h k", h=8)
            p.op('dve', lambda e, gv=gv: e.tensor_tensor(out=gv, in0=best[:],
                                                         in1=best[:, :, 0:1].to_broadcast([128, 8, 16]), op=ALU.subtract),
                 r=['best'], w=[('gate', t)])
            p.op('act', lambda e, t=t: e.activation(out=gate_all[:, t, :], in_=gate_all[:, t, :], func=AF.Exp),
                 r=[('gate', t)], w=[('gate', t)])
            p.op('dve', lambda e, gv=gv: e.tensor_reduce(out=g8[:], in_=gv, axis=AX.X, op=ALU.add), r=[('gate', t)], w=['g8'])
            p.op('dve', lambda e: e.reciprocal(out=g8[:], in_=g8[:]), r=['g8'], w=['g8'])
            p.op('dve', lambda e, gv=gv: e.tensor_tensor(out=gv, in0=gv, in1=g8[:].unsqueeze(2).to_broadcast([128, 8, 16]),
                                                         op=ALU.mult), r=[('gate', t), 'g8'], w=[('gate', t)])
        p.barrier()
        p.sb_reset(m1)
        h2 = [p.sb([128, D], F32, f"h2{i}") for i in range(2)]
        xt = [p.sb([128, D], F32, f"xp{i}") for i in range(2)]
        act = [p.sb([128, 128], F32, f"actv{i}") for i in range(2)]
        wg = [p.sb([128, 128], F32, f"wg{i}") for i in range(2)]
        NACC = 1
        acc = [[p.sb([128, D], F32, f"acc{i}{k}") for k in range(NACC)] for i in range(2)]
        junk = p.sb([128, D], BF16, "pjunk")
        NG = 32
        GS = 8
        gbuf = [p.sb([128, 2 * D], BF16, f"gb{i}") for i in range(NG)]
        NDG = 8
        dg = [p.sb([128, 128], BF16, f"dg{i}") for i in range(NDG)]
        gi = 0
        di = 0
        for t in tiles:
            b = t % 2
            s = 1 if t < 2 else 0
            rows = slice(t * 128, (t + 1) * 128)
            p.dma(lambda e, b=b, rows=rows: e.dma_start(out=h2[b][:], in_=S['h2'][rows, :]), w=[('h2', b)])
            p.dma(lambda e, b=b, rows=rows: e.dma_start(out=xt[b][:], in_=S['xs'][rows, :]), w=[('xp', b)])
            p.op('dve', lambda e, b=b: e.memset(act[b][:], 0.0), w=[('actv', b)])
            for g in range(128 // GS):
                ks = []
                for sidx in range(g * GS, (g + 1) * GS):
                    k = gi % NG
                    gi += 1
                    ks.append(k)
                    p.dma(lambda e, k=k, t=t, sidx=sidx: e.indirect_dma_start(
                        out=gbuf[k][:], out_offset=None, in_=S['T'][l],
                        in_offset=bass.IndirectOffsetOnAxis(ap=eu_all[:, t, sidx:sidx + 1], axis=0)),
                        r=[], w=[('gb', k)], eng="pool")
                    p.op('dve', lambda e, k=k, b=b, sidx=sidx: e.scalar_tensor_tensor(
                        out=junk[:], in0=gbuf[k][:, 0:D], scalar=1.0, in1=h2[b][:], op0=ALU.mult, op1=ALU.mult,
                        accum_out=act[b][:, sidx:sidx + 1]), r=[('gb', k), ('h2', b), ('actv', b)], w=[('actc', b, sidx)])
                gs = slice(g * GS, (g + 1) * GS)
                p.op('act', lambda e, b=b, gs=gs: e.activation(out=wg[b][:, gs], in_=act[b][:, gs], func=AF.Gelu),
                     r=[('actc', b, sidx) for sidx in range(g * GS, (g + 1) * GS)], w=[('wg', b, g)])
                p.op('dve', lambda e, b=b, gs=gs, t=t: e.tensor_tensor(out=wg[b][:, gs], in0=wg[b][:, gs],
                                                                      in1=gate_all[:, t, gs], op=ALU.mult),
                     r=[('wg', b, g)], w=[('wg', b, g)])
                for j, sidx in enumerate(range(g * GS, (g + 1) * GS)):
                    k = ks[j]
                    dj = di % NDG
                    di += 1
                    p.op('act', lambda e, dj=dj, b=b, sidx=sidx: e.activation(
                        out=dg[dj][:], in_=ident_f[:], func=AF.Copy, scale=wg[b][:, sidx:sidx + 1]),
                        r=[('wg', b, g), 'identf'], w=[('dg', dj)])
                    for half in range(2):
                        bank = 2 * b + half
                        p.op('pe', lambda e, dj=dj, k=k, half=half, bank=bank, sidx=sidx: e.matmul(
                            ps[bank][:, :], dg[dj][:], gbuf[k][:, D + half * 512:D + (half + 1) * 512],
                            start=(sidx == 0), stop=(sidx == 127)), r=[('dg', dj), ('gb', k)], w=[PS(bank), ('gbr', k, half)])
            a0 = acc[b][0]
            for half in range(2):
                hs_ = slice(half * 512, (half + 1) * 512)
                p.op('dve', lambda e, a0=a0, s=s, b=b, half=half, hs_=hs_: e.tensor_tensor(
                    out=a0[:, hs_], in0=ps[2 * b + half][:, :], in1=G2b[s][:, hs_], op=ALU.mult),
                    r=[PS(2 * b + half), ('G2b', s)], w=[('acc', b, 0)])
            p.op('dve', lambda e, a0=a0, b=b: e.tensor_tensor(out=xt[b][:], in0=xt[b][:], in1=a0[:], op=ALU.add),
                 r=[('acc', b, 0), ('xp', b)], w=[('xp', b)])
            if last:
                p.dma(lambda e, b=b, t=t: e.dma_start(out=out_d[(t - 2) * 128:(t - 1) * 128, :], in_=xt[b][:]),
                      r=[('xp', b)], w=[('outd', t)])
            else:
                p.dma(lambda e, b=b, rows=rows: e.dma_start(out=S['xs'][rows, :], in_=xt[b][:]),
                      r=[('xp', b)], w=[('Sxs', t)])
        p.barrier()

    PHASES = cfg.get("phases", ["proj", "rprep", "scan", "rout", "peer"])

    phase_mod()
    for l in range(cfg.get("layers", DEPTH)):
        if 'proj' in PHASES:
            qkT, Vaug, mp = phase_proj(l)
            phase_attn(l, qkT, Vaug, mp)
        if 'rprep' in PHASES:
            phase_rprep(l)
        if 'scan' in PHASES:
            phase_scan(l)
        if 'rout' in PHASES:
            phase_rout(l)
        if 'peer' in PHASES:
            phase_peer(l)
    p.barrier()
    p.emit()
    return nc


def prep_inputs(inputs):
    f = lambda a: np.ascontiguousarray(np.asarray(a, dtype=np.float32))
    x, c, ctx, c_ctx = f(inputs['x']), f(inputs['c']), f(inputs['ctx']), f(inputs['c_ctx'])
    shared = {}
    for n in ['norm_mix', 'norm_ffn', 'w_mod', 'b_mod', 'w_in', 'w_out', 'a_qnorm', 'a_knorm', 'b_qnorm', 'b_knorm',
              'a_sink']:
        shared[n] = f(inputs[n])
    rpb = f(inputs['b_rpb'])
    btab = np.zeros((DEPTH, 128, NTAB, 4, 128), np.float32)
    bmask = np.zeros((128, NTAB, 128), np.float32)
    for i, (dr, dc, valid) in enumerate(NA_TABS):
        g = rpb[:, :, dr, dc]
        btab[:, :, i, :, :] = np.where(valid[None, None], g, 0.0).transpose(0, 2, 1, 3)
        bmask[:, i, :] = valid
    shared['btab'] = btab
    shared['bmask'] = bmask
    ar = np.arange(128)
    am = np.zeros((128, 2, 128), np.float32)
    am[:, 0, :] = (ar[:, None] >= ar[None, :])
    am[:, 1, :] = (ar[:, None] <= ar[None, :])
    shared['amask'] = am
    shared['ident'] = np.eye(128, dtype=np.float32)
    cos, sin = rope_tables()
    shared['cos'], shared['sin'] = cos, sin
    rc = f(inputs['r7_conv'])
    for n in ['r7_w0', 'r7_a0', 'r7_w2', 'r7_a2', 'r7_g2', 'r7_kk', 'r7_ka', 'r7_lnw', 'r7_lnb', 'r7_rk', 'peer_wq', 'peer_keys']:
        shared[n] = f(inputs[n])
    for l in range(DEPTH):
        shared[f'peer_u{l}'] = f(inputs['peer_u'][l])
        shared[f'peer_v{l}'] = f(inputs['peer_v'][l])
    a64 = np.arange(64)
    tri = np.zeros((64, 2, 64), np.float32)
    tri[:, 0, :] = a64[:, None] <= a64[None, :]
    tri[:, 1, :] = a64[:, None] >= a64[None, :]
    mg = np.zeros((64, 2, 128), np.float32)
    mg[:, 0, 0:64] = a64[:, None] < a64[None, :]
    mg[:, 0, 64:128] = a64[:, None] <= a64[None, :]
    mg[:, 1, 0:64] = a64[:, None] > a64[None, :]
    mg[:, 1, 64:128] = a64[:, None] >= a64[None, :]
    mn = np.zeros((64, 2, 64), np.float32)
    mn[:, 0, :] = a64[None, :] < a64[:, None]
    mn[:, 1, :] = a64[None, :] > a64[:, None]
    shared['tri'], shared['mg'], shared['mn'] = tri, mg, mn
    shared['iota'] = np.ascontiguousarray(np.broadcast_to(np.arange(256, dtype=np.float32), (128, 256)))
    shared['r7_conv'] = np.ascontiguousarray(rc.reshape(DEPTH, 3, 15, 128).transpose(0, 3, 2, 1))
    maps = []
    for b in range(8):
        m = dict(shared)
        m['x'] = np.ascontiguousarray(np.concatenate([ctx[b], x[b]], axis=0))
        cc = np.stack([c[b], c_ctx], axis=-1)
        m['cc'] = np.ascontiguousarray(cc.reshape(8, 128, 2).transpose(1, 0, 2))
        maps.append(m)
    return maps


_NC_CACHE = {}


def kernel(**inputs):
    if 'nc' not in _NC_CACHE:
        _NC_CACHE['nc'] = build({})
    nc = _NC_CACHE['nc']
    maps = prep_inputs(inputs)
    res = run_bass_kernel_spmd(nc, maps, core_ids=list(range(8)))
    return np.stack([np.asarray(r['out'], dtype=np.float32) for r in res.results], axis=0)
```

```python
import numpy as np
import ml_dtypes
import concourse.bass as bass
import concourse.mybir as mybir
from concourse.bass_utils import run_bass_kernel_spmd

F32 = mybir.dt.float32
BF16 = mybir.dt.bfloat16
U32 = mybir.dt.uint32
I32 = mybir.dt.int32
AF = mybir.ActivationFunctionType
ALU = mybir.AluOpType
AX = mybir.AxisListType

ENGS = ["pe", "act", "dve", "pool", "sp"]
DT_SIZE = {F32: 4, BF16: 2, U32: 4, I32: 4}

D = 1024
NCTX = 256
NLAT = 2048
NTOK = NCTX + NLAT
NT = NTOK // 128
DEPTH = 2
EPS = 1e-6


class Prog:
    def __init__(self, nc, n_dma_sems=32):
        self.nc = nc
        self.ops = {e: [] for e in ENGS}
        self.cnt = {e: 0 for e in ENGS}
        self.waited = {e: {} for e in ENGS}
        self.res = {}
        self.n_dma_sems = n_dma_sems
        self.dma_use = [0] * n_dma_sems
        self.dma_last = [None] * n_dma_sems
        self.dma_rr = 0
        self.sb_off = 16 * 1024
        self.sb_id = 0
        self.SB_CAP = 216 * 1024

    def sb_mark(self):
        return self.sb_off

    def sb_reset(self, off=0):
        self.sb_off = off

    def sb(self, shape, dtype, name=""):
        nbytes = int(np.prod(shape[1:])) * DT_SIZE[dtype]
        off = (self.sb_off + 63) // 64 * 64
        assert off + nbytes <= self.SB_CAP, f"SBUF overflow {off}+{nbytes} ({name})"
        self.sb_off = off + nbytes
        self.sb_id += 1
        return self.nc.alloc_sbuf_tensor_at(f"sb{self.sb_id}_{name}", list(shape), dtype, offset=off)

    def _deps(self, r, w):
        deps = []
        for k in r:
            st = self.res.get(k)
            if st and st[0] is not None:
                deps.append(st[0])
        for k in w:
            st = self.res.get(k)
            if st:
                if st[0] is not None:
                    deps.append(st[0])
                deps.extend(st[1])
        return deps

    def _commit(self, tok, r, w):
        for k in r:
            st = self.res.setdefault(k, [None, []])
            st[1].append(tok)
        for k in w:
            self.res[k] = [tok, []]

    def _waits_for(self, eng, deps):
        wd = self.waited[eng]
        best = {}
        for t in deps:
            if t[0] == 'c':
                if t[1] == eng and eng == 'pe':
                    continue
                key = ('c', t[1])
            else:
                key = ('d', t[1])
            if wd.get(key, 0) >= t[2]:
                continue
            best[key] = max(best.get(key, 0), t[2])
        for k, v in best.items():
            wd[k] = v
        return list(best.items())

    def op(self, eng, fn, r=(), w=()):
        deps = self._deps(r, w)
        waits = self._waits_for(eng, deps)
        self.cnt[eng] += 1
        tok = ('c', eng, self.cnt[eng])
        self.ops[eng].append((waits, fn, ('c', eng), 1))
        self._commit(tok, r, w)
        return tok

    def dma(self, fn, r=(), w=(), eng="sp"):
        deps = list(self._deps(r, w))
        i = self.dma_rr
        self.dma_rr = (self.dma_rr + 1) % self.n_dma_sems
        if self.dma_last[i] is not None:
            deps.append(self.dma_last[i])
        waits = self._waits_for(eng, deps)
        self.dma_use[i] += 1
        tok = ('d', i, 16 * self.dma_use[i])
        self.dma_last[i] = tok
        self.ops[eng].append((waits, fn, ('d', i), 16))
        self._commit(tok, r, w)
        return tok

    def barrier(self):
        toks = [('c', e, self.cnt[e]) for e in ENGS if self.cnt[e] > 0]
        toks += [t for t in self.dma_last if t is not None]
        for e in ENGS:
            waits = self._waits_for(e, toks)
            if waits:
                self.ops[e].append((waits, None, None, 0))
        self.res = {}

    def emit(self):
        nc = self.nc
        from contextlib import ExitStack
        with ExitStack() as es:
            csem = {e: es.enter_context(nc.semaphore(f"c_{e}")) for e in ENGS}
            dsem = [es.enter_context(nc.semaphore(f"d_{i}")) for i in range(self.n_dma_sems)]
            block = es.enter_context(nc.Block())

            def sem_of(key):
                return csem[key[1]] if key[0] == 'c' else dsem[key[1]]

            def run(engname, e):
                for waits, fn, inc_key, inc in self.ops[engname]:
                    for k, v in waits:
                        e.wait_ge(sem_of(k), v)
                    if fn is None:
                        continue
                    ins = fn(e)
                    ins.then_inc(sem_of(inc_key), inc)

            @block.tensor
            def _(e):
                run("pe", e)

            @block.scalar
            def _(e):
                run("act", e)

            @block.vector
            def _(e):
                run("dve", e)

            @block.gpsimd
            def _(e):
                run("pool", e)

            @block.sync
            def _(e):
                run("sp", e)


def na_tables():
    cases = {}
    tabs = []
    keys = {}
    ar = np.arange(128)
    for p in range(16):
        for kb in range(16):
            krow = 2 * kb + ar // 64
            kcol = ar % 64
            qrow = 2 * p + ar // 64
            qcol = ar % 64
            rs = np.clip(qrow - 4, 0, 24)
            vr = (krow[:, None] >= rs[None, :]) & (krow[:, None] < rs[None, :] + 8)
            ws = np.clip(qcol - 8, 0, 48)
            vc = (kcol[:, None] >= ws[None, :]) & (kcol[:, None] < ws[None, :] + 16)
            valid = vr & vc
            if not valid.any():
                continue
            dr = krow[:, None] - qrow[None, :] + 7
            dc = np.clip(kcol[:, None] - qcol[None, :] + 15, 0, 30)
            dr = np.where(valid, dr, 0)
            dc = np.where(valid, dc, 0)
            key = (dr.tobytes(), dc.tobytes(), valid.tobytes())
            if key not in keys:
                keys[key] = len(tabs)
                tabs.append((dr, dc, valid))
            cases[(p, kb)] = keys[key]
    return cases, tabs


NA_CASES, NA_TABS = na_tables()
NTAB = len(NA_TABS)


def rope_tables():
    t = np.arange(NLAT)
    inv_freq = 10000.0 ** (-np.arange(0, 32, 2) / 32)
    ang = np.stack([(t // 64)[:, None] * inv_freq[None], (t % 64)[:, None] * inv_freq[None]], axis=1)
    return np.cos(ang).astype(np.float32).reshape(NLAT, 32), np.sin(ang).astype(np.float32).reshape(NLAT, 32)


def build(cfg=None):
    cfg = cfg or {}
    dbg = cfg.get("dbg", [])
    nc = bass.Bass("TRN2", target_bir_lowering=False)
    p = Prog(nc)

    def din(name, shape, dt=F32):
        return nc.dram_tensor(name, list(shape), dt, kind="ExternalInput").ap()

    def dscr(name, shape, dt=F32):
        kind = "Internal"
        if name in cfg.get("dump", []):
            kind = "ExternalOutput"
        if name in cfg.get("feed", []):
            kind = "ExternalInput"
        return nc.dram_tensor(name, list(shape), dt, kind=kind).ap()

    I = {}
    I['x'] = din('x', [NTOK, D])
    I['cc'] = din('cc', [128, 8, 2])
    I['norm_mix'] = din('norm_mix', [DEPTH, D])
    I['norm_ffn'] = din('norm_ffn', [DEPTH, D])
    I['w_mod'] = din('w_mod', [DEPTH, D, 6 * D])
    I['b_mod'] = din('b_mod', [DEPTH, 6 * D])
    I['w_in'] = din('w_in', [DEPTH, D, 3200])
    I['w_out'] = din('w_out', [DEPTH, D, D])
    for n in ['a_qnorm', 'a_knorm', 'b_qnorm', 'b_knorm']:
        I[n] = din(n, [DEPTH, 64])
    I['a_sink'] = din('a_sink', [DEPTH, 4])
    I['btab'] = din('btab', [DEPTH, 128, NTAB, 4, 128])
    I['bmask'] = din('bmask', [128, NTAB, 128])
    I['amask'] = din('amask', [128, 2, 128])
    I['ident'] = din('ident', [128, 128])
    I['cos'] = din('cos', [NLAT, 32])
    I['sin'] = din('sin', [NLAT, 32])
    I['r7_conv'] = din('r7_conv', [DEPTH, 128, 15, 3])
    I['r7_w0'] = din('r7_w0', [DEPTH, 2, 512])
    I['r7_a0'] = din('r7_a0', [DEPTH, 2, 512])
    I['r7_w2'] = din('r7_w2', [DEPTH, 2, 64, 512])
    I['r7_a2'] = din('r7_a2', [DEPTH, 2, 64, 512])
    I['r7_g2'] = din('r7_g2', [DEPTH, 128, 512])
    for n in ['r7_kk', 'r7_ka', 'r7_lnw', 'r7_lnb']:
        I[n] = din(n, [DEPTH, 512])
    I['r7_rk'] = din('r7_rk', [DEPTH, 8, 64])
    I['peer_wq'] = din('peer_wq', [DEPTH, D, 2048])
    I['peer_keys'] = din('peer_keys', [DEPTH, 8, 2, 128, 128])
    I['peer_u'] = [din(f'peer_u{l}', [16384, D]) for l in range(DEPTH)]
    I['peer_v'] = [din(f'peer_v{l}', [16384, D]) for l in range(DEPTH)]
    I['iota'] = din('iota', [128, 256])
    I['tri'] = din('tri', [64, 2, 64])
    I['mg'] = din('mg', [64, 2, 128])
    I['mn'] = din('mn', [64, 2, 64])
    out_d = nc.dram_tensor('out', [NLAT, D], F32, kind="ExternalOutput").ap()

    S = {}
    S['mod'] = dscr('s_mod', [DEPTH, 2, 6 * D])
    S['xs'] = dscr('s_xs', [NTOK, D])
    S['o'] = dscr('s_o', [NTOK, D])
    S['pcT'] = dscr('s_pcT', [1920, NTOK])
    S['tm'] = dscr('s_tm', [NTOK, 10, 512])
    S['bon'] = dscr('s_bon', [NTOK, 8])
    S['y'] = dscr('s_y', [2, NTOK, 512])
    S['h2'] = dscr('s_h2', [NTOK, D])
    S['T'] = [dscr(f's_T{l}', [16384, 2 * D], BF16) for l in range(DEPTH)]
    DBG = {}
    for name, shape in cfg.get("dbg_out", {}).items():
        DBG[name] = nc.dram_tensor(name, list(shape), F32, kind="ExternalOutput").ap()

    ps = [nc.alloc_psum_tensor(f"ps{i}", [128, 512], F32) for i in range(8)]

    def PS(i):
        return ('ps', i)

    ident_f = p.sb([128, 128], F32, "identf")
    ident_b = p.sb([128, 128], BF16, "identb")
    eps_col = p.sb([128, 1], F32, "eps")
    p.dma(lambda e: e.dma_start(out=ident_f[:], in_=I['ident']), w=['identf'])
    p.op('dve', lambda e: e.tensor_copy(out=ident_b[:], in_=ident_f[:]), r=['identf'], w=['identb'])
    p.op('dve', lambda e: e.memset(eps_col[:], EPS), w=['eps'])
    p.barrier()
    base_mark = p.sb_mark()

    def phase_mod():
        p.sb_reset(base_mark)
        cc = p.sb([128, 8, 2], F32, "cc")
        scc = p.sb([128, 8, 2], F32, "scc")
        p.dma(lambda e: e.dma_start(out=cc[:], in_=I['cc']), w=['cc'])
        p.op('act', lambda e: e.activation(out=scc[:], in_=cc[:], func=AF.Silu), r=['cc'], w=['scc'])
        wt = [p.sb([128, 8, 512], F32, f"wmod{i}") for i in range(2)]
        bm = p.sb([2, 6 * D], F32, "bm")
        mo = p.sb([2, 6 * D], F32, "mo")
        k = 0
        for l in range(DEPTH):
            p.dma(lambda e, l=l: e.dma_start(out=bm[:], in_=I['b_mod'][l].partition_broadcast(2)),
                  w=['bm'])
            for cch in range(12):
                b = k % 2
                k += 1
                src = I['w_mod'][l, :, cch * 512:(cch + 1) * 512].rearrange("(j p) n -> p j n", p=128)
                p.dma(lambda e, b=b, src=src: e.dma_start(out=wt[b][:], in_=src), w=[('wmod', b)])
                pb = cch % 2
                for j in range(8):
                    p.op('pe', lambda e, b=b, j=j, pb=pb: e.matmul(ps[pb][0:2, :], scc[:, j, :], wt[b][:, j, :],
                                                                    start=(j == 0), stop=(j == 7)),
                         r=['scc', ('wmod', b)], w=[PS(pb)])
                p.op('dve', lambda e, pb=pb, cch=cch: e.tensor_tensor(
                    out=mo[:, cch * 512:(cch + 1) * 512], in0=ps[pb][0:2, :], in1=bm[:, cch * 512:(cch + 1) * 512],
                    op=ALU.add), r=[PS(pb), 'bm'], w=['mo'])
            p.dma(lambda e, l=l: e.dma_start(out=S['mod'][l], in_=mo[:]), r=['mo'], w=['S_mod'])
        p.barrier()

    def load_bc(dst, src_1d, key):
        P = dst.shape[0]
        p.dma(lambda e: e.dma_start(out=dst, in_=src_1d.partition_broadcast(P)), w=[key])

    def norm_tiles(l, which, src, hT, hT_off, tm_dram=None):
        nv = I['norm_mix'] if which == 0 else I['norm_ffn']
        so = 0 if which == 0 else 3
        G = [p.sb([128, D], F32, f"G{s}") for s in range(2)]
        SH = [p.sb([128, D], F32, f"SH{s}") for s in range(2)]
        tmp = p.sb([128, D], F32, "gtmp")
        for s in range(2):
            load_bc(tmp[:], nv[l], 'gtmp')
            load_bc(G[s][:], S['mod'][l, s, (so + 1) * D:(so + 2) * D], ('G', s))
            load_bc(SH[s][:], S['mod'][l, s, so * D:(so + 1) * D], ('SH', s))
            p.op('dve', lambda e, s=s: e.scalar_tensor_tensor(out=G[s][:], in0=G[s][:], scalar=1.0, in1=tmp[:],
                                                             op0=ALU.add, op1=ALU.mult),
                 r=['gtmp', ('G', s)], w=[('G', s)])
        NBUF = 4
        xt = [p.sb([128, D], F32, f"xt{i}") for i in range(NBUF)]
        junk = p.sb([128, D], F32, "junk")
        hb = [p.sb([128, D], BF16, f"hb{i}") for i in range(NBUF)]
        ss = [p.sb([128, 1], F32, f"ss{i}") for i in range(NBUF)]
        def stageA(t):
            b = t % NBUF
            s = 1 if t < 2 else 0
            p.dma(lambda e, b=b, t=t: e.dma_start(out=xt[b][:], in_=src[t * 128:(t + 1) * 128, :]), w=[('xt', b)])
            p.op('act', lambda e, b=b: e.activation(out=junk[:], in_=xt[b][:], func=AF.Square, accum_out=ss[b][:]),
                 r=[('xt', b)], w=[('ss', b)])
            p.op('act', lambda e, b=b: e.activation(out=ss[b][:], in_=ss[b][:], func=AF.Sqrt, bias=eps_col[:],
                                                    scale=1.0 / D), r=[('ss', b)], w=[('ss', b)])
            p.op('dve', lambda e, b=b: e.reciprocal(out=ss[b][:], in_=ss[b][:]), r=[('ss', b)], w=[('ss', b)])
            p.op('dve', lambda e, b=b, s=s: e.scalar_tensor_tensor(out=xt[b][:], in0=xt[b][:], scalar=ss[b][:, 0:1],
                                                                 in1=G[s][:], op0=ALU.mult, op1=ALU.mult),
                 r=[('xt', b), ('ss', b), ('G', s)], w=[('xt', b)])
            if tm_dram is not None:
                p.op('dve', lambda e, b=b, s=s: e.tensor_tensor(out=xt[b][:], in0=xt[b][:], in1=SH[s][:], op=ALU.add),
                     r=[('xt', b), ('SH', s)], w=[('xt', b)])
                p.dma(lambda e, b=b, t=t: e.dma_start(out=tm_dram[t * 128:(t + 1) * 128, :], in_=xt[b][:]),
                      r=[('xt', b)], w=[('tmd', t)])
                p.op('act', lambda e, b=b: e.activation(out=hb[b][:], in_=xt[b][:], func=AF.Copy),
                     r=[('xt', b)], w=[('hb', b)])
            else:
                p.op('dve', lambda e, b=b, s=s: e.tensor_tensor(out=hb[b][:], in0=xt[b][:], in1=SH[s][:], op=ALU.add),
                     r=[('xt', b), ('SH', s)], w=[('hb', b)])

        def stageB(t):
            b = t % NBUF
            pbank = 4 + b
            pv = ps[pbank][:, 0:512].bitcast(BF16)
            for j in range(8):
                p.op('pe', lambda e, b=b, j=j, pv=pv: e.transpose(out=pv[:, j * 128:(j + 1) * 128],
                                                                 in_=hb[b][:, j * 128:(j + 1) * 128],
                                                                 identity=ident_b[:]),
                     r=[('hb', b), 'identb'], w=[PS(pbank)])
            o = hT_off(t)
            p.op('act', lambda e, pv=pv, o=o: e.activation(
                out=hT[:, :, o:o + 128], in_=pv.rearrange("p (j t) -> p j t", j=8), func=AF.Copy),
                r=[PS(pbank)], w=[('hT', t)])


        tl_ = [t for t in range(NT) if not (t < 2 and l == DEPTH - 1 and which == 1)]
        SKEW = 2
        for i in range(len(tl_) + SKEW):
            if i < len(tl_):
                stageA(tl_[i])
            if i >= SKEW:
                stageB(tl_[i - SKEW])
    def phase_proj(l):
        p.sb_reset(base_mark)
        qkT = p.sb([64, 14, NTOK], BF16, "qkT")
        Vaug = p.sb([128, NT, 6, 65], BF16, "Vaug")
        mark_persist = p.sb_mark()
        hT = p.sb([128, 8, NTOK], BF16, "hT")
        m_afterh = p.sb_mark()
        wAB = p.sb([128, 8, 1280], BF16, "wAB")
        for j in range(8):
            p.dma(lambda e, j=j: e.dma_start(out=wAB[:, j, :], in_=I['w_in'][l, j * 128:(j + 1) * 128, 0:1280]),
                  w=[('wAB', j)], eng="pool")
        p.op('pool', lambda e: e.memset(Vaug[:, :, :, 64:65], 1.0), w=['Vones'])
        m0 = p.sb_mark()
        norm_tiles(l, 0, I['x'] if l == 0 else S['xs'], hT, lambda t: t * 128)
        p.barrier()
        p.sb_reset(m0)
        wC = p.sb([128, 8, 1920], BF16, "wC")
        for j in range(8):
            p.dma(lambda e, j=j: e.dma_start(out=wC[:, j, :], in_=I['w_in'][l, j * 128:(j + 1) * 128, 1280:3200]),
                  w=[('wC', j)], eng="pool")
        m_afterwc = p.sb_mark()
        GA = p.sb([128, 6, 64], F32, "GA")
        GB = p.sb([128, 8, 64], F32, "GB")
        for h in range(6):
            load_bc(GA[:, h, :], I['a_qnorm'][l] if h < 4 else I['a_knorm'][l], 'GA')
        for h in range(8):
            load_bc(GB[:, h, :], I['b_qnorm'][l] if h < 4 else I['b_knorm'][l], 'GB')
        p.op('act', lambda e: e.mul(out=GA[:, 0:4, :], in_=GA[:, 0:4, :], mul=0.125), r=['GA'], w=['GA'])
        p.op('act', lambda e: e.mul(out=GB[:, 0:4, :], in_=GB[:, 0:4, :], mul=0.125), r=['GB'], w=['GB'])
        cs = [p.sb([128, 2, 32], F32, f"cs{i}") for i in range(2)]
        xn = [p.sb([128, 14, 64], F32, f"xn{i}") for i in range(2)]
        sq = p.sb([128, 14, 64], F32, "sq")
        ssq = [p.sb([128, 14], F32, f"ssq{i}") for i in range(2)]
        xr = [p.sb([128, 14, 64], BF16, f"xr{i}") for i in range(2)]
        RT = [p.sb([128, 6, 2, 16], F32, f"ropeT{i}") for i in range(4)]
        for t in range(NT):
            b = t % 2
            lat = t >= 2
            bA, bB, bV = (0, 1, 2) if t % 2 == 0 else (3, 6, 7)
            for bank, c0, c1 in ((bA, 0, 512), (bB, 512, 1024), (bV, 1024, 1280)):
                for j in range(8):
                    p.op('pe', lambda e, bank=bank, c0=c0, c1=c1, j=j, t=t: e.matmul(
                        ps[bank][:, 0:c1 - c0], hT[:, j, t * 128:(t + 1) * 128], wAB[:, j, c0:c1],
                        start=(j == 0), stop=(j == 7)),
                        r=[('hT', t), ('wAB', j)], w=[PS(bank)])
            if lat:
                tl = t - 2
                p.dma(lambda e, b=b, tl=tl: e.dma_start(out=cs[b][:, 0, :], in_=I['cos'][tl * 128:(tl + 1) * 128, :]),
                      w=[('cs', b)])
                p.dma(lambda e, b=b, tl=tl: e.dma_start(out=cs[b][:, 1, :], in_=I['sin'][tl * 128:(tl + 1) * 128, :]),
                      w=[('cs', b)])
            p.op('act', lambda e, t=t, bA=bA: e.activation(out=Vaug[:, t, 0:2, 0:64],
                                                    in_=ps[bA][:, 384:512].rearrange("p (h d) -> p h d", h=2),
                                                    func=AF.Copy), r=[PS(bA)], w=[('V', t)])
            p.op('act', lambda e, t=t, bV=bV: e.activation(out=Vaug[:, t, 2:6, 0:64],
                                                    in_=ps[bV][:, 0:256].rearrange("p (h d) -> p h d", h=4),
                                                    func=AF.Copy), r=[PS(bV)], w=[('V', t)])
            p.op('act', lambda e, b=b, bA=bA: e.activation(out=xn[b][:, 0:6, :],
                                                    in_=ps[bA][:, 0:384].rearrange("p (h d) -> p h d", h=6),
                                                    func=AF.Copy), r=[PS(bA)], w=[('xn', b)])
            p.op('act', lambda e, b=b, bB=bB: e.activation(out=xn[b][:, 6:14, :],
                                                    in_=ps[bB][:, 0:512].rearrange("p (h d) -> p h d", h=8),
                                                    func=AF.Copy), r=[PS(bB)], w=[('xn', b)])
            p.op('dve', lambda e, b=b: e.tensor_tensor(out=sq[:], in0=xn[b][:], in1=xn[b][:], op=ALU.mult),
                 r=[('xn', b)], w=['sq'])
            p.op('dve', lambda e, b=b: e.tensor_reduce(out=ssq[b][:], in_=sq[:], axis=AX.X, op=ALU.add),
                 r=['sq'], w=[('ssq', b)])
            p.op('act', lambda e, b=b: e.activation(out=ssq[b][:], in_=ssq[b][:], func=AF.Sqrt, bias=eps_col[:],
                                                    scale=1.0 / 64), r=[('ssq', b)], w=[('ssq', b)])
            p.op('dve', lambda e, b=b: e.reciprocal(out=ssq[b][:], in_=ssq[b][:]), r=[('ssq', b)], w=[('ssq', b)])
            p.op('dve', lambda e, b=b: e.tensor_tensor(out=xn[b][:], in0=xn[b][:],
                                                       in1=ssq[b][:].unsqueeze(2).to_broadcast([128, 14, 64]),
                                                       op=ALU.mult), r=[('xn', b), ('ssq', b)], w=[('xn', b)])
            p.op('dve', lambda e, b=b: e.tensor_tensor(out=xr[b][:, 6:14, :], in0=xn[b][:, 6:14, :], in1=GB[:],
                                                       op=ALU.mult), r=[('xn', b), 'GB'], w=[('xr', b)])
            if lat:
                p.op('dve', lambda e, b=b: e.tensor_tensor(out=xn[b][:, 0:6, :], in0=xn[b][:, 0:6, :], in1=GA[:],
                                                           op=ALU.mult), r=[('xn', b), 'GA'], w=[('xn', b)])
                xv = xn[b][:, 0:6, :].rearrange("p h (a g f) -> p h a g f", a=2, g=2)
                x1 = xv[:, :, :, 0, :]
                x2 = xv[:, :, :, 1, :]
                ov = xr[b][:, 0:6, :].rearrange("p h (a g f) -> p h a g f", a=2, g=2)
                cosb = cs[b][:, 0, :].rearrange("p (a f) -> p a f", a=2).unsqueeze(1).to_broadcast([128, 6, 2, 16])
                sinb = cs[b][:, 1, :].rearrange("p (a f) -> p a f", a=2).unsqueeze(1).to_broadcast([128, 6, 2, 16])
                rk = [('xn', b), ('cs', b)]
                for i, (xa, tb) in enumerate(((x1, cosb), (x2, sinb), (x2, cosb), (x1, sinb))):
                    p.op('dve', lambda e, i=i, xa=xa, tb=tb: e.tensor_tensor(out=RT[i][:], in0=xa, in1=tb, op=ALU.mult),
                         r=rk, w=[('RT', i)])
                p.op('dve', lambda e, ov=ov: e.tensor_tensor(out=ov[:, :, :, 0, :], in0=RT[0][:], in1=RT[1][:],
                                                             op=ALU.subtract), r=[('RT', 0), ('RT', 1)], w=[('xr', b)])
                p.op('dve', lambda e, ov=ov: e.tensor_tensor(out=ov[:, :, :, 1, :], in0=RT[2][:], in1=RT[3][:],
                                                             op=ALU.add), r=[('RT', 2), ('RT', 3)], w=[('xr', b)])
            else:
                p.op('dve', lambda e, b=b: e.tensor_tensor(out=xr[b][:, 0:6, :], in0=xn[b][:, 0:6, :], in1=GA[:],
                                                           op=ALU.mult), r=[('xn', b), 'GA'], w=[('xr', b)])
            for half in range(2):
                bank = 4 + half
                pv = ps[bank][0:64, 0:448].bitcast(BF16)
                for hh in range(7):
                    h = half * 7 + hh
                    p.op('pe', lambda e, b=b, h=h, hh=hh, pv=pv: e.transpose(
                        out=pv[:, hh * 128:(hh + 1) * 128], in_=xr[b][:, h, :], identity=ident_b[:]),
                        r=[('xr', b), 'identb'], w=[PS(bank)])
                p.op('act', lambda e, half=half, pv=pv, t=t: e.activation(
                    out=qkT[:, half * 7:(half + 1) * 7, t * 128:(t + 1) * 128],
                    in_=pv.rearrange("p (h t) -> p h t", h=7), func=AF.Copy), r=[PS(bank)], w=[('qkT', t)])
        p.barrier()
        p.sb_reset(m_afterwc)
        cw = p.sb([128, 15, 3], F32, "cw")
        p.dma(lambda e: e.dma_start(out=cw[:], in_=I['r7_conv'][l]), w=['cw'])
        rawc = [p.sb([128, NCTX + 2], F32, f"rawc{i}") for i in range(2)]
        rawl = [p.sb([128, NLAT + 2], F32, f"rawl{i}") for i in range(2)]
        cvo = [p.sb([128, NTOK], F32, f"cvo{i}") for i in range(2)]
        for i in range(2):
            p.op('pool', lambda e, i=i: e.memset(rawc[i][:], 0.0), w=[('rawc', i)])
            p.op('pool', lambda e, i=i: e.memset(rawl[i][:], 0.0), w=[('rawl', i)])
        bk = 0
        for ch in range(15):
            b = ch % 2
            groups = [(rawc[b], ('rawc', b), 1, 0, 256)] + [(rawl[b], ('rawl', b), 1 + 512 * g, 256 + 512 * g, 512)
                                                             for g in range(4)]
            for (raw, rkey, ro, tok0, n) in groups:
                bank = bk % 4
                bk += 1
                for j in range(8):
                    p.op('pe', lambda e, bank=bank, j=j, ch=ch, tok0=tok0, n=n: e.matmul(
                        ps[bank][:, 0:n], wC[:, j, ch * 128:(ch + 1) * 128], hT[:, j, tok0:tok0 + n],
                        start=(j == 0), stop=(j == 7)), r=[('wC', j)], w=[PS(bank)])
                p.op('act', lambda e, raw=raw, ro=ro, n=n, bank=bank: e.activation(
                    out=raw[:, ro:ro + n], in_=ps[bank][:, 0:n], func=AF.Copy), r=[PS(bank)], w=[rkey])
            for (raw, rkey, n, o0) in ((rawc[b], ('rawc', b), NCTX, 0), (rawl[b], ('rawl', b), NLAT, NCTX)):
                dst = cvo[b][:, o0:o0 + n]
                p.op('dve', lambda e, raw=raw, n=n, dst=dst, ch=ch: e.tensor_scalar(
                    out=dst, in0=raw[:, 1:1 + n], scalar1=cw[:, ch, 1:2], scalar2=None, op0=ALU.mult),
                    r=[rkey, 'cw'], w=[('cvo', b)])
                p.op('dve', lambda e, raw=raw, n=n, dst=dst, ch=ch: e.scalar_tensor_tensor(
                    out=dst, in0=raw[:, 0:n], scalar=cw[:, ch, 0:1], in1=dst, op0=ALU.mult, op1=ALU.add),
                    r=[rkey, 'cw'], w=[('cvo', b)])
                p.op('dve', lambda e, raw=raw, n=n, dst=dst, ch=ch: e.scalar_tensor_tensor(
                    out=dst, in0=raw[:, 2:2 + n], scalar=cw[:, ch, 2:3], in1=dst, op0=ALU.mult, op1=ALU.add),
                    r=[rkey, 'cw'], w=[('cvo', b)])
            if ch == 12:
                p.op('act', lambda e, b=b: e.activation(out=cvo[b][:], in_=cvo[b][:], func=AF.Tanh),
                     r=[('cvo', b)], w=[('cvo', b)])
            if ch == 14:
                p.op('act', lambda e, b=b: e.activation(out=cvo[b][:], in_=cvo[b][:], func=AF.Sigmoid),
                     r=[('cvo', b)], w=[('cvo', b)])
            p.dma(lambda e, b=b, ch=ch: e.dma_start(out=S['pcT'][ch * 128:(ch + 1) * 128, :], in_=cvo[b][:]),
                  r=[('cvo', b)], w=[('pcT', ch)])
        p.barrier()
        return qkT, Vaug, mark_persist

    def phase_attn(l, qkT, Vaug, mark_persist):
        with_ctx = l < DEPTH - 1
        p.sb_reset(mark_persist)
        btab = p.sb([128, NTAB, 4, 128], F32, "btab")
        bmask = p.sb([128, NTAB, 128], F32, "bmask")
        amask = p.sb([128, 2, 128], F32, "amask")
        esink = p.sb([128, 4], F32, "esink")
        o_all = [p.sb([128, 512], F32, f"oall{i}") for i in range(2)]
        ex = [p.sb([128, 8, 128], F32, f"ex{i}") for i in range(2)]
        pT = [p.sb([128, 8, 128], BF16, f"pT{i}") for i in range(2)]
        den = [p.sb([128, 1], F32, f"den{i}") for i in range(2)]
        for tb in range(NTAB):
            p.dma(lambda e, tb=tb: e.dma_start(out=btab[:, tb], in_=I['btab'][l, :, tb]), w=['btab'])
        p.dma(lambda e: e.dma_start(out=bmask[:], in_=I['bmask']), w=['bmask'])
        p.dma(lambda e: e.dma_start(out=amask[:], in_=I['amask']), w=['amask'])
        load_bc(esink[:], I['a_sink'][l], 'esink')
        p.op('act', lambda e: e.activation(out=esink[:], in_=esink[:], func=AF.Exp), r=['esink'], w=['esink'])
        p.op('act', lambda e: e.activation(out=btab[:], in_=btab[:], func=AF.Exp), r=['btab'], w=['btab'])
        for h in range(4):
            p.op('dve', lambda e, h=h: e.tensor_tensor(out=btab[:, :, h, :], in0=btab[:, :, h, :], in1=bmask[:],
                                                       op=ALU.mult), r=['btab', 'bmask'], w=['btab'])
        it = 0
        for t in range(NT):
            if t < 2 and not with_ctx:
                continue
            ob = t % 2
            for grp in range(2):
                for h in range(4):
                    b = it % 2
                    it += 1
                    if grp == 0:
                        qs, ks, vs = h, 4 + h // 2, h // 2
                    else:
                        qs, ks, vs = 6 + h, 10 + h, 2 + h
                    if t < 2:
                        blocks = [(0, None), (1, None)]
                    elif grp == 0:
                        n = t - 2
                        blocks = [(t, None), (0, None), (1, None)]
                        if n > 0:
                            blocks.append((t - 1, amask[:, 0, :]))
                        if n < 15:
                            blocks.append((t + 1, amask[:, 1, :]))
                    else:
                        pq = t - 2
                        blocks = [(0, None), (1, None)]
                        for kb in range(16):
                            if (pq, kb) in NA_CASES:
                                blocks.append((kb + 2, btab[:, NA_CASES[(pq, kb)], h, :]))
                    nb = len(blocks)
                    nn = sum(1 for _, tb in blocks if tb is None)
                    sb0, sb1 = (0, 1) if b == 0 else (2, 3)
                    ob_ps = 4 + b
                    for i, (kt, tb) in enumerate(blocks):
                        bank = sb0 if i < 4 else sb1
                        p.op('pe', lambda e, bank=bank, i=i, kt=kt, ks=ks, qs=qs, t=t: e.matmul(
                            ps[bank][:, (i % 4) * 128:(i % 4 + 1) * 128], qkT[:, ks, kt * 128:(kt + 1) * 128],
                            qkT[:, qs, t * 128:(t + 1) * 128], start=True, stop=True), w=[PS(bank)])
                    n0 = min(nb, 4)
                    p.op('act', lambda e, b=b, n0=n0, sb0=sb0: e.activation(
                        out=ex[b][:, 0:n0, :], in_=ps[sb0][:, 0:n0 * 128].rearrange("p (n k) -> p n k", n=n0),
                        func=AF.Exp), r=[PS(sb0)], w=[('ex', b)])
                    if nb > 4:
                        n1 = nb - 4
                        p.op('act', lambda e, b=b, n1=n1, sb1=sb1: e.activation(
                            out=ex[b][:, 4:4 + n1, :], in_=ps[sb1][:, 0:n1 * 128].rearrange("p (n k) -> p n k", n=n1),
                            func=AF.Exp), r=[PS(sb1)], w=[('ex', b)])
                    p.op('pool', lambda e, b=b, nn=nn: e.tensor_copy(out=pT[b][:, 0:nn, :], in_=ex[b][:, 0:nn, :]),
                         r=[('ex', b)], w=[('pT', b)])
                    for i, (kt, tb) in enumerate(blocks):
                        if tb is None:
                            continue
                        p.op('dve', lambda e, b=b, i=i, tb=tb: e.tensor_tensor(out=pT[b][:, i, :], in0=ex[b][:, i, :],
                                                                              in1=tb, op=ALU.mult),
                             r=[('ex', b), 'btab', 'amask'], w=[('pT', b)])
                    for i, (kt, tb) in enumerate(blocks):
                        p.op('pe', lambda e, b=b, i=i, kt=kt, vs=vs, ob_ps=ob_ps, nb=nb: e.matmul(
                            ps[ob_ps][:, 0:65], pT[b][:, i, :], Vaug[:, kt, vs, :], start=(i == 0), stop=(i == nb - 1)),
                            r=[('pT', b)], w=[PS(ob_ps)])
                    if grp == 0:
                        p.op('dve', lambda e, b=b, h=h, ob_ps=ob_ps: e.tensor_scalar(
                            out=den[b][:], in0=ps[ob_ps][:, 64:65], scalar1=esink[:, h:h + 1], scalar2=None,
                            op0=ALU.add), r=[PS(ob_ps), 'esink'], w=[('den', b)])
                        p.op('dve', lambda e, b=b: e.reciprocal(out=den[b][:], in_=den[b][:]),
                             r=[('den', b)], w=[('den', b)])
                    else:
                        p.op('dve', lambda e, b=b, ob_ps=ob_ps: e.reciprocal(out=den[b][:], in_=ps[ob_ps][:, 64:65]),
                             r=[PS(ob_ps)], w=[('den', b)])
                    col = grp * 256 + h * 64
                    p.op('dve', lambda e, b=b, ob=ob, col=col, ob_ps=ob_ps: e.tensor_scalar(
                        out=o_all[ob][:, col:col + 64], in0=ps[ob_ps][:, 0:64], scalar1=den[b][:, 0:1], scalar2=None,
                        op0=ALU.mult), r=[PS(ob_ps), ('den', b)], w=[('oall', ob)])
            p.dma(lambda e, ob=ob, t=t: e.dma_start(out=S['o'][t * 128:(t + 1) * 128, 0:512], in_=o_all[ob][:]),
                  r=[('oall', ob)], w=[('So', t)])
        p.barrier()


    def phase_rprep(l):
        p.sb_reset(base_mark)
        w2 = p.sb([128, 512], F32, "w2")
        a2 = p.sb([128, 512], F32, "a2")
        g2 = p.sb([128, 512], F32, "g2")
        w0 = p.sb([1, 2, 512], F32, "w0")
        a0 = p.sb([1, 2, 512], F32, "a0")
        ones = p.sb([1, 128], F32, "ones")
        KKW = p.sb([128, 512], F32, "KKW")
        KA = p.sb([128, 512], F32, "KA")
        RK = p.sb([128, 512], F32, "RK")
        p.dma(lambda e: e.dma_start(out=w2[:], in_=I['r7_w2'][l].rearrange("d r c -> (d r) c")), w=['w2'])
        p.dma(lambda e: e.dma_start(out=a2[:], in_=I['r7_a2'][l].rearrange("d r c -> (d r) c")), w=['a2'])
        p.dma(lambda e: e.dma_start(out=g2[:], in_=I['r7_g2'][l]), w=['g2'])
        p.dma(lambda e: e.dma_start(out=w0[:], in_=I['r7_w0'][l:l + 1]), w=['w0'])
        p.dma(lambda e: e.dma_start(out=a0[:], in_=I['r7_a0'][l:l + 1]), w=['a0'])
        p.op('dve', lambda e: e.memset(ones[:], 1.0), w=['ones'])
        load_bc(KKW[:], I['r7_kk'][l], 'KKW')
        load_bc(KA[:], I['r7_ka'][l], 'KA')
        load_bc(RK[:], I['r7_rk'][l].rearrange("h d -> (h d)"), 'RK')
        fm = [p.sb([128, 15, 128], F32, f"fm{i}") for i in range(2)]
        TM = [p.sb([128, 10, 512], F32, f"TM{i}") for i in range(2)]
        kt = p.sb([128, 512], F32, "kt")
        av = [p.sb([128, 512], F32, f"av{i}") for i in range(2)]
        tmp = p.sb([128, 512], F32, "tmp")
        tmp2 = p.sb([128, 512], F32, "tmp2")
        s8 = p.sb([128, 8], F32, "s8")
        bs = [p.sb([128, 8], F32, f"bs{i}") for i in range(2)]
        for t in range(NT):
            b = t % 2
            p.dma(lambda e, b=b, t=t: e.dma_start(
                out=fm[b][:], in_=S['pcT'][:, t * 128:(t + 1) * 128].rearrange("(c p) t -> p c t", p=128)),
                w=[('fm', b)])
            for q in range(3):
                for c4 in range(4):
                    p.op('pe', lambda e, b=b, q=q, c4=c4: e.transpose(
                        out=ps[q][:, c4 * 128:(c4 + 1) * 128], in_=fm[b][:, q * 4 + c4, :], identity=ident_f[:]),
                        r=[('fm', b), 'identf'], w=[PS(q)])
            for d in range(2):
                pr = slice(d * 64, d * 64 + 64)
                p.op('pe', lambda e, b=b, d=d, pr=pr: e.matmul(ps[3 + d][:, :], fm[b][pr, 12, :], w2[pr, :],
                                                              start=True, stop=False), r=[('fm', b), 'w2'], w=[PS(3 + d)])
                p.op('pe', lambda e, d=d: e.matmul(ps[3 + d][:, :], ones[0:1, :], w0[0:1, d, :], start=False, stop=True),
                     r=['ones', 'w0'], w=[PS(3 + d)])
                p.op('pe', lambda e, b=b, d=d, pr=pr: e.matmul(ps[5 + d][:, :], fm[b][pr, 13, :], a2[pr, :],
                                                              start=True, stop=False), r=[('fm', b), 'a2'], w=[PS(5 + d)])
                p.op('pe', lambda e, d=d: e.matmul(ps[5 + d][:, :], ones[0:1, :], a0[0:1, d, :], start=False, stop=True),
                     r=['ones', 'a0'], w=[PS(5 + d)])
            p.op('pe', lambda e, b=b: e.matmul(ps[7][:, :], fm[b][:, 14, :], g2[:], start=True, stop=True),
                 r=[('fm', b), 'g2'], w=[PS(7)])
            T = TM[b]
            wk = [('TM', b)]
            p.op('act', lambda e, T=T: e.activation(out=T[:, 0, :], in_=ps[0][:, :], func=AF.Copy), r=[PS(0)], w=wk)
            p.op('act', lambda e: e.activation(out=kt[:], in_=ps[1][:, :], func=AF.Copy), r=[PS(1)], w=['kt'])
            p.op('act', lambda e, T=T: e.activation(out=T[:, 1, :], in_=ps[2][:, :], func=AF.Copy), r=[PS(2)], w=wk)
            p.op('act', lambda e, T=T: e.activation(out=T[:, 2, :], in_=ps[7][:, :], func=AF.Copy), r=[PS(7)], w=wk)
            for d in range(2):
                p.op('act', lambda e, T=T, d=d: e.activation(out=T[:, 8 + d, :], in_=ps[3 + d][:, :], func=AF.Sigmoid),
                     r=[PS(3 + d)], w=wk)
                p.op('act', lambda e, d=d: e.activation(out=av[d][:], in_=ps[5 + d][:, :], func=AF.Sigmoid),
                     r=[PS(5 + d)], w=[('av', d)])
                p.op('dve', lambda e, T=T, d=d: e.tensor_scalar(out=T[:, 8 + d, :], in0=T[:, 8 + d, :],
                                                                scalar1=-0.6065306597126334, scalar2=None, op0=ALU.mult),
                     r=wk, w=wk)
            p.op('dve', lambda e: e.tensor_tensor(out=tmp[:], in0=kt[:], in1=KKW[:], op=ALU.mult), r=['kt', 'KKW'], w=['tmp'])
            p.op('dve', lambda e: e.tensor_tensor(out=tmp2[:], in0=tmp[:], in1=tmp[:], op=ALU.mult), r=['tmp'], w=['tmp2'])
            p.op('dve', lambda e: e.tensor_reduce(out=s8[:], in_=tmp2[:].rearrange("p (h d) -> p h d", h=8), axis=AX.X,
                                                  op=ALU.add), r=['tmp2'], w=['s8'])
            p.op('act', lambda e: e.activation(out=s8[:], in_=s8[:], func=AF.Sqrt), r=['s8'], w=['s8'])
            p.op('dve', lambda e: e.tensor_scalar(out=s8[:], in0=s8[:], scalar1=1e-12, scalar2=None, op0=ALU.max),
                 r=['s8'], w=['s8'])
            p.op('dve', lambda e: e.reciprocal(out=s8[:], in_=s8[:]), r=['s8'], w=['s8'])
            p.op('dve', lambda e, T=T: e.tensor_tensor(out=T[:, 3, :].rearrange("p (h d) -> p h d", h=8),
                                                       in0=tmp[:].rearrange("p (h d) -> p h d", h=8),
                                                       in1=s8[:].unsqueeze(2).to_broadcast([128, 8, 64]), op=ALU.mult),
                 r=['tmp', 's8'], w=wk)
            for d in range(2):
                p.op('dve', lambda e, d=d: e.scalar_tensor_tensor(out=tmp2[:], in0=av[d][:], scalar=-1.0, in1=KA[:],
                                                                  op0=ALU.add, op1=ALU.mult),
                     r=[('av', d), 'KA'], w=['tmp2'])
                p.op('dve', lambda e, T=T, d=d: e.scalar_tensor_tensor(out=T[:, 4 + d, :], in0=tmp2[:], scalar=1.0,
                                                                       in1=kt[:], op0=ALU.add, op1=ALU.mult),
                     r=['tmp2', 'kt'], w=wk)
                p.op('dve', lambda e, T=T, d=d: e.tensor_tensor(out=T[:, 6 + d, :], in0=T[:, 3, :], in1=av[d][:],
                                                                op=ALU.mult), r=wk + [('av', d)], w=wk)
            p.op('dve', lambda e, T=T: e.tensor_tensor(out=tmp[:], in0=T[:, 4, :], in1=T[:, 5, :], op=ALU.add),
                 r=wk, w=['tmp'])
            p.op('dve', lambda e: e.tensor_tensor(out=tmp[:], in0=tmp[:], in1=RK[:], op=ALU.mult), r=['tmp', 'RK'], w=['tmp'])
            p.op('dve', lambda e, T=T: e.tensor_tensor(out=tmp[:], in0=tmp[:], in1=T[:, 0, :], op=ALU.mult),
                 r=['tmp'] + wk, w=['tmp'])
            p.op('dve', lambda e, b=b: e.tensor_reduce(out=bs[b][:], in_=tmp[:].rearrange("p (h d) -> p h d", h=8),
                                                       axis=AX.X, op=ALU.add), r=['tmp'], w=[('bs', b)])
            p.dma(lambda e, T=T, t=t: e.dma_start(out=S['tm'][t * 128:(t + 1) * 128], in_=T[:]), r=wk, w=[('Stm', t)])
            p.dma(lambda e, b=b, t=t: e.dma_start(out=S['bon'][t * 128:(t + 1) * 128], in_=bs[b][:]),
                  r=[('bs', b)], w=[('Sbon', t)])
        p.barrier()

    def phase_scan(l):
        p.sb_reset(base_mark)
        PSB = ps
        C = 64
        NCH = NTOK // C
        tri = p.sb([64, 2, 64], F32, "tri")
        mg = p.sb([64, 2, 128], F32, "mg")
        mn = p.sb([64, 2, 64], F32, "mn")
        ones = p.sb([64, 1], F32, "ones1")
        p.dma(lambda e: e.dma_start(out=tri[:], in_=I['tri']), w=['tri'])
        p.dma(lambda e: e.dma_start(out=mg[:], in_=I['mg']), w=['mg'])
        p.dma(lambda e: e.dma_start(out=mn[:], in_=I['mn']), w=['mn'])
        p.op('dve', lambda e: e.memset(ones[:], 1.0), w=['ones1'])
        M = [p.sb([64, 8, 64], F32, f"M{d}") for d in range(2)]
        for d in range(2):
            M0_PLACEHOLDER = None
        X = [[p.sb([64, 6, 512], F32, f"X{d}{i}") for i in range(2)] for d in range(2)]
        def mk(shape, name):
            return [p.sb(shape, F32, f"{name}{d}") for d in range(2)]
        E0s, E1s, E2s = mk([64, 512], "E0"), mk([64, 512], "E1"), mk([64, 512], "E2")
        Ats, Rts, Bts, Kts = mk([64, 512], "At"), mk([64, 512], "Rt"), mk([64, 512], "Bt"), mk([64, 512], "Kt")
        FARs, FBs, FKs = mk([64, 8, 128], "FAR"), mk([64, 8, 64], "FB"), mk([64, 8, 64], "FK")
        G1s, G2s = mk([64, 8, 128], "G1"), mk([64, 8, 128], "G2")
        Tms = [mk([64, 8, 64], f"Tm{i}_") for i in range(2)]
        Nms = [mk([64, 8, 64], f"Nm{i}_") for i in range(2)]
        Zs, Wss, Uss, PCs = mk([64, 8, 64], "Z"), mk([64, 512], "Ws"), mk([64, 512], "Us"), mk([64, 8], "PC")
        Ys = [p.sb([64, 512], F32, f"Ys{d}") for d in range(2)]
        order = {0: list(range(0, 4)) + list(range(4, NCH)), 1: list(range(3, -1, -1)) + list(range(NCH - 1, 3, -1))}
        v3 = lambda ap: ap.rearrange("p (h d) -> p h d", h=8)
        F32R = mybir.dt.float32r
        use_r = cfg.get("fp32r", True)

        def RR(ap):
            return ap.bitcast(F32R) if use_r else ap

        Vrs = mk([64, 512], "Vr")
        Mts = mk([64, 8, 64], "Mt")
        for d in range(2):
            p.op('dve', lambda e, d=d: e.memset(Mts[d][:], 0.0), w=[('Mt', d)])
            p.op('dve', lambda e, d=d: e.tensor_copy(out=RR(M[d][:]), in_=Mts[d][:]), r=[('Mt', d)], w=[('M', d)])

        def mmr(e, out, lhsT, rhs, **kw):
            if use_r:
                return e.matmul(out, lhsT.bitcast(F32R), rhs.bitcast(F32R), **kw)
            return e.matmul(out, lhsT, rhs, **kw)

        def scan_unit(d, c):
            if True:
                tok0 = c * C
                Xd = X[d][c % 2]
                E0, E1, E2, At, Rt, Bt, Kt = E0s[d], E1s[d], E2s[d], Ats[d], Rts[d], Bts[d], Kts[d]
                FAR, FB, FK, G1, G2 = FARs[d], FBs[d], FKs[d], G1s[d], G2s[d]
                Tm = [Tms[0][d], Tms[1][d]]
                Nm = [Nms[0][d], Nms[1][d]]
                Z, Ws, Us, PC = Zs[d], Wss[d], Uss[d], PCs[d]
                ps = [PSB[4 * d + (i % 4)] for i in range(8)]
                PS = lambda i: ('ps', 4 * d + (i % 4))
                xk = [('X', d, c % 2)]
                srcs = [0, 1, 3, 4 + d, 6 + d, 8 + d]
                yield
                for i, s in enumerate(srcs):
                    p.dma(lambda e, Xd=Xd, i=i, s=s, tok0=tok0: e.dma_start(out=Xd[:, i, :],
                                                                           in_=S['tm'][tok0:tok0 + C, s, :]), w=xk)
                r_, v_, kk_, k_, b_, lw_ = [Xd[:, i, :] for i in range(6)]
                Vr = Vrs[d]
                yield
                p.op('act', lambda e, v_=v_: e.activation(out=RR(Vr[:]), in_=v_, func=AF.Copy), r=xk, w=[('Vr', d)])
                v_ = Vr[:]
                vk = [('Vr', d)]
                yield
                p.op('pe', lambda e, d=d, lw_=lw_: e.matmul(ps[0][0:64, :], tri[:, d, :], lw_, start=True, stop=True),
                     r=xk + ['tri'], w=[PS(0)])
                yield
                for h in range(8):
                    p.op('pe', lambda e, h=h, lw_=lw_: e.matmul(ps[1][0:64, h:h + 1], lw_[:, h * 64:(h + 1) * 64],
                                                               ones[:, 0:1], start=True, stop=True),
                         r=xk + ['ones1'], w=[PS(1)])
                yield
                p.op('act', lambda e: e.activation(out=PC[:], in_=ps[1][0:64, 0:8], func=AF.Exp), r=[PS(1)], w=[('PC', d)])
                yield
                p.op('act', lambda e: e.activation(out=E1[:], in_=ps[0][0:64, :], func=AF.Exp), r=[PS(0)], w=[('E1', d)])
                yield
                p.op('act', lambda e: e.activation(out=E2[:], in_=ps[0][0:64, :], func=AF.Exp, scale=-1.0),
                     r=[PS(0)], w=[('E2', d)])
                yield
                p.op('dve', lambda e, lw_=lw_: e.tensor_tensor(out=E0[:], in0=ps[0][0:64, :], in1=lw_, op=ALU.subtract),
                     r=[PS(0)] + xk, w=[('E0', d)])
                yield
                p.op('act', lambda e: e.activation(out=E0[:], in_=E0[:], func=AF.Exp), r=[('E0', d)], w=[('E0', d)])
                yield
                p.op('dve', lambda e, kk_=kk_: e.scalar_tensor_tensor(out=At[:], in0=kk_, scalar=-1.0, in1=E0[:],
                                                                      op0=ALU.mult, op1=ALU.mult),
                     r=xk + [('E0', d)], w=[('At', d)])
                yield
                p.op('dve', lambda e, r_=r_: e.tensor_tensor(out=Rt[:], in0=r_, in1=E1[:], op=ALU.mult),
                     r=xk + [('E1', d)], w=[('Rt', d)])
                yield
                p.op('dve', lambda e, b_=b_: e.tensor_tensor(out=RR(Bt[:]), in0=b_, in1=E2[:], op=ALU.mult),
                     r=xk + [('E2', d)], w=[('Bt', d)])
                yield
                p.op('dve', lambda e, k_=k_: e.tensor_tensor(out=RR(Kt[:]), in0=k_, in1=E2[:], op=ALU.mult),
                     r=xk + [('E2', d)], w=[('Kt', d)])
                yield
                for bank, src, key in ((2, At, ('At', d)), (3, Rt, ('Rt', d)), (4, Bt, ('Bt', d)), (5, Kt, ('Kt', d))):
                    for h in range(8):
                        p.op('pe', lambda e, bank=bank, src=src, h=h: e.transpose(
                            out=ps[bank][0:64, h * 64:(h + 1) * 64], in_=src[:, h * 64:(h + 1) * 64],
                            identity=ident_f[0:64, 0:64]), r=[key, 'identf'], w=[PS(bank)])
                yield
                p.op('act', lambda e: e.activation(out=RR(FAR[:, :, 0:64]), in_=v3(ps[2][0:64, :]), func=AF.Copy),
                     r=[PS(2)], w=[('FAR', d)])
                yield
                p.op('act', lambda e: e.activation(out=RR(FAR[:, :, 64:128]), in_=v3(ps[3][0:64, :]), func=AF.Copy),
                     r=[PS(3)], w=[('FAR', d)])
                yield
                p.op('dve', lambda e: e.tensor_copy(out=RR(FB[:]), in_=v3(ps[4][0:64, :])), r=[PS(4)], w=[('FB', d)])
                yield
                p.op('dve', lambda e: e.tensor_copy(out=RR(FK[:]), in_=v3(ps[5][0:64, :])), r=[PS(5)], w=[('FK', d)])
                yield
                for h in range(8):
                    bank = 6 + (h // 4)
                    p.op('pe', lambda e, h=h, bank=bank: mmr(e, ps[bank][0:64, (h % 4) * 128:(h % 4 + 1) * 128],
                                                                   FB[:, h, :], FAR[:, h, :], start=True, stop=True),
                         r=[('FB', d), ('FAR', d)], w=[PS(bank)])
                yield
                for hb in range(2):
                    p.op('dve', lambda e, hb=hb, d=d: e.tensor_tensor(
                        out=RR(G1[:, hb * 4:(hb + 1) * 4, :]), in0=ps[6 + hb][0:64, :].rearrange("p (h t) -> p h t", h=4),
                        in1=mg[:, d, :].unsqueeze(1).to_broadcast([64, 4, 128]), op=ALU.mult),
                        r=[PS(6 + hb), 'mg'], w=[('G1', d)])
                yield
                for h in range(8):
                    bank = 2 + (h // 4)
                    p.op('pe', lambda e, h=h, bank=bank: mmr(e, ps[bank][0:64, (h % 4) * 128:(h % 4 + 1) * 128],
                                                                   FK[:, h, :], FAR[:, h, :], start=True, stop=True),
                         r=[('FK', d), ('FAR', d)], w=[PS(bank)])
                yield
                for hb in range(2):
                    p.op('dve', lambda e, hb=hb, d=d: e.tensor_tensor(
                        out=RR(G2[:, hb * 4:(hb + 1) * 4, :]), in0=ps[2 + hb][0:64, :].rearrange("p (h t) -> p h t", h=4),
                        in1=mg[:, d, :].unsqueeze(1).to_broadcast([64, 4, 128]), op=ALU.mult),
                        r=[PS(2 + hb), 'mg'], w=[('G2', d)])
                yield
                for h in range(8):
                    p.op('pe', lambda e, h=h: mmr(e, ps[4][0:64, h * 64:(h + 1) * 64], FAR[:, h, 0:64], FB[:, h, :],
                                                       start=True, stop=True), r=[('FAR', d), ('FB', d)], w=[PS(4)])
                yield
                p.op('dve', lambda e, d=d: e.tensor_tensor(out=RR(Nm[0][:]), in0=v3(ps[4][0:64, :]),
                                                           in1=mn[:, d, :].unsqueeze(1).to_broadcast([64, 8, 64]),
                                                           op=ALU.mult), r=[PS(4), 'mn'], w=[('Nm', d, 0)])
                yield
                p.op('dve', lambda e: e.tensor_copy(out=RR(Tm[0][:]), in_=G1[:, :, 0:64]), r=[('G1', d)], w=[('Tm', d, 0)])
                yield
                p.op('dve', lambda e: e.tensor_tensor(out=RR(Z[:]), in0=G1[:, :, 0:64],
                                                      in1=ident_f[0:64, 0:64].unsqueeze(1).to_broadcast([64, 8, 64]),
                                                      op=ALU.add), r=[('G1', d), 'identf'], w=[('Z', d)])
                cur = 0
                yield
                for lev in range(5):
                    nxt = 1 - cur
                    last = lev == 4
                    for h in range(8):
                        p.op('pe', lambda e, h=h, cur=cur: mmr(e, ps[5][0:64, h * 64:(h + 1) * 64], Tm[cur][:, h, :],
                                                                    Nm[cur][:, h, :], start=True, stop=True),
                             r=[('Tm', d, cur), ('Nm', d, cur)], w=[PS(5)])
                    p.op('act', lambda e, nxt=nxt: e.activation(out=RR(Nm[nxt][:]), in_=v3(ps[5][0:64, :]), func=AF.Copy),
                         r=[PS(5)], w=[('Nm', d, nxt)])
                    if not last:
                        for h in range(8):
                            p.op('pe', lambda e, h=h, cur=cur: mmr(e, ps[6][0:64, h * 64:(h + 1) * 64],
                                                                        Nm[cur][:, h, :], Tm[cur][:, h, :],
                                                                        start=True, stop=True),
                                 r=[('Tm', d, cur), ('Nm', d, cur)], w=[PS(6)])
                        p.op('dve', lambda e, nxt=nxt: e.tensor_copy(out=RR(Tm[nxt][:]), in_=v3(ps[6][0:64, :])),
                             r=[PS(6)], w=[('Tm', d, nxt)])
                    for h in range(8):
                        p.op('pe', lambda e, h=h, nxt=nxt: mmr(e, ps[7][0:64, h * 64:(h + 1) * 64], Nm[nxt][:, h, :],
                                                                    Z[:, h, :], start=True, stop=True),
                             r=[('Nm', d, nxt), ('Z', d)], w=[PS(7)])
                    p.op('dve', lambda e: e.tensor_tensor(out=RR(Z[:]), in0=Z[:], in1=v3(ps[7][0:64, :]), op=ALU.add),
                         r=[PS(7), ('Z', d)], w=[('Z', d)])
                    cur = nxt
                Md = M[d]
                yield
                for h in range(8):
                    o = ps[0][0:64, h * 64:(h + 1) * 64]
                    p.op('pe', lambda e, h=h, o=o, Md=Md: mmr(e, o, FAR[:, h, 0:64], Md[:, h, :], start=True, stop=False),
                         r=[('FAR', d), ('M', d)], w=[PS(0)])
                    p.op('pe', lambda e, h=h, o=o, v_=v_: mmr(e, o, G2[:, h, 0:64], v_[:, h * 64:(h + 1) * 64],
                                                                   start=False, stop=True), r=[('G2', d)] + vk, w=[PS(0)])
                yield
                p.op('act', lambda e: e.activation(out=RR(Ws[:]), in_=ps[0][0:64, :], func=AF.Copy), r=[PS(0)], w=[('Ws', d)])
                yield
                for h in range(8):
                    p.op('pe', lambda e, h=h: mmr(e, ps[1][0:64, h * 64:(h + 1) * 64], Z[:, h, :],
                                                       Ws[:, h * 64:(h + 1) * 64], start=True, stop=True),
                         r=[('Z', d), ('Ws', d)], w=[PS(1)])
                yield
                p.op('act', lambda e: e.activation(out=RR(Us[:]), in_=ps[1][0:64, :], func=AF.Copy), r=[PS(1)], w=[('Us', d)])
                yield
                for h in range(8):
                    o = ps[2][0:64, h * 64:(h + 1) * 64]
                    hs = slice(h * 64, (h + 1) * 64)
                    p.op('pe', lambda e, h=h, o=o, Md=Md: mmr(e, o, FAR[:, h, 64:128], Md[:, h, :], start=True, stop=False),
                         r=[('FAR', d), ('M', d)], w=[PS(2)])
                    p.op('pe', lambda e, h=h, o=o, hs=hs: mmr(e, o, G1[:, h, 64:128], Us[:, hs], start=False, stop=False),
                         r=[('G1', d), ('Us', d)], w=[PS(2)])
                    p.op('pe', lambda e, h=h, o=o, hs=hs, v_=v_: mmr(e, o, G2[:, h, 64:128], v_[:, hs], start=False, stop=True),
                         r=[('G2', d)] + vk, w=[PS(2)])
                yield
                p.op('act', lambda e, d=d: e.activation(out=Ys[d][:], in_=ps[2][0:64, :], func=AF.Copy),
                     r=[PS(2)], w=[('Ys', d)])
                yield
                p.dma(lambda e, d=d, tok0=tok0: e.dma_start(out=S['y'][d, tok0:tok0 + C, :], in_=Ys[d][:]),
                      r=[('Ys', d)], w=[('Sy', d, c)])
                yield
                for h in range(8):
                    o = ps[3][0:64, h * 64:(h + 1) * 64]
                    hs = slice(h * 64, (h + 1) * 64)
                    p.op('pe', lambda e, o=o, hs=hs: mmr(e, o, Bt[:, hs], Us[:, hs], start=True, stop=False),
                         r=[('Bt', d), ('Us', d)], w=[PS(3)])
                    p.op('pe', lambda e, o=o, hs=hs, v_=v_: mmr(e, o, Kt[:, hs], v_[:, hs], start=False, stop=True),
                         r=[('Kt', d)] + vk, w=[PS(3)])
                Mt = Mts[d]
                yield
                p.op('dve', lambda e, Md=Md, Mt=Mt: e.tensor_tensor(out=Mt[:], in0=Md[:], in1=v3(ps[3][0:64, :]), op=ALU.add),
                     r=[PS(3), ('M', d)], w=[('Mt', d)])
                yield
                p.op('dve', lambda e, Md=Md, Mt=Mt: e.tensor_tensor(out=RR(Md[:]), in0=Mt[:],
                                                             in1=PC[:].unsqueeze(2).to_broadcast([64, 8, 64]),
                                                             op=ALU.mult), r=[('PC', d), ('Mt', d)], w=[('M', d)])
        cin = [[p.sb([128, 1, D], F32, f"cin{i}{q}") for q in range(2)] for i in range(2)]
        cout = [p.sb([128, 1, 2 * D], BF16, f"cout{i}") for i in range(2)]

        def conv_block(blk):
            b = blk % 2
            rows = slice(blk * 128, (blk + 1) * 128)
            for q, tabn in enumerate(('peer_u', 'peer_v')):
                p.dma(lambda e, b=b, q=q, tabn=tabn, rows=rows: e.dma_start(
                    out=cin[b][q][:], in_=I[tabn][l][rows, :].rearrange("(j p) d -> p j d", p=128)), w=[('cin', b, q)],
                    eng="pool")
                p.op('pool', lambda e, b=b, q=q: e.tensor_copy(out=cout[b][:, :, q * D:(q + 1) * D], in_=cin[b][q][:]),
                     r=[('cin', b, q)], w=[('cout', b, q)])
            p.dma(lambda e, b=b, rows=rows: e.dma_start(
                out=S['T'][l][rows, :].rearrange("(j p) d -> p j d", p=128), in_=cout[b][:]),
                r=[('cout', b, 0), ('cout', b, 1)], w=[('cout', b, 0), ('cout', b, 1)], eng="pool")

        nblk = 0
        for step in range(NCH):
            gens = [scan_unit(d, order[d][step]) for d in range(2)]
            while gens:
                for g_ in list(gens):
                    try:
                        next(g_)
                    except StopIteration:
                        gens.remove(g_)
            for _ in range(4):
                if nblk < 128:
                    conv_block(nblk)
                    nblk += 1
        while nblk < 128:
            conv_block(nblk)
            nblk += 1
        p.barrier()


    def phase_rout(l):
        p.sb_reset(base_mark)
        with_ctx = l < DEPTH - 1
        wo = p.sb([128, 8, D], BF16, "wo")
        for j in range(8):
            p.dma(lambda e, j=j: e.dma_start(out=wo[:, j, :], in_=I['w_out'][l, j * 128:(j + 1) * 128, :]),
                  w=[('wo', j)], eng="pool")
        LNW = p.sb([128, 512], F32, "LNW")
        LNB = p.sb([128, 512], F32, "LNB")
        G1b = [p.sb([128, D], F32, f"G1b{s}") for s in range(2)]
        load_bc(LNW[:], I['r7_lnw'][l], 'LNW')
        load_bc(LNB[:], I['r7_lnb'][l], 'LNB')
        gn_eps = p.sb([128, 1], F32, "gneps")
        p.op('dve', lambda e: e.memset(gn_eps[:], 64e-5), w=['gneps'])
        for s in range(2):
            load_bc(G1b[s][:], S['mod'][l, s, 2 * D:3 * D], ('G1b', s))
        yb = [[p.sb([128, 512], F32, f"y{d}{i}") for d in range(2)] for i in range(2)]
        vg = [p.sb([128, 2, 512], F32, f"vg{i}") for i in range(2)]
        bon = [p.sb([128, 8], F32, f"bon{i}") for i in range(2)]
        O = [p.sb([128, D], F32, f"O{i}") for i in range(2)]
        Ob = [p.sb([128, D], BF16, f"Ob{i}") for i in range(2)]
        oT = [p.sb([128, 8, 128], BF16, f"oT{i}") for i in range(2)]
        xt = [p.sb([128, D], F32, f"xr{i}") for i in range(2)]
        yc = p.sb([128, 512], F32, "yc")
        sq = p.sb([128, 512], F32, "sq2")
        m8 = p.sb([128, 8], F32, "m8")
        v8 = p.sb([128, 8], F32, "v8")
        src = I['x'] if l == 0 else S['xs']
        v3 = lambda ap: ap.rearrange("p (h d) -> p h d", h=8)
        bc8 = lambda ap: ap.unsqueeze(2).to_broadcast([128, 8, 64])
        for t in range(NT):
            if t < 2 and not with_ctx:
                continue
            b = t % 2
            s = 1 if t < 2 else 0
            rows = slice(t * 128, (t + 1) * 128)
            for d in range(2):
                p.dma(lambda e, b=b, d=d, rows=rows: e.dma_start(out=yb[b][d][:], in_=S['y'][d, rows, :]), w=[('y', b, d)])
            p.dma(lambda e, b=b, rows=rows: e.dma_start(out=vg[b][:], in_=S['tm'][rows, 1:3, :]), w=[('vg', b)])
            p.dma(lambda e, b=b, rows=rows: e.dma_start(out=bon[b][:], in_=S['bon'][rows, :]), w=[('bon', b)])
            p.dma(lambda e, b=b, rows=rows: e.dma_start(out=O[b][:, 0:512], in_=S['o'][rows, 0:512]), w=[('O', b)])
            p.dma(lambda e, b=b, rows=rows: e.dma_start(out=xt[b][:], in_=src[rows, :]), w=[('xr', b)])
            p.op('dve', lambda e, b=b: e.tensor_tensor(out=yc[:], in0=yb[b][0][:], in1=yb[b][1][:], op=ALU.add),
                 r=[('y', b, 0), ('y', b, 1)], w=['yc'])
            p.op('dve', lambda e: e.tensor_reduce(out=m8[:], in_=v3(yc[:]), axis=AX.X, op=ALU.add), r=['yc'], w=['m8'])
            p.op('dve', lambda e: e.tensor_scalar(out=m8[:], in0=m8[:], scalar1=1.0 / 64, scalar2=None, op0=ALU.mult),
                 r=['m8'], w=['m8'])
            p.op('dve', lambda e: e.tensor_tensor(out=v3(yc[:]), in0=v3(yc[:]), in1=bc8(m8[:]), op=ALU.subtract),
                 r=['yc', 'm8'], w=['yc'])
            p.op('dve', lambda e: e.tensor_tensor(out=sq[:], in0=yc[:], in1=yc[:], op=ALU.mult), r=['yc'], w=['sq2'])
            p.op('dve', lambda e: e.tensor_reduce(out=v8[:], in_=v3(sq[:]), axis=AX.X, op=ALU.add), r=['sq2'], w=['v8'])
            p.op('act', lambda e: e.activation(out=v8[:], in_=v8[:], func=AF.Sqrt, bias=gn_eps[:], scale=1.0 / 64),
                 r=['v8', 'gneps'], w=['v8'])
            p.op('dve', lambda e: e.reciprocal(out=v8[:], in_=v8[:]), r=['v8'], w=['v8'])
            p.op('dve', lambda e: e.tensor_tensor(out=v3(yc[:]), in0=v3(yc[:]), in1=bc8(v8[:]), op=ALU.mult),
                 r=['yc', 'v8'], w=['yc'])
            p.op('dve', lambda e: e.tensor_tensor(out=yc[:], in0=yc[:], in1=LNW[:], op=ALU.mult), r=['yc', 'LNW'], w=['yc'])
            p.op('dve', lambda e: e.tensor_tensor(out=yc[:], in0=yc[:], in1=LNB[:], op=ALU.add), r=['yc', 'LNB'], w=['yc'])
            p.op('dve', lambda e, b=b: e.tensor_tensor(out=v3(sq[:]), in0=v3(vg[b][:, 0, :]), in1=bc8(bon[b][:]),
                                                       op=ALU.mult), r=[('vg', b), ('bon', b)], w=['sq2'])
            p.op('dve', lambda e: e.tensor_tensor(out=yc[:], in0=yc[:], in1=sq[:], op=ALU.add), r=['yc', 'sq2'], w=['yc'])
            p.op('dve', lambda e, b=b: e.tensor_tensor(out=O[b][:, 512:1024], in0=yc[:], in1=vg[b][:, 1, :], op=ALU.mult),
                 r=['yc', ('vg', b)], w=[('O2', b)])
            p.dma(lambda e, b=b, rows=rows: e.dma_start(out=S['o'][rows, 512:1024], in_=O[b][:, 512:1024]),
                  r=[('O2', b)], w=[('So2', t)])
            p.op('act', lambda e, b=b: e.activation(out=Ob[b][:], in_=O[b][:], func=AF.Copy),
                 r=[('O', b), ('O2', b)], w=[('Ob', b)])
            bank = 6 + b
            pv = ps[bank][:, 0:512].bitcast(BF16)
            for j in range(8):
                p.op('pe', lambda e, b=b, j=j, pv=pv: e.transpose(out=pv[:, j * 128:(j + 1) * 128],
                                                                 in_=Ob[b][:, j * 128:(j + 1) * 128], identity=ident_b[:]),
                     r=[('Ob', b), 'identb'], w=[PS(bank)])
            p.op('act', lambda e, b=b, pv=pv: e.activation(out=oT[b][:], in_=pv.rearrange("p (j t) -> p j t", j=8),
                                                            func=AF.Copy), r=[PS(bank)], w=[('oT', b)])
            for half in range(2):
                ybank = 2 * b + half
                for j in range(8):
                    p.op('pe', lambda e, b=b, j=j, half=half, ybank=ybank: e.matmul(
                        ps[ybank][:, :], oT[b][:, j, :], wo[:, j, half * 512:(half + 1) * 512],
                        start=(j == 0), stop=(j == 7)), r=[('oT', b), ('wo', j)], w=[PS(ybank)])
                cs_ = slice(half * 512, (half + 1) * 512)
                p.op('dve', lambda e, b=b, s=s, cs_=cs_, ybank=ybank: e.tensor_tensor(
                    out=O[b][:, cs_], in0=ps[ybank][:, :], in1=G1b[s][:, cs_], op=ALU.mult),
                    r=[PS(ybank), ('G1b', s), ('Ob', b), ('So2', t)], w=[('O', b), ('O2', b)])
                p.op('dve', lambda e, b=b, cs_=cs_: e.tensor_tensor(out=xt[b][:, cs_], in0=xt[b][:, cs_], in1=O[b][:, cs_],
                                                                   op=ALU.add), r=[('O', b), ('xr', b)], w=[('xr', b)])
            p.dma(lambda e, b=b, rows=rows: e.dma_start(out=S['xs'][rows, :], in_=xt[b][:]), r=[('xr', b)], w=[('Sxs', t)])
        p.barrier()

    def phase_peer(l):
        p.sb_reset(base_mark)
        last = l == DEPTH - 1
        eu_all = p.sb([128, NT, 128], U32, "eu_all")
        gate_all = p.sb([128, NT, 128], F32, "gate_all")
        G2b = [p.sb([128, D], F32, f"G2b{s}") for s in range(2)]
        m1 = p.sb_mark()
        hT = p.sb([128, 8, NTOK], BF16, "hT2")
        wq = p.sb([128, 8, 2048], BF16, "wq")
        for j in range(8):
            p.dma(lambda e, j=j: e.dma_start(out=wq[:, j, :], in_=I['peer_wq'][l, j * 128:(j + 1) * 128, :]),
                  w=[('wq', j)], eng="pool")
        keysT = p.sb([128, 16, 128], F32, "keysT")
        m0 = p.sb_mark()
        kraw = p.sb([128, 16, 128], F32, "kraw")
        p.dma(lambda e: e.dma_start(out=kraw[:], in_=I['peer_keys'][l].rearrange("h q n d -> n (h q) d")), w=['kraw'])
        for g in range(4):
            for i in range(4):
                hp = g * 4 + i
                p.op('pe', lambda e, g=g, i=i, hp=hp: e.transpose(out=ps[g][:, i * 128:(i + 1) * 128], in_=kraw[:, hp, :],
                                                                 identity=ident_f[:]), r=['kraw', 'identf'], w=[PS(g)])
            p.op('act', lambda e, g=g: e.activation(out=keysT[:, g * 4:(g + 1) * 4, :],
                                                    in_=ps[g][:, :].rearrange("p (i n) -> p i n", i=4), func=AF.Copy),
                 r=[PS(g)], w=['keysT'])
        p.barrier()
        p.sb_reset(m0)
        norm_tiles(l, 1, S['xs'], hT, lambda t: t * 128, tm_dram=S['h2'])
        p.barrier()
        p.sb_reset(m0)
        for s in range(2):
            load_bc(G2b[s][:], S['mod'][l, s, 5 * D:6 * D], ('G2b', s))
        qT = p.sb([128, 16, 128], F32, "qT")
        sc = p.sb([128, 16, 128], F32, "sc")
        sc2 = p.sb([128, 16, 128], F32, "sc2")
        sv = p.sb([128, 16, 16], F32, "sv")
        si = p.sb([128, 16, 16], U32, "si")
        sif = p.sb([128, 16, 16], F32, "sif")
        cand = p.sb([128, 8, 16, 16], F32, "cand")
        cand2 = p.sb([128, 8, 16, 16], F32, "cand2")
        eidx = p.sb([128, 8, 16, 16], F32, "eidx")
        best = p.sb([128, 8, 16], F32, "best")
        ci = p.sb([128, 8, 16], U32, "ci")
        cif = p.sb([128, 8, 16], F32, "cif")
        iota = p.sb([128, 256], F32, "iota")
        p.dma(lambda e: e.dma_start(out=iota[:], in_=I['iota']), w=['iota'])
        eq4 = p.sb([128, 8, 16, 16], F32, "eq4")
        cu = p.sb([128, 2, 8, 16], U32, "cu")
        cf = p.sb([128, 2, 8, 16], F32, "cf")
        e12 = p.sb([128, 2, 8, 16], F32, "e12")
        esel = p.sb([128, 128], F32, "esel")
        g8 = p.sb([128, 8], F32, "g8")
        tiles = [t for t in range(NT) if not (t < 2 and last)]
        for t in tiles:
            for g in range(4):
                for i in range(4):
                    hp = g * 4 + i
                    for j in range(8):
                        p.op('pe', lambda e, g=g, i=i, hp=hp, j=j, t=t: e.matmul(
                            ps[g][:, i * 128:(i + 1) * 128], wq[:, j, hp * 128:(hp + 1) * 128],
                            hT[:, j, t * 128:(t + 1) * 128], start=(j == 0), stop=(j == 7)),
                            r=[('hT', t), ('wq', j)], w=[PS(g)])
                p.op('act', lambda e, g=g: e.activation(out=qT[:, g * 4:(g + 1) * 4, :],
                                                        in_=ps[g][:, :].rearrange("p (i n) -> p i n", i=4), func=AF.Copy),
                     r=[PS(g)], w=['qT'])
            for g in range(4):
                for i in range(4):
                    hp = g * 4 + i
                    p.op('pe', lambda e, g=g, i=i, hp=hp: e.matmul(ps[4 + g][:, i * 128:(i + 1) * 128], qT[:, hp, :],
                                                                   keysT[:, hp, :], start=True, stop=True),
                         r=['qT', 'keysT'], w=[PS(4 + g)])
                p.op('act', lambda e, g=g: e.activation(out=sc[:, g * 4:(g + 1) * 4, :],
                                                        in_=ps[4 + g][:, :].rearrange("p (i n) -> p i n", i=4), func=AF.Copy),
                     r=[PS(4 + g)], w=['sc'])
            for hp in range(16):
                p.op('dve', lambda e, hp=hp: e.max(out=sv[:, hp, 0:8], in_=sc[:, hp, :]), r=['sc'], w=['sv'])
                p.op('dve', lambda e, hp=hp: e.max_index(out=si[:, hp, 0:8], in_max=sv[:, hp, 0:8], in_values=sc[:, hp, :]),
                     r=['sc', 'sv'], w=['si'])
                p.op('dve', lambda e, hp=hp: e.match_replace(out=sc2[:, hp, :], in_to_replace=sv[:, hp, 0:8],
                                                             in_values=sc[:, hp, :], imm_value=-1e30),
                     r=['sc', 'sv'], w=['sc2'])
                p.op('dve', lambda e, hp=hp: e.max(out=sv[:, hp, 8:16], in_=sc2[:, hp, :]), r=['sc2'], w=['sv'])
                p.op('dve', lambda e, hp=hp: e.max_index(out=si[:, hp, 8:16], in_max=sv[:, hp, 8:16], in_values=sc2[:, hp, :]),
                     r=['sc2', 'sv'], w=['si'])
            p.op('dve', lambda e: e.tensor_copy(out=sif[:], in_=si[:]), r=['si'], w=['sif'])
            svv = sv[:].rearrange("p (h q) k -> p h q k", q=2)
            sfv = sif[:].rearrange("p (h q) k -> p h q k", q=2)
            p.op('dve', lambda e, svv=svv: e.tensor_tensor(
                out=cand[:], in0=svv[:, :, 0, :].unsqueeze(3).to_broadcast([128, 8, 16, 16]),
                in1=svv[:, :, 1, :].unsqueeze(2).to_broadcast([128, 8, 16, 16]), op=ALU.add), r=['sv'], w=['cand'])
            p.op('dve', lambda e, sfv=sfv: e.tensor_scalar(out=sfv[:, :, 0, :], in0=sfv[:, :, 0, :], scalar1=128.0,
                                                           scalar2=None, op0=ALU.mult), r=['sif'], w=['sif'])
            for h in range(8):
                ch = cand[:, h].rearrange("p a b -> p (a b)")
                ch2 = cand2[:, h].rearrange("p a b -> p (a b)")
                p.op('dve', lambda e, h=h, ch=ch: e.max(out=best[:, h, 0:8], in_=ch), r=['cand'], w=['best'])
                p.op('dve', lambda e, h=h, ch=ch: e.max_index(out=ci[:, h, 0:8], in_max=best[:, h, 0:8], in_values=ch),
                     r=['cand', 'best'], w=['ci'])
                p.op('dve', lambda e, h=h, ch=ch, ch2=ch2: e.match_replace(out=ch2, in_to_replace=best[:, h, 0:8],
                                                                           in_values=ch, imm_value=-1e30),
                     r=['cand', 'best'], w=['cand2'])
                p.op('dve', lambda e, h=h, ch2=ch2: e.max(out=best[:, h, 8:16], in_=ch2), r=['cand2'], w=['best'])
                p.op('dve', lambda e, h=h, ch2=ch2: e.max_index(out=ci[:, h, 8:16], in_max=best[:, h, 8:16], in_values=ch2),
                     r=['cand2', 'best'], w=['ci'])
            p.op('dve', lambda e: e.tensor_scalar(out=cu[:, 0], in0=ci[:], scalar1=4, scalar2=None,
                                                  op0=ALU.logical_shift_right), r=['ci'], w=['cu'])
            p.op('dve', lambda e: e.tensor_scalar(out=cu[:, 1], in0=ci[:], scalar1=15, scalar2=None,
                                                  op0=ALU.bitwise_and), r=['ci'], w=['cu'])
            p.op('dve', lambda e: e.tensor_copy(out=cf[:], in_=cu[:]), r=['cu'], w=['cf'])
            io16 = iota[:, 0:16].unsqueeze(1).unsqueeze(1).to_broadcast([128, 8, 16, 16])
            for q in range(2):
                p.op('dve', lambda e, q=q, io16=io16: e.tensor_tensor(
                    out=eq4[:], in0=io16, in1=cf[:, q].unsqueeze(3).to_broadcast([128, 8, 16, 16]), op=ALU.is_equal),
                    r=['iota', 'cf'], w=['eq4'])
                p.op('dve', lambda e, q=q, sfv=sfv: e.tensor_tensor(
                    out=eq4[:], in0=eq4[:], in1=sfv[:, :, q, :].unsqueeze(2).to_broadcast([128, 8, 16, 16]), op=ALU.mult),
                    r=['eq4', 'sif'], w=['eq4'])
                p.op('dve', lambda e, q=q: e.tensor_reduce(out=e12[:, q], in_=eq4[:], axis=AX.X, op=ALU.add),
                     r=['eq4'], w=['e12'])
            p.op('dve', lambda e: e.tensor_tensor(out=esel[:], in0=e12[:, 0].rearrange("p h k -> p (h k)"),
                                                  in1=e12[:, 1].rearrange("p h k -> p (h k)"), op=ALU.add),
                 r=['e12'], w=['esel'])
            p.op('dve', lambda e, t=t: e.tensor_copy(out=eu_all[:, t, :], in_=esel[:]), r=['esel'], w=[('eu', t)])
            gv = gate_all[:, t, :].rearrange("p (h k) -> p h k", h=8)
            p.op('dve', lambda e, gv=gv: e.tensor_tensor(out=gv, in0=best[:],
                                                         in1=best[:, :, 0:1].to_broadcast([128, 8, 16]), op=ALU.subtract),
                 r=['best'], w=[('gate', t)])
            p.op('act', lambda e, t=t: e.activation(out=gate_all[:, t, :], in_=gate_all[:, t, :], func=AF.Exp),
                 r=[('gate', t)], w=[('gate', t)])
            p.op('dve', lambda e, gv=gv: e.tensor_reduce(out=g8[:], in_=gv, axis=AX.X, op=ALU.add), r=[('gate', t)], w=['g8'])
            p.op('dve', lambda e: e.reciprocal(out=g8[:], in_=g8[:]), r=['g8'], w=['g8'])
            p.op('dve', lambda e, gv=gv: e.tensor_tensor(out=gv, in0=gv, in1=g8[:].unsqueeze(2).to_broadcast([128, 8, 16]),
                                                         op=ALU.mult), r=[('gate', t), 'g8'], w=[('gate', t)])
        p.barrier()
        p.sb_reset(m1)
        h2 = [p.sb([128, D], F32, f"h2{i}") for i in range(2)]
        xt = [p.sb([128, D], F32, f"xp{i}") for i in range(2)]
        act = [p.sb([128, 128], F32, f"actv{i}") for i in range(2)]
        wg = [p.sb([128, 128], F32, f"wg{i}") for i in range(2)]
        NACC = 1
        acc = [[p.sb([128, D], F32, f"acc{i}{k}") for k in range(NACC)] for i in range(2)]
        junk = p.sb([128, D], BF16, "pjunk")
        NG = 32
        GS = 8
        gbuf = [p.sb([128, 2 * D], BF16, f"gb{i}") for i in range(NG)]
        NDG = 8
        dg = [p.sb([128, 128], BF16, f"dg{i}") for i in range(NDG)]
        gi = 0
        di = 0
        for t in tiles:
            b = t % 2
            s = 1 if t < 2 else 0
            rows = slice(t * 128, (t + 1) * 128)
            p.dma(lambda e, b=b, rows=rows: e.dma_start(out=h2[b][:], in_=S['h2'][rows, :]), w=[('h2', b)])
            p.dma(lambda e, b=b, rows=rows: e.dma_start(out=xt[b][:], in_=S['xs'][rows, :]), w=[('xp', b)])
            p.op('dve', lambda e, b=b: e.memset(act[b][:], 0.0), w=[('actv', b)])
            for g in range(128 // GS):
                ks = []
                for sidx in range(g * GS, (g + 1) * GS):
                    k = gi % NG
                    gi += 1
                    ks.append(k)
                    p.dma(lambda e, k=k, t=t, sidx=sidx: e.indirect_dma_start(
                        out=gbuf[k][:], out_offset=None, in_=S['T'][l],
                        in_offset=bass.IndirectOffsetOnAxis(ap=eu_all[:, t, sidx:sidx + 1], axis=0)),
                        r=[], w=[('gb', k)], eng="pool")
                    p.op('dve', lambda e, k=k, b=b, sidx=sidx: e.scalar_tensor_tensor(
                        out=junk[:], in0=gbuf[k][:, 0:D], scalar=1.0, in1=h2[b][:], op0=ALU.mult, op1=ALU.mult,
                        accum_out=act[b][:, sidx:sidx + 1]), r=[('gb', k), ('h2', b), ('actv', b)], w=[('actc', b, sidx)])
                gs = slice(g * GS, (g + 1) * GS)
                p.op('act', lambda e, b=b, gs=gs: e.activation(out=wg[b][:, gs], in_=act[b][:, gs], func=AF.Gelu),
                     r=[('actc', b, sidx) for sidx in range(g * GS, (g + 1) * GS)], w=[('wg', b, g)])
                p.op('dve', lambda e, b=b, gs=gs, t=t: e.tensor_tensor(out=wg[b][:, gs], in0=wg[b][:, gs],
                                                                      in1=gate_all[:, t, gs], op=ALU.mult),
                     r=[('wg', b, g)], w=[('wg', b, g)])
                for j, sidx in enumerate(range(g * GS, (g + 1) * GS)):
                    k = ks[j]
                    dj = di % NDG
                    di += 1
                    p.op('act', lambda e, dj=dj, b=b, sidx=sidx: e.activation(
                        out=dg[dj][:], in_=ident_f[:], func=AF.Copy, scale=wg[b][:, sidx:sidx + 1]),
                        r=[('wg', b, g), 'identf'], w=[('dg', dj)])
                    for half in range(2):
                        bank = 2 * b + half
                        p.op('pe', lambda e, dj=dj, k=k, half=half, bank=bank, sidx=sidx: e.matmul(
                            ps[bank][:, :], dg[dj][:], gbuf[k][:, D + half * 512:D + (half + 1) * 512],
                            start=(sidx == 0), stop=(sidx == 127)), r=[('dg', dj), ('gb', k)], w=[PS(bank), ('gbr', k, half)])
            a0 = acc[b][0]
            for half in range(2):
                hs_ = slice(half * 512, (half + 1) * 512)
                p.op('dve', lambda e, a0=a0, s=s, b=b, half=half, hs_=hs_: e.tensor_tensor(
                    out=a0[:, hs_], in0=ps[2 * b + half][:, :], in1=G2b[s][:, hs_], op=ALU.mult),
                    r=[PS(2 * b + half), ('G2b', s)], w=[('acc', b, 0)])
            p.op('dve', lambda e, a0=a0, b=b: e.tensor_tensor(out=xt[b][:], in0=xt[b][:], in1=a0[:], op=ALU.add),
                 r=[('acc', b, 0), ('xp', b)], w=[('xp', b)])
            if last:
                p.dma(lambda e, b=b, t=t: e.dma_start(out=out_d[(t - 2) * 128:(t - 1) * 128, :], in_=xt[b][:]),
                      r=[('xp', b)], w=[('outd', t)])
            else:
                p.dma(lambda e, b=b, rows=rows: e.dma_start(out=S['xs'][rows, :], in_=xt[b][:]),
                      r=[('xp', b)], w=[('Sxs', t)])
        p.barrier()

    PHASES = cfg.get("phases", ["proj", "rprep", "scan", "rout", "peer"])

    phase_mod()
    for l in range(cfg.get("layers", DEPTH)):
        if 'proj' in PHASES:
            qkT, Vaug, mp = phase_proj(l)
            phase_attn(l, qkT, Vaug, mp)
        if 'rprep' in PHASES:
            phase_rprep(l)
        if 'scan' in PHASES:
            phase_scan(l)
        if 'rout' in PHASES:
            phase_rout(l)
        if 'peer' in PHASES:
            phase_peer(l)
    p.barrier()
    p.emit()
    return nc


def prep_inputs(inputs):
    f = lambda a: np.ascontiguousarray(np.asarray(a, dtype=np.float32))
    x, c, ctx, c_ctx = f(inputs['x']), f(inputs['c']), f(inputs['ctx']), f(inputs['c_ctx'])
    shared = {}
    for n in ['norm_mix', 'norm_ffn', 'w_mod', 'b_mod', 'w_in', 'w_out', 'a_qnorm', 'a_knorm', 'b_qnorm', 'b_knorm',
              'a_sink']:
        shared[n] = f(inputs[n])
    rpb = f(inputs['b_rpb'])
    btab = np.zeros((DEPTH, 128, NTAB, 4, 128), np.float32)
    bmask = np.zeros((128, NTAB, 128), np.float32)
    for i, (dr, dc, valid) in enumerate(NA_TABS):
        g = rpb[:, :, dr, dc]
        btab[:, :, i, :, :] = np.where(valid[None, None], g, 0.0).transpose(0, 2, 1, 3)
        bmask[:, i, :] = valid
    shared['btab'] = btab
    shared['bmask'] = bmask
    ar = np.arange(128)
    am = np.zeros((128, 2, 128), np.float32)
    am[:, 0, :] = (ar[:, None] >= ar[None, :])
    am[:, 1, :] = (ar[:, None] <= ar[None, :])
    shared['amask'] = am
    shared['ident'] = np.eye(128, dtype=np.float32)
    cos, sin = rope_tables()
    shared['cos'], shared['sin'] = cos, sin
    rc = f(inputs['r7_conv'])
    for n in ['r7_w0', 'r7_a0', 'r7_w2', 'r7_a2', 'r7_g2', 'r7_kk', 'r7_ka', 'r7_lnw', 'r7_lnb', 'r7_rk', 'peer_wq', 'peer_keys']:
        shared[n] = f(inputs[n])
    for l in range(DEPTH):
        shared[f'peer_u{l}'] = f(inputs['peer_u'][l])
        shared[f'peer_v{l}'] = f(inputs['peer_v'][l])
    a64 = np.arange(64)
    tri = np.zeros((64, 2, 64), np.float32)
    tri[:, 0, :] = a64[:, None] <= a64[None, :]
    tri[:, 1, :] = a64[:, None] >= a64[None, :]
    mg = np.zeros((64, 2, 128), np.float32)
    mg[:, 0, 0:64] = a64[:, None] < a64[None, :]
    mg[:, 0, 64:128] = a64[:, None] <= a64[None, :]
    mg[:, 1, 0:64] = a64[:, None] > a64[None, :]
    mg[:, 1, 64:128] = a64[:, None] >= a64[None, :]
    mn = np.zeros((64, 2, 64), np.float32)
    mn[:, 0, :] = a64[None, :] < a64[:, None]
    mn[:, 1, :] = a64[None, :] > a64[:, None]
    shared['tri'], shared['mg'], shared['mn'] = tri, mg, mn
    shared['iota'] = np.ascontiguousarray(np.broadcast_to(np.arange(256, dtype=np.float32), (128, 256)))
    shared['r7_conv'] = np.ascontiguousarray(rc.reshape(DEPTH, 3, 15, 128).transpose(0, 3, 2, 1))
    maps = []
    for b in range(8):
        m = dict(shared)
        m['x'] = np.ascontiguousarray(np.concatenate([ctx[b], x[b]], axis=0))
        cc = np.stack([c[b], c_ctx], axis=-1)
        m['cc'] = np.ascontiguousarray(cc.reshape(8, 128, 2).transpose(1, 0, 2))
        maps.append(m)
    return maps


_NC_CACHE = {}


def kernel(**inputs):
    if 'nc' not in _NC_CACHE:
        _NC_CACHE['nc'] = build({})
    nc = _NC_CACHE['nc']
    maps = prep_inputs(inputs)
    res = run_bass_kernel_spmd(nc, maps, core_ids=list(range(8)))
    return np.stack([np.asarray(r['out'], dtype=np.float32) for r in res.results], axis=0)
```

```python
import numpy as np
import ml_dtypes
import concourse.bass as bass
import concourse.mybir as mybir
from concourse.bass_utils import run_bass_kernel_spmd

F32 = mybir.dt.float32
BF16 = mybir.dt.bfloat16
U32 = mybir.dt.uint32
I32 = mybir.dt.int32
AF = mybir.ActivationFunctionType
ALU = mybir.AluOpType
AX = mybir.AxisListType

ENGS = ["pe", "act", "dve", "pool", "sp"]
DT_SIZE = {F32: 4, BF16: 2, U32: 4, I32: 4}

D = 1024
NCTX = 256
NLAT = 2048
NTOK = NCTX + NLAT
NT = NTOK // 128
DEPTH = 2
EPS = 1e-6


class Prog:
    def __init__(self, nc, n_dma_sems=32):
        self.nc = nc
        self.ops = {e: [] for e in ENGS}
        self.cnt = {e: 0 for e in ENGS}
        self.waited = {e: {} for e in ENGS}
        self.res = {}
        self.n_dma_sems = n_dma_sems
        self.dma_use = [0] * n_dma_sems
        self.dma_last = [None] * n_dma_sems
        self.dma_rr = 0
        self.sb_off = 16 * 1024
        self.sb_id = 0
        self.SB_CAP = 216 * 1024

    def sb_mark(self):
        return self.sb_off

    def sb_reset(self, off=0):
        self.sb_off = off

    def sb(self, shape, dtype, name=""):
        nbytes = int(np.prod(shape[1:])) * DT_SIZE[dtype]
        off = (self.sb_off + 63) // 64 * 64
        assert off + nbytes <= self.SB_CAP, f"SBUF overflow {off}+{nbytes} ({name})"
        self.sb_off = off + nbytes
        self.sb_id += 1
        return self.nc.alloc_sbuf_tensor_at(f"sb{self.sb_id}_{name}", list(shape), dtype, offset=off)

    def _deps(self, r, w):
        deps = []
        for k in r:
            st = self.res.get(k)
            if st and st[0] is not None:
                deps.append(st[0])
        for k in w:
            st = self.res.get(k)
            if st:
                if st[0] is not None:
                    deps.append(st[0])
                deps.extend(st[1])
        return deps

    def _commit(self, tok, r, w):
        for k in r:
            st = self.res.setdefault(k, [None, []])
            st[1].append(tok)
        for k in w:
            self.res[k] = [tok, []]

    def _waits_for(self, eng, deps):
        wd = self.waited[eng]
        best = {}
        for t in deps:
            if t[0] == 'c':
                if t[1] == eng and eng == 'pe':
                    continue
                key = ('c', t[1])
            else:
                key = ('d', t[1])
            if wd.get(key, 0) >= t[2]:
                continue
            best[key] = max(best.get(key, 0), t[2])
        for k, v in best.items():
            wd[k] = v
        return list(best.items())

    def op(self, eng, fn, r=(), w=()):
        deps = self._deps(r, w)
        waits = self._waits_for(eng, deps)
        self.cnt[eng] += 1
        tok = ('c', eng, self.cnt[eng])
        self.ops[eng].append((waits, fn, ('c', eng), 1))
        self._commit(tok, r, w)
        return tok

    def dma(self, fn, r=(), w=(), eng="sp"):
        deps = list(self._deps(r, w))
        i = self.dma_rr
        self.dma_rr = (self.dma_rr + 1) % self.n_dma_sems
        if self.dma_last[i] is not None:
            deps.append(self.dma_last[i])
        waits = self._waits_for(eng, deps)
        self.dma_use[i] += 1
        tok = ('d', i, 16 * self.dma_use[i])
        self.dma_last[i] = tok
        self.ops[eng].append((waits, fn, ('d', i), 16))
        self._commit(tok, r, w)
        return tok

    def barrier(self):
        toks = [('c', e, self.cnt[e]) for e in ENGS if self.cnt[e] > 0]
        toks += [t for t in self.dma_last if t is not None]
        for e in ENGS:
            waits = self._waits_for(e, toks)
            if waits:
                self.ops[e].append((waits, None, None, 0))
        self.res = {}

    def emit(self):
        nc = self.nc
        from contextlib import ExitStack
        with ExitStack() as es:
            csem = {e: es.enter_context(nc.semaphore(f"c_{e}")) for e in ENGS}
            dsem = [es.enter_context(nc.semaphore(f"d_{i}")) for i in range(self.n_dma_sems)]
            block = es.enter_context(nc.Block())

            def sem_of(key):
                return csem[key[1]] if key[0] == 'c' else dsem[key[1]]

            def run(engname, e):
                for waits, fn, inc_key, inc in self.ops[engname]:
                    for k, v in waits:
                        e.wait_ge(sem_of(k), v)
                    if fn is None:
                        continue
                    ins = fn(e)
                    ins.then_inc(sem_of(inc_key), inc)

            @block.tensor
            def _(e):
                run("pe", e)

            @block.scalar
            def _(e):
                run("act", e)

            @block.vector
            def _(e):
                run("dve", e)

            @block.gpsimd
            def _(e):
                run("pool", e)

            @block.sync
            def _(e):
                run("sp", e)


def na_tables():
    cases = {}
    tabs = []
    keys = {}
    ar = np.arange(128)
    for p in range(16):
        for kb in range(16):
            krow = 2 * kb + ar // 64
            kcol = ar % 64
            qrow = 2 * p + ar // 64
            qcol = ar % 64
            rs = np.clip(qrow - 4, 0, 24)
            vr = (krow[:, None] >= rs[None, :]) & (krow[:, None] < rs[None, :] + 8)
            ws = np.clip(qcol - 8, 0, 48)
            vc = (kcol[:, None] >= ws[None, :]) & (kcol[:, None] < ws[None, :] + 16)
            valid = vr & vc
            if not valid.any():
                continue
            dr = krow[:, None] - qrow[None, :] + 7
            dc = np.clip(kcol[:, None] - qcol[None, :] + 15, 0, 30)
            dr = np.where(valid, dr, 0)
            dc = np.where(valid, dc, 0)
            key = (dr.tobytes(), dc.tobytes(), valid.tobytes())
            if key not in keys:
                keys[key] = len(tabs)
                tabs.append((dr, dc, valid))
            cases[(p, kb)] = keys[key]
    return cases, tabs


NA_CASES, NA_TABS = na_tables()
NTAB = len(NA_TABS)


def rope_tables():
    t = np.arange(NLAT)
    inv_freq = 10000.0 ** (-np.arange(0, 32, 2) / 32)
    ang = np.stack([(t // 64)[:, None] * inv_freq[None], (t % 64)[:, None] * inv_freq[None]], axis=1)
    return np.cos(ang).astype(np.float32).reshape(NLAT, 32), np.sin(ang).astype(np.float32).reshape(NLAT, 32)


def build(cfg=None):
    cfg = cfg or {}
    dbg = cfg.get("dbg", [])
    nc = bass.Bass("TRN2", target_bir_lowering=False)
    p = Prog(nc)

    def din(name, shape, dt=F32):
        return nc.dram_tensor(name, list(shape), dt, kind="ExternalInput").ap()

    def dscr(name, shape, dt=F32):
        kind = "Internal"
        if name in cfg.get("dump", []):
            kind = "ExternalOutput"
        if name in cfg.get("feed", []):
            kind = "ExternalInput"
        return nc.dram_tensor(name, list(shape), dt, kind=kind).ap()

    I = {}
    I['x'] = din('x', [NTOK, D])
    I['cc'] = din('cc', [128, 8, 2])
    I['norm_mix'] = din('norm_mix', [DEPTH, D])
    I['norm_ffn'] = din('norm_ffn', [DEPTH, D])
    I['w_mod'] = din('w_mod', [DEPTH, D, 6 * D])
    I['b_mod'] = din('b_mod', [DEPTH, 6 * D])
    I['w_in'] = din('w_in', [DEPTH, D, 3200])
    I['w_out'] = din('w_out', [DEPTH, D, D])
    for n in ['a_qnorm', 'a_knorm', 'b_qnorm', 'b_knorm']:
        I[n] = din(n, [DEPTH, 64])
    I['a_sink'] = din('a_sink', [DEPTH, 4])
    I['btab'] = din('btab', [DEPTH, 128, NTAB, 4, 128])
    I['bmask'] = din('bmask', [128, NTAB, 128])
    I['amask'] = din('amask', [128, 2, 128])
    I['ident'] = din('ident', [128, 128])
    I['cos'] = din('cos', [NLAT, 32])
    I['sin'] = din('sin', [NLAT, 32])
    I['r7_conv'] = din('r7_conv', [DEPTH, 128, 15, 3])
    I['r7_w0'] = din('r7_w0', [DEPTH, 2, 512])
    I['r7_a0'] = din('r7_a0', [DEPTH, 2, 512])
    I['r7_w2'] = din('r7_w2', [DEPTH, 2, 64, 512])
    I['r7_a2'] = din('r7_a2', [DEPTH, 2, 64, 512])
    I['r7_g2'] = din('r7_g2', [DEPTH, 128, 512])
    for n in ['r7_kk', 'r7_ka', 'r7_lnw', 'r7_lnb']:
        I[n] = din(n, [DEPTH, 512])
    I['r7_rk'] = din('r7_rk', [DEPTH, 8, 64])
    I['peer_wq'] = din('peer_wq', [DEPTH, D, 2048])
    I['peer_keys'] = din('peer_keys', [DEPTH, 8, 2, 128, 128])
    I['peer_u'] = [din(f'peer_u{l}', [16384, D]) for l in range(DEPTH)]
    I['peer_v'] = [din(f'peer_v{l}', [16384, D]) for l in range(DEPTH)]
    I['iota'] = din('iota', [128, 256])
    I['tri'] = din('tri', [64, 2, 64])
    I['mg'] = din('mg', [64, 2, 128])
    I['mn'] = din('mn', [64, 2, 64])
    out_d = nc.dram_tensor('out', [NLAT, D], F32, kind="ExternalOutput").ap()

    S = {}
    S['mod'] = dscr('s_mod', [DEPTH, 2, 6 * D])
    S['xs'] = dscr('s_xs', [NTOK, D])
    S['o'] = dscr('s_o', [NTOK, D])
    S['pcT'] = dscr('s_pcT', [1920, NTOK])
    S['tm'] = dscr('s_tm', [NTOK, 10, 512])
    S['bon'] = dscr('s_bon', [NTOK, 8])
    S['y'] = dscr('s_y', [2, NTOK, 512])
    S['h2'] = dscr('s_h2', [NTOK, D])
    S['T'] = [dscr(f's_T{l}', [16384, 2 * D], BF16) for l in range(DEPTH)]
    DBG = {}
    for name, shape in cfg.get("dbg_out", {}).items():
        DBG[name] = nc.dram_tensor(name, list(shape), F32, kind="ExternalOutput").ap()

    ps = [nc.alloc_psum_tensor(f"ps{i}", [128, 512], F32) for i in range(8)]

    def PS(i):
        return ('ps', i)

    ident_f = p.sb([128, 128], F32, "identf")
    ident_b = p.sb([128, 128], BF16, "identb")
    eps_col = p.sb([128, 1], F32, "eps")
    p.dma(lambda e: e.dma_start(out=ident_f[:], in_=I['ident']), w=['identf'])
    p.op('dve', lambda e: e.tensor_copy(out=ident_b[:], in_=ident_f[:]), r=['identf'], w=['identb'])
    p.op('dve', lambda e: e.memset(eps_col[:], EPS), w=['eps'])
    p.barrier()
    base_mark = p.sb_mark()

    def phase_mod():
        p.sb_reset(base_mark)
        cc = p.sb([128, 8, 2], F32, "cc")
        scc = p.sb([128, 8, 2], F32, "scc")
        p.dma(lambda e: e.dma_start(out=cc[:], in_=I['cc']), w=['cc'])
        p.op('act', lambda e: e.activation(out=scc[:], in_=cc[:], func=AF.Silu), r=['cc'], w=['scc'])
        wt = [p.sb([128, 8, 512], F32, f"wmod{i}") for i in range(2)]
        bm = p.sb([2, 6 * D], F32, "bm")
        mo = p.sb([2, 6 * D], F32, "mo")
        k = 0
        for l in range(DEPTH):
            p.dma(lambda e, l=l: e.dma_start(out=bm[:], in_=I['b_mod'][l].partition_broadcast(2)),
                  w=['bm'])
            for cch in range(12):
                b = k % 2
                k += 1
                src = I['w_mod'][l, :, cch * 512:(cch + 1) * 512].rearrange("(j p) n -> p j n", p=128)
                p.dma(lambda e, b=b, src=src: e.dma_start(out=wt[b][:], in_=src), w=[('wmod', b)])
                pb = cch % 2
                for j in range(8):
                    p.op('pe', lambda e, b=b, j=j, pb=pb: e.matmul(ps[pb][0:2, :], scc[:, j, :], wt[b][:, j, :],
                                                                    start=(j == 0), stop=(j == 7)),
                         r=['scc', ('wmod', b)], w=[PS(pb)])
                p.op('dve', lambda e, pb=pb, cch=cch: e.tensor_tensor(
                    out=mo[:, cch * 512:(cch + 1) * 512], in0=ps[pb][0:2, :], in1=bm[:, cch * 512:(cch + 1) * 512],
                    op=ALU.add), r=[PS(pb), 'bm'], w=['mo'])
            p.dma(lambda e, l=l: e.dma_start(out=S['mod'][l], in_=mo[:]), r=['mo'], w=['S_mod'])
        p.barrier()

    def load_bc(dst, src_1d, key):
        P = dst.shape[0]
        p.dma(lambda e: e.dma_start(out=dst, in_=src_1d.partition_broadcast(P)), w=[key])

    def norm_tiles(l, which, src, hT, hT_off, tm_dram=None):
        nv = I['norm_mix'] if which == 0 else I['norm_ffn']
        so = 0 if which == 0 else 3
        G = [p.sb([128, D], F32, f"G{s}") for s in range(2)]
        SH = [p.sb([128, D], F32, f"SH{s}") for s in range(2)]
        tmp = p.sb([128, D], F32, "gtmp")
        for s in range(2):
            load_bc(tmp[:], nv[l], 'gtmp')
            load_bc(G[s][:], S['mod'][l, s, (so + 1) * D:(so + 2) * D], ('G', s))
            load_bc(SH[s][:], S['mod'][l, s, so * D:(so + 1) * D], ('SH', s))
            p.op('dve', lambda e, s=s: e.scalar_tensor_tensor(out=G[s][:], in0=G[s][:], scalar=1.0, in1=tmp[:],
                                                             op0=ALU.add, op1=ALU.mult),
                 r=['gtmp', ('G', s)], w=[('G', s)])
        NBUF = 4
        xt = [p.sb([128, D], F32, f"xt{i}") for i in range(NBUF)]
        junk = p.sb([128, D], F32, "junk")
        hb = [p.sb([128, D], BF16, f"hb{i}") for i in range(NBUF)]
        ss = [p.sb([128, 1], F32, f"ss{i}") for i in range(NBUF)]
        def stageA(t):
            b = t % NBUF
            s = 1 if t < 2 else 0
            yield
            p.dma(lambda e, b=b, t=t: e.dma_start(out=xt[b][:], in_=src[t * 128:(t + 1) * 128, :]), w=[('xt', b)])
            yield
            p.op('act', lambda e, b=b: e.activation(out=junk[:], in_=xt[b][:], func=AF.Square, accum_out=ss[b][:]),
                 r=[('xt', b)], w=[('ss', b)])
            yield
            p.op('act', lambda e, b=b: e.activation(out=ss[b][:], in_=ss[b][:], func=AF.Sqrt, bias=eps_col[:],
                                                    scale=1.0 / D), r=[('ss', b)], w=[('ss', b)])
            yield
            p.op('dve', lambda e, b=b: e.reciprocal(out=ss[b][:], in_=ss[b][:]), r=[('ss', b)], w=[('ss', b)])
            yield
            p.op('dve', lambda e, b=b, s=s: e.scalar_tensor_tensor(out=xt[b][:], in0=xt[b][:], scalar=ss[b][:, 0:1],
                                                                 in1=G[s][:], op0=ALU.mult, op1=ALU.mult),
                 r=[('xt', b), ('ss', b), ('G', s)], w=[('xt', b)])
            yield
            if tm_dram is not None:
                p.op('dve', lambda e, b=b, s=s: e.tensor_tensor(out=xt[b][:], in0=xt[b][:], in1=SH[s][:], op=ALU.add),
                     r=[('xt', b), ('SH', s)], w=[('xt', b)])
                p.dma(lambda e, b=b, t=t: e.dma_start(out=tm_dram[t * 128:(t + 1) * 128, :], in_=xt[b][:]),
                      r=[('xt', b)], w=[('tmd', t)])
                p.op('act', lambda e, b=b: e.activation(out=hb[b][:], in_=xt[b][:], func=AF.Copy),
                     r=[('xt', b)], w=[('hb', b)])
            else:
                p.op('dve', lambda e, b=b, s=s: e.tensor_tensor(out=hb[b][:], in0=xt[b][:], in1=SH[s][:], op=ALU.add),
                     r=[('xt', b), ('SH', s)], w=[('hb', b)])

        def stageB(t):
            b = t % NBUF
            pbank = 4 + b
            pv = ps[pbank][:, 0:512].bitcast(BF16)
            yield
            for j in range(8):
                p.op('pe', lambda e, b=b, j=j, pv=pv: e.transpose(out=pv[:, j * 128:(j + 1) * 128],
                                                                 in_=hb[b][:, j * 128:(j + 1) * 128],
                                                                 identity=ident_b[:]),
                     r=[('hb', b), 'identb'], w=[PS(pbank)])
            o = hT_off(t)
            yield
            p.op('act', lambda e, pv=pv, o=o: e.activation(
                out=hT[:, :, o:o + 128], in_=pv.rearrange("p (j t) -> p j t", j=8), func=AF.Copy),
                r=[PS(pbank)], w=[('hT', t)])


        tl_ = [t for t in range(NT) if not (t < 2 and l == DEPTH - 1 and which == 1)]
        def rr(gens):
            gens = list(gens)
            while gens:
                for g_ in list(gens):
                    try:
                        next(g_)
                    except StopIteration:
                        gens.remove(g_)

        groups = [tl_[i:i + NBUF] for i in range(0, len(tl_), NBUF)]
        for gi_ in range(len(groups) + 1):
            gl = []
            if gi_ < len(groups):
                gl += [stageA(t) for t in groups[gi_]]
            if gi_ > 0:
                gl += [stageB(t) for t in groups[gi_ - 1]]
            rr(gl)

    def phase_proj(l):
        p.sb_reset(base_mark)
        qkT = p.sb([64, 14, NTOK], BF16, "qkT")
        Vaug = p.sb([128, NT, 6, 65], BF16, "Vaug")
        mark_persist = p.sb_mark()
        hT = p.sb([128, 8, NTOK], BF16, "hT")
        m_afterh = p.sb_mark()
        wAB = p.sb([128, 8, 1280], BF16, "wAB")
        for j in range(8):
            p.dma(lambda e, j=j: e.dma_start(out=wAB[:, j, :], in_=I['w_in'][l, j * 128:(j + 1) * 128, 0:1280]),
                  w=[('wAB', j)], eng="pool")
        p.op('pool', lambda e: e.memset(Vaug[:, :, :, 64:65], 1.0), w=['Vones'])
        m0 = p.sb_mark()
        norm_tiles(l, 0, I['x'] if l == 0 else S['xs'], hT, lambda t: t * 128)
        p.barrier()
        p.sb_reset(m0)
        wC = p.sb([128, 8, 1920], BF16, "wC")
        for j in range(8):
            p.dma(lambda e, j=j: e.dma_start(out=wC[:, j, :], in_=I['w_in'][l, j * 128:(j + 1) * 128, 1280:3200]),
                  w=[('wC', j)], eng="pool")
        m_afterwc = p.sb_mark()
        GA = p.sb([128, 6, 64], F32, "GA")
        GB = p.sb([128, 8, 64], F32, "GB")
        for h in range(6):
            load_bc(GA[:, h, :], I['a_qnorm'][l] if h < 4 else I['a_knorm'][l], 'GA')
        for h in range(8):
            load_bc(GB[:, h, :], I['b_qnorm'][l] if h < 4 else I['b_knorm'][l], 'GB')
        p.op('act', lambda e: e.mul(out=GA[:, 0:4, :], in_=GA[:, 0:4, :], mul=0.125), r=['GA'], w=['GA'])
        p.op('act', lambda e: e.mul(out=GB[:, 0:4, :], in_=GB[:, 0:4, :], mul=0.125), r=['GB'], w=['GB'])
        cs = [p.sb([128, 2, 32], F32, f"cs{i}") for i in range(2)]
        xn = [p.sb([128, 14, 64], F32, f"xn{i}") for i in range(2)]
        sq = p.sb([128, 14, 64], F32, "sq")
        ssq = [p.sb([128, 14], F32, f"ssq{i}") for i in range(2)]
        xr = [p.sb([128, 14, 64], BF16, f"xr{i}") for i in range(2)]
        RT = [p.sb([128, 6, 2, 16], F32, f"ropeT{i}") for i in range(4)]
        for t in range(NT):
            b = t % 2
            lat = t >= 2
            bA, bB, bV = (0, 1, 2) if t % 2 == 0 else (3, 6, 7)
            for bank, c0, c1 in ((bA, 0, 512), (bB, 512, 1024), (bV, 1024, 1280)):
                for j in range(8):
                    p.op('pe', lambda e, bank=bank, c0=c0, c1=c1, j=j, t=t: e.matmul(
                        ps[bank][:, 0:c1 - c0], hT[:, j, t * 128:(t + 1) * 128], wAB[:, j, c0:c1],
                        start=(j == 0), stop=(j == 7)),
                        r=[('hT', t), ('wAB', j)], w=[PS(bank)])
            if lat:
                tl = t - 2
                p.dma(lambda e, b=b, tl=tl: e.dma_start(out=cs[b][:, 0, :], in_=I['cos'][tl * 128:(tl + 1) * 128, :]),
                      w=[('cs', b)])
                p.dma(lambda e, b=b, tl=tl: e.dma_start(out=cs[b][:, 1, :], in_=I['sin'][tl * 128:(tl + 1) * 128, :]),
                      w=[('cs', b)])
            p.op('act', lambda e, t=t, bA=bA: e.activation(out=Vaug[:, t, 0:2, 0:64],
                                                    in_=ps[bA][:, 384:512].rearrange("p (h d) -> p h d", h=2),
                                                    func=AF.Copy), r=[PS(bA)], w=[('V', t)])
            p.op('act', lambda e, t=t, bV=bV: e.activation(out=Vaug[:, t, 2:6, 0:64],
                                                    in_=ps[bV][:, 0:256].rearrange("p (h d) -> p h d", h=4),
                                                    func=AF.Copy), r=[PS(bV)], w=[('V', t)])
            p.op('act', lambda e, b=b, bA=bA: e.activation(out=xn[b][:, 0:6, :],
                                                    in_=ps[bA][:, 0:384].rearrange("p (h d) -> p h d", h=6),
                                                    func=AF.Copy), r=[PS(bA)], w=[('xn', b)])
            p.op('act', lambda e, b=b, bB=bB: e.activation(out=xn[b][:, 6:14, :],
                                                    in_=ps[bB][:, 0:512].rearrange("p (h d) -> p h d", h=8),
                                                    func=AF.Copy), r=[PS(bB)], w=[('xn', b)])
            p.op('dve', lambda e, b=b: e.tensor_tensor(out=sq[:], in0=xn[b][:], in1=xn[b][:], op=ALU.mult),
                 r=[('xn', b)], w=['sq'])
            p.op('dve', lambda e, b=b: e.tensor_reduce(out=ssq[b][:], in_=sq[:], axis=AX.X, op=ALU.add),
                 r=['sq'], w=[('ssq', b)])
            p.op('act', lambda e, b=b: e.activation(out=ssq[b][:], in_=ssq[b][:], func=AF.Sqrt, bias=eps_col[:],
                                                    scale=1.0 / 64), r=[('ssq', b)], w=[('ssq', b)])
            p.op('dve', lambda e, b=b: e.reciprocal(out=ssq[b][:], in_=ssq[b][:]), r=[('ssq', b)], w=[('ssq', b)])
            p.op('dve', lambda e, b=b: e.tensor_tensor(out=xn[b][:], in0=xn[b][:],
                                                       in1=ssq[b][:].unsqueeze(2).to_broadcast([128, 14, 64]),
                                                       op=ALU.mult), r=[('xn', b), ('ssq', b)], w=[('xn', b)])
            p.op('dve', lambda e, b=b: e.tensor_tensor(out=xr[b][:, 6:14, :], in0=xn[b][:, 6:14, :], in1=GB[:],
                                                       op=ALU.mult), r=[('xn', b), 'GB'], w=[('xr', b)])
            if lat:
                p.op('dve', lambda e, b=b: e.tensor_tensor(out=xn[b][:, 0:6, :], in0=xn[b][:, 0:6, :], in1=GA[:],
                                                           op=ALU.mult), r=[('xn', b), 'GA'], w=[('xn', b)])
                xv = xn[b][:, 0:6, :].rearrange("p h (a g f) -> p h a g f", a=2, g=2)
                x1 = xv[:, :, :, 0, :]
                x2 = xv[:, :, :, 1, :]
                ov = xr[b][:, 0:6, :].rearrange("p h (a g f) -> p h a g f", a=2, g=2)
                cosb = cs[b][:, 0, :].rearrange("p (a f) -> p a f", a=2).unsqueeze(1).to_broadcast([128, 6, 2, 16])
                sinb = cs[b][:, 1, :].rearrange("p (a f) -> p a f", a=2).unsqueeze(1).to_broadcast([128, 6, 2, 16])
                rk = [('xn', b), ('cs', b)]
                for i, (xa, tb) in enumerate(((x1, cosb), (x2, sinb), (x2, cosb), (x1, sinb))):
                    p.op('dve', lambda e, i=i, xa=xa, tb=tb: e.tensor_tensor(out=RT[i][:], in0=xa, in1=tb, op=ALU.mult),
                         r=rk, w=[('RT', i)])
                p.op('dve', lambda e, ov=ov: e.tensor_tensor(out=ov[:, :, :, 0, :], in0=RT[0][:], in1=RT[1][:],
                                                             op=ALU.subtract), r=[('RT', 0), ('RT', 1)], w=[('xr', b)])
                p.op('dve', lambda e, ov=ov: e.tensor_tensor(out=ov[:, :, :, 1, :], in0=RT[2][:], in1=RT[3][:],
                                                             op=ALU.add), r=[('RT', 2), ('RT', 3)], w=[('xr', b)])
            else:
                p.op('dve', lambda e, b=b: e.tensor_tensor(out=xr[b][:, 0:6, :], in0=xn[b][:, 0:6, :], in1=GA[:],
                                                           op=ALU.mult), r=[('xn', b), 'GA'], w=[('xr', b)])
            for half in range(2):
                bank = 4 + half
                pv = ps[bank][0:64, 0:448].bitcast(BF16)
                for hh in range(7):
                    h = half * 7 + hh
                    p.op('pe', lambda e, b=b, h=h, hh=hh, pv=pv: e.transpose(
                        out=pv[:, hh * 128:(hh + 1) * 128], in_=xr[b][:, h, :], identity=ident_b[:]),
                        r=[('xr', b), 'identb'], w=[PS(bank)])
                p.op('act', lambda e, half=half, pv=pv, t=t: e.activation(
                    out=qkT[:, half * 7:(half + 1) * 7, t * 128:(t + 1) * 128],
                    in_=pv.rearrange("p (h t) -> p h t", h=7), func=AF.Copy), r=[PS(bank)], w=[('qkT', t)])
        p.barrier()
        p.sb_reset(m_afterwc)
        cw = p.sb([128, 15, 3], F32, "cw")
        p.dma(lambda e: e.dma_start(out=cw[:], in_=I['r7_conv'][l]), w=['cw'])
        rawc = [p.sb([128, NCTX + 2], F32, f"rawc{i}") for i in range(2)]
        rawl = [p.sb([128, NLAT + 2], F32, f"rawl{i}") for i in range(2)]
        cvo = [p.sb([128, NTOK], F32, f"cvo{i}") for i in range(2)]
        for i in range(2):
            p.op('pool', lambda e, i=i: e.memset(rawc[i][:], 0.0), w=[('rawc', i)])
            p.op('pool', lambda e, i=i: e.memset(rawl[i][:], 0.0), w=[('rawl', i)])
        bk = 0
        for ch in range(15):
            b = ch % 2
            groups = [(rawc[b], ('rawc', b), 1, 0, 256)] + [(rawl[b], ('rawl', b), 1 + 512 * g, 256 + 512 * g, 512)
                                                             for g in range(4)]
            for (raw, rkey, ro, tok0, n) in groups:
                bank = bk % 4
                bk += 1
                for j in range(8):
                    p.op('pe', lambda e, bank=bank, j=j, ch=ch, tok0=tok0, n=n: e.matmul(
                        ps[bank][:, 0:n], wC[:, j, ch * 128:(ch + 1) * 128], hT[:, j, tok0:tok0 + n],
                        start=(j == 0), stop=(j == 7)), r=[('wC', j)], w=[PS(bank)])
                p.op('act', lambda e, raw=raw, ro=ro, n=n, bank=bank: e.activation(
                    out=raw[:, ro:ro + n], in_=ps[bank][:, 0:n], func=AF.Copy), r=[PS(bank)], w=[rkey])
            for (raw, rkey, n, o0) in ((rawc[b], ('rawc', b), NCTX, 0), (rawl[b], ('rawl', b), NLAT, NCTX)):
                dst = cvo[b][:, o0:o0 + n]
                p.op('dve', lambda e, raw=raw, n=n, dst=dst, ch=ch: e.tensor_scalar(
                    out=dst, in0=raw[:, 1:1 + n], scalar1=cw[:, ch, 1:2], scalar2=None, op0=ALU.mult),
                    r=[rkey, 'cw'], w=[('cvo', b)])
                p.op('dve', lambda e, raw=raw, n=n, dst=dst, ch=ch: e.scalar_tensor_tensor(
                    out=dst, in0=raw[:, 0:n], scalar=cw[:, ch, 0:1], in1=dst, op0=ALU.mult, op1=ALU.add),
                    r=[rkey, 'cw'], w=[('cvo', b)])
                p.op('dve', lambda e, raw=raw, n=n, dst=dst, ch=ch: e.scalar_tensor_tensor(
                    out=dst, in0=raw[:, 2:2 + n], scalar=cw[:, ch, 2:3], in1=dst, op0=ALU.mult, op1=ALU.add),
                    r=[rkey, 'cw'], w=[('cvo', b)])
            if ch == 12:
                p.op('act', lambda e, b=b: e.activation(out=cvo[b][:], in_=cvo[b][:], func=AF.Tanh),
                     r=[('cvo', b)], w=[('cvo', b)])
            if ch == 14:
                p.op('act', lambda e, b=b: e.activation(out=cvo[b][:], in_=cvo[b][:], func=AF.Sigmoid),
                     r=[('cvo', b)], w=[('cvo', b)])
            p.dma(lambda e, b=b, ch=ch: e.dma_start(out=S['pcT'][ch * 128:(ch + 1) * 128, :], in_=cvo[b][:]),
                  r=[('cvo', b)], w=[('pcT', ch)])
        p.barrier()
        return qkT, Vaug, mark_persist

    def phase_attn(l, qkT, Vaug, mark_persist):
        with_ctx = l < DEPTH - 1
        p.sb_reset(mark_persist)
        btab = p.sb([128, NTAB, 4, 128], F32, "btab")
        bmask = p.sb([128, NTAB, 128], F32, "bmask")
        amask = p.sb([128, 2, 128], F32, "amask")
        esink = p.sb([128, 4], F32, "esink")
        o_all = [p.sb([128, 512], F32, f"oall{i}") for i in range(2)]
        ex = [p.sb([128, 8, 128], F32, f"ex{i}") for i in range(2)]
        pT = [p.sb([128, 8, 128], BF16, f"pT{i}") for i in range(2)]
        den = [p.sb([128, 1], F32, f"den{i}") for i in range(2)]
        for tb in range(NTAB):
            p.dma(lambda e, tb=tb: e.dma_start(out=btab[:, tb], in_=I['btab'][l, :, tb]), w=['btab'])
        p.dma(lambda e: e.dma_start(out=bmask[:], in_=I['bmask']), w=['bmask'])
        p.dma(lambda e: e.dma_start(out=amask[:], in_=I['amask']), w=['amask'])
        load_bc(esink[:], I['a_sink'][l], 'esink')
        p.op('act', lambda e: e.activation(out=esink[:], in_=esink[:], func=AF.Exp), r=['esink'], w=['esink'])
        p.op('act', lambda e: e.activation(out=btab[:], in_=btab[:], func=AF.Exp), r=['btab'], w=['btab'])
        for h in range(4):
            p.op('dve', lambda e, h=h: e.tensor_tensor(out=btab[:, :, h, :], in0=btab[:, :, h, :], in1=bmask[:],
                                                       op=ALU.mult), r=['btab', 'bmask'], w=['btab'])
        it = 0
        for t in range(NT):
            if t < 2 and not with_ctx:
                continue
            ob = t % 2
            for grp in range(2):
                for h in range(4):
                    b = it % 2
                    it += 1
                    if grp == 0:
                        qs, ks, vs = h, 4 + h // 2, h // 2
                    else:
                        qs, ks, vs = 6 + h, 10 + h, 2 + h
                    if t < 2:
                        blocks = [(0, None), (1, None)]
                    elif grp == 0:
                        n = t - 2
                        blocks = [(t, None), (0, None), (1, None)]
                        if n > 0:
                            blocks.append((t - 1, amask[:, 0, :]))
                        if n < 15:
                            blocks.append((t + 1, amask[:, 1, :]))
                    else:
                        pq = t - 2
                        blocks = [(0, None), (1, None)]
                        for kb in range(16):
                            if (pq, kb) in NA_CASES:
                                blocks.append((kb + 2, btab[:, NA_CASES[(pq, kb)], h, :]))
                    nb = len(blocks)
                    nn = sum(1 for _, tb in blocks if tb is None)
                    sb0, sb1 = (0, 1) if b == 0 else (2, 3)
                    ob_ps = 4 + b
                    for i, (kt, tb) in enumerate(blocks):
                        bank = sb0 if i < 4 else sb1
                        p.op('pe', lambda e, bank=bank, i=i, kt=kt, ks=ks, qs=qs, t=t: e.matmul(
                            ps[bank][:, (i % 4) * 128:(i % 4 + 1) * 128], qkT[:, ks, kt * 128:(kt + 1) * 128],
                            qkT[:, qs, t * 128:(t + 1) * 128], start=True, stop=True), w=[PS(bank)])
                    n0 = min(nb, 4)
                    p.op('act', lambda e, b=b, n0=n0, sb0=sb0: e.activation(
                        out=ex[b][:, 0:n0, :], in_=ps[sb0][:, 0:n0 * 128].rearrange("p (n k) -> p n k", n=n0),
                        func=AF.Exp), r=[PS(sb0)], w=[('ex', b)])
                    if nb > 4:
                        n1 = nb - 4
                        p.op('act', lambda e, b=b, n1=n1, sb1=sb1: e.activation(
                            out=ex[b][:, 4:4 + n1, :], in_=ps[sb1][:, 0:n1 * 128].rearrange("p (n k) -> p n k", n=n1),
                            func=AF.Exp), r=[PS(sb1)], w=[('ex', b)])
                    p.op('pool', lambda e, b=b, nn=nn: e.tensor_copy(out=pT[b][:, 0:nn, :], in_=ex[b][:, 0:nn, :]),
                         r=[('ex', b)], w=[('pT', b)])
                    for i, (kt, tb) in enumerate(blocks):
                        if tb is None:
                            continue
                        p.op('dve', lambda e, b=b, i=i, tb=tb: e.tensor_tensor(out=pT[b][:, i, :], in0=ex[b][:, i, :],
                                                                              in1=tb, op=ALU.mult),
                             r=[('ex', b), 'btab', 'amask'], w=[('pT', b)])
                    for i, (kt, tb) in enumerate(blocks):
                        p.op('pe', lambda e, b=b, i=i, kt=kt, vs=vs, ob_ps=ob_ps, nb=nb: e.matmul(
                            ps[ob_ps][:, 0:65], pT[b][:, i, :], Vaug[:, kt, vs, :], start=(i == 0), stop=(i == nb - 1)),
                            r=[('pT', b)], w=[PS(ob_ps)])
                    if grp == 0:
                        p.op('dve', lambda e, b=b, h=h, ob_ps=ob_ps: e.tensor_scalar(
                            out=den[b][:], in0=ps[ob_ps][:, 64:65], scalar1=esink[:, h:h + 1], scalar2=None,
                            op0=ALU.add), r=[PS(ob_ps), 'esink'], w=[('den', b)])
                        p.op('dve', lambda e, b=b: e.reciprocal(out=den[b][:], in_=den[b][:]),
                             r=[('den', b)], w=[('den', b)])
                    else:
                        p.op('dve', lambda e, b=b, ob_ps=ob_ps: e.reciprocal(out=den[b][:], in_=ps[ob_ps][:, 64:65]),
                             r=[PS(ob_ps)], w=[('den', b)])
                    col = grp * 256 + h * 64
                    p.op('dve', lambda e, b=b, ob=ob, col=col, ob_ps=ob_ps: e.tensor_scalar(
                        out=o_all[ob][:, col:col + 64], in0=ps[ob_ps][:, 0:64], scalar1=den[b][:, 0:1], scalar2=None,
                        op0=ALU.mult), r=[PS(ob_ps), ('den', b)], w=[('oall', ob)])
            p.dma(lambda e, ob=ob, t=t: e.dma_start(out=S['o'][t * 128:(t + 1) * 128, 0:512], in_=o_all[ob][:]),
                  r=[('oall', ob)], w=[('So', t)])
        p.barrier()


    def phase_rprep(l):
        p.sb_reset(base_mark)
        w2 = p.sb([128, 512], F32, "w2")
        a2 = p.sb([128, 512], F32, "a2")
        g2 = p.sb([128, 512], F32, "g2")
        w0 = p.sb([1, 2, 512], F32, "w0")
        a0 = p.sb([1, 2, 512], F32, "a0")
        ones = p.sb([1, 128], F32, "ones")
        KKW = p.sb([128, 512], F32, "KKW")
        KA = p.sb([128, 512], F32, "KA")
        RK = p.sb([128, 512], F32, "RK")
        p.dma(lambda e: e.dma_start(out=w2[:], in_=I['r7_w2'][l].rearrange("d r c -> (d r) c")), w=['w2'])
        p.dma(lambda e: e.dma_start(out=a2[:], in_=I['r7_a2'][l].rearrange("d r c -> (d r) c")), w=['a2'])
        p.dma(lambda e: e.dma_start(out=g2[:], in_=I['r7_g2'][l]), w=['g2'])
        p.dma(lambda e: e.dma_start(out=w0[:], in_=I['r7_w0'][l:l + 1]), w=['w0'])
        p.dma(lambda e: e.dma_start(out=a0[:], in_=I['r7_a0'][l:l + 1]), w=['a0'])
        p.op('dve', lambda e: e.memset(ones[:], 1.0), w=['ones'])
        load_bc(KKW[:], I['r7_kk'][l], 'KKW')
        load_bc(KA[:], I['r7_ka'][l], 'KA')
        load_bc(RK[:], I['r7_rk'][l].rearrange("h d -> (h d)"), 'RK')
        fm = [p.sb([128, 15, 128], F32, f"fm{i}") for i in range(2)]
        TM = [p.sb([128, 10, 512], F32, f"TM{i}") for i in range(2)]
        kt = p.sb([128, 512], F32, "kt")
        av = [p.sb([128, 512], F32, f"av{i}") for i in range(2)]
        tmp = p.sb([128, 512], F32, "tmp")
        tmp2 = p.sb([128, 512], F32, "tmp2")
        s8 = p.sb([128, 8], F32, "s8")
        bs = [p.sb([128, 8], F32, f"bs{i}") for i in range(2)]
        for t in range(NT):
            b = t % 2
            p.dma(lambda e, b=b, t=t: e.dma_start(
                out=fm[b][:], in_=S['pcT'][:, t * 128:(t + 1) * 128].rearrange("(c p) t -> p c t", p=128)),
                w=[('fm', b)])
            for q in range(3):
                for c4 in range(4):
                    p.op('pe', lambda e, b=b, q=q, c4=c4: e.transpose(
                        out=ps[q][:, c4 * 128:(c4 + 1) * 128], in_=fm[b][:, q * 4 + c4, :], identity=ident_f[:]),
                        r=[('fm', b), 'identf'], w=[PS(q)])
            for d in range(2):
                pr = slice(d * 64, d * 64 + 64)
                p.op('pe', lambda e, b=b, d=d, pr=pr: e.matmul(ps[3 + d][:, :], fm[b][pr, 12, :], w2[pr, :],
                                                              start=True, stop=False), r=[('fm', b), 'w2'], w=[PS(3 + d)])
                p.op('pe', lambda e, d=d: e.matmul(ps[3 + d][:, :], ones[0:1, :], w0[0:1, d, :], start=False, stop=True),
                     r=['ones', 'w0'], w=[PS(3 + d)])
                p.op('pe', lambda e, b=b, d=d, pr=pr: e.matmul(ps[5 + d][:, :], fm[b][pr, 13, :], a2[pr, :],
                                                              start=True, stop=False), r=[('fm', b), 'a2'], w=[PS(5 + d)])
                p.op('pe', lambda e, d=d: e.matmul(ps[5 + d][:, :], ones[0:1, :], a0[0:1, d, :], start=False, stop=True),
                     r=['ones', 'a0'], w=[PS(5 + d)])
            p.op('pe', lambda e, b=b: e.matmul(ps[7][:, :], fm[b][:, 14, :], g2[:], start=True, stop=True),
                 r=[('fm', b), 'g2'], w=[PS(7)])
            T = TM[b]
            wk = [('TM', b)]
            p.op('act', lambda e, T=T: e.activation(out=T[:, 0, :], in_=ps[0][:, :], func=AF.Copy), r=[PS(0)], w=wk)
            p.op('act', lambda e: e.activation(out=kt[:], in_=ps[1][:, :], func=AF.Copy), r=[PS(1)], w=['kt'])
            p.op('act', lambda e, T=T: e.activation(out=T[:, 1, :], in_=ps[2][:, :], func=AF.Copy), r=[PS(2)], w=wk)
            p.op('act', lambda e, T=T: e.activation(out=T[:, 2, :], in_=ps[7][:, :], func=AF.Copy), r=[PS(7)], w=wk)
            for d in range(2):
                p.op('act', lambda e, T=T, d=d: e.activation(out=T[:, 8 + d, :], in_=ps[3 + d][:, :], func=AF.Sigmoid),
                     r=[PS(3 + d)], w=wk)
                p.op('act', lambda e, d=d: e.activation(out=av[d][:], in_=ps[5 + d][:, :], func=AF.Sigmoid),
                     r=[PS(5 + d)], w=[('av', d)])
                p.op('dve', lambda e, T=T, d=d: e.tensor_scalar(out=T[:, 8 + d, :], in0=T[:, 8 + d, :],
                                                                scalar1=-0.6065306597126334, scalar2=None, op0=ALU.mult),
                     r=wk, w=wk)
            p.op('dve', lambda e: e.tensor_tensor(out=tmp[:], in0=kt[:], in1=KKW[:], op=ALU.mult), r=['kt', 'KKW'], w=['tmp'])
            p.op('dve', lambda e: e.tensor_tensor(out=tmp2[:], in0=tmp[:], in1=tmp[:], op=ALU.mult), r=['tmp'], w=['tmp2'])
            p.op('dve', lambda e: e.tensor_reduce(out=s8[:], in_=tmp2[:].rearrange("p (h d) -> p h d", h=8), axis=AX.X,
                                                  op=ALU.add), r=['tmp2'], w=['s8'])
            p.op('act', lambda e: e.activation(out=s8[:], in_=s8[:], func=AF.Sqrt), r=['s8'], w=['s8'])
            p.op('dve', lambda e: e.tensor_scalar(out=s8[:], in0=s8[:], scalar1=1e-12, scalar2=None, op0=ALU.max),
                 r=['s8'], w=['s8'])
            p.op('dve', lambda e: e.reciprocal(out=s8[:], in_=s8[:]), r=['s8'], w=['s8'])
            p.op('dve', lambda e, T=T: e.tensor_tensor(out=T[:, 3, :].rearrange("p (h d) -> p h d", h=8),
                                                       in0=tmp[:].rearrange("p (h d) -> p h d", h=8),
                                                       in1=s8[:].unsqueeze(2).to_broadcast([128, 8, 64]), op=ALU.mult),
                 r=['tmp', 's8'], w=wk)
            for d in range(2):
                p.op('dve', lambda e, d=d: e.scalar_tensor_tensor(out=tmp2[:], in0=av[d][:], scalar=-1.0, in1=KA[:],
                                                                  op0=ALU.add, op1=ALU.mult),
                     r=[('av', d), 'KA'], w=['tmp2'])
                p.op('dve', lambda e, T=T, d=d: e.scalar_tensor_tensor(out=T[:, 4 + d, :], in0=tmp2[:], scalar=1.0,
                                                                       in1=kt[:], op0=ALU.add, op1=ALU.mult),
                     r=['tmp2', 'kt'], w=wk)
                p.op('dve', lambda e, T=T, d=d: e.tensor_tensor(out=T[:, 6 + d, :], in0=T[:, 3, :], in1=av[d][:],
                                                                op=ALU.mult), r=wk + [('av', d)], w=wk)
            p.op('dve', lambda e, T=T: e.tensor_tensor(out=tmp[:], in0=T[:, 4, :], in1=T[:, 5, :], op=ALU.add),
                 r=wk, w=['tmp'])
            p.op('dve', lambda e: e.tensor_tensor(out=tmp[:], in0=tmp[:], in1=RK[:], op=ALU.mult), r=['tmp', 'RK'], w=['tmp'])
            p.op('dve', lambda e, T=T: e.tensor_tensor(out=tmp[:], in0=tmp[:], in1=T[:, 0, :], op=ALU.mult),
                 r=['tmp'] + wk, w=['tmp'])
            p.op('dve', lambda e, b=b: e.tensor_reduce(out=bs[b][:], in_=tmp[:].rearrange("p (h d) -> p h d", h=8),
                                                       axis=AX.X, op=ALU.add), r=['tmp'], w=[('bs', b)])
            p.dma(lambda e, T=T, t=t: e.dma_start(out=S['tm'][t * 128:(t + 1) * 128], in_=T[:]), r=wk, w=[('Stm', t)])
            p.dma(lambda e, b=b, t=t: e.dma_start(out=S['bon'][t * 128:(t + 1) * 128], in_=bs[b][:]),
                  r=[('bs', b)], w=[('Sbon', t)])
        p.barrier()

    def phase_scan(l):
        p.sb_reset(base_mark)
        PSB = ps
        C = 64
        NCH = NTOK // C
        tri = p.sb([64, 2, 64], F32, "tri")
        mg = p.sb([64, 2, 128], F32, "mg")
        mn = p.sb([64, 2, 64], F32, "mn")
        ones = p.sb([64, 1], F32, "ones1")
        p.dma(lambda e: e.dma_start(out=tri[:], in_=I['tri']), w=['tri'])
        p.dma(lambda e: e.dma_start(out=mg[:], in_=I['mg']), w=['mg'])
        p.dma(lambda e: e.dma_start(out=mn[:], in_=I['mn']), w=['mn'])
        p.op('dve', lambda e: e.memset(ones[:], 1.0), w=['ones1'])
        M = [p.sb([64, 8, 64], F32, f"M{d}") for d in range(2)]
        for d in range(2):
            M0_PLACEHOLDER = None
        X = [[p.sb([64, 6, 512], F32, f"X{d}{i}") for i in range(2)] for d in range(2)]
        def mk(shape, name):
            return [p.sb(shape, F32, f"{name}{d}") for d in range(2)]
        E0s, E1s, E2s = mk([64, 512], "E0"), mk([64, 512], "E1"), mk([64, 512], "E2")
        Ats, Rts, Bts, Kts = mk([64, 512], "At"), mk([64, 512], "Rt"), mk([64, 512], "Bt"), mk([64, 512], "Kt")
        FARs, FBs, FKs = mk([64, 8, 128], "FAR"), mk([64, 8, 64], "FB"), mk([64, 8, 64], "FK")
        G1s, G2s = mk([64, 8, 128], "G1"), mk([64, 8, 128], "G2")
        Tms = [mk([64, 8, 64], f"Tm{i}_") for i in range(2)]
        Nms = [mk([64, 8, 64], f"Nm{i}_") for i in range(2)]
        Zs, Wss, Uss, PCs = mk([64, 8, 64], "Z"), mk([64, 512], "Ws"), mk([64, 512], "Us"), mk([64, 8], "PC")
        Ys = [p.sb([64, 512], F32, f"Ys{d}") for d in range(2)]
        order = {0: list(range(0, 4)) + list(range(4, NCH)), 1: list(range(3, -1, -1)) + list(range(NCH - 1, 3, -1))}
        v3 = lambda ap: ap.rearrange("p (h d) -> p h d", h=8)
        F32R = mybir.dt.float32r
        use_r = cfg.get("fp32r", True)

        def RR(ap):
            return ap.bitcast(F32R) if use_r else ap

        Vrs = mk([64, 512], "Vr")
        Mts = mk([64, 8, 64], "Mt")
        for d in range(2):
            p.op('dve', lambda e, d=d: e.memset(Mts[d][:], 0.0), w=[('Mt', d)])
            p.op('dve', lambda e, d=d: e.tensor_copy(out=RR(M[d][:]), in_=Mts[d][:]), r=[('Mt', d)], w=[('M', d)])

        def mmr(e, out, lhsT, rhs, **kw):
            if use_r:
                return e.matmul(out, lhsT.bitcast(F32R), rhs.bitcast(F32R), **kw)
            return e.matmul(out, lhsT, rhs, **kw)

        def scan_unit(d, c):
            if True:
                tok0 = c * C
                Xd = X[d][c % 2]
                E0, E1, E2, At, Rt, Bt, Kt = E0s[d], E1s[d], E2s[d], Ats[d], Rts[d], Bts[d], Kts[d]
                FAR, FB, FK, G1, G2 = FARs[d], FBs[d], FKs[d], G1s[d], G2s[d]
                Tm = [Tms[0][d], Tms[1][d]]
                Nm = [Nms[0][d], Nms[1][d]]
                Z, Ws, Us, PC = Zs[d], Wss[d], Uss[d], PCs[d]
                ps = [PSB[4 * d + (i % 4)] for i in range(8)]
                PS = lambda i: ('ps', 4 * d + (i % 4))
                xk = [('X', d, c % 2)]
                srcs = [0, 1, 3, 4 + d, 6 + d, 8 + d]
                yield
                for i, s in enumerate(srcs):
                    p.dma(lambda e, Xd=Xd, i=i, s=s, tok0=tok0: e.dma_start(out=Xd[:, i, :],
                                                                           in_=S['tm'][tok0:tok0 + C, s, :]), w=xk)
                r_, v_, kk_, k_, b_, lw_ = [Xd[:, i, :] for i in range(6)]
                Vr = Vrs[d]
                yield
                p.op('act', lambda e, v_=v_: e.activation(out=RR(Vr[:]), in_=v_, func=AF.Copy), r=xk, w=[('Vr', d)])
                v_ = Vr[:]
                vk = [('Vr', d)]
                yield
                p.op('pe', lambda e, d=d, lw_=lw_: e.matmul(ps[0][0:64, :], tri[:, d, :], lw_, start=True, stop=True),
                     r=xk + ['tri'], w=[PS(0)])
                yield
                for h in range(8):
                    p.op('pe', lambda e, h=h, lw_=lw_: e.matmul(ps[1][0:64, h:h + 1], lw_[:, h * 64:(h + 1) * 64],
                                                               ones[:, 0:1], start=True, stop=True),
                         r=xk + ['ones1'], w=[PS(1)])
                yield
                p.op('act', lambda e: e.activation(out=PC[:], in_=ps[1][0:64, 0:8], func=AF.Exp), r=[PS(1)], w=[('PC', d)])
                yield
                p.op('act', lambda e: e.activation(out=E1[:], in_=ps[0][0:64, :], func=AF.Exp), r=[PS(0)], w=[('E1', d)])
                yield
                p.op('act', lambda e: e.activation(out=E2[:], in_=ps[0][0:64, :], func=AF.Exp, scale=-1.0),
                     r=[PS(0)], w=[('E2', d)])
                yield
                p.op('dve', lambda e, lw_=lw_: e.tensor_tensor(out=E0[:], in0=ps[0][0:64, :], in1=lw_, op=ALU.subtract),
                     r=[PS(0)] + xk, w=[('E0', d)])
                yield
                p.op('act', lambda e: e.activation(out=E0[:], in_=E0[:], func=AF.Exp), r=[('E0', d)], w=[('E0', d)])
                yield
                p.op('dve', lambda e, kk_=kk_: e.scalar_tensor_tensor(out=At[:], in0=kk_, scalar=-1.0, in1=E0[:],
                                                                      op0=ALU.mult, op1=ALU.mult),
                     r=xk + [('E0', d)], w=[('At', d)])
                yield
                p.op('dve', lambda e, r_=r_: e.tensor_tensor(out=Rt[:], in0=r_, in1=E1[:], op=ALU.mult),
                     r=xk + [('E1', d)], w=[('Rt', d)])
                yield
                p.op('dve', lambda e, b_=b_: e.tensor_tensor(out=RR(Bt[:]), in0=b_, in1=E2[:], op=ALU.mult),
                     r=xk + [('E2', d)], w=[('Bt', d)])
                yield
                p.op('dve', lambda e, k_=k_: e.tensor_tensor(out=RR(Kt[:]), in0=k_, in1=E2[:], op=ALU.mult),
                     r=xk + [('E2', d)], w=[('Kt', d)])
                yield
                for bank, src, key in ((2, At, ('At', d)), (3, Rt, ('Rt', d)), (4, Bt, ('Bt', d)), (5, Kt, ('Kt', d))):
                    for h in range(8):
                        p.op('pe', lambda e, bank=bank, src=src, h=h: e.transpose(
                            out=ps[bank][0:64, h * 64:(h + 1) * 64], in_=src[:, h * 64:(h + 1) * 64],
                            identity=ident_f[0:64, 0:64]), r=[key, 'identf'], w=[PS(bank)])
                yield
                p.op('act', lambda e: e.activation(out=RR(FAR[:, :, 0:64]), in_=v3(ps[2][0:64, :]), func=AF.Copy),
                     r=[PS(2)], w=[('FAR', d)])
                yield
                p.op('act', lambda e: e.activation(out=RR(FAR[:, :, 64:128]), in_=v3(ps[3][0:64, :]), func=AF.Copy),
                     r=[PS(3)], w=[('FAR', d)])
                yield
                p.op('dve', lambda e: e.tensor_copy(out=RR(FB[:]), in_=v3(ps[4][0:64, :])), r=[PS(4)], w=[('FB', d)])
                yield
                p.op('dve', lambda e: e.tensor_copy(out=RR(FK[:]), in_=v3(ps[5][0:64, :])), r=[PS(5)], w=[('FK', d)])
                yield
                for h in range(8):
                    bank = 6 + (h // 4)
                    p.op('pe', lambda e, h=h, bank=bank: mmr(e, ps[bank][0:64, (h % 4) * 128:(h % 4 + 1) * 128],
                                                                   FB[:, h, :], FAR[:, h, :], start=True, stop=True),
                         r=[('FB', d), ('FAR', d)], w=[PS(bank)])
                yield
                for hb in range(2):
                    p.op('dve', lambda e, hb=hb, d=d: e.tensor_tensor(
                        out=RR(G1[:, hb * 4:(hb + 1) * 4, :]), in0=ps[6 + hb][0:64, :].rearrange("p (h t) -> p h t", h=4),
                        in1=mg[:, d, :].unsqueeze(1).to_broadcast([64, 4, 128]), op=ALU.mult),
                        r=[PS(6 + hb), 'mg'], w=[('G1', d)])
                yield
                for h in range(8):
                    bank = 2 + (h // 4)
                    p.op('pe', lambda e, h=h, bank=bank: mmr(e, ps[bank][0:64, (h % 4) * 128:(h % 4 + 1) * 128],
                                                                   FK[:, h, :], FAR[:, h, :], start=True, stop=True),
                         r=[('FK', d), ('FAR', d)], w=[PS(bank)])
                yield
                for hb in range(2):
                    p.op('dve', lambda e, hb=hb, d=d: e.tensor_tensor(
                        out=RR(G2[:, hb * 4:(hb + 1) * 4, :]), in0=ps[2 + hb][0:64, :].rearrange("p (h t) -> p h t", h=4),
                        in1=mg[:, d, :].unsqueeze(1).to_broadcast([64, 4, 128]), op=ALU.mult),
                        r=[PS(2 + hb), 'mg'], w=[('G2', d)])
                yield
                for h in range(8):
                    p.op('pe', lambda e, h=h: mmr(e, ps[4][0:64, h * 64:(h + 1) * 64], FAR[:, h, 0:64], FB[:, h, :],
                                                       start=True, stop=True), r=[('FAR', d), ('FB', d)], w=[PS(4)])
                yield
                p.op('dve', lambda e, d=d: e.tensor_tensor(out=RR(Nm[0][:]), in0=v3(ps[4][0:64, :]),
                                                           in1=mn[:, d, :].unsqueeze(1).to_broadcast([64, 8, 64]),
                                                           op=ALU.mult), r=[PS(4), 'mn'], w=[('Nm', d, 0)])
                yield
                p.op('dve', lambda e: e.tensor_copy(out=RR(Tm[0][:]), in_=G1[:, :, 0:64]), r=[('G1', d)], w=[('Tm', d, 0)])
                yield
                p.op('dve', lambda e: e.tensor_tensor(out=RR(Z[:]), in0=G1[:, :, 0:64],
                                                      in1=ident_f[0:64, 0:64].unsqueeze(1).to_broadcast([64, 8, 64]),
                                                      op=ALU.add), r=[('G1', d), 'identf'], w=[('Z', d)])
                cur = 0
                yield
                for lev in range(5):
                    nxt = 1 - cur
                    last = lev == 4
                    for h in range(8):
                        p.op('pe', lambda e, h=h, cur=cur: mmr(e, ps[5][0:64, h * 64:(h + 1) * 64], Tm[cur][:, h, :],
                                                                    Nm[cur][:, h, :], start=True, stop=True),
                             r=[('Tm', d, cur), ('Nm', d, cur)], w=[PS(5)])
                    p.op('act', lambda e, nxt=nxt: e.activation(out=RR(Nm[nxt][:]), in_=v3(ps[5][0:64, :]), func=AF.Copy),
                         r=[PS(5)], w=[('Nm', d, nxt)])
                    if not last:
                        for h in range(8):
                            p.op('pe', lambda e, h=h, cur=cur: mmr(e, ps[6][0:64, h * 64:(h + 1) * 64],
                                                                        Nm[cur][:, h, :], Tm[cur][:, h, :],
                                                                        start=True, stop=True),
                                 r=[('Tm', d, cur), ('Nm', d, cur)], w=[PS(6)])
                        p.op('dve', lambda e, nxt=nxt: e.tensor_copy(out=RR(Tm[nxt][:]), in_=v3(ps[6][0:64, :])),
                             r=[PS(6)], w=[('Tm', d, nxt)])
                    for h in range(8):
                        p.op('pe', lambda e, h=h, nxt=nxt: mmr(e, ps[7][0:64, h * 64:(h + 1) * 64], Nm[nxt][:, h, :],
                                                                    Z[:, h, :], start=True, stop=True),
                             r=[('Nm', d, nxt), ('Z', d)], w=[PS(7)])
                    p.op('dve', lambda e: e.tensor_tensor(out=RR(Z[:]), in0=Z[:], in1=v3(ps[7][0:64, :]), op=ALU.add),
                         r=[PS(7), ('Z', d)], w=[('Z', d)])
                    cur = nxt
                Md = M[d]
                yield
                for h in range(8):
                    o = ps[0][0:64, h * 64:(h + 1) * 64]
                    p.op('pe', lambda e, h=h, o=o, Md=Md: mmr(e, o, FAR[:, h, 0:64], Md[:, h, :], start=True, stop=False),
                         r=[('FAR', d), ('M', d)], w=[PS(0)])
                    p.op('pe', lambda e, h=h, o=o, v_=v_: mmr(e, o, G2[:, h, 0:64], v_[:, h * 64:(h + 1) * 64],
                                                                   start=False, stop=True), r=[('G2', d)] + vk, w=[PS(0)])
                yield
                p.op('act', lambda e: e.activation(out=RR(Ws[:]), in_=ps[0][0:64, :], func=AF.Copy), r=[PS(0)], w=[('Ws', d)])
                yield
                for h in range(8):
                    p.op('pe', lambda e, h=h: mmr(e, ps[1][0:64, h * 64:(h + 1) * 64], Z[:, h, :],
                                                       Ws[:, h * 64:(h + 1) * 64], start=True, stop=True),
                         r=[('Z', d), ('Ws', d)], w=[PS(1)])
                yield
                p.op('act', lambda e: e.activation(out=RR(Us[:]), in_=ps[1][0:64, :], func=AF.Copy), r=[PS(1)], w=[('Us', d)])
                yield
                for h in range(8):
                    o = ps[2][0:64, h * 64:(h + 1) * 64]
                    hs = slice(h * 64, (h + 1) * 64)
                    p.op('pe', lambda e, h=h, o=o, Md=Md: mmr(e, o, FAR[:, h, 64:128], Md[:, h, :], start=True, stop=False),
                         r=[('FAR', d), ('M', d)], w=[PS(2)])
                    p.op('pe', lambda e, h=h, o=o, hs=hs: mmr(e, o, G1[:, h, 64:128], Us[:, hs], start=False, stop=False),
                         r=[('G1', d), ('Us', d)], w=[PS(2)])
                    p.op('pe', lambda e, h=h, o=o, hs=hs, v_=v_: mmr(e, o, G2[:, h, 64:128], v_[:, hs], start=False, stop=True),
                         r=[('G2', d)] + vk, w=[PS(2)])
                yield
                p.op('act', lambda e, d=d: e.activation(out=Ys[d][:], in_=ps[2][0:64, :], func=AF.Copy),
                     r=[PS(2)], w=[('Ys', d)])
                yield
                p.dma(lambda e, d=d, tok0=tok0: e.dma_start(out=S['y'][d, tok0:tok0 + C, :], in_=Ys[d][:]),
                      r=[('Ys', d)], w=[('Sy', d, c)])
                yield
                for h in range(8):
                    o = ps[3][0:64, h * 64:(h + 1) * 64]
                    hs = slice(h * 64, (h + 1) * 64)
                    p.op('pe', lambda e, o=o, hs=hs: mmr(e, o, Bt[:, hs], Us[:, hs], start=True, stop=False),
                         r=[('Bt', d), ('Us', d)], w=[PS(3)])
                    p.op('pe', lambda e, o=o, hs=hs, v_=v_: mmr(e, o, Kt[:, hs], v_[:, hs], start=False, stop=True),
                         r=[('Kt', d)] + vk, w=[PS(3)])
                Mt = Mts[d]
                yield
                p.op('dve', lambda e, Md=Md, Mt=Mt: e.tensor_tensor(out=Mt[:], in0=Md[:], in1=v3(ps[3][0:64, :]), op=ALU.add),
                     r=[PS(3), ('M', d)], w=[('Mt', d)])
                yield
                p.op('dve', lambda e, Md=Md, Mt=Mt: e.tensor_tensor(out=RR(Md[:]), in0=Mt[:],
                                                             in1=PC[:].unsqueeze(2).to_broadcast([64, 8, 64]),
                                                             op=ALU.mult), r=[('PC', d), ('Mt', d)], w=[('M', d)])
        cin = [[p.sb([128, 1, D], F32, f"cin{i}{q}") for q in range(2)] for i in range(2)]
        cout = [p.sb([128, 1, 2 * D], BF16, f"cout{i}") for i in range(2)]

        def conv_block(blk):
            b = blk % 2
            rows = slice(blk * 128, (blk + 1) * 128)
            for q, tabn in enumerate(('peer_u', 'peer_v')):
                p.dma(lambda e, b=b, q=q, tabn=tabn, rows=rows: e.dma_start(
                    out=cin[b][q][:], in_=I[tabn][l][rows, :].rearrange("(j p) d -> p j d", p=128)), w=[('cin', b, q)],
                    eng="pool")
                p.op('pool', lambda e, b=b, q=q: e.tensor_copy(out=cout[b][:, :, q * D:(q + 1) * D], in_=cin[b][q][:]),
                     r=[('cin', b, q)], w=[('cout', b, q)])
            p.dma(lambda e, b=b, rows=rows: e.dma_start(
                out=S['T'][l][rows, :].rearrange("(j p) d -> p j d", p=128), in_=cout[b][:]),
                r=[('cout', b, 0), ('cout', b, 1)], w=[('cout', b, 0), ('cout', b, 1)], eng="pool")

        nblk = 0
        for step in range(NCH):
            gens = [scan_unit(d, order[d][step]) for d in range(2)]
            while gens:
                for g_ in list(gens):
                    try:
                        next(g_)
                    except StopIteration:
                        gens.remove(g_)
            for _ in range(4):
                if nblk < 128:
                    conv_block(nblk)
                    nblk += 1
        while nblk < 128:
            conv_block(nblk)
            nblk += 1
        p.barrier()


    def phase_rout(l):
        p.sb_reset(base_mark)
        with_ctx = l < DEPTH - 1
        wo = p.sb([128, 8, D], BF16, "wo")
        for j in range(8):
            p.dma(lambda e, j=j: e.dma_start(out=wo[:, j, :], in_=I['w_out'][l, j * 128:(j + 1) * 128, :]),
                  w=[('wo', j)], eng="pool")
        LNW = p.sb([128, 512], F32, "LNW")
        LNB = p.sb([128, 512], F32, "LNB")
        G1b = [p.sb([128, D], F32, f"G1b{s}") for s in range(2)]
        load_bc(LNW[:], I['r7_lnw'][l], 'LNW')
        load_bc(LNB[:], I['r7_lnb'][l], 'LNB')
        gn_eps = p.sb([128, 1], F32, "gneps")
        p.op('dve', lambda e: e.memset(gn_eps[:], 64e-5), w=['gneps'])
        for s in range(2):
            load_bc(G1b[s][:], S['mod'][l, s, 2 * D:3 * D], ('G1b', s))
        yb = [[p.sb([128, 512], F32, f"y{d}{i}") for d in range(2)] for i in range(2)]
        vg = [p.sb([128, 2, 512], F32, f"vg{i}") for i in range(2)]
        bon = [p.sb([128, 8], F32, f"bon{i}") for i in range(2)]
        O = [p.sb([128, D], F32, f"O{i}") for i in range(2)]
        Ob = [p.sb([128, D], BF16, f"Ob{i}") for i in range(2)]
        oT = [p.sb([128, 8, 128], BF16, f"oT{i}") for i in range(2)]
        xt = [p.sb([128, D], F32, f"xr{i}") for i in range(2)]
        yc = p.sb([128, 512], F32, "yc")
        sq = p.sb([128, 512], F32, "sq2")
        m8 = p.sb([128, 8], F32, "m8")
        v8 = p.sb([128, 8], F32, "v8")
        src = I['x'] if l == 0 else S['xs']
        v3 = lambda ap: ap.rearrange("p (h d) -> p h d", h=8)
        bc8 = lambda ap: ap.unsqueeze(2).to_broadcast([128, 8, 64])
        for t in range(NT):
            if t < 2 and not with_ctx:
                continue
            b = t % 2
            s = 1 if t < 2 else 0
            rows = slice(t * 128, (t + 1) * 128)
            for d in range(2):
                p.dma(lambda e, b=b, d=d, rows=rows: e.dma_start(out=yb[b][d][:], in_=S['y'][d, rows, :]), w=[('y', b, d)])
            p.dma(lambda e, b=b, rows=rows: e.dma_start(out=vg[b][:], in_=S['tm'][rows, 1:3, :]), w=[('vg', b)])
            p.dma(lambda e, b=b, rows=rows: e.dma_start(out=bon[b][:], in_=S['bon'][rows, :]), w=[('bon', b)])
            p.dma(lambda e, b=b, rows=rows: e.dma_start(out=O[b][:, 0:512], in_=S['o'][rows, 0:512]), w=[('O', b)])
            p.dma(lambda e, b=b, rows=rows: e.dma_start(out=xt[b][:], in_=src[rows, :]), w=[('xr', b)])
            p.op('dve', lambda e, b=b: e.tensor_tensor(out=yc[:], in0=yb[b][0][:], in1=yb[b][1][:], op=ALU.add),
                 r=[('y', b, 0), ('y', b, 1)], w=['yc'])
            p.op('dve', lambda e: e.tensor_reduce(out=m8[:], in_=v3(yc[:]), axis=AX.X, op=ALU.add), r=['yc'], w=['m8'])
            p.op('dve', lambda e: e.tensor_scalar(out=m8[:], in0=m8[:], scalar1=1.0 / 64, scalar2=None, op0=ALU.mult),
                 r=['m8'], w=['m8'])
            p.op('dve', lambda e: e.tensor_tensor(out=v3(yc[:]), in0=v3(yc[:]), in1=bc8(m8[:]), op=ALU.subtract),
                 r=['yc', 'm8'], w=['yc'])
            p.op('dve', lambda e: e.tensor_tensor(out=sq[:], in0=yc[:], in1=yc[:], op=ALU.mult), r=['yc'], w=['sq2'])
            p.op('dve', lambda e: e.tensor_reduce(out=v8[:], in_=v3(sq[:]), axis=AX.X, op=ALU.add), r=['sq2'], w=['v8'])
            p.op('act', lambda e: e.activation(out=v8[:], in_=v8[:], func=AF.Sqrt, bias=gn_eps[:], scale=1.0 / 64),
                 r=['v8', 'gneps'], w=['v8'])
            p.op('dve', lambda e: e.reciprocal(out=v8[:], in_=v8[:]), r=['v8'], w=['v8'])
            p.op('dve', lambda e: e.tensor_tensor(out=v3(yc[:]), in0=v3(yc[:]), in1=bc8(v8[:]), op=ALU.mult),
                 r=['yc', 'v8'], w=['yc'])
            p.op('dve', lambda e: e.tensor_tensor(out=yc[:], in0=yc[:], in1=LNW[:], op=ALU.mult), r=['yc', 'LNW'], w=['yc'])
            p.op('dve', lambda e: e.tensor_tensor(out=yc[:], in0=yc[:], in1=LNB[:], op=ALU.add), r=['yc', 'LNB'], w=['yc'])
            p.op('dve', lambda e, b=b: e.tensor_tensor(out=v3(sq[:]), in0=v3(vg[b][:, 0, :]), in1=bc8(bon[b][:]),
                                                       op=ALU.mult), r=[('vg', b), ('bon', b)], w=['sq2'])
            p.op('dve', lambda e: e.tensor_tensor(out=yc[:], in0=yc[:], in1=sq[:], op=ALU.add), r=['yc', 'sq2'], w=['yc'])
            p.op('dve', lambda e, b=b: e.tensor_tensor(out=O[b][:, 512:1024], in0=yc[:], in1=vg[b][:, 1, :], op=ALU.mult),
                 r=['yc', ('vg', b)], w=[('O2', b)])
            p.dma(lambda e, b=b, rows=rows: e.dma_start(out=S['o'][rows, 512:1024], in_=O[b][:, 512:1024]),
                  r=[('O2', b)], w=[('So2', t)])
            p.op('act', lambda e, b=b: e.activation(out=Ob[b][:], in_=O[b][:], func=AF.Copy),
                 r=[('O', b), ('O2', b)], w=[('Ob', b)])
            bank = 6 + b
            pv = ps[bank][:, 0:512].bitcast(BF16)
            for j in range(8):
                p.op('pe', lambda e, b=b, j=j, pv=pv: e.transpose(out=pv[:, j * 128:(j + 1) * 128],
                                                                 in_=Ob[b][:, j * 128:(j + 1) * 128], identity=ident_b[:]),
                     r=[('Ob', b), 'identb'], w=[PS(bank)])
            p.op('act', lambda e, b=b, pv=pv: e.activation(out=oT[b][:], in_=pv.rearrange("p (j t) -> p j t", j=8),
                                                            func=AF.Copy), r=[PS(bank)], w=[('oT', b)])
            for half in range(2):
                ybank = 2 * b + half
                for j in range(8):
                    p.op('pe', lambda e, b=b, j=j, half=half, ybank=ybank: e.matmul(
                        ps[ybank][:, :], oT[b][:, j, :], wo[:, j, half * 512:(half + 1) * 512],
                        start=(j == 0), stop=(j == 7)), r=[('oT', b), ('wo', j)], w=[PS(ybank)])
                cs_ = slice(half * 512, (half + 1) * 512)
                p.op('dve', lambda e, b=b, s=s, cs_=cs_, ybank=ybank: e.tensor_tensor(
                    out=O[b][:, cs_], in0=ps[ybank][:, :], in1=G1b[s][:, cs_], op=ALU.mult),
                    r=[PS(ybank), ('G1b', s), ('Ob', b), ('So2', t)], w=[('O', b), ('O2', b)])
                p.op('dve', lambda e, b=b, cs_=cs_: e.tensor_tensor(out=xt[b][:, cs_], in0=xt[b][:, cs_], in1=O[b][:, cs_],
                                                                   op=ALU.add), r=[('O', b), ('xr', b)], w=[('xr', b)])
            p.dma(lambda e, b=b, rows=rows: e.dma_start(out=S['xs'][rows, :], in_=xt[b][:]), r=[('xr', b)], w=[('Sxs', t)])
        p.barrier()

    def phase_peer(l):
        p.sb_reset(base_mark)
        last = l == DEPTH - 1
        eu_all = p.sb([128, NT, 128], U32, "eu_all")
        gate_all = p.sb([128, NT, 128], F32, "gate_all")
        G2b = [p.sb([128, D], F32, f"G2b{s}") for s in range(2)]
        m1 = p.sb_mark()
        hT = p.sb([128, 8, NTOK], BF16, "hT2")
        wq = p.sb([128, 8, 2048], BF16, "wq")
        for j in range(8):
            p.dma(lambda e, j=j: e.dma_start(out=wq[:, j, :], in_=I['peer_wq'][l, j * 128:(j + 1) * 128, :]),
                  w=[('wq', j)], eng="pool")
        keysT = p.sb([128, 16, 128], F32, "keysT")
        m0 = p.sb_mark()
        kraw = p.sb([128, 16, 128], F32, "kraw")
        p.dma(lambda e: e.dma_start(out=kraw[:], in_=I['peer_keys'][l].rearrange("h q n d -> n (h q) d")), w=['kraw'])
        for g in range(4):
            for i in range(4):
                hp = g * 4 + i
                p.op('pe', lambda e, g=g, i=i, hp=hp: e.transpose(out=ps[g][:, i * 128:(i + 1) * 128], in_=kraw[:, hp, :],
                                                                 identity=ident_f[:]), r=['kraw', 'identf'], w=[PS(g)])
            p.op('act', lambda e, g=g: e.activation(out=keysT[:, g * 4:(g + 1) * 4, :],
                                                    in_=ps[g][:, :].rearrange("p (i n) -> p i n", i=4), func=AF.Copy),
                 r=[PS(g)], w=['keysT'])
        p.barrier()
        p.sb_reset(m0)
        norm_tiles(l, 1, S['xs'], hT, lambda t: t * 128, tm_dram=S['h2'])
        p.barrier()
        p.sb_reset(m0)
        for s in range(2):
            load_bc(G2b[s][:], S['mod'][l, s, 5 * D:6 * D], ('G2b', s))
        qT = p.sb([128, 16, 128], F32, "qT")
        sc = p.sb([128, 16, 128], F32, "sc")
        sc2 = p.sb([128, 16, 128], F32, "sc2")
        sv = p.sb([128, 16, 16], F32, "sv")
        si = p.sb([128, 16, 16], U32, "si")
        sif = p.sb([128, 16, 16], F32, "sif")
        cand = p.sb([128, 8, 16, 16], F32, "cand")
        cand2 = p.sb([128, 8, 16, 16], F32, "cand2")
        eidx = p.sb([128, 8, 16, 16], F32, "eidx")
        best = p.sb([128, 8, 16], F32, "best")
        ci = p.sb([128, 8, 16], U32, "ci")
        cif = p.sb([128, 8, 16], F32, "cif")
        iota = p.sb([128, 256], F32, "iota")
        p.dma(lambda e: e.dma_start(out=iota[:], in_=I['iota']), w=['iota'])
        eq4 = p.sb([128, 8, 16, 16], F32, "eq4")
        cu = p.sb([128, 2, 8, 16], U32, "cu")
        cf = p.sb([128, 2, 8, 16], F32, "cf")
        e12 = p.sb([128, 2, 8, 16], F32, "e12")
        esel = p.sb([128, 128], F32, "esel")
        g8 = p.sb([128, 8], F32, "g8")
        tiles = [t for t in range(NT) if not (t < 2 and last)]
        for t in tiles:
            for g in range(4):
                for i in range(4):
                    hp = g * 4 + i
                    for j in range(8):
                        p.op('pe', lambda e, g=g, i=i, hp=hp, j=j, t=t: e.matmul(
                            ps[g][:, i * 128:(i + 1) * 128], wq[:, j, hp * 128:(hp + 1) * 128],
                            hT[:, j, t * 128:(t + 1) * 128], start=(j == 0), stop=(j == 7)),
                            r=[('hT', t), ('wq', j)], w=[PS(g)])
                p.op('act', lambda e, g=g: e.activation(out=qT[:, g * 4:(g + 1) * 4, :],
                                                        in_=ps[g][:, :].rearrange("p (i n) -> p i n", i=4), func=AF.Copy),
                     r=[PS(g)], w=['qT'])
            for g in range(4):
                for i in range(4):
                    hp = g * 4 + i
                    p.op('pe', lambda e, g=g, i=i, hp=hp: e.matmul(ps[4 + g][:, i * 128:(i + 1) * 128], qT[:, hp, :],
                                                                   keysT[:, hp, :], start=True, stop=True),
                         r=['qT', 'keysT'], w=[PS(4 + g)])
                p.op('act', lambda e, g=g: e.activation(out=sc[:, g * 4:(g + 1) * 4, :],
                                                        in_=ps[4 + g][:, :].rearrange("p (i n) -> p i n", i=4), func=AF.Copy),
                     r=[PS(4 + g)], w=['sc'])
            SVK = [('sv', hp) for hp in range(16)]
            SIK = [('si', hp) for hp in range(16)]
            for hp in range(16):
                p.op('dve', lambda e, hp=hp: e.max(out=sv[:, hp, 0:8], in_=sc[:, hp, :]), r=['sc'], w=[('sv', hp)])
            for hp in range(16):
                p.op('dve', lambda e, hp=hp: e.max_index(out=si[:, hp, 0:8], in_max=sv[:, hp, 0:8], in_values=sc[:, hp, :]),
                     r=['sc', ('sv', hp)], w=[('si', hp)])
            for hp in range(16):
                p.op('dve', lambda e, hp=hp: e.match_replace(out=sc2[:, hp, :], in_to_replace=sv[:, hp, 0:8],
                                                             in_values=sc[:, hp, :], imm_value=-1e30),
                     r=['sc', ('sv', hp)], w=[('sc2', hp)])
            for hp in range(16):
                p.op('dve', lambda e, hp=hp: e.max(out=sv[:, hp, 8:16], in_=sc2[:, hp, :]), r=[('sc2', hp)], w=[('sv8', hp)])
            for hp in range(16):
                p.op('dve', lambda e, hp=hp: e.max_index(out=si[:, hp, 8:16], in_max=sv[:, hp, 8:16], in_values=sc2[:, hp, :]),
                     r=[('sc2', hp), ('sv8', hp)], w=[('si8', hp)])
            SVK = SVK + [('sv8', hp) for hp in range(16)]
            SIK = SIK + [('si8', hp) for hp in range(16)]
            p.op('dve', lambda e: e.tensor_copy(out=sif[:], in_=si[:]), r=SIK, w=['sif'])
            svv = sv[:].rearrange("p (h q) k -> p h q k", q=2)
            sfv = sif[:].rearrange("p (h q) k -> p h q k", q=2)
            p.op('dve', lambda e, svv=svv: e.tensor_tensor(
                out=cand[:], in0=svv[:, :, 0, :].unsqueeze(3).to_broadcast([128, 8, 16, 16]),
                in1=svv[:, :, 1, :].unsqueeze(2).to_broadcast([128, 8, 16, 16]), op=ALU.add), r=SVK, w=['cand'])
            p.op('dve', lambda e, sfv=sfv: e.tensor_scalar(out=sfv[:, :, 0, :], in0=sfv[:, :, 0, :], scalar1=128.0,
                                                           scalar2=None, op0=ALU.mult), r=['sif'], w=['sif'])
            chs = [cand[:, h].rearrange("p a b -> p (a b)") for h in range(8)]
            ch2s = [cand2[:, h].rearrange("p a b -> p (a b)") for h in range(8)]
            for h in range(8):
                p.op('dve', lambda e, h=h: e.max(out=best[:, h, 0:8], in_=chs[h]), r=['cand'], w=[('best', h)])
            for h in range(8):
                p.op('dve', lambda e, h=h: e.max_index(out=ci[:, h, 0:8], in_max=best[:, h, 0:8], in_values=chs[h]),
                     r=['cand', ('best', h)], w=[('ci', h)])
            for h in range(8):
                p.op('dve', lambda e, h=h: e.match_replace(out=ch2s[h], in_to_replace=best[:, h, 0:8],
                                                           in_values=chs[h], imm_value=-1e30),
                     r=['cand', ('best', h)], w=[('cand2', h)])
            for h in range(8):
                p.op('dve', lambda e, h=h: e.max(out=best[:, h, 8:16], in_=ch2s[h]), r=[('cand2', h)], w=[('best8', h)])
            for h in range(8):
                p.op('dve', lambda e, h=h: e.max_index(out=ci[:, h, 8:16], in_max=best[:, h, 8:16], in_values=ch2s[h]),
                     r=[('cand2', h), ('best8', h)], w=[('ci8', h)])
            BK = [('best', h) for h in range(8)] + [('best8', h) for h in range(8)]
            CIK = [('ci', h) for h in range(8)] + [('ci8', h) for h in range(8)]
            p.op('dve', lambda e: e.tensor_scalar(out=cu[:, 0], in0=ci[:], scalar1=4, scalar2=None,
                                                  op0=ALU.logical_shift_right), r=CIK, w=['cu'])
            p.op('dve', lambda e: e.tensor_scalar(out=cu[:, 1], in0=ci[:], scalar1=15, scalar2=None,
                                                  op0=ALU.bitwise_and), r=CIK, w=['cu'])
            p.op('dve', lambda e: e.tensor_copy(out=cf[:], in_=cu[:]), r=['cu'], w=['cf'])
            io16 = iota[:, 0:16].unsqueeze(1).unsqueeze(1).to_broadcast([128, 8, 16, 16])
            for q in range(2):
                p.op('dve', lambda e, q=q, io16=io16: e.tensor_tensor(
                    out=eq4[:], in0=io16, in1=cf[:, q].unsqueeze(3).to_broadcast([128, 8, 16, 16]), op=ALU.is_equal),
                    r=['iota', 'cf'], w=['eq4'])
                p.op('dve', lambda e, q=q, sfv=sfv: e.tensor_tensor(
                    out=eq4[:], in0=eq4[:], in1=sfv[:, :, q, :].unsqueeze(2).to_broadcast([128, 8, 16, 16]), op=ALU.mult),
                    r=['eq4', 'sif'], w=['eq4'])
                p.op('dve', lambda e, q=q: e.tensor_reduce(out=e12[:, q], in_=eq4[:], axis=AX.X, op=ALU.add),
                     r=['eq4'], w=['e12'])
            p.op('dve', lambda e: e.tensor_tensor(out=esel[:], in0=e12[:, 0].rearrange("p h k -> p (h k)"),
                                                  in1=e12[:, 1].rearrange("p h k -> p (h k)"), op=ALU.add),
                 r=['e12'], w=['esel'])
            p.op('dve', lambda e, t=t: e.tensor_copy(out=eu_all[:, t, :], in_=esel[:]), r=['esel'], w=[('eu', t)])
            gv = gate_all[:, t, :].rearrange("p (h k) -> p h k", h=8)
            p.op('dve', lambda e, gv=gv: e.tensor_tensor(out=gv, in0=best[:],
                                                         in1=best[:, :, 0:1].to_broadcast([128, 8, 16]), op=ALU.subtract),
                 r=BK, w=[('gate', t)])
            p.op('act', lambda e, t=t: e.activation(out=gate_all[:, t, :], in_=gate_all[:, t, :], func=AF.Exp),
                 r=[('gate', t)], w=[('gate', t)])
            p.op('dve', lambda e, gv=gv: e.tensor_reduce(out=g8[:], in_=gv, axis=AX.X, op=ALU.add), r=[('gate', t)], w=['g8'])
            p.op('dve', lambda e: e.reciprocal(out=g8[:], in_=g8[:]), r=['g8'], w=['g8'])
            p.op('dve', lambda e, gv=gv: e.tensor_tensor(out=gv, in0=gv, in1=g8[:].unsqueeze(2).to_broadcast([128, 8, 16]),
                                                         op=ALU.mult), r=[('gate', t), 'g8'], w=[('gate', t)])
        p.barrier()
        p.sb_reset(m1)
        h2 = [p.sb([128, D], F32, f"h2{i}") for i in range(2)]
        xt = [p.sb([128, D], F32, f"xp{i}") for i in range(2)]
        act = [p.sb([128, 128], F32, f"actv{i}") for i in range(2)]
        wg = [p.sb([128, 128], F32, f"wg{i}") for i in range(2)]
        NACC = 1
        acc = [[p.sb([128, D], F32, f"acc{i}{k}") for k in range(NACC)] for i in range(2)]
        junk = p.sb([128, D], BF16, "pjunk")
        NG = 32
        GS = 8
        gbuf = [p.sb([128, 2 * D], BF16, f"gb{i}") for i in range(NG)]
        NDG = 8
        dg = [p.sb([128, 128], BF16, f"dg{i}") for i in range(NDG)]
        gi = 0
        di = 0
        for t in tiles:
            b = t % 2
            s = 1 if t < 2 else 0
            rows = slice(t * 128, (t + 1) * 128)
            p.dma(lambda e, b=b, rows=rows: e.dma_start(out=h2[b][:], in_=S['h2'][rows, :]), w=[('h2', b)])
            p.dma(lambda e, b=b, rows=rows: e.dma_start(out=xt[b][:], in_=S['xs'][rows, :]), w=[('xp', b)])
            p.op('dve', lambda e, b=b: e.memset(act[b][:], 0.0), w=[('actv', b)])
            for g in range(128 // GS):
                ks = []
                for sidx in range(g * GS, (g + 1) * GS):
                    k = gi % NG
                    gi += 1
                    ks.append(k)
                    p.dma(lambda e, k=k, t=t, sidx=sidx: e.indirect_dma_start(
                        out=gbuf[k][:], out_offset=None, in_=S['T'][l],
                        in_offset=bass.IndirectOffsetOnAxis(ap=eu_all[:, t, sidx:sidx + 1], axis=0)),
                        r=[], w=[('gb', k)], eng="pool")
                    p.op('dve', lambda e, k=k, b=b, sidx=sidx: e.scalar_tensor_tensor(
                        out=junk[:], in0=gbuf[k][:, 0:D], scalar=1.0, in1=h2[b][:], op0=ALU.mult, op1=ALU.mult,
                        accum_out=act[b][:, sidx:sidx + 1]), r=[('gb', k), ('h2', b), ('actv', b)], w=[('actc', b, sidx)])
                gs = slice(g * GS, (g + 1) * GS)
                p.op('act', lambda e, b=b, gs=gs: e.activation(out=wg[b][:, gs], in_=act[b][:, gs], func=AF.Gelu),
                     r=[('actc', b, sidx) for sidx in range(g * GS, (g + 1) * GS)], w=[('wg', b, g)])
                p.op('dve', lambda e, b=b, gs=gs, t=t: e.tensor_tensor(out=wg[b][:, gs], in0=wg[b][:, gs],
                                                                      in1=gate_all[:, t, gs], op=ALU.mult),
                     r=[('wg', b, g)], w=[('wg', b, g)])
                for j, sidx in enumerate(range(g * GS, (g + 1) * GS)):
                    k = ks[j]
                    dj = di % NDG
                    di += 1
                    p.op('act', lambda e, dj=dj, b=b, sidx=sidx: e.activation(
                        out=dg[dj][:], in_=ident_f[:], func=AF.Copy, scale=wg[b][:, sidx:sidx + 1]),
                        r=[('wg', b, g), 'identf'], w=[('dg', dj)])
                    for half in range(2):
                        bank = 2 * b + half
                        p.op('pe', lambda e, dj=dj, k=k, half=half, bank=bank, sidx=sidx: e.matmul(
                            ps[bank][:, :], dg[dj][:], gbuf[k][:, D + half * 512:D + (half + 1) * 512],
                            start=(sidx == 0), stop=(sidx == 127)), r=[('dg', dj), ('gb', k)], w=[PS(bank), ('gbr', k, half)])
            a0 = acc[b][0]
            for half in range(2):
                hs_ = slice(half * 512, (half + 1) * 512)
                p.op('dve', lambda e, a0=a0, s=s, b=b, half=half, hs_=hs_: e.tensor_tensor(
                    out=a0[:, hs_], in0=ps[2 * b + half][:, :], in1=G2b[s][:, hs_], op=ALU.mult),
                    r=[PS(2 * b + half), ('G2b', s)], w=[('acc', b, 0)])
            p.op('dve', lambda e, a0=a0, b=b: e.tensor_tensor(out=xt[b][:], in0=xt[b][:], in1=a0[:], op=ALU.add),
                 r=[('acc', b, 0), ('xp', b)], w=[('xp', b)])
            if last:
                p.dma(lambda e, b=b, t=t: e.dma_start(out=out_d[(t - 2) * 128:(t - 1) * 128, :], in_=xt[b][:]),
                      r=[('xp', b)], w=[('outd', t)])
            else:
                p.dma(lambda e, b=b, rows=rows: e.dma_start(out=S['xs'][rows, :], in_=xt[b][:]),
                      r=[('xp', b)], w=[('Sxs', t)])
        p.barrier()

    PHASES = cfg.get("phases", ["proj", "rprep", "scan", "rout", "peer"])

    phase_mod()
    for l in range(cfg.get("layers", DEPTH)):
        if 'proj' in PHASES:
            qkT, Vaug, mp = phase_proj(l)
            phase_attn(l, qkT, Vaug, mp)
        if 'rprep' in PHASES:
            phase_rprep(l)
        if 'scan' in PHASES:
            phase_scan(l)
        if 'rout' in PHASES:
            phase_rout(l)
        if 'peer' in PHASES:
            phase_peer(l)
    p.barrier()
    p.emit()
    return nc


def prep_inputs(inputs):
    f = lambda a: np.ascontiguousarray(np.asarray(a, dtype=np.float32))
    x, c, ctx, c_ctx = f(inputs['x']), f(inputs['c']), f(inputs['ctx']), f(inputs['c_ctx'])
    shared = {}
    for n in ['norm_mix', 'norm_ffn', 'w_mod', 'b_mod', 'w_in', 'w_out', 'a_qnorm', 'a_knorm', 'b_qnorm', 'b_knorm',
              'a_sink']:
        shared[n] = f(inputs[n])
    rpb = f(inputs['b_rpb'])
    btab = np.zeros((DEPTH, 128, NTAB, 4, 128), np.float32)
    bmask = np.zeros((128, NTAB, 128), np.float32)
    for i, (dr, dc, valid) in enumerate(NA_TABS):
        g = rpb[:, :, dr, dc]
        btab[:, :, i, :, :] = np.where(valid[None, None], g, 0.0).transpose(0, 2, 1, 3)
        bmask[:, i, :] = valid
    shared['btab'] = btab
    shared['bmask'] = bmask
    ar = np.arange(128)
    am = np.zeros((128, 2, 128), np.float32)
    am[:, 0, :] = (ar[:, None] >= ar[None, :])
    am[:, 1, :] = (ar[:, None] <= ar[None, :])
    shared['amask'] = am
    shared['ident'] = np.eye(128, dtype=np.float32)
    cos, sin = rope_tables()
    shared['cos'], shared['sin'] = cos, sin
    rc = f(inputs['r7_conv'])
    for n in ['r7_w0', 'r7_a0', 'r7_w2', 'r7_a2', 'r7_g2', 'r7_kk', 'r7_ka', 'r7_lnw', 'r7_lnb', 'r7_rk', 'peer_wq', 'peer_keys']:
        shared[n] = f(inputs[n])
    for l in range(DEPTH):
        shared[f'peer_u{l}'] = f(inputs['peer_u'][l])
        shared[f'peer_v{l}'] = f(inputs['peer_v'][l])
    a64 = np.arange(64)
    tri = np.zeros((64, 2, 64), np.float32)
    tri[:, 0, :] = a64[:, None] <= a64[None, :]
    tri[:, 1, :] = a64[:, None] >= a64[None, :]
    mg = np.zeros((64, 2, 128), np.float32)
    mg[:, 0, 0:64] = a64[:, None] < a64[None, :]
    mg[:, 0, 64:128] = a64[:, None] <= a64[None, :]
    mg[:, 1, 0:64] = a64[:, None] > a64[None, :]
    mg[:, 1, 64:128] = a64[:, None] >= a64[None, :]
    mn = np.zeros((64, 2, 64), np.float32)
    mn[:, 0, :] = a64[None, :] < a64[:, None]
    mn[:, 1, :] = a64[None, :] > a64[:, None]
    shared['tri'], shared['mg'], shared['mn'] = tri, mg, mn
    shared['iota'] = np.ascontiguousarray(np.broadcast_to(np.arange(256, dtype=np.float32), (128, 256)))
    shared['r7_conv'] = np.ascontiguousarray(rc.reshape(DEPTH, 3, 15, 128).transpose(0, 3, 2, 1))
    maps = []
    for b in range(8):
        m = dict(shared)
        m['x'] = np.ascontiguousarray(np.concatenate([ctx[b], x[b]], axis=0))
        cc = np.stack([c[b], c_ctx], axis=-1)
        m['cc'] = np.ascontiguousarray(cc.reshape(8, 128, 2).transpose(1, 0, 2))
        maps.append(m)
    return maps


_NC_CACHE = {}


def kernel(**inputs):
    if 'nc' not in _NC_CACHE:
        _NC_CACHE['nc'] = build({})
    nc = _NC_CACHE['nc']
    maps = prep_inputs(inputs)
    res = run_bass_kernel_spmd(nc, maps, core_ids=list(range(8)))
    return np.stack([np.asarray(r['out'], dtype=np.float32) for r in res.results], axis=0)
```

```python
import numpy as np
import ml_dtypes
import concourse.bass as bass
import concourse.mybir as mybir
from concourse.bass_utils import run_bass_kernel_spmd

F32 = mybir.dt.float32
BF16 = mybir.dt.bfloat16
U32 = mybir.dt.uint32
I32 = mybir.dt.int32
AF = mybir.ActivationFunctionType
ALU = mybir.AluOpType
AX = mybir.AxisListType

ENGS = ["pe", "act", "dve", "pool", "sp"]
DT_SIZE = {F32: 4, BF16: 2, U32: 4, I32: 4}

D = 1024
NCTX = 256
NLAT = 2048
NTOK = NCTX + NLAT
NT = NTOK // 128
DEPTH = 2
EPS = 1e-6


class Prog:
    def __init__(self, nc, n_dma_sems=32):
        self.nc = nc
        self.ops = {e: [] for e in ENGS}
        self.cnt = {e: 0 for e in ENGS}
        self.waited = {e: {} for e in ENGS}
        self.res = {}
        self.n_dma_sems = n_dma_sems
        self.dma_use = [0] * n_dma_sems
        self.dma_last = [None] * n_dma_sems
        self.dma_rr = 0
        self.sb_off = 16 * 1024
        self.sb_id = 0
        self.SB_CAP = 216 * 1024

    def sb_mark(self):
        return self.sb_off

    def sb_reset(self, off=0):
        self.sb_off = off

    def sb(self, shape, dtype, name=""):
        nbytes = int(np.prod(shape[1:])) * DT_SIZE[dtype]
        off = (self.sb_off + 63) // 64 * 64
        assert off + nbytes <= self.SB_CAP, f"SBUF overflow {off}+{nbytes} ({name})"
        self.sb_off = off + nbytes
        self.sb_id += 1
        return self.nc.alloc_sbuf_tensor_at(f"sb{self.sb_id}_{name}", list(shape), dtype, offset=off)

    def _deps(self, r, w):
        deps = []
        for k in r:
            st = self.res.get(k)
            if st and st[0] is not None:
                deps.append(st[0])
        for k in w:
            st = self.res.get(k)
            if st:
                if st[0] is not None:
                    deps.append(st[0])
                deps.extend(st[1])
        return deps

    def _commit(self, tok, r, w):
        for k in r:
            st = self.res.setdefault(k, [None, []])
            st[1].append(tok)
        for k in w:
            self.res[k] = [tok, []]

    def _waits_for(self, eng, deps):
        wd = self.waited[eng]
        best = {}
        for t in deps:
            if t[0] == 'c':
                if t[1] == eng and eng == 'pe':
                    continue
                key = ('c', t[1])
            else:
                key = ('d', t[1])
            if wd.get(key, 0) >= t[2]:
                continue
            best[key] = max(best.get(key, 0), t[2])
        for k, v in best.items():
            wd[k] = v
        return list(best.items())

    def op(self, eng, fn, r=(), w=()):
        deps = self._deps(r, w)
        waits = self._waits_for(eng, deps)
        self.cnt[eng] += 1
        tok = ('c', eng, self.cnt[eng])
        self.ops[eng].append((waits, fn, ('c', eng), 1))
        self._commit(tok, r, w)
        return tok

    def dma(self, fn, r=(), w=(), eng="sp"):
        deps = list(self._deps(r, w))
        i = self.dma_rr
        self.dma_rr = (self.dma_rr + 1) % self.n_dma_sems
        if self.dma_last[i] is not None:
            deps.append(self.dma_last[i])
        waits = self._waits_for(eng, deps)
        self.dma_use[i] += 1
        tok = ('d', i, 16 * self.dma_use[i])
        self.dma_last[i] = tok
        self.ops[eng].append((waits, fn, ('d', i), 16))
        self._commit(tok, r, w)
        return tok

    def barrier(self):
        toks = [('c', e, self.cnt[e]) for e in ENGS if self.cnt[e] > 0]
        toks += [t for t in self.dma_last if t is not None]
        for e in ENGS:
            waits = self._waits_for(e, toks)
            if waits:
                self.ops[e].append((waits, None, None, 0))
        self.res = {}

    def emit(self):
        nc = self.nc
        from contextlib import ExitStack
        with ExitStack() as es:
            csem = {e: es.enter_context(nc.semaphore(f"c_{e}")) for e in ENGS}
            dsem = [es.enter_context(nc.semaphore(f"d_{i}")) for i in range(self.n_dma_sems)]
            block = es.enter_context(nc.Block())

            def sem_of(key):
                return csem[key[1]] if key[0] == 'c' else dsem[key[1]]

            def run(engname, e):
                for waits, fn, inc_key, inc in self.ops[engname]:
                    for k, v in waits:
                        e.wait_ge(sem_of(k), v)
                    if fn is None:
                        continue
                    ins = fn(e)
                    ins.then_inc(sem_of(inc_key), inc)

            @block.tensor
            def _(e):
                run("pe", e)

            @block.scalar
            def _(e):
                run("act", e)

            @block.vector
            def _(e):
                run("dve", e)

            @block.gpsimd
            def _(e):
                run("pool", e)

            @block.sync
            def _(e):
                run("sp", e)


def na_tables():
    cases = {}
    tabs = []
    keys = {}
    ar = np.arange(128)
    for p in range(16):
        for kb in range(16):
            krow = 2 * kb + ar // 64
            kcol = ar % 64
            qrow = 2 * p + ar // 64
            qcol = ar % 64
            rs = np.clip(qrow - 4, 0, 24)
            vr = (krow[:, None] >= rs[None, :]) & (krow[:, None] < rs[None, :] + 8)
            ws = np.clip(qcol - 8, 0, 48)
            vc = (kcol[:, None] >= ws[None, :]) & (kcol[:, None] < ws[None, :] + 16)
            valid = vr & vc
            if not valid.any():
                continue
            dr = krow[:, None] - qrow[None, :] + 7
            dc = np.clip(kcol[:, None] - qcol[None, :] + 15, 0, 30)
            dr = np.where(valid, dr, 0)
            dc = np.where(valid, dc, 0)
            key = (dr.tobytes(), dc.tobytes(), valid.tobytes())
            if key not in keys:
                keys[key] = len(tabs)
                tabs.append((dr, dc, valid))
            cases[(p, kb)] = keys[key]
    return cases, tabs


NA_CASES, NA_TABS = na_tables()
NTAB = len(NA_TABS)


def rope_tables():
    t = np.arange(NLAT)
    inv_freq = 10000.0 ** (-np.arange(0, 32, 2) / 32)
    ang = np.stack([(t // 64)[:, None] * inv_freq[None], (t % 64)[:, None] * inv_freq[None]], axis=1)
    return np.cos(ang).astype(np.float32).reshape(NLAT, 32), np.sin(ang).astype(np.float32).reshape(NLAT, 32)


def build(cfg=None):
    cfg = cfg or {}
    dbg = cfg.get("dbg", [])
    nc = bass.Bass("TRN2", target_bir_lowering=False)
    p = Prog(nc)

    def din(name, shape, dt=F32):
        return nc.dram_tensor(name, list(shape), dt, kind="ExternalInput").ap()

    def dscr(name, shape, dt=F32):
        kind = "Internal"
        if name in cfg.get("dump", []):
            kind = "ExternalOutput"
        if name in cfg.get("feed", []):
            kind = "ExternalInput"
        return nc.dram_tensor(name, list(shape), dt, kind=kind).ap()

    I = {}
    I['x'] = din('x', [NTOK, D])
    I['cc'] = din('cc', [128, 8, 2])
    I['norm_mix'] = din('norm_mix', [DEPTH, D])
    I['norm_ffn'] = din('norm_ffn', [DEPTH, D])
    I['w_mod'] = din('w_mod', [DEPTH, D, 6 * D])
    I['b_mod'] = din('b_mod', [DEPTH, 6 * D])
    I['w_in'] = din('w_in', [DEPTH, D, 3200])
    I['w_out'] = din('w_out', [DEPTH, D, D])
    for n in ['a_qnorm', 'a_knorm', 'b_qnorm', 'b_knorm']:
        I[n] = din(n, [DEPTH, 64])
    I['a_sink'] = din('a_sink', [DEPTH, 4])
    I['btab'] = din('btab', [DEPTH, 128, NTAB, 4, 128])
    I['bmask'] = din('bmask', [128, NTAB, 128])
    I['amask'] = din('amask', [128, 2, 128])
    I['ident'] = din('ident', [128, 128])
    I['cos'] = din('cos', [NLAT, 32])
    I['sin'] = din('sin', [NLAT, 32])
    I['r7_conv'] = din('r7_conv', [DEPTH, 128, 15, 3])
    I['r7_w0'] = din('r7_w0', [DEPTH, 2, 512])
    I['r7_a0'] = din('r7_a0', [DEPTH, 2, 512])
    I['r7_w2'] = din('r7_w2', [DEPTH, 2, 64, 512])
    I['r7_a2'] = din('r7_a2', [DEPTH, 2, 64, 512])
    I['r7_g2'] = din('r7_g2', [DEPTH, 128, 512])
    for n in ['r7_kk', 'r7_ka', 'r7_lnw', 'r7_lnb']:
        I[n] = din(n, [DEPTH, 512])
    I['r7_rk'] = din('r7_rk', [DEPTH, 8, 64])
    I['peer_wq'] = din('peer_wq', [DEPTH, D, 2048])
    I['peer_keys'] = din('peer_keys', [DEPTH, 8, 2, 128, 128])
    I['peer_u'] = [din(f'peer_u{l}', [16384, D]) for l in range(DEPTH)]
    I['peer_v'] = [din(f'peer_v{l}', [16384, D]) for l in range(DEPTH)]
    I['iota'] = din('iota', [128, 256])
    I['tri'] = din('tri', [64, 2, 64])
    I['mg'] = din('mg', [64, 2, 128])
    I['mn'] = din('mn', [64, 2, 64])
    out_d = nc.dram_tensor('out', [NLAT, D], F32, kind="ExternalOutput").ap()

    S = {}
    S['mod'] = dscr('s_mod', [DEPTH, 2, 6 * D])
    S['xs'] = dscr('s_xs', [NTOK, D])
    S['o'] = dscr('s_o', [NTOK, D])
    S['pcT'] = dscr('s_pcT', [1920, NTOK])
    S['tm'] = dscr('s_tm', [NTOK, 10, 512])
    S['bon'] = dscr('s_bon', [NTOK, 8])
    S['y'] = dscr('s_y', [2, NTOK, 512])
    S['h2'] = dscr('s_h2', [NTOK, D])
    S['T'] = [dscr(f's_T{l}', [16384, 2 * D], BF16) for l in range(DEPTH)]
    DBG = {}
    for name, shape in cfg.get("dbg_out", {}).items():
        DBG[name] = nc.dram_tensor(name, list(shape), F32, kind="ExternalOutput").ap()

    ps = [nc.alloc_psum_tensor(f"ps{i}", [128, 512], F32) for i in range(8)]

    def PS(i):
        return ('ps', i)

    ident_f = p.sb([128, 128], F32, "identf")
    ident_b = p.sb([128, 128], BF16, "identb")
    eps_col = p.sb([128, 1], F32, "eps")
    p.dma(lambda e: e.dma_start(out=ident_f[:], in_=I['ident']), w=['identf'])
    p.op('dve', lambda e: e.tensor_copy(out=ident_b[:], in_=ident_f[:]), r=['identf'], w=['identb'])
    p.op('dve', lambda e: e.memset(eps_col[:], EPS), w=['eps'])
    p.barrier()
    base_mark = p.sb_mark()

    def phase_mod():
        p.sb_reset(base_mark)
        cc = p.sb([128, 8, 2], F32, "cc")
        scc = p.sb([128, 8, 2], F32, "scc")
        p.dma(lambda e: e.dma_start(out=cc[:], in_=I['cc']), w=['cc'])
        p.op('act', lambda e: e.activation(out=scc[:], in_=cc[:], func=AF.Silu), r=['cc'], w=['scc'])
        wt = [p.sb([128, 8, 512], F32, f"wmod{i}") for i in range(2)]
        bm = p.sb([2, 6 * D], F32, "bm")
        mo = p.sb([2, 6 * D], F32, "mo")
        k = 0
        for l in range(DEPTH):
            p.dma(lambda e, l=l: e.dma_start(out=bm[:], in_=I['b_mod'][l].partition_broadcast(2)),
                  w=['bm'])
            for cch in range(12):
                b = k % 2
                k += 1
                src = I['w_mod'][l, :, cch * 512:(cch + 1) * 512].rearrange("(j p) n -> p j n", p=128)
                p.dma(lambda e, b=b, src=src: e.dma_start(out=wt[b][:], in_=src), w=[('wmod', b)])
                pb = cch % 2
                for j in range(8):
                    p.op('pe', lambda e, b=b, j=j, pb=pb: e.matmul(ps[pb][0:2, :], scc[:, j, :], wt[b][:, j, :],
                                                                    start=(j == 0), stop=(j == 7)),
                         r=['scc', ('wmod', b)], w=[PS(pb)])
                p.op('dve', lambda e, pb=pb, cch=cch: e.tensor_tensor(
                    out=mo[:, cch * 512:(cch + 1) * 512], in0=ps[pb][0:2, :], in1=bm[:, cch * 512:(cch + 1) * 512],
                    op=ALU.add), r=[PS(pb), 'bm'], w=['mo'])
            p.dma(lambda e, l=l: e.dma_start(out=S['mod'][l], in_=mo[:]), r=['mo'], w=['S_mod'])
        p.barrier()

    def load_bc(dst, src_1d, key):
        P = dst.shape[0]
        p.dma(lambda e: e.dma_start(out=dst, in_=src_1d.partition_broadcast(P)), w=[key])

    def norm_tiles(l, which, src, hT, hT_off, tm_dram=None):
        nv = I['norm_mix'] if which == 0 else I['norm_ffn']
        so = 0 if which == 0 else 3
        G = [p.sb([128, D], F32, f"G{s}") for s in range(2)]
        SH = [p.sb([128, D], F32, f"SH{s}") for s in range(2)]
        tmp = p.sb([128, D], F32, "gtmp")
        for s in range(2):
            load_bc(tmp[:], nv[l], 'gtmp')
            load_bc(G[s][:], S['mod'][l, s, (so + 1) * D:(so + 2) * D], ('G', s))
            load_bc(SH[s][:], S['mod'][l, s, so * D:(so + 1) * D], ('SH', s))
            p.op('dve', lambda e, s=s: e.scalar_tensor_tensor(out=G[s][:], in0=G[s][:], scalar=1.0, in1=tmp[:],
                                                             op0=ALU.add, op1=ALU.mult),
                 r=['gtmp', ('G', s)], w=[('G', s)])
        NBUF = 4
        xt = [p.sb([128, D], F32, f"xt{i}") for i in range(NBUF)]
        junk = p.sb([128, D], F32, "junk")
        hb = [p.sb([128, D], BF16, f"hb{i}") for i in range(NBUF)]
        ss = [p.sb([128, 1], F32, f"ss{i}") for i in range(NBUF)]
        def stageA(t):
            b = t % NBUF
            s = 1 if t < 2 else 0
            yield
            p.dma(lambda e, b=b, t=t: e.dma_start(out=xt[b][:], in_=src[t * 128:(t + 1) * 128, :]), w=[('xt', b)])
            yield
            p.op('act', lambda e, b=b: e.activation(out=junk[:], in_=xt[b][:], func=AF.Square, accum_out=ss[b][:]),
                 r=[('xt', b)], w=[('ss', b)])
            yield
            p.op('act', lambda e, b=b: e.activation(out=ss[b][:], in_=ss[b][:], func=AF.Sqrt, bias=eps_col[:],
                                                    scale=1.0 / D), r=[('ss', b)], w=[('ss', b)])
            yield
            p.op('dve', lambda e, b=b: e.reciprocal(out=ss[b][:], in_=ss[b][:]), r=[('ss', b)], w=[('ss', b)])
            yield
            p.op('dve', lambda e, b=b, s=s: e.scalar_tensor_tensor(out=xt[b][:], in0=xt[b][:], scalar=ss[b][:, 0:1],
                                                                 in1=G[s][:], op0=ALU.mult, op1=ALU.mult),
                 r=[('xt', b), ('ss', b), ('G', s)], w=[('xt', b)])
            yield
            if tm_dram is not None:
                p.op('dve', lambda e, b=b, s=s: e.tensor_tensor(out=xt[b][:], in0=xt[b][:], in1=SH[s][:], op=ALU.add),
                     r=[('xt', b), ('SH', s)], w=[('xt', b)])
                p.dma(lambda e, b=b, t=t: e.dma_start(out=tm_dram[t * 128:(t + 1) * 128, :], in_=xt[b][:]),
                      r=[('xt', b)], w=[('tmd', t)])
                p.op('act', lambda e, b=b: e.activation(out=hb[b][:], in_=xt[b][:], func=AF.Copy),
                     r=[('xt', b)], w=[('hb', b)])
            else:
                p.op('dve', lambda e, b=b, s=s: e.tensor_tensor(out=hb[b][:], in0=xt[b][:], in1=SH[s][:], op=ALU.add),
                     r=[('xt', b), ('SH', s)], w=[('hb', b)])

        def stageB(t):
            b = t % NBUF
            pbank = 4 + b
            pv = ps[pbank][:, 0:512].bitcast(BF16)
            yield
            for j in range(8):
                p.op('pe', lambda e, b=b, j=j, pv=pv: e.transpose(out=pv[:, j * 128:(j + 1) * 128],
                                                                 in_=hb[b][:, j * 128:(j + 1) * 128],
                                                                 identity=ident_b[:]),
                     r=[('hb', b), 'identb'], w=[PS(pbank)])
            o = hT_off(t)
            yield
            p.op('act', lambda e, pv=pv, o=o: e.activation(
                out=hT[:, :, o:o + 128], in_=pv.rearrange("p (j t) -> p j t", j=8), func=AF.Copy),
                r=[PS(pbank)], w=[('hT', t)])


        tl_ = [t for t in range(NT) if not (t < 2 and l == DEPTH - 1 and which == 1)]
        def rr(gens):
            gens = list(gens)
            while gens:
                for g_ in list(gens):
                    try:
                        next(g_)
                    except StopIteration:
                        gens.remove(g_)

        groups = [tl_[i:i + NBUF] for i in range(0, len(tl_), NBUF)]
        for gi_ in range(len(groups) + 1):
            gl = []
            if gi_ < len(groups):
                gl += [stageA(t) for t in groups[gi_]]
            if gi_ > 0:
                gl += [stageB(t) for t in groups[gi_ - 1]]
            rr(gl)

    def phase_proj(l):
        p.sb_reset(base_mark)
        qkT = p.sb([64, 14, NTOK], BF16, "qkT")
        Vaug = p.sb([128, NT, 6, 65], BF16, "Vaug")
        mark_persist = p.sb_mark()
        hT = p.sb([128, 8, NTOK], BF16, "hT")
        m_afterh = p.sb_mark()
        wAB = p.sb([128, 8, 1280], BF16, "wAB")
        for j in range(8):
            p.dma(lambda e, j=j: e.dma_start(out=wAB[:, j, :], in_=I['w_in'][l, j * 128:(j + 1) * 128, 0:1280]),
                  w=[('wAB', j)], eng="pool")
        p.op('pool', lambda e: e.memset(Vaug[:, :, :, 64:65], 1.0), w=['Vones'])
        m0 = p.sb_mark()
        norm_tiles(l, 0, I['x'] if l == 0 else S['xs'], hT, lambda t: t * 128)
        p.barrier()
        p.sb_reset(m0)
        wC = p.sb([128, 8, 1920], BF16, "wC")
        for j in range(8):
            p.dma(lambda e, j=j: e.dma_start(out=wC[:, j, :], in_=I['w_in'][l, j * 128:(j + 1) * 128, 1280:3200]),
                  w=[('wC', j)], eng="pool")
        m_afterwc = p.sb_mark()
        GA = p.sb([128, 6, 64], F32, "GA")
        GB = p.sb([128, 8, 64], F32, "GB")
        for h in range(6):
            load_bc(GA[:, h, :], I['a_qnorm'][l] if h < 4 else I['a_knorm'][l], 'GA')
        for h in range(8):
            load_bc(GB[:, h, :], I['b_qnorm'][l] if h < 4 else I['b_knorm'][l], 'GB')
        p.op('act', lambda e: e.mul(out=GA[:, 0:4, :], in_=GA[:, 0:4, :], mul=0.125), r=['GA'], w=['GA'])
        p.op('act', lambda e: e.mul(out=GB[:, 0:4, :], in_=GB[:, 0:4, :], mul=0.125), r=['GB'], w=['GB'])
        cs = [p.sb([128, 2, 32], F32, f"cs{i}") for i in range(2)]
        xn = [p.sb([128, 14, 64], F32, f"xn{i}") for i in range(2)]
        sq = [p.sb([128, 14, 64], F32, f"sq{i}") for i in range(2)]
        ssq = [p.sb([128, 14], F32, f"ssq{i}") for i in range(2)]
        xr = [p.sb([128, 14, 64], BF16, f"xr{i}") for i in range(2)]
        RT = [[p.sb([128, 6, 2, 16], F32, f"ropeT{b_}{i}") for i in range(4)] for b_ in range(2)]
        def qk_iter(t):
            b = t % 2
            lat = t >= 2
            bA, bB, bV = (0, 1, 2) if t % 2 == 0 else (3, 6, 7)
            yield
            for bank, c0, c1 in ((bA, 0, 512), (bB, 512, 1024), (bV, 1024, 1280)):
                for j in range(8):
                    p.op('pe', lambda e, bank=bank, c0=c0, c1=c1, j=j, t=t: e.matmul(
                        ps[bank][:, 0:c1 - c0], hT[:, j, t * 128:(t + 1) * 128], wAB[:, j, c0:c1],
                        start=(j == 0), stop=(j == 7)),
                        r=[('hT', t), ('wAB', j)], w=[PS(bank)])
            yield
            if lat:
                tl = t - 2
                p.dma(lambda e, b=b, tl=tl: e.dma_start(out=cs[b][:, 0, :], in_=I['cos'][tl * 128:(tl + 1) * 128, :]),
                      w=[('cs', b)])
                p.dma(lambda e, b=b, tl=tl: e.dma_start(out=cs[b][:, 1, :], in_=I['sin'][tl * 128:(tl + 1) * 128, :]),
                      w=[('cs', b)])
            yield
            p.op('act', lambda e, t=t, bA=bA: e.activation(out=Vaug[:, t, 0:2, 0:64],
                                                    in_=ps[bA][:, 384:512].rearrange("p (h d) -> p h d", h=2),
                                                    func=AF.Copy), r=[PS(bA)], w=[('V', t)])
            yield
            p.op('act', lambda e, t=t, bV=bV: e.activation(out=Vaug[:, t, 2:6, 0:64],
                                                    in_=ps[bV][:, 0:256].rearrange("p (h d) -> p h d", h=4),
                                                    func=AF.Copy), r=[PS(bV)], w=[('V', t)])
            yield
            p.op('act', lambda e, b=b, bA=bA: e.activation(out=xn[b][:, 0:6, :],
                                                    in_=ps[bA][:, 0:384].rearrange("p (h d) -> p h d", h=6),
                                                    func=AF.Copy), r=[PS(bA)], w=[('xn', b)])
            yield
            p.op('act', lambda e, b=b, bB=bB: e.activation(out=xn[b][:, 6:14, :],
                                                    in_=ps[bB][:, 0:512].rearrange("p (h d) -> p h d", h=8),
                                                    func=AF.Copy), r=[PS(bB)], w=[('xn', b)])
            yield
            p.op('dve', lambda e, b=b: e.tensor_tensor(out=sq[b][:], in0=xn[b][:], in1=xn[b][:], op=ALU.mult),
                 r=[('xn', b)], w=[('sq', b)])
            yield
            p.op('dve', lambda e, b=b: e.tensor_reduce(out=ssq[b][:], in_=sq[b][:], axis=AX.X, op=ALU.add),
                 r=[('sq', b)], w=[('ssq', b)])
            yield
            p.op('act', lambda e, b=b: e.activation(out=ssq[b][:], in_=ssq[b][:], func=AF.Sqrt, bias=eps_col[:],
                                                    scale=1.0 / 64), r=[('ssq', b)], w=[('ssq', b)])
            yield
            p.op('dve', lambda e, b=b: e.reciprocal(out=ssq[b][:], in_=ssq[b][:]), r=[('ssq', b)], w=[('ssq', b)])
            yield
            p.op('dve', lambda e, b=b: e.tensor_tensor(out=xn[b][:], in0=xn[b][:],
                                                       in1=ssq[b][:].unsqueeze(2).to_broadcast([128, 14, 64]),
                                                       op=ALU.mult), r=[('xn', b), ('ssq', b)], w=[('xn', b)])
            yield
            p.op('dve', lambda e, b=b: e.tensor_tensor(out=xr[b][:, 6:14, :], in0=xn[b][:, 6:14, :], in1=GB[:],
                                                       op=ALU.mult), r=[('xn', b), 'GB'], w=[('xr', b)])
            yield
            if lat:
                p.op('dve', lambda e, b=b: e.tensor_tensor(out=xn[b][:, 0:6, :], in0=xn[b][:, 0:6, :], in1=GA[:],
                                                           op=ALU.mult), r=[('xn', b), 'GA'], w=[('xn', b)])
                xv = xn[b][:, 0:6, :].rearrange("p h (a g f) -> p h a g f", a=2, g=2)
                x1 = xv[:, :, :, 0, :]
                x2 = xv[:, :, :, 1, :]
                ov = xr[b][:, 0:6, :].rearrange("p h (a g f) -> p h a g f", a=2, g=2)
                cosb = cs[b][:, 0, :].rearrange("p (a f) -> p a f", a=2).unsqueeze(1).to_broadcast([128, 6, 2, 16])
                sinb = cs[b][:, 1, :].rearrange("p (a f) -> p a f", a=2).unsqueeze(1).to_broadcast([128, 6, 2, 16])
                rk = [('xn', b), ('cs', b)]
                for i, (xa, tb) in enumerate(((x1, cosb), (x2, sinb), (x2, cosb), (x1, sinb))):
                    p.op('dve', lambda e, i=i, xa=xa, tb=tb, b=b: e.tensor_tensor(out=RT[b][i][:], in0=xa, in1=tb, op=ALU.mult),
                         r=rk, w=[('RT', b, i)])
                p.op('dve', lambda e, ov=ov, b=b: e.tensor_tensor(out=ov[:, :, :, 0, :], in0=RT[b][0][:], in1=RT[b][1][:],
                                                             op=ALU.subtract), r=[('RT', b, 0), ('RT', b, 1)], w=[('xr', b)])
                p.op('dve', lambda e, ov=ov, b=b: e.tensor_tensor(out=ov[:, :, :, 1, :], in0=RT[b][2][:], in1=RT[b][3][:],
                                                             op=ALU.add), r=[('RT', b, 2), ('RT', b, 3)], w=[('xr', b)])
            else:
                p.op('dve', lambda e, b=b: e.tensor_tensor(out=xr[b][:, 0:6, :], in0=xn[b][:, 0:6, :], in1=GA[:],
                                                           op=ALU.mult), r=[('xn', b), 'GA'], w=[('xr', b)])
            yield
            for half in range(2):
                bank = 4 + half
                pv = ps[bank][0:64, 0:448].bitcast(BF16)
                for hh in range(7):
                    h = half * 7 + hh
                    p.op('pe', lambda e, b=b, h=h, hh=hh, pv=pv: e.transpose(
                        out=pv[:, hh * 128:(hh + 1) * 128], in_=xr[b][:, h, :], identity=ident_b[:]),
                        r=[('xr', b), 'identb'], w=[PS(bank)])
                p.op('act', lambda e, half=half, pv=pv, t=t: e.activation(
                    out=qkT[:, half * 7:(half + 1) * 7, t * 128:(t + 1) * 128],
                    in_=pv.rearrange("p (h t) -> p h t", h=7), func=AF.Copy), r=[PS(bank)], w=[('qkT', t)])
        def rr3(gens):
            gens = list(gens)
            while gens:
                for g_ in list(gens):
                    try:
                        next(g_)
                    except StopIteration:
                        gens.remove(g_)

        for t in range(0, NT, 2):
            rr3([qk_iter(t), qk_iter(t + 1)])
        p.barrier()
        p.sb_reset(m_afterwc)
        cw = p.sb([128, 15, 3], F32, "cw")
        p.dma(lambda e: e.dma_start(out=cw[:], in_=I['r7_conv'][l]), w=['cw'])
        rawc = [p.sb([128, NCTX + 2], F32, f"rawc{i}") for i in range(2)]
        rawl = [p.sb([128, NLAT + 2], F32, f"rawl{i}") for i in range(2)]
        cvo = [p.sb([128, NTOK], F32, f"cvo{i}") for i in range(2)]
        for i in range(2):
            p.op('pool', lambda e, i=i: e.memset(rawc[i][:], 0.0), w=[('rawc', i)])
            p.op('pool', lambda e, i=i: e.memset(rawl[i][:], 0.0), w=[('rawl', i)])
        bk = 0
        for ch in range(15):
            b = ch % 2
            groups = [(rawc[b], ('rawc', b), 1, 0, 256)] + [(rawl[b], ('rawl', b), 1 + 512 * g, 256 + 512 * g, 512)
                                                             for g in range(4)]
            for (raw, rkey, ro, tok0, n) in groups:
                bank = bk % 4
                bk += 1
                for j in range(8):
                    p.op('pe', lambda e, bank=bank, j=j, ch=ch, tok0=tok0, n=n: e.matmul(
                        ps[bank][:, 0:n], wC[:, j, ch * 128:(ch + 1) * 128], hT[:, j, tok0:tok0 + n],
                        start=(j == 0), stop=(j == 7)), r=[('wC', j)], w=[PS(bank)])
                p.op('act', lambda e, raw=raw, ro=ro, n=n, bank=bank: e.activation(
                    out=raw[:, ro:ro + n], in_=ps[bank][:, 0:n], func=AF.Copy), r=[PS(bank)], w=[rkey])
            for (raw, rkey, n, o0) in ((rawc[b], ('rawc', b), NCTX, 0), (rawl[b], ('rawl', b), NLAT, NCTX)):
                dst = cvo[b][:, o0:o0 + n]
                p.op('dve', lambda e, raw=raw, n=n, dst=dst, ch=ch: e.tensor_scalar(
                    out=dst, in0=raw[:, 1:1 + n], scalar1=cw[:, ch, 1:2], scalar2=None, op0=ALU.mult),
                    r=[rkey, 'cw'], w=[('cvo', b)])
                p.op('dve', lambda e, raw=raw, n=n, dst=dst, ch=ch: e.scalar_tensor_tensor(
                    out=dst, in0=raw[:, 0:n], scalar=cw[:, ch, 0:1], in1=dst, op0=ALU.mult, op1=ALU.add),
                    r=[rkey, 'cw'], w=[('cvo', b)])
                p.op('dve', lambda e, raw=raw, n=n, dst=dst, ch=ch: e.scalar_tensor_tensor(
                    out=dst, in0=raw[:, 2:2 + n], scalar=cw[:, ch, 2:3], in1=dst, op0=ALU.mult, op1=ALU.add),
                    r=[rkey, 'cw'], w=[('cvo', b)])
            if ch == 12:
                p.op('act', lambda e, b=b: e.activation(out=cvo[b][:], in_=cvo[b][:], func=AF.Tanh),
                     r=[('cvo', b)], w=[('cvo', b)])
            if ch == 14:
                p.op('act', lambda e, b=b: e.activation(out=cvo[b][:], in_=cvo[b][:], func=AF.Sigmoid),
                     r=[('cvo', b)], w=[('cvo', b)])
            p.dma(lambda e, b=b, ch=ch: e.dma_start(out=S['pcT'][ch * 128:(ch + 1) * 128, :], in_=cvo[b][:]),
                  r=[('cvo', b)], w=[('pcT', ch)])
        p.barrier()
        return qkT, Vaug, mark_persist

    def phase_attn(l, qkT, Vaug, mark_persist):
        with_ctx = l < DEPTH - 1
        p.sb_reset(mark_persist)
        btab = p.sb([128, NTAB, 4, 128], F32, "btab")
        bmask = p.sb([128, NTAB, 128], F32, "bmask")
        amask = p.sb([128, 2, 128], F32, "amask")
        esink = p.sb([128, 4], F32, "esink")
        o_all = [p.sb([128, 512], F32, f"oall{i}") for i in range(2)]
        ex = [p.sb([128, 8, 128], F32, f"ex{i}") for i in range(2)]
        pT = [p.sb([128, 8, 128], BF16, f"pT{i}") for i in range(2)]
        den = [p.sb([128, 1], F32, f"den{i}") for i in range(2)]
        for tb in range(NTAB):
            p.dma(lambda e, tb=tb: e.dma_start(out=btab[:, tb], in_=I['btab'][l, :, tb]), w=['btab'])
        p.dma(lambda e: e.dma_start(out=bmask[:], in_=I['bmask']), w=['bmask'])
        p.dma(lambda e: e.dma_start(out=amask[:], in_=I['amask']), w=['amask'])
        load_bc(esink[:], I['a_sink'][l], 'esink')
        p.op('act', lambda e: e.activation(out=esink[:], in_=esink[:], func=AF.Exp), r=['esink'], w=['esink'])
        p.op('act', lambda e: e.activation(out=btab[:], in_=btab[:], func=AF.Exp), r=['btab'], w=['btab'])
        for h in range(4):
            p.op('dve', lambda e, h=h: e.tensor_tensor(out=btab[:, :, h, :], in0=btab[:, :, h, :], in1=bmask[:],
                                                       op=ALU.mult), r=['btab', 'bmask'], w=['btab'])
        def attn_iter(t, grp, h, b, ob):
            if grp == 0:
                qs, ks, vs = h, 4 + h // 2, h // 2
            else:
                qs, ks, vs = 6 + h, 10 + h, 2 + h
            if t < 2:
                blocks = [(0, None), (1, None)]
            elif grp == 0:
                n = t - 2
                blocks = [(t, None), (0, None), (1, None)]
                if n > 0:
                    blocks.append((t - 1, amask[:, 0, :]))
                if n < 15:
                    blocks.append((t + 1, amask[:, 1, :]))
            else:
                pq = t - 2
                blocks = [(0, None), (1, None)]
                for kb in range(16):
                    if (pq, kb) in NA_CASES:
                        blocks.append((kb + 2, btab[:, NA_CASES[(pq, kb)], h, :]))
            nb = len(blocks)
            nn = sum(1 for _, tb in blocks if tb is None)
            sb0, sb1 = (0, 1) if b == 0 else (2, 3)
            ob_ps = 4 + b
            yield
            for i, (kt, tb) in enumerate(blocks):
                bank = sb0 if i < 4 else sb1
                p.op('pe', lambda e, bank=bank, i=i, kt=kt, ks=ks, qs=qs, t=t: e.matmul(
                    ps[bank][:, (i % 4) * 128:(i % 4 + 1) * 128], qkT[:, ks, kt * 128:(kt + 1) * 128],
                    qkT[:, qs, t * 128:(t + 1) * 128], start=True, stop=True), w=[PS(bank)])
            n0 = min(nb, 4)
            yield
            p.op('act', lambda e, b=b, n0=n0, sb0=sb0: e.activation(
                out=ex[b][:, 0:n0, :], in_=ps[sb0][:, 0:n0 * 128].rearrange("p (n k) -> p n k", n=n0),
                func=AF.Exp), r=[PS(sb0)], w=[('ex', b)])
            if nb > 4:
                n1 = nb - 4
                p.op('act', lambda e, b=b, n1=n1, sb1=sb1: e.activation(
                    out=ex[b][:, 4:4 + n1, :], in_=ps[sb1][:, 0:n1 * 128].rearrange("p (n k) -> p n k", n=n1),
                    func=AF.Exp), r=[PS(sb1)], w=[('ex', b)])
            yield
            p.op('pool', lambda e, b=b, nn=nn: e.tensor_copy(out=pT[b][:, 0:nn, :], in_=ex[b][:, 0:nn, :]),
                 r=[('ex', b)], w=[('pT', b)])
            yield
            for i, (kt, tb) in enumerate(blocks):
                if tb is None:
                    continue
                p.op('dve', lambda e, b=b, i=i, tb=tb: e.tensor_tensor(out=pT[b][:, i, :], in0=ex[b][:, i, :],
                                                                      in1=tb, op=ALU.mult),
                     r=[('ex', b), 'btab', 'amask'], w=[('pT', b)])
            yield
            for i, (kt, tb) in enumerate(blocks):
                p.op('pe', lambda e, b=b, i=i, kt=kt, vs=vs, ob_ps=ob_ps, nb=nb: e.matmul(
                    ps[ob_ps][:, 0:65], pT[b][:, i, :], Vaug[:, kt, vs, :], start=(i == 0), stop=(i == nb - 1)),
                    r=[('pT', b)], w=[PS(ob_ps)])
            if grp == 0:
                p.op('dve', lambda e, b=b, h=h, ob_ps=ob_ps: e.tensor_scalar(
                    out=den[b][:], in0=ps[ob_ps][:, 64:65], scalar1=esink[:, h:h + 1], scalar2=None,
                    op0=ALU.add), r=[PS(ob_ps), 'esink'], w=[('den', b)])
                p.op('dve', lambda e, b=b: e.reciprocal(out=den[b][:], in_=den[b][:]),
                     r=[('den', b)], w=[('den', b)])
            else:
                p.op('dve', lambda e, b=b, ob_ps=ob_ps: e.reciprocal(out=den[b][:], in_=ps[ob_ps][:, 64:65]),
                     r=[PS(ob_ps)], w=[('den', b)])
            col = grp * 256 + h * 64
            yield
            p.op('dve', lambda e, b=b, ob=ob, col=col, ob_ps=ob_ps: e.tensor_scalar(
                out=o_all[ob][:, col:col + 64], in0=ps[ob_ps][:, 0:64], scalar1=den[b][:, 0:1], scalar2=None,
                op0=ALU.mult), r=[PS(ob_ps), ('den', b)], w=[('oall', ob)])

        def rr2(gens):
            gens = list(gens)
            while gens:
                for g_ in list(gens):
                    try:
                        next(g_)
                    except StopIteration:
                        gens.remove(g_)

        it = 0
        for t in range(NT):
            if t < 2 and not with_ctx:
                continue
            ob = t % 2
            its = []
            for grp in range(2):
                for h in range(4):
                    its.append((t, grp, h, it % 2, ob))
                    it += 1
            for i in range(0, 8, 2):
                rr2([attn_iter(*its[i]), attn_iter(*its[i + 1])])
            p.dma(lambda e, ob=ob, t=t: e.dma_start(out=S['o'][t * 128:(t + 1) * 128, 0:512], in_=o_all[ob][:]),
                  r=[('oall', ob)], w=[('So', t)])
        p.barrier()


    def phase_rprep(l):
        p.sb_reset(base_mark)
        w2 = p.sb([128, 512], F32, "w2")
        a2 = p.sb([128, 512], F32, "a2")
        g2 = p.sb([128, 512], F32, "g2")
        w0 = p.sb([1, 2, 512], F32, "w0")
        a0 = p.sb([1, 2, 512], F32, "a0")
        ones = p.sb([1, 128], F32, "ones")
        KKW = p.sb([128, 512], F32, "KKW")
        KA = p.sb([128, 512], F32, "KA")
        RK = p.sb([128, 512], F32, "RK")
        p.dma(lambda e: e.dma_start(out=w2[:], in_=I['r7_w2'][l].rearrange("d r c -> (d r) c")), w=['w2'])
        p.dma(lambda e: e.dma_start(out=a2[:], in_=I['r7_a2'][l].rearrange("d r c -> (d r) c")), w=['a2'])
        p.dma(lambda e: e.dma_start(out=g2[:], in_=I['r7_g2'][l]), w=['g2'])
        p.dma(lambda e: e.dma_start(out=w0[:], in_=I['r7_w0'][l:l + 1]), w=['w0'])
        p.dma(lambda e: e.dma_start(out=a0[:], in_=I['r7_a0'][l:l + 1]), w=['a0'])
        p.op('dve', lambda e: e.memset(ones[:], 1.0), w=['ones'])
        load_bc(KKW[:], I['r7_kk'][l], 'KKW')
        load_bc(KA[:], I['r7_ka'][l], 'KA')
        load_bc(RK[:], I['r7_rk'][l].rearrange("h d -> (h d)"), 'RK')
        fm = [p.sb([128, 15, 128], F32, f"fm{i}") for i in range(2)]
        TM = [p.sb([128, 10, 512], F32, f"TM{i}") for i in range(2)]
        kt = p.sb([128, 512], F32, "kt")
        av = [p.sb([128, 512], F32, f"av{i}") for i in range(2)]
        tmp = p.sb([128, 512], F32, "tmp")
        tmp2 = p.sb([128, 512], F32, "tmp2")
        s8 = p.sb([128, 8], F32, "s8")
        bs = [p.sb([128, 8], F32, f"bs{i}") for i in range(2)]
        for t in range(NT):
            b = t % 2
            p.dma(lambda e, b=b, t=t: e.dma_start(
                out=fm[b][:], in_=S['pcT'][:, t * 128:(t + 1) * 128].rearrange("(c p) t -> p c t", p=128)),
                w=[('fm', b)])
            for q in range(3):
                for c4 in range(4):
                    p.op('pe', lambda e, b=b, q=q, c4=c4: e.transpose(
                        out=ps[q][:, c4 * 128:(c4 + 1) * 128], in_=fm[b][:, q * 4 + c4, :], identity=ident_f[:]),
                        r=[('fm', b), 'identf'], w=[PS(q)])
            for d in range(2):
                pr = slice(d * 64, d * 64 + 64)
                p.op('pe', lambda e, b=b, d=d, pr=pr: e.matmul(ps[3 + d][:, :], fm[b][pr, 12, :], w2[pr, :],
                                                              start=True, stop=False), r=[('fm', b), 'w2'], w=[PS(3 + d)])
                p.op('pe', lambda e, d=d: e.matmul(ps[3 + d][:, :], ones[0:1, :], w0[0:1, d, :], start=False, stop=True),
                     r=['ones', 'w0'], w=[PS(3 + d)])
                p.op('pe', lambda e, b=b, d=d, pr=pr: e.matmul(ps[5 + d][:, :], fm[b][pr, 13, :], a2[pr, :],
                                                              start=True, stop=False), r=[('fm', b), 'a2'], w=[PS(5 + d)])
                p.op('pe', lambda e, d=d: e.matmul(ps[5 + d][:, :], ones[0:1, :], a0[0:1, d, :], start=False, stop=True),
                     r=['ones', 'a0'], w=[PS(5 + d)])
            p.op('pe', lambda e, b=b: e.matmul(ps[7][:, :], fm[b][:, 14, :], g2[:], start=True, stop=True),
                 r=[('fm', b), 'g2'], w=[PS(7)])
            T = TM[b]
            wk = [('TM', b)]
            p.op('act', lambda e, T=T: e.activation(out=T[:, 0, :], in_=ps[0][:, :], func=AF.Copy), r=[PS(0)], w=wk)
            p.op('act', lambda e: e.activation(out=kt[:], in_=ps[1][:, :], func=AF.Copy), r=[PS(1)], w=['kt'])
            p.op('act', lambda e, T=T: e.activation(out=T[:, 1, :], in_=ps[2][:, :], func=AF.Copy), r=[PS(2)], w=wk)
            p.op('act', lambda e, T=T: e.activation(out=T[:, 2, :], in_=ps[7][:, :], func=AF.Copy), r=[PS(7)], w=wk)
            for d in range(2):
                p.op('act', lambda e, T=T, d=d: e.activation(out=T[:, 8 + d, :], in_=ps[3 + d][:, :], func=AF.Sigmoid),
                     r=[PS(3 + d)], w=wk)
                p.op('act', lambda e, d=d: e.activation(out=av[d][:], in_=ps[5 + d][:, :], func=AF.Sigmoid),
                     r=[PS(5 + d)], w=[('av', d)])
                p.op('dve', lambda e, T=T, d=d: e.tensor_scalar(out=T[:, 8 + d, :], in0=T[:, 8 + d, :],
                                                                scalar1=-0.6065306597126334, scalar2=None, op0=ALU.mult),
                     r=wk, w=wk)
            p.op('dve', lambda e: e.tensor_tensor(out=tmp[:], in0=kt[:], in1=KKW[:], op=ALU.mult), r=['kt', 'KKW'], w=['tmp'])
            p.op('dve', lambda e: e.tensor_tensor(out=tmp2[:], in0=tmp[:], in1=tmp[:], op=ALU.mult), r=['tmp'], w=['tmp2'])
            p.op('dve', lambda e: e.tensor_reduce(out=s8[:], in_=tmp2[:].rearrange("p (h d) -> p h d", h=8), axis=AX.X,
                                                  op=ALU.add), r=['tmp2'], w=['s8'])
            p.op('act', lambda e: e.activation(out=s8[:], in_=s8[:], func=AF.Sqrt), r=['s8'], w=['s8'])
            p.op('dve', lambda e: e.tensor_scalar(out=s8[:], in0=s8[:], scalar1=1e-12, scalar2=None, op0=ALU.max),
                 r=['s8'], w=['s8'])
            p.op('dve', lambda e: e.reciprocal(out=s8[:], in_=s8[:]), r=['s8'], w=['s8'])
            p.op('dve', lambda e, T=T: e.tensor_tensor(out=T[:, 3, :].rearrange("p (h d) -> p h d", h=8),
                                                       in0=tmp[:].rearrange("p (h d) -> p h d", h=8),
                                                       in1=s8[:].unsqueeze(2).to_broadcast([128, 8, 64]), op=ALU.mult),
                 r=['tmp', 's8'], w=wk)
            for d in range(2):
                p.op('dve', lambda e, d=d: e.scalar_tensor_tensor(out=tmp2[:], in0=av[d][:], scalar=-1.0, in1=KA[:],
                                                                  op0=ALU.add, op1=ALU.mult),
                     r=[('av', d), 'KA'], w=['tmp2'])
                p.op('dve', lambda e, T=T, d=d: e.scalar_tensor_tensor(out=T[:, 4 + d, :], in0=tmp2[:], scalar=1.0,
                                                                       in1=kt[:], op0=ALU.add, op1=ALU.mult),
                     r=['tmp2', 'kt'], w=wk)
                p.op('dve', lambda e, T=T, d=d: e.tensor_tensor(out=T[:, 6 + d, :], in0=T[:, 3, :], in1=av[d][:],
                                                                op=ALU.mult), r=wk + [('av', d)], w=wk)
            p.op('dve', lambda e, T=T: e.tensor_tensor(out=tmp[:], in0=T[:, 4, :], in1=T[:, 5, :], op=ALU.add),
                 r=wk, w=['tmp'])
            p.op('dve', lambda e: e.tensor_tensor(out=tmp[:], in0=tmp[:], in1=RK[:], op=ALU.mult), r=['tmp', 'RK'], w=['tmp'])
            p.op('dve', lambda e, T=T: e.tensor_tensor(out=tmp[:], in0=tmp[:], in1=T[:, 0, :], op=ALU.mult),
                 r=['tmp'] + wk, w=['tmp'])
            p.op('dve', lambda e, b=b: e.tensor_reduce(out=bs[b][:], in_=tmp[:].rearrange("p (h d) -> p h d", h=8),
                                                       axis=AX.X, op=ALU.add), r=['tmp'], w=[('bs', b)])
            p.dma(lambda e, T=T, t=t: e.dma_start(out=S['tm'][t * 128:(t + 1) * 128], in_=T[:]), r=wk, w=[('Stm', t)])
            p.dma(lambda e, b=b, t=t: e.dma_start(out=S['bon'][t * 128:(t + 1) * 128], in_=bs[b][:]),
                  r=[('bs', b)], w=[('Sbon', t)])
        p.barrier()

    def phase_scan(l):
        p.sb_reset(base_mark)
        PSB = ps
        C = 64
        NCH = NTOK // C
        tri = p.sb([64, 2, 64], F32, "tri")
        mg = p.sb([64, 2, 128], F32, "mg")
        mn = p.sb([64, 2, 64], F32, "mn")
        ones = p.sb([64, 1], F32, "ones1")
        p.dma(lambda e: e.dma_start(out=tri[:], in_=I['tri']), w=['tri'])
        p.dma(lambda e: e.dma_start(out=mg[:], in_=I['mg']), w=['mg'])
        p.dma(lambda e: e.dma_start(out=mn[:], in_=I['mn']), w=['mn'])
        p.op('dve', lambda e: e.memset(ones[:], 1.0), w=['ones1'])
        M = [p.sb([64, 8, 64], F32, f"M{d}") for d in range(2)]
        for d in range(2):
            M0_PLACEHOLDER = None
        X = [[p.sb([64, 6, 512], F32, f"X{d}{i}") for i in range(2)] for d in range(2)]
        def mk(shape, name):
            return [p.sb(shape, F32, f"{name}{d}") for d in range(2)]
        E0s, E1s, E2s = mk([64, 512], "E0"), mk([64, 512], "E1"), mk([64, 512], "E2")
        Ats, Rts, Bts, Kts = mk([64, 512], "At"), mk([64, 512], "Rt"), mk([64, 512], "Bt"), mk([64, 512], "Kt")
        FARs, FBs, FKs = mk([64, 8, 128], "FAR"), mk([64, 8, 64], "FB"), mk([64, 8, 64], "FK")
        G1s, G2s = mk([64, 8, 128], "G1"), mk([64, 8, 128], "G2")
        Tms = [mk([64, 8, 64], f"Tm{i}_") for i in range(2)]
        Nms = [mk([64, 8, 64], f"Nm{i}_") for i in range(2)]
        Zs, Wss, Uss, PCs = mk([64, 8, 64], "Z"), mk([64, 512], "Ws"), mk([64, 512], "Us"), mk([64, 8], "PC")
        Ys = [p.sb([64, 512], F32, f"Ys{d}") for d in range(2)]
        order = {0: list(range(0, 4)) + list(range(4, NCH)), 1: list(range(3, -1, -1)) + list(range(NCH - 1, 3, -1))}
        v3 = lambda ap: ap.rearrange("p (h d) -> p h d", h=8)
        F32R = mybir.dt.float32r
        use_r = cfg.get("fp32r", True)

        def RR(ap):
            return ap.bitcast(F32R) if use_r else ap

        Vrs = mk([64, 512], "Vr")
        Mts = mk([64, 8, 64], "Mt")
        for d in range(2):
            p.op('dve', lambda e, d=d: e.memset(Mts[d][:], 0.0), w=[('Mt', d)])
            p.op('dve', lambda e, d=d: e.tensor_copy(out=RR(M[d][:]), in_=Mts[d][:]), r=[('Mt', d)], w=[('M', d)])

        def mmr(e, out, lhsT, rhs, **kw):
            if use_r:
                return e.matmul(out, lhsT.bitcast(F32R), rhs.bitcast(F32R), **kw)
            return e.matmul(out, lhsT, rhs, **kw)

        def scan_unit(d, c):
            if True:
                tok0 = c * C
                Xd = X[d][c % 2]
                E0, E1, E2, At, Rt, Bt, Kt = E0s[d], E1s[d], E2s[d], Ats[d], Rts[d], Bts[d], Kts[d]
                FAR, FB, FK, G1, G2 = FARs[d], FBs[d], FKs[d], G1s[d], G2s[d]
                Tm = [Tms[0][d], Tms[1][d]]
                Nm = [Nms[0][d], Nms[1][d]]
                Z, Ws, Us, PC = Zs[d], Wss[d], Uss[d], PCs[d]
                ps = [PSB[4 * d + (i % 4)] for i in range(8)]
                PS = lambda i: ('ps', 4 * d + (i % 4))
                xk = [('X', d, c % 2)]
                srcs = [0, 1, 3, 4 + d, 6 + d, 8 + d]
                yield
                for i, s in enumerate(srcs):
                    p.dma(lambda e, Xd=Xd, i=i, s=s, tok0=tok0: e.dma_start(out=Xd[:, i, :],
                                                                           in_=S['tm'][tok0:tok0 + C, s, :]), w=xk)
                r_, v_, kk_, k_, b_, lw_ = [Xd[:, i, :] for i in range(6)]
                Vr = Vrs[d]
                yield
                p.op('act', lambda e, v_=v_: e.activation(out=RR(Vr[:]), in_=v_, func=AF.Copy), r=xk, w=[('Vr', d)])
                v_ = Vr[:]
                vk = [('Vr', d)]
                yield
                p.op('pe', lambda e, d=d, lw_=lw_: e.matmul(ps[0][0:64, :], tri[:, d, :], lw_, start=True, stop=True),
                     r=xk + ['tri'], w=[PS(0)])
                yield
                for h in range(8):
                    p.op('pe', lambda e, h=h, lw_=lw_: e.matmul(ps[1][0:64, h:h + 1], lw_[:, h * 64:(h + 1) * 64],
                                                               ones[:, 0:1], start=True, stop=True),
                         r=xk + ['ones1'], w=[PS(1)])
                yield
                p.op('act', lambda e: e.activation(out=PC[:], in_=ps[1][0:64, 0:8], func=AF.Exp), r=[PS(1)], w=[('PC', d)])
                yield
                p.op('act', lambda e: e.activation(out=E1[:], in_=ps[0][0:64, :], func=AF.Exp), r=[PS(0)], w=[('E1', d)])
                yield
                p.op('act', lambda e: e.activation(out=E2[:], in_=ps[0][0:64, :], func=AF.Exp, scale=-1.0),
                     r=[PS(0)], w=[('E2', d)])
                yield
                p.op('dve', lambda e, lw_=lw_: e.tensor_tensor(out=E0[:], in0=ps[0][0:64, :], in1=lw_, op=ALU.subtract),
                     r=[PS(0)] + xk, w=[('E0', d)])
                yield
                p.op('act', lambda e: e.activation(out=E0[:], in_=E0[:], func=AF.Exp), r=[('E0', d)], w=[('E0', d)])
                yield
                p.op('dve', lambda e, kk_=kk_: e.scalar_tensor_tensor(out=At[:], in0=kk_, scalar=-1.0, in1=E0[:],
                                                                      op0=ALU.mult, op1=ALU.mult),
                     r=xk + [('E0', d)], w=[('At', d)])
                yield
                p.op('dve', lambda e, r_=r_: e.tensor_tensor(out=Rt[:], in0=r_, in1=E1[:], op=ALU.mult),
                     r=xk + [('E1', d)], w=[('Rt', d)])
                yield
                p.op('dve', lambda e, b_=b_: e.tensor_tensor(out=RR(Bt[:]), in0=b_, in1=E2[:], op=ALU.mult),
                     r=xk + [('E2', d)], w=[('Bt', d)])
                yield
                p.op('dve', lambda e, k_=k_: e.tensor_tensor(out=RR(Kt[:]), in0=k_, in1=E2[:], op=ALU.mult),
                     r=xk + [('E2', d)], w=[('Kt', d)])
                yield
                for bank, src, key in ((2, At, ('At', d)), (3, Rt, ('Rt', d)), (4, Bt, ('Bt', d)), (5, Kt, ('Kt', d))):
                    for h in range(8):
                        p.op('pe', lambda e, bank=bank, src=src, h=h: e.transpose(
                            out=ps[bank][0:64, h * 64:(h + 1) * 64], in_=src[:, h * 64:(h + 1) * 64],
                            identity=ident_f[0:64, 0:64]), r=[key, 'identf'], w=[PS(bank)])
                yield
                p.op('act', lambda e: e.activation(out=RR(FAR[:, :, 0:64]), in_=v3(ps[2][0:64, :]), func=AF.Copy),
                     r=[PS(2)], w=[('FAR', d)])
                yield
                p.op('act', lambda e: e.activation(out=RR(FAR[:, :, 64:128]), in_=v3(ps[3][0:64, :]), func=AF.Copy),
                     r=[PS(3)], w=[('FAR', d)])
                yield
                p.op('dve', lambda e: e.tensor_copy(out=RR(FB[:]), in_=v3(ps[4][0:64, :])), r=[PS(4)], w=[('FB', d)])
                yield
                p.op('dve', lambda e: e.tensor_copy(out=RR(FK[:]), in_=v3(ps[5][0:64, :])), r=[PS(5)], w=[('FK', d)])
                yield
                for h in range(8):
                    bank = 6 + (h // 4)
                    p.op('pe', lambda e, h=h, bank=bank: mmr(e, ps[bank][0:64, (h % 4) * 128:(h % 4 + 1) * 128],
                                                                   FB[:, h, :], FAR[:, h, :], start=True, stop=True),
                         r=[('FB', d), ('FAR', d)], w=[PS(bank)])
                yield
                for hb in range(2):
                    p.op('dve', lambda e, hb=hb, d=d: e.tensor_tensor(
                        out=RR(G1[:, hb * 4:(hb + 1) * 4, :]), in0=ps[6 + hb][0:64, :].rearrange("p (h t) -> p h t", h=4),
                        in1=mg[:, d, :].unsqueeze(1).to_broadcast([64, 4, 128]), op=ALU.mult),
                        r=[PS(6 + hb), 'mg'], w=[('G1', d)])
                yield
                for h in range(8):
                    bank = 2 + (h // 4)
                    p.op('pe', lambda e, h=h, bank=bank: mmr(e, ps[bank][0:64, (h % 4) * 128:(h % 4 + 1) * 128],
                                                                   FK[:, h, :], FAR[:, h, :], start=True, stop=True),
                         r=[('FK', d), ('FAR', d)], w=[PS(bank)])
                yield
                for hb in range(2):
                    p.op('dve', lambda e, hb=hb, d=d: e.tensor_tensor(
                        out=RR(G2[:, hb * 4:(hb + 1) * 4, :]), in0=ps[2 + hb][0:64, :].rearrange("p (h t) -> p h t", h=4),
                        in1=mg[:, d, :].unsqueeze(1).to_broadcast([64, 4, 128]), op=ALU.mult),
                        r=[PS(2 + hb), 'mg'], w=[('G2', d)])
                yield
                for h in range(8):
                    p.op('pe', lambda e, h=h: mmr(e, ps[4][0:64, h * 64:(h + 1) * 64], FAR[:, h, 0:64], FB[:, h, :],
                                                       start=True, stop=True), r=[('FAR', d), ('FB', d)], w=[PS(4)])
                yield
                p.op('dve', lambda e, d=d: e.tensor_tensor(out=RR(Nm[0][:]), in0=v3(ps[4][0:64, :]),
                                                           in1=mn[:, d, :].unsqueeze(1).to_broadcast([64, 8, 64]),
                                                           op=ALU.mult), r=[PS(4), 'mn'], w=[('Nm', d, 0)])
                yield
                p.op('dve', lambda e: e.tensor_copy(out=RR(Tm[0][:]), in_=G1[:, :, 0:64]), r=[('G1', d)], w=[('Tm', d, 0)])
                yield
                p.op('dve', lambda e: e.tensor_tensor(out=RR(Z[:]), in0=G1[:, :, 0:64],
                                                      in1=ident_f[0:64, 0:64].unsqueeze(1).to_broadcast([64, 8, 64]),
                                                      op=ALU.add), r=[('G1', d), 'identf'], w=[('Z', d)])
                cur = 0
                yield
                for lev in range(5):
                    nxt = 1 - cur
                    last = lev == 4
                    for h in range(8):
                        p.op('pe', lambda e, h=h, cur=cur: mmr(e, ps[5][0:64, h * 64:(h + 1) * 64], Tm[cur][:, h, :],
                                                                    Nm[cur][:, h, :], start=True, stop=True),
                             r=[('Tm', d, cur), ('Nm', d, cur)], w=[PS(5)])
                    p.op('act', lambda e, nxt=nxt: e.activation(out=RR(Nm[nxt][:]), in_=v3(ps[5][0:64, :]), func=AF.Copy),
                         r=[PS(5)], w=[('Nm', d, nxt)])
                    if not last:
                        for h in range(8):
                            p.op('pe', lambda e, h=h, cur=cur: mmr(e, ps[6][0:64, h * 64:(h + 1) * 64],
                                                                        Nm[cur][:, h, :], Tm[cur][:, h, :],
                                                                        start=True, stop=True),
                                 r=[('Tm', d, cur), ('Nm', d, cur)], w=[PS(6)])
                        p.op('dve', lambda e, nxt=nxt: e.tensor_copy(out=RR(Tm[nxt][:]), in_=v3(ps[6][0:64, :])),
                             r=[PS(6)], w=[('Tm', d, nxt)])
                    for h in range(8):
                        p.op('pe', lambda e, h=h, nxt=nxt: mmr(e, ps[7][0:64, h * 64:(h + 1) * 64], Nm[nxt][:, h, :],
                                                                    Z[:, h, :], start=True, stop=True),
                             r=[('Nm', d, nxt), ('Z', d)], w=[PS(7)])
                    p.op('dve', lambda e: e.tensor_tensor(out=RR(Z[:]), in0=Z[:], in1=v3(ps[7][0:64, :]), op=ALU.add),
                         r=[PS(7), ('Z', d)], w=[('Z', d)])
                    cur = nxt
                Md = M[d]
                yield
                for h in range(8):
                    o = ps[0][0:64, h * 64:(h + 1) * 64]
                    p.op('pe', lambda e, h=h, o=o, Md=Md: mmr(e, o, FAR[:, h, 0:64], Md[:, h, :], start=True, stop=False),
                         r=[('FAR', d), ('M', d)], w=[PS(0)])
                    p.op('pe', lambda e, h=h, o=o, v_=v_: mmr(e, o, G2[:, h, 0:64], v_[:, h * 64:(h + 1) * 64],
                                                                   start=False, stop=True), r=[('G2', d)] + vk, w=[PS(0)])
                yield
                p.op('act', lambda e: e.activation(out=RR(Ws[:]), in_=ps[0][0:64, :], func=AF.Copy), r=[PS(0)], w=[('Ws', d)])
                yield
                for h in range(8):
                    p.op('pe', lambda e, h=h: mmr(e, ps[1][0:64, h * 64:(h + 1) * 64], Z[:, h, :],
                                                       Ws[:, h * 64:(h + 1) * 64], start=True, stop=True),
                         r=[('Z', d), ('Ws', d)], w=[PS(1)])
                yield
                p.op('act', lambda e: e.activation(out=RR(Us[:]), in_=ps[1][0:64, :], func=AF.Copy), r=[PS(1)], w=[('Us', d)])
                yield
                for h in range(8):
                    o = ps[2][0:64, h * 64:(h + 1) * 64]
                    hs = slice(h * 64, (h + 1) * 64)
                    p.op('pe', lambda e, h=h, o=o, Md=Md: mmr(e, o, FAR[:, h, 64:128], Md[:, h, :], start=True, stop=False),
                         r=[('FAR', d), ('M', d)], w=[PS(2)])
                    p.op('pe', lambda e, h=h, o=o, hs=hs: mmr(e, o, G1[:, h, 64:128], Us[:, hs], start=False, stop=False),
                         r=[('G1', d), ('Us', d)], w=[PS(2)])
                    p.op('pe', lambda e, h=h, o=o, hs=hs, v_=v_: mmr(e, o, G2[:, h, 64:128], v_[:, hs], start=False, stop=True),
                         r=[('G2', d)] + vk, w=[PS(2)])
                yield
                p.op('act', lambda e, d=d: e.activation(out=Ys[d][:], in_=ps[2][0:64, :], func=AF.Copy),
                     r=[PS(2)], w=[('Ys', d)])
                yield
                p.dma(lambda e, d=d, tok0=tok0: e.dma_start(out=S['y'][d, tok0:tok0 + C, :], in_=Ys[d][:]),
                      r=[('Ys', d)], w=[('Sy', d, c)])
                yield
                for h in range(8):
                    o = ps[3][0:64, h * 64:(h + 1) * 64]
                    hs = slice(h * 64, (h + 1) * 64)
                    p.op('pe', lambda e, o=o, hs=hs: mmr(e, o, Bt[:, hs], Us[:, hs], start=True, stop=False),
                         r=[('Bt', d), ('Us', d)], w=[PS(3)])
                    p.op('pe', lambda e, o=o, hs=hs, v_=v_: mmr(e, o, Kt[:, hs], v_[:, hs], start=False, stop=True),
                         r=[('Kt', d)] + vk, w=[PS(3)])
                Mt = Mts[d]
                yield
                p.op('dve', lambda e, Md=Md, Mt=Mt: e.tensor_tensor(out=Mt[:], in0=Md[:], in1=v3(ps[3][0:64, :]), op=ALU.add),
                     r=[PS(3), ('M', d)], w=[('Mt', d)])
                yield
                p.op('dve', lambda e, Md=Md, Mt=Mt: e.tensor_tensor(out=RR(Md[:]), in0=Mt[:],
                                                             in1=PC[:].unsqueeze(2).to_broadcast([64, 8, 64]),
                                                             op=ALU.mult), r=[('PC', d), ('Mt', d)], w=[('M', d)])
        cin = [[p.sb([128, 1, D], F32, f"cin{i}{q}") for q in range(2)] for i in range(2)]
        cout = [p.sb([128, 1, 2 * D], BF16, f"cout{i}") for i in range(2)]

        def conv_block(blk):
            b = blk % 2
            rows = slice(blk * 128, (blk + 1) * 128)
            for q, tabn in enumerate(('peer_u', 'peer_v')):
                p.dma(lambda e, b=b, q=q, tabn=tabn, rows=rows: e.dma_start(
                    out=cin[b][q][:], in_=I[tabn][l][rows, :].rearrange("(j p) d -> p j d", p=128)), w=[('cin', b, q)],
                    eng="pool")
                p.op('pool', lambda e, b=b, q=q: e.tensor_copy(out=cout[b][:, :, q * D:(q + 1) * D], in_=cin[b][q][:]),
                     r=[('cin', b, q)], w=[('cout', b, q)])
            p.dma(lambda e, b=b, rows=rows: e.dma_start(
                out=S['T'][l][rows, :].rearrange("(j p) d -> p j d", p=128), in_=cout[b][:]),
                r=[('cout', b, 0), ('cout', b, 1)], w=[('cout', b, 0), ('cout', b, 1)], eng="pool")

        nblk = 0
        for step in range(NCH):
            gens = [scan_unit(d, order[d][step]) for d in range(2)]
            while gens:
                for g_ in list(gens):
                    try:
                        next(g_)
                    except StopIteration:
                        gens.remove(g_)
            for _ in range(4):
                if nblk < 128:
                    conv_block(nblk)
                    nblk += 1
        while nblk < 128:
            conv_block(nblk)
            nblk += 1
        p.barrier()


    def phase_rout(l):
        p.sb_reset(base_mark)
        with_ctx = l < DEPTH - 1
        wo = p.sb([128, 8, D], BF16, "wo")
        for j in range(8):
            p.dma(lambda e, j=j: e.dma_start(out=wo[:, j, :], in_=I['w_out'][l, j * 128:(j + 1) * 128, :]),
                  w=[('wo', j)], eng="pool")
        LNW = p.sb([128, 512], F32, "LNW")
        LNB = p.sb([128, 512], F32, "LNB")
        G1b = [p.sb([128, D], F32, f"G1b{s}") for s in range(2)]
        load_bc(LNW[:], I['r7_lnw'][l], 'LNW')
        load_bc(LNB[:], I['r7_lnb'][l], 'LNB')
        gn_eps = p.sb([128, 1], F32, "gneps")
        p.op('dve', lambda e: e.memset(gn_eps[:], 64e-5), w=['gneps'])
        for s in range(2):
            load_bc(G1b[s][:], S['mod'][l, s, 2 * D:3 * D], ('G1b', s))
        yb = [[p.sb([128, 512], F32, f"y{d}{i}") for d in range(2)] for i in range(2)]
        vg = [p.sb([128, 2, 512], F32, f"vg{i}") for i in range(2)]
        bon = [p.sb([128, 8], F32, f"bon{i}") for i in range(2)]
        O = [p.sb([128, D], F32, f"O{i}") for i in range(2)]
        Ob = [p.sb([128, D], BF16, f"Ob{i}") for i in range(2)]
        oT = [p.sb([128, 8, 128], BF16, f"oT{i}") for i in range(2)]
        xt = [p.sb([128, D], F32, f"xr{i}") for i in range(2)]
        yc = p.sb([128, 512], F32, "yc")
        sq = p.sb([128, 512], F32, "sq2")
        m8 = p.sb([128, 8], F32, "m8")
        v8 = p.sb([128, 8], F32, "v8")
        src = I['x'] if l == 0 else S['xs']
        v3 = lambda ap: ap.rearrange("p (h d) -> p h d", h=8)
        bc8 = lambda ap: ap.unsqueeze(2).to_broadcast([128, 8, 64])
        for t in range(NT):
            if t < 2 and not with_ctx:
                continue
            b = t % 2
            s = 1 if t < 2 else 0
            rows = slice(t * 128, (t + 1) * 128)
            for d in range(2):
                p.dma(lambda e, b=b, d=d, rows=rows: e.dma_start(out=yb[b][d][:], in_=S['y'][d, rows, :]), w=[('y', b, d)])
            p.dma(lambda e, b=b, rows=rows: e.dma_start(out=vg[b][:], in_=S['tm'][rows, 1:3, :]), w=[('vg', b)])
            p.dma(lambda e, b=b, rows=rows: e.dma_start(out=bon[b][:], in_=S['bon'][rows, :]), w=[('bon', b)])
            p.dma(lambda e, b=b, rows=rows: e.dma_start(out=O[b][:, 0:512], in_=S['o'][rows, 0:512]), w=[('O', b)])
            p.dma(lambda e, b=b, rows=rows: e.dma_start(out=xt[b][:], in_=src[rows, :]), w=[('xr', b)])
            p.op('dve', lambda e, b=b: e.tensor_tensor(out=yc[:], in0=yb[b][0][:], in1=yb[b][1][:], op=ALU.add),
                 r=[('y', b, 0), ('y', b, 1)], w=['yc'])
            p.op('dve', lambda e: e.tensor_reduce(out=m8[:], in_=v3(yc[:]), axis=AX.X, op=ALU.add), r=['yc'], w=['m8'])
            p.op('dve', lambda e: e.tensor_scalar(out=m8[:], in0=m8[:], scalar1=1.0 / 64, scalar2=None, op0=ALU.mult),
                 r=['m8'], w=['m8'])
            p.op('dve', lambda e: e.tensor_tensor(out=v3(yc[:]), in0=v3(yc[:]), in1=bc8(m8[:]), op=ALU.subtract),
                 r=['yc', 'm8'], w=['yc'])
            p.op('dve', lambda e: e.tensor_tensor(out=sq[:], in0=yc[:], in1=yc[:], op=ALU.mult), r=['yc'], w=['sq2'])
            p.op('dve', lambda e: e.tensor_reduce(out=v8[:], in_=v3(sq[:]), axis=AX.X, op=ALU.add), r=['sq2'], w=['v8'])
            p.op('act', lambda e: e.activation(out=v8[:], in_=v8[:], func=AF.Sqrt, bias=gn_eps[:], scale=1.0 / 64),
                 r=['v8', 'gneps'], w=['v8'])
            p.op('dve', lambda e: e.reciprocal(out=v8[:], in_=v8[:]), r=['v8'], w=['v8'])
            p.op('dve', lambda e: e.tensor_tensor(out=v3(yc[:]), in0=v3(yc[:]), in1=bc8(v8[:]), op=ALU.mult),
                 r=['yc', 'v8'], w=['yc'])
            p.op('dve', lambda e: e.tensor_tensor(out=yc[:], in0=yc[:], in1=LNW[:], op=ALU.mult), r=['yc', 'LNW'], w=['yc'])
            p.op('dve', lambda e: e.tensor_tensor(out=yc[:], in0=yc[:], in1=LNB[:], op=ALU.add), r=['yc', 'LNB'], w=['yc'])
            p.op('dve', lambda e, b=b: e.tensor_tensor(out=v3(sq[:]), in0=v3(vg[b][:, 0, :]), in1=bc8(bon[b][:]),
                                                       op=ALU.mult), r=[('vg', b), ('bon', b)], w=['sq2'])
            p.op('dve', lambda e: e.tensor_tensor(out=yc[:], in0=yc[:], in1=sq[:], op=ALU.add), r=['yc', 'sq2'], w=['yc'])
            p.op('dve', lambda e, b=b: e.tensor_tensor(out=O[b][:, 512:1024], in0=yc[:], in1=vg[b][:, 1, :], op=ALU.mult),
                 r=['yc', ('vg', b)], w=[('O2', b)])
            p.dma(lambda e, b=b, rows=rows: e.dma_start(out=S['o'][rows, 512:1024], in_=O[b][:, 512:1024]),
                  r=[('O2', b)], w=[('So2', t)])
            p.op('act', lambda e, b=b: e.activation(out=Ob[b][:], in_=O[b][:], func=AF.Copy),
                 r=[('O', b), ('O2', b)], w=[('Ob', b)])
            bank = 6 + b
            pv = ps[bank][:, 0:512].bitcast(BF16)
            for j in range(8):
                p.op('pe', lambda e, b=b, j=j, pv=pv: e.transpose(out=pv[:, j * 128:(j + 1) * 128],
                                                                 in_=Ob[b][:, j * 128:(j + 1) * 128], identity=ident_b[:]),
                     r=[('Ob', b), 'identb'], w=[PS(bank)])
            p.op('act', lambda e, b=b, pv=pv: e.activation(out=oT[b][:], in_=pv.rearrange("p (j t) -> p j t", j=8),
                                                            func=AF.Copy), r=[PS(bank)], w=[('oT', b)])
            for half in range(2):
                ybank = 2 * b + half
                for j in range(8):
                    p.op('pe', lambda e, b=b, j=j, half=half, ybank=ybank: e.matmul(
                        ps[ybank][:, :], oT[b][:, j, :], wo[:, j, half * 512:(half + 1) * 512],
                        start=(j == 0), stop=(j == 7)), r=[('oT', b), ('wo', j)], w=[PS(ybank)])
                cs_ = slice(half * 512, (half + 1) * 512)
                p.op('dve', lambda e, b=b, s=s, cs_=cs_, ybank=ybank: e.tensor_tensor(
                    out=O[b][:, cs_], in0=ps[ybank][:, :], in1=G1b[s][:, cs_], op=ALU.mult),
                    r=[PS(ybank), ('G1b', s), ('Ob', b), ('So2', t)], w=[('O', b), ('O2', b)])
                p.op('dve', lambda e, b=b, cs_=cs_: e.tensor_tensor(out=xt[b][:, cs_], in0=xt[b][:, cs_], in1=O[b][:, cs_],
                                                                   op=ALU.add), r=[('O', b), ('xr', b)], w=[('xr', b)])
            p.dma(lambda e, b=b, rows=rows: e.dma_start(out=S['xs'][rows, :], in_=xt[b][:]), r=[('xr', b)], w=[('Sxs', t)])
        p.barrier()

    def phase_peer(l):
        p.sb_reset(base_mark)
        last = l == DEPTH - 1
        eu_all = p.sb([128, NT, 128], U32, "eu_all")
        gate_all = p.sb([128, NT, 128], F32, "gate_all")
        G2b = [p.sb([128, D], F32, f"G2b{s}") for s in range(2)]
        m1 = p.sb_mark()
        hT = p.sb([128, 8, NTOK], BF16, "hT2")
        wq = p.sb([128, 8, 2048], BF16, "wq")
        for j in range(8):
            p.dma(lambda e, j=j: e.dma_start(out=wq[:, j, :], in_=I['peer_wq'][l, j * 128:(j + 1) * 128, :]),
                  w=[('wq', j)], eng="pool")
        keysT = p.sb([128, 16, 128], F32, "keysT")
        m0 = p.sb_mark()
        kraw = p.sb([128, 16, 128], F32, "kraw")
        p.dma(lambda e: e.dma_start(out=kraw[:], in_=I['peer_keys'][l].rearrange("h q n d -> n (h q) d")), w=['kraw'])
        for g in range(4):
            for i in range(4):
                hp = g * 4 + i
                p.op('pe', lambda e, g=g, i=i, hp=hp: e.transpose(out=ps[g][:, i * 128:(i + 1) * 128], in_=kraw[:, hp, :],
                                                                 identity=ident_f[:]), r=['kraw', 'identf'], w=[PS(g)])
            p.op('act', lambda e, g=g: e.activation(out=keysT[:, g * 4:(g + 1) * 4, :],
                                                    in_=ps[g][:, :].rearrange("p (i n) -> p i n", i=4), func=AF.Copy),
                 r=[PS(g)], w=['keysT'])
        p.barrier()
        p.sb_reset(m0)
        norm_tiles(l, 1, S['xs'], hT, lambda t: t * 128, tm_dram=S['h2'])
        p.barrier()
        p.sb_reset(m0)
        for s in range(2):
            load_bc(G2b[s][:], S['mod'][l, s, 5 * D:6 * D], ('G2b', s))
        qT = p.sb([128, 16, 128], F32, "qT")
        sc = p.sb([128, 16, 128], F32, "sc")
        sc2 = p.sb([128, 16, 128], F32, "sc2")
        sv = p.sb([128, 16, 16], F32, "sv")
        si = p.sb([128, 16, 16], U32, "si")
        sif = p.sb([128, 16, 16], F32, "sif")
        cand = p.sb([128, 8, 16, 16], F32, "cand")
        cand2 = p.sb([128, 8, 16, 16], F32, "cand2")
        eidx = p.sb([128, 8, 16, 16], F32, "eidx")
        best = p.sb([128, 8, 16], F32, "best")
        ci = p.sb([128, 8, 16], U32, "ci")
        cif = p.sb([128, 8, 16], F32, "cif")
        iota = p.sb([128, 256], F32, "iota")
        p.dma(lambda e: e.dma_start(out=iota[:], in_=I['iota']), w=['iota'])
        eq4 = p.sb([128, 8, 16, 16], F32, "eq4")
        cu = p.sb([128, 2, 8, 16], U32, "cu")
        cf = p.sb([128, 2, 8, 16], F32, "cf")
        e12 = p.sb([128, 2, 8, 16], F32, "e12")
        esel = p.sb([128, 128], F32, "esel")
        g8 = p.sb([128, 8], F32, "g8")
        tiles = [t for t in range(NT) if not (t < 2 and last)]
        for t in tiles:
            for g in range(4):
                for i in range(4):
                    hp = g * 4 + i
                    for j in range(8):
                        p.op('pe', lambda e, g=g, i=i, hp=hp, j=j, t=t: e.matmul(
                            ps[g][:, i * 128:(i + 1) * 128], wq[:, j, hp * 128:(hp + 1) * 128],
                            hT[:, j, t * 128:(t + 1) * 128], start=(j == 0), stop=(j == 7)),
                            r=[('hT', t), ('wq', j)], w=[PS(g)])
                p.op('act', lambda e, g=g: e.activation(out=qT[:, g * 4:(g + 1) * 4, :],
                                                        in_=ps[g][:, :].rearrange("p (i n) -> p i n", i=4), func=AF.Copy),
                     r=[PS(g)], w=['qT'])
            for g in range(4):
                for i in range(4):
                    hp = g * 4 + i
                    p.op('pe', lambda e, g=g, i=i, hp=hp: e.matmul(ps[4 + g][:, i * 128:(i + 1) * 128], qT[:, hp, :],
                                                                   keysT[:, hp, :], start=True, stop=True),
                         r=['qT', 'keysT'], w=[PS(4 + g)])
                p.op('act', lambda e, g=g: e.activation(out=sc[:, g * 4:(g + 1) * 4, :],
                                                        in_=ps[4 + g][:, :].rearrange("p (i n) -> p i n", i=4), func=AF.Copy),
                     r=[PS(4 + g)], w=['sc'])
            SVK = [('sv', hp) for hp in range(16)]
            SIK = [('si', hp) for hp in range(16)]
            for hp in range(16):
                p.op('dve', lambda e, hp=hp: e.max(out=sv[:, hp, 0:8], in_=sc[:, hp, :]), r=['sc'], w=[('sv', hp)])
            for hp in range(16):
                p.op('dve', lambda e, hp=hp: e.max_index(out=si[:, hp, 0:8], in_max=sv[:, hp, 0:8], in_values=sc[:, hp, :]),
                     r=['sc', ('sv', hp)], w=[('si', hp)])
            for hp in range(16):
                p.op('dve', lambda e, hp=hp: e.match_replace(out=sc2[:, hp, :], in_to_replace=sv[:, hp, 0:8],
                                                             in_values=sc[:, hp, :], imm_value=-1e30),
                     r=['sc', ('sv', hp)], w=[('sc2', hp)])
            for hp in range(16):
                p.op('dve', lambda e, hp=hp: e.max(out=sv[:, hp, 8:16], in_=sc2[:, hp, :]), r=[('sc2', hp)], w=[('sv8', hp)])
            for hp in range(16):
                p.op('dve', lambda e, hp=hp: e.max_index(out=si[:, hp, 8:16], in_max=sv[:, hp, 8:16], in_values=sc2[:, hp, :]),
                     r=[('sc2', hp), ('sv8', hp)], w=[('si8', hp)])
            SVK = SVK + [('sv8', hp) for hp in range(16)]
            SIK = SIK + [('si8', hp) for hp in range(16)]
            p.op('dve', lambda e: e.tensor_copy(out=sif[:], in_=si[:]), r=SIK, w=['sif'])
            svv = sv[:].rearrange("p (h q) k -> p h q k", q=2)
            sfv = sif[:].rearrange("p (h q) k -> p h q k", q=2)
            p.op('dve', lambda e, svv=svv: e.tensor_tensor(
                out=cand[:], in0=svv[:, :, 0, :].unsqueeze(3).to_broadcast([128, 8, 16, 16]),
                in1=svv[:, :, 1, :].unsqueeze(2).to_broadcast([128, 8, 16, 16]), op=ALU.add), r=SVK, w=['cand'])
            p.op('dve', lambda e, sfv=sfv: e.tensor_scalar(out=sfv[:, :, 0, :], in0=sfv[:, :, 0, :], scalar1=128.0,
                                                           scalar2=None, op0=ALU.mult), r=['sif'], w=['sif'])
            chs = [cand[:, h].rearrange("p a b -> p (a b)") for h in range(8)]
            ch2s = [cand2[:, h].rearrange("p a b -> p (a b)") for h in range(8)]
            for h in range(8):
                p.op('dve', lambda e, h=h: e.max(out=best[:, h, 0:8], in_=chs[h]), r=['cand'], w=[('best', h)])
            for h in range(8):
                p.op('dve', lambda e, h=h: e.max_index(out=ci[:, h, 0:8], in_max=best[:, h, 0:8], in_values=chs[h]),
                     r=['cand', ('best', h)], w=[('ci', h)])
            for h in range(8):
                p.op('dve', lambda e, h=h: e.match_replace(out=ch2s[h], in_to_replace=best[:, h, 0:8],
                                                           in_values=chs[h], imm_value=-1e30),
                     r=['cand', ('best', h)], w=[('cand2', h)])
            for h in range(8):
                p.op('dve', lambda e, h=h: e.max(out=best[:, h, 8:16], in_=ch2s[h]), r=[('cand2', h)], w=[('best8', h)])
            for h in range(8):
                p.op('dve', lambda e, h=h: e.max_index(out=ci[:, h, 8:16], in_max=best[:, h, 8:16], in_values=ch2s[h]),
                     r=[('cand2', h), ('best8', h)], w=[('ci8', h)])
            BK = [('best', h) for h in range(8)] + [('best8', h) for h in range(8)]
            CIK = [('ci', h) for h in range(8)] + [('ci8', h) for h in range(8)]
            p.op('dve', lambda e: e.tensor_scalar(out=cu[:, 0], in0=ci[:], scalar1=4, scalar2=None,
                                                  op0=ALU.logical_shift_right), r=CIK, w=['cu'])
            p.op('dve', lambda e: e.tensor_scalar(out=cu[:, 1], in0=ci[:], scalar1=15, scalar2=None,
                                                  op0=ALU.bitwise_and), r=CIK, w=['cu'])
            p.op('dve', lambda e: e.tensor_copy(out=cf[:], in_=cu[:]), r=['cu'], w=['cf'])
            io16 = iota[:, 0:16].unsqueeze(1).unsqueeze(1).to_broadcast([128, 8, 16, 16])
            for q in range(2):
                p.op('dve', lambda e, q=q, io16=io16: e.tensor_tensor(
                    out=eq4[:], in0=io16, in1=cf[:, q].unsqueeze(3).to_broadcast([128, 8, 16, 16]), op=ALU.is_equal),
                    r=['iota', 'cf'], w=['eq4'])
                p.op('dve', lambda e, q=q, sfv=sfv: e.tensor_tensor(
                    out=eq4[:], in0=eq4[:], in1=sfv[:, :, q, :].unsqueeze(2).to_broadcast([128, 8, 16, 16]), op=ALU.mult),
                    r=['eq4', 'sif'], w=['eq4'])
                p.op('dve', lambda e, q=q: e.tensor_reduce(out=e12[:, q], in_=eq4[:], axis=AX.X, op=ALU.add),
                     r=['eq4'], w=['e12'])
            p.op('dve', lambda e: e.tensor_tensor(out=esel[:], in0=e12[:, 0].rearrange("p h k -> p (h k)"),
                                                  in1=e12[:, 1].rearrange("p h k -> p (h k)"), op=ALU.add),
                 r=['e12'], w=['esel'])
            p.op('dve', lambda e, t=t: e.tensor_copy(out=eu_all[:, t, :], in_=esel[:]), r=['esel'], w=[('eu', t)])
            gv = gate_all[:, t, :].rearrange("p (h k) -> p h k", h=8)
            p.op('dve', lambda e, gv=gv: e.tensor_tensor(out=gv, in0=best[:],
                                                         in1=best[:, :, 0:1].to_broadcast([128, 8, 16]), op=ALU.subtract),
                 r=BK, w=[('gate', t)])
            p.op('act', lambda e, t=t: e.activation(out=gate_all[:, t, :], in_=gate_all[:, t, :], func=AF.Exp),
                 r=[('gate', t)], w=[('gate', t)])
            p.op('dve', lambda e, gv=gv: e.tensor_reduce(out=g8[:], in_=gv, axis=AX.X, op=ALU.add), r=[('gate', t)], w=['g8'])
            p.op('dve', lambda e: e.reciprocal(out=g8[:], in_=g8[:]), r=['g8'], w=['g8'])
            p.op('dve', lambda e, gv=gv: e.tensor_tensor(out=gv, in0=gv, in1=g8[:].unsqueeze(2).to_broadcast([128, 8, 16]),
                                                         op=ALU.mult), r=[('gate', t), 'g8'], w=[('gate', t)])
        p.barrier()
        p.sb_reset(m1)
        h2 = [p.sb([128, D], F32, f"h2{i}") for i in range(2)]
        xt = [p.sb([128, D], F32, f"xp{i}") for i in range(2)]
        act = [p.sb([128, 128], F32, f"actv{i}") for i in range(2)]
        wg = [p.sb([128, 128], F32, f"wg{i}") for i in range(2)]
        NACC = 1
        acc = [[p.sb([128, D], F32, f"acc{i}{k}") for k in range(NACC)] for i in range(2)]
        junk = p.sb([128, D], BF16, "pjunk")
        NG = 32
        GS = 8
        gbuf = [p.sb([128, 2 * D], BF16, f"gb{i}") for i in range(NG)]
        NDG = 8
        dg = [p.sb([128, 128], BF16, f"dg{i}") for i in range(NDG)]
        gi = 0
        di = 0
        for t in tiles:
            b = t % 2
            s = 1 if t < 2 else 0
            rows = slice(t * 128, (t + 1) * 128)
            p.dma(lambda e, b=b, rows=rows: e.dma_start(out=h2[b][:], in_=S['h2'][rows, :]), w=[('h2', b)])
            p.dma(lambda e, b=b, rows=rows: e.dma_start(out=xt[b][:], in_=S['xs'][rows, :]), w=[('xp', b)])
            p.op('dve', lambda e, b=b: e.memset(act[b][:], 0.0), w=[('actv', b)])
            for g in range(128 // GS):
                ks = []
                for sidx in range(g * GS, (g + 1) * GS):
                    k = gi % NG
                    gi += 1
                    ks.append(k)
                    p.dma(lambda e, k=k, t=t, sidx=sidx: e.indirect_dma_start(
                        out=gbuf[k][:], out_offset=None, in_=S['T'][l],
                        in_offset=bass.IndirectOffsetOnAxis(ap=eu_all[:, t, sidx:sidx + 1], axis=0)),
                        r=[], w=[('gb', k)], eng="pool")
                    p.op('dve', lambda e, k=k, b=b, sidx=sidx: e.scalar_tensor_tensor(
                        out=junk[:], in0=gbuf[k][:, 0:D], scalar=1.0, in1=h2[b][:], op0=ALU.mult, op1=ALU.mult,
                        accum_out=act[b][:, sidx:sidx + 1]), r=[('gb', k), ('h2', b), ('actv', b)], w=[('actc', b, sidx)])
                gs = slice(g * GS, (g + 1) * GS)
                p.op('act', lambda e, b=b, gs=gs: e.activation(out=wg[b][:, gs], in_=act[b][:, gs], func=AF.Gelu),
                     r=[('actc', b, sidx) for sidx in range(g * GS, (g + 1) * GS)], w=[('wg', b, g)])
                p.op('dve', lambda e, b=b, gs=gs, t=t: e.tensor_tensor(out=wg[b][:, gs], in0=wg[b][:, gs],
                                                                      in1=gate_all[:, t, gs], op=ALU.mult),
                     r=[('wg', b, g)], w=[('wg', b, g)])
                for j, sidx in enumerate(range(g * GS, (g + 1) * GS)):
                    k = ks[j]
                    dj = di % NDG
                    di += 1
                    p.op('act', lambda e, dj=dj, b=b, sidx=sidx: e.activation(
                        out=dg[dj][:], in_=ident_f[:], func=AF.Copy, scale=wg[b][:, sidx:sidx + 1]),
                        r=[('wg', b, g), 'identf'], w=[('dg', dj)])
                    for half in range(2):
                        bank = 2 * b + half
                        p.op('pe', lambda e, dj=dj, k=k, half=half, bank=bank, sidx=sidx: e.matmul(
                            ps[bank][:, :], dg[dj][:], gbuf[k][:, D + half * 512:D + (half + 1) * 512],
                            start=(sidx == 0), stop=(sidx == 127)), r=[('dg', dj), ('gb', k)], w=[PS(bank), ('gbr', k, half)])
            a0 = acc[b][0]
            for half in range(2):
                hs_ = slice(half * 512, (half + 1) * 512)
                p.op('dve', lambda e, a0=a0, s=s, b=b, half=half, hs_=hs_: e.tensor_tensor(
                    out=a0[:, hs_], in0=ps[2 * b + half][:, :], in1=G2b[s][:, hs_], op=ALU.mult),
                    r=[PS(2 * b + half), ('G2b', s)], w=[('acc', b, 0)])
            p.op('dve', lambda e, a0=a0, b=b: e.tensor_tensor(out=xt[b][:], in0=xt[b][:], in1=a0[:], op=ALU.add),
                 r=[('acc', b, 0), ('xp', b)], w=[('xp', b)])
            if last:
                p.dma(lambda e, b=b, t=t: e.dma_start(out=out_d[(t - 2) * 128:(t - 1) * 128, :], in_=xt[b][:]),
                      r=[('xp', b)], w=[('outd', t)])
            else:
                p.dma(lambda e, b=b, rows=rows: e.dma_start(out=S['xs'][rows, :], in_=xt[b][:]),
                      r=[('xp', b)], w=[('Sxs', t)])
        p.barrier()

    PHASES = cfg.get("phases", ["proj", "rprep", "scan", "rout", "peer"])

    phase_mod()
    for l in range(cfg.get("layers", DEPTH)):
        if 'proj' in PHASES:
            qkT, Vaug, mp = phase_proj(l)
            phase_attn(l, qkT, Vaug, mp)
        if 'rprep' in PHASES:
            phase_rprep(l)
        if 'scan' in PHASES:
            phase_scan(l)
        if 'rout' in PHASES:
            phase_rout(l)
        if 'peer' in PHASES:
            phase_peer(l)
    p.barrier()
    p.emit()
    return nc


def prep_inputs(inputs):
    f = lambda a: np.ascontiguousarray(np.asarray(a, dtype=np.float32))
    x, c, ctx, c_ctx = f(inputs['x']), f(inputs['c']), f(inputs['ctx']), f(inputs['c_ctx'])
    shared = {}
    for n in ['norm_mix', 'norm_ffn', 'w_mod', 'b_mod', 'w_in', 'w_out', 'a_qnorm', 'a_knorm', 'b_qnorm', 'b_knorm',
              'a_sink']:
        shared[n] = f(inputs[n])
    rpb = f(inputs['b_rpb'])
    btab = np.zeros((DEPTH, 128, NTAB, 4, 128), np.float32)
    bmask = np.zeros((128, NTAB, 128), np.float32)
    for i, (dr, dc, valid) in enumerate(NA_TABS):
        g = rpb[:, :, dr, dc]
        btab[:, :, i, :, :] = np.where(valid[None, None], g, 0.0).transpose(0, 2, 1, 3)
        bmask[:, i, :] = valid
    shared['btab'] = btab
    shared['bmask'] = bmask
    ar = np.arange(128)
    am = np.zeros((128, 2, 128), np.float32)
    am[:, 0, :] = (ar[:, None] >= ar[None, :])
    am[:, 1, :] = (ar[:, None] <= ar[None, :])
    shared['amask'] = am
    shared['ident'] = np.eye(128, dtype=np.float32)
    cos, sin = rope_tables()
    shared['cos'], shared['sin'] = cos, sin
    rc = f(inputs['r7_conv'])
    for n in ['r7_w0', 'r7_a0', 'r7_w2', 'r7_a2', 'r7_g2', 'r7_kk', 'r7_ka', 'r7_lnw', 'r7_lnb', 'r7_rk', 'peer_wq', 'peer_keys']:
        shared[n] = f(inputs[n])
    for l in range(DEPTH):
        shared[f'peer_u{l}'] = f(inputs['peer_u'][l])
        shared[f'peer_v{l}'] = f(inputs['peer_v'][l])
    a64 = np.arange(64)
    tri = np.zeros((64, 2, 64), np.float32)
    tri[:, 0, :] = a64[:, None] <= a64[None, :]
    tri[:, 1, :] = a64[:, None] >= a64[None, :]
    mg = np.zeros((64, 2, 128), np.float32)
    mg[:, 0, 0:64] = a64[:, None] < a64[None, :]
    mg[:, 0, 64:128] = a64[:, None] <= a64[None, :]
    mg[:, 1, 0:64] = a64[:, None] > a64[None, :]
    mg[:, 1, 64:128] = a64[:, None] >= a64[None, :]
    mn = np.zeros((64, 2, 64), np.float32)
    mn[:, 0, :] = a64[None, :] < a64[:, None]
    mn[:, 1, :] = a64[None, :] > a64[:, None]
    shared['tri'], shared['mg'], shared['mn'] = tri, mg, mn
    shared['iota'] = np.ascontiguousarray(np.broadcast_to(np.arange(256, dtype=np.float32), (128, 256)))
    shared['r7_conv'] = np.ascontiguousarray(rc.reshape(DEPTH, 3, 15, 128).transpose(0, 3, 2, 1))
    maps = []
    for b in range(8):
        m = dict(shared)
        m['x'] = np.ascontiguousarray(np.concatenate([ctx[b], x[b]], axis=0))
        cc = np.stack([c[b], c_ctx], axis=-1)
        m['cc'] = np.ascontiguousarray(cc.reshape(8, 128, 2).transpose(1, 0, 2))
        maps.append(m)
    return maps


_NC_CACHE = {}


def kernel(**inputs):
    if 'nc' not in _NC_CACHE:
        _NC_CACHE['nc'] = build({})
    nc = _NC_CACHE['nc']
    maps = prep_inputs(inputs)
    res = run_bass_kernel_spmd(nc, maps, core_ids=list(range(8)))
    return np.stack([np.asarray(r['out'], dtype=np.float32) for r in res.results], axis=0)
```

```python
import numpy as np
import ml_dtypes
import concourse.bass as bass
import concourse.mybir as mybir
from concourse.bass_utils import run_bass_kernel_spmd

F32 = mybir.dt.float32
BF16 = mybir.dt.bfloat16
U32 = mybir.dt.uint32
I32 = mybir.dt.int32
AF = mybir.ActivationFunctionType
ALU = mybir.AluOpType
AX = mybir.AxisListType

ENGS = ["pe", "act", "dve", "pool", "sp"]
DT_SIZE = {F32: 4, BF16: 2, U32: 4, I32: 4}

D = 1024
NCTX = 256
NLAT = 2048
NTOK = NCTX + NLAT
NT = NTOK // 128
DEPTH = 2
EPS = 1e-6


class Prog:
    def __init__(self, nc, n_dma_sems=32):
        self.nc = nc
        self.ops = {e: [] for e in ENGS}
        self.cnt = {e: 0 for e in ENGS}
        self.waited = {e: {} for e in ENGS}
        self.res = {}
        self.n_dma_sems = n_dma_sems
        self.dma_use = [0] * n_dma_sems
        self.dma_last = [None] * n_dma_sems
        self.dma_rr = 0
        self.sb_off = 16 * 1024
        self.sb_id = 0
        self.SB_CAP = 216 * 1024

    def sb_mark(self):
        return self.sb_off

    def sb_reset(self, off=0):
        self.sb_off = off

    def sb(self, shape, dtype, name=""):
        nbytes = int(np.prod(shape[1:])) * DT_SIZE[dtype]
        off = (self.sb_off + 63) // 64 * 64
        assert off + nbytes <= self.SB_CAP, f"SBUF overflow {off}+{nbytes} ({name})"
        self.sb_off = off + nbytes
        self.sb_id += 1
        return self.nc.alloc_sbuf_tensor_at(f"sb{self.sb_id}_{name}", list(shape), dtype, offset=off)

    def _deps(self, r, w):
        deps = []
        for k in r:
            st = self.res.get(k)
            if st and st[0] is not None:
                deps.append(st[0])
        for k in w:
            st = self.res.get(k)
            if st:
                if st[0] is not None:
                    deps.append(st[0])
                deps.extend(st[1])
        return deps

    def _commit(self, tok, r, w):
        for k in r:
            st = self.res.setdefault(k, [None, []])
            st[1].append(tok)
        for k in w:
            self.res[k] = [tok, []]

    def _waits_for(self, eng, deps):
        wd = self.waited[eng]
        best = {}
        for t in deps:
            if t[0] == 'c':
                if t[1] == eng and eng == 'pe':
                    continue
                key = ('c', t[1])
            else:
                key = ('d', t[1])
            if wd.get(key, 0) >= t[2]:
                continue
            best[key] = max(best.get(key, 0), t[2])
        for k, v in best.items():
            wd[k] = v
        return list(best.items())

    def op(self, eng, fn, r=(), w=()):
        deps = self._deps(r, w)
        waits = self._waits_for(eng, deps)
        self.cnt[eng] += 1
        tok = ('c', eng, self.cnt[eng])
        self.ops[eng].append((waits, fn, ('c', eng), 1))
        self._commit(tok, r, w)
        return tok

    def dma(self, fn, r=(), w=(), eng="sp"):
        deps = list(self._deps(r, w))
        i = self.dma_rr
        self.dma_rr = (self.dma_rr + 1) % self.n_dma_sems
        if self.dma_last[i] is not None:
            deps.append(self.dma_last[i])
        waits = self._waits_for(eng, deps)
        self.dma_use[i] += 1
        tok = ('d', i, 16 * self.dma_use[i])
        self.dma_last[i] = tok
        self.ops[eng].append((waits, fn, ('d', i), 16))
        self._commit(tok, r, w)
        return tok

    def barrier(self):
        toks = [('c', e, self.cnt[e]) for e in ENGS if self.cnt[e] > 0]
        toks += [t for t in self.dma_last if t is not None]
        for e in ENGS:
            waits = self._waits_for(e, toks)
            if waits:
                self.ops[e].append((waits, None, None, 0))
        self.res = {}

    def emit(self):
        nc = self.nc
        from contextlib import ExitStack
        with ExitStack() as es:
            csem = {e: es.enter_context(nc.semaphore(f"c_{e}")) for e in ENGS}
            dsem = [es.enter_context(nc.semaphore(f"d_{i}")) for i in range(self.n_dma_sems)]
            block = es.enter_context(nc.Block())

            def sem_of(key):
                return csem[key[1]] if key[0] == 'c' else dsem[key[1]]

            def run(engname, e):
                for waits, fn, inc_key, inc in self.ops[engname]:
                    for k, v in waits:
                        e.wait_ge(sem_of(k), v)
                    if fn is None:
                        continue
                    ins = fn(e)
                    ins.then_inc(sem_of(inc_key), inc)

            @block.tensor
            def _(e):
                run("pe", e)

            @block.scalar
            def _(e):
                run("act", e)

            @block.vector
            def _(e):
                run("dve", e)

            @block.gpsimd
            def _(e):
                run("pool", e)

            @block.sync
            def _(e):
                run("sp", e)


def na_tables():
    cases = {}
    tabs = []
    keys = {}
    ar = np.arange(128)
    for p in range(16):
        for kb in range(16):
            krow = 2 * kb + ar // 64
            kcol = ar % 64
            qrow = 2 * p + ar // 64
            qcol = ar % 64
            rs = np.clip(qrow - 4, 0, 24)
            vr = (krow[:, None] >= rs[None, :]) & (krow[:, None] < rs[None, :] + 8)
            ws = np.clip(qcol - 8, 0, 48)
            vc = (kcol[:, None] >= ws[None, :]) & (kcol[:, None] < ws[None, :] + 16)
            valid = vr & vc
            if not valid.any():
                continue
            dr = krow[:, None] - qrow[None, :] + 7
            dc = np.clip(kcol[:, None] - qcol[None, :] + 15, 0, 30)
            dr = np.where(valid, dr, 0)
            dc = np.where(valid, dc, 0)
            key = (dr.tobytes(), dc.tobytes(), valid.tobytes())
            if key not in keys:
                keys[key] = len(tabs)
                tabs.append((dr, dc, valid))
            cases[(p, kb)] = keys[key]
    return cases, tabs


NA_CASES, NA_TABS = na_tables()
NTAB = len(NA_TABS)


def rope_tables():
    t = np.arange(NLAT)
    inv_freq = 10000.0 ** (-np.arange(0, 32, 2) / 32)
    ang = np.stack([(t // 64)[:, None] * inv_freq[None], (t % 64)[:, None] * inv_freq[None]], axis=1)
    return np.cos(ang).astype(np.float32).reshape(NLAT, 32), np.sin(ang).astype(np.float32).reshape(NLAT, 32)


def build(cfg=None):
    cfg = cfg or {}
    dbg = cfg.get("dbg", [])
    nc = bass.Bass("TRN2", target_bir_lowering=False)
    p = Prog(nc)

    def din(name, shape, dt=F32):
        return nc.dram_tensor(name, list(shape), dt, kind="ExternalInput").ap()

    def dscr(name, shape, dt=F32):
        kind = "Internal"
        if name in cfg.get("dump", []):
            kind = "ExternalOutput"
        if name in cfg.get("feed", []):
            kind = "ExternalInput"
        return nc.dram_tensor(name, list(shape), dt, kind=kind).ap()

    I = {}
    I['x'] = din('x', [NTOK, D])
    I['cc'] = din('cc', [128, 8, 2])
    I['norm_mix'] = din('norm_mix', [DEPTH, D])
    I['norm_ffn'] = din('norm_ffn', [DEPTH, D])
    I['w_mod'] = din('w_mod', [DEPTH, D, 6 * D])
    I['b_mod'] = din('b_mod', [DEPTH, 6 * D])
    I['w_in'] = din('w_in', [DEPTH, D, 3200])
    I['w_out'] = din('w_out', [DEPTH, D, D])
    for n in ['a_qnorm', 'a_knorm', 'b_qnorm', 'b_knorm']:
        I[n] = din(n, [DEPTH, 64])
    I['a_sink'] = din('a_sink', [DEPTH, 4])
    I['btab'] = din('btab', [DEPTH, 128, NTAB, 4, 128])
    I['bmask'] = din('bmask', [128, NTAB, 128])
    I['amask'] = din('amask', [128, 2, 128])
    I['ident'] = din('ident', [128, 128])
    I['cos'] = din('cos', [NLAT, 32])
    I['sin'] = din('sin', [NLAT, 32])
    I['r7_conv'] = din('r7_conv', [DEPTH, 128, 15, 3])
    I['r7_w0'] = din('r7_w0', [DEPTH, 2, 512])
    I['r7_a0'] = din('r7_a0', [DEPTH, 2, 512])
    I['r7_w2'] = din('r7_w2', [DEPTH, 2, 64, 512])
    I['r7_a2'] = din('r7_a2', [DEPTH, 2, 64, 512])
    I['r7_g2'] = din('r7_g2', [DEPTH, 128, 512])
    for n in ['r7_kk', 'r7_ka', 'r7_lnw', 'r7_lnb']:
        I[n] = din(n, [DEPTH, 512])
    I['r7_rk'] = din('r7_rk', [DEPTH, 8, 64])
    I['peer_wq'] = din('peer_wq', [DEPTH, D, 2048])
    I['peer_keys'] = din('peer_keys', [DEPTH, 8, 2, 128, 128])
    I['peer_u'] = [din(f'peer_u{l}', [16384, D]) for l in range(DEPTH)]
    I['peer_v'] = [din(f'peer_v{l}', [16384, D]) for l in range(DEPTH)]
    I['iota'] = din('iota', [128, 256])
    I['tri'] = din('tri', [64, 2, 64])
    I['mg'] = din('mg', [64, 2, 128])
    I['mn'] = din('mn', [64, 2, 64])
    out_d = nc.dram_tensor('out', [NLAT, D], F32, kind="ExternalOutput").ap()

    S = {}
    S['mod'] = dscr('s_mod', [DEPTH, 2, 6 * D])
    S['xs'] = dscr('s_xs', [NTOK, D])
    S['o'] = dscr('s_o', [NTOK, D])
    S['pcT'] = dscr('s_pcT', [1920, NTOK])
    S['tm'] = dscr('s_tm', [NTOK, 10, 512])
    S['bon'] = dscr('s_bon', [NTOK, 8])
    S['y'] = dscr('s_y', [2, NTOK, 512])
    S['h2'] = dscr('s_h2', [NTOK, D])
    S['T'] = [dscr(f's_T{l}', [16384, 2 * D], BF16) for l in range(DEPTH)]
    DBG = {}
    for name, shape in cfg.get("dbg_out", {}).items():
        DBG[name] = nc.dram_tensor(name, list(shape), F32, kind="ExternalOutput").ap()

    ps = [nc.alloc_psum_tensor(f"ps{i}", [128, 512], F32) for i in range(8)]

    def PS(i):
        return ('ps', i)

    ident_f = p.sb([128, 128], F32, "identf")
    ident_b = p.sb([128, 128], BF16, "identb")
    eps_col = p.sb([128, 1], F32, "eps")
    p.dma(lambda e: e.dma_start(out=ident_f[:], in_=I['ident']), w=['identf'])
    p.op('dve', lambda e: e.tensor_copy(out=ident_b[:], in_=ident_f[:]), r=['identf'], w=['identb'])
    p.op('dve', lambda e: e.memset(eps_col[:], EPS), w=['eps'])
    p.barrier()
    base_mark = p.sb_mark()

    def phase_mod():
        p.sb_reset(base_mark)
        cc = p.sb([128, 8, 2], F32, "cc")
        scc = p.sb([128, 8, 2], F32, "scc")
        p.dma(lambda e: e.dma_start(out=cc[:], in_=I['cc']), w=['cc'])
        p.op('act', lambda e: e.activation(out=scc[:], in_=cc[:], func=AF.Silu), r=['cc'], w=['scc'])
        wt = [p.sb([128, 8, 512], F32, f"wmod{i}") for i in range(2)]
        bm = p.sb([2, 6 * D], F32, "bm")
        mo = p.sb([2, 6 * D], F32, "mo")
        k = 0
        for l in range(DEPTH):
            p.dma(lambda e, l=l: e.dma_start(out=bm[:], in_=I['b_mod'][l].partition_broadcast(2)),
                  w=['bm'])
            for cch in range(12):
                b = k % 2
                k += 1
                src = I['w_mod'][l, :, cch * 512:(cch + 1) * 512].rearrange("(j p) n -> p j n", p=128)
                p.dma(lambda e, b=b, src=src: e.dma_start(out=wt[b][:], in_=src), w=[('wmod', b)])
                pb = cch % 2
                for j in range(8):
                    p.op('pe', lambda e, b=b, j=j, pb=pb: e.matmul(ps[pb][0:2, :], scc[:, j, :], wt[b][:, j, :],
                                                                    start=(j == 0), stop=(j == 7)),
                         r=['scc', ('wmod', b)], w=[PS(pb)])
                p.op('dve', lambda e, pb=pb, cch=cch: e.tensor_tensor(
                    out=mo[:, cch * 512:(cch + 1) * 512], in0=ps[pb][0:2, :], in1=bm[:, cch * 512:(cch + 1) * 512],
                    op=ALU.add), r=[PS(pb), 'bm'], w=['mo'])
            p.dma(lambda e, l=l: e.dma_start(out=S['mod'][l], in_=mo[:]), r=['mo'], w=['S_mod'])
        p.barrier()

    def load_bc(dst, src_1d, key):
        P = dst.shape[0]
        p.dma(lambda e: e.dma_start(out=dst, in_=src_1d.partition_broadcast(P)), w=[key])

    def norm_tiles(l, which, src, hT, hT_off, tm_dram=None):
        nv = I['norm_mix'] if which == 0 else I['norm_ffn']
        so = 0 if which == 0 else 3
        G = [p.sb([128, D], F32, f"G{s}") for s in range(2)]
        SH = [p.sb([128, D], F32, f"SH{s}") for s in range(2)]
        tmp = p.sb([128, D], F32, "gtmp")
        for s in range(2):
            load_bc(tmp[:], nv[l], 'gtmp')
            load_bc(G[s][:], S['mod'][l, s, (so + 1) * D:(so + 2) * D], ('G', s))
            load_bc(SH[s][:], S['mod'][l, s, so * D:(so + 1) * D], ('SH', s))
            p.op('dve', lambda e, s=s: e.scalar_tensor_tensor(out=G[s][:], in0=G[s][:], scalar=1.0, in1=tmp[:],
                                                             op0=ALU.add, op1=ALU.mult),
                 r=['gtmp', ('G', s)], w=[('G', s)])
        NBUF = 4
        xt = [p.sb([128, D], F32, f"xt{i}") for i in range(NBUF)]
        junk = p.sb([128, D], F32, "junk")
        hb = [p.sb([128, D], BF16, f"hb{i}") for i in range(NBUF)]
        ss = [p.sb([128, 1], F32, f"ss{i}") for i in range(NBUF)]
        def stageA(t):
            b = t % NBUF
            s = 1 if t < 2 else 0
            yield
            p.dma(lambda e, b=b, t=t: e.dma_start(out=xt[b][:], in_=src[t * 128:(t + 1) * 128, :]), w=[('xt', b)])
            yield
            p.op('act', lambda e, b=b: e.activation(out=junk[:], in_=xt[b][:], func=AF.Square, accum_out=ss[b][:]),
                 r=[('xt', b)], w=[('ss', b)])
            yield
            p.op('act', lambda e, b=b: e.activation(out=ss[b][:], in_=ss[b][:], func=AF.Sqrt, bias=eps_col[:],
                                                    scale=1.0 / D), r=[('ss', b)], w=[('ss', b)])
            yield
            p.op('dve', lambda e, b=b: e.reciprocal(out=ss[b][:], in_=ss[b][:]), r=[('ss', b)], w=[('ss', b)])
            yield
            p.op('dve', lambda e, b=b, s=s: e.scalar_tensor_tensor(out=xt[b][:], in0=xt[b][:], scalar=ss[b][:, 0:1],
                                                                 in1=G[s][:], op0=ALU.mult, op1=ALU.mult),
                 r=[('xt', b), ('ss', b), ('G', s)], w=[('xt', b)])
            yield
            if tm_dram is not None:
                p.op('dve', lambda e, b=b, s=s: e.tensor_tensor(out=xt[b][:], in0=xt[b][:], in1=SH[s][:], op=ALU.add),
                     r=[('xt', b), ('SH', s)], w=[('xt', b)])
                p.dma(lambda e, b=b, t=t: e.dma_start(out=tm_dram[t * 128:(t + 1) * 128, :], in_=xt[b][:]),
                      r=[('xt', b)], w=[('tmd', t)])
                p.op('act', lambda e, b=b: e.activation(out=hb[b][:], in_=xt[b][:], func=AF.Copy),
                     r=[('xt', b)], w=[('hb', b)])
            else:
                p.op('dve', lambda e, b=b, s=s: e.tensor_tensor(out=hb[b][:], in0=xt[b][:], in1=SH[s][:], op=ALU.add),
                     r=[('xt', b), ('SH', s)], w=[('hb', b)])

        def stageB(t):
            b = t % NBUF
            pbank = 4 + b
            pv = ps[pbank][:, 0:512].bitcast(BF16)
            yield
            for j in range(8):
                p.op('pe', lambda e, b=b, j=j, pv=pv: e.transpose(out=pv[:, j * 128:(j + 1) * 128],
                                                                 in_=hb[b][:, j * 128:(j + 1) * 128],
                                                                 identity=ident_b[:]),
                     r=[('hb', b), 'identb'], w=[PS(pbank)])
            o = hT_off(t)
            yield
            p.op('act', lambda e, pv=pv, o=o: e.activation(
                out=hT[:, :, o:o + 128], in_=pv.rearrange("p (j t) -> p j t", j=8), func=AF.Copy),
                r=[PS(pbank)], w=[('hT', t)])


        tl_ = [t for t in range(NT) if not (t < 2 and l == DEPTH - 1 and which == 1)]
        def rr(gens):
            gens = list(gens)
            while gens:
                for g_ in list(gens):
                    try:
                        next(g_)
                    except StopIteration:
                        gens.remove(g_)

        groups = [tl_[i:i + NBUF] for i in range(0, len(tl_), NBUF)]
        for gi_ in range(len(groups) + 1):
            gl = []
            if gi_ < len(groups):
                gl += [stageA(t) for t in groups[gi_]]
            if gi_ > 0:
                gl += [stageB(t) for t in groups[gi_ - 1]]
            rr(gl)

    def phase_proj(l):
        p.sb_reset(base_mark)
        qkT = p.sb([64, 14, NTOK], BF16, "qkT")
        Vaug = p.sb([128, NT, 6, 65], BF16, "Vaug")
        mark_persist = p.sb_mark()
        hT = p.sb([128, 8, NTOK], BF16, "hT")
        m_afterh = p.sb_mark()
        wAB = p.sb([128, 8, 1280], BF16, "wAB")
        for j in range(8):
            p.dma(lambda e, j=j: e.dma_start(out=wAB[:, j, :], in_=I['w_in'][l, j * 128:(j + 1) * 128, 0:1280]),
                  w=[('wAB', j)], eng="pool")
        p.op('pool', lambda e: e.memset(Vaug[:, :, :, 64:65], 1.0), w=['Vones'])
        m0 = p.sb_mark()
        norm_tiles(l, 0, I['x'] if l == 0 else S['xs'], hT, lambda t: t * 128)
        p.barrier()
        p.sb_reset(m0)
        wC = p.sb([128, 8, 1920], BF16, "wC")
        for j in range(8):
            p.dma(lambda e, j=j: e.dma_start(out=wC[:, j, :], in_=I['w_in'][l, j * 128:(j + 1) * 128, 1280:3200]),
                  w=[('wC', j)], eng="pool")
        m_afterwc = p.sb_mark()
        GA = p.sb([128, 6, 64], F32, "GA")
        GB = p.sb([128, 8, 64], F32, "GB")
        for h in range(6):
            load_bc(GA[:, h, :], I['a_qnorm'][l] if h < 4 else I['a_knorm'][l], 'GA')
        for h in range(8):
            load_bc(GB[:, h, :], I['b_qnorm'][l] if h < 4 else I['b_knorm'][l], 'GB')
        p.op('act', lambda e: e.mul(out=GA[:, 0:4, :], in_=GA[:, 0:4, :], mul=0.125), r=['GA'], w=['GA'])
        p.op('act', lambda e: e.mul(out=GB[:, 0:4, :], in_=GB[:, 0:4, :], mul=0.125), r=['GB'], w=['GB'])
        cs = [p.sb([128, 2, 32], F32, f"cs{i}") for i in range(2)]
        xn = [p.sb([128, 14, 64], F32, f"xn{i}") for i in range(2)]
        sq = [p.sb([128, 14, 64], F32, f"sq{i}") for i in range(2)]
        ssq = [p.sb([128, 14], F32, f"ssq{i}") for i in range(2)]
        xr = [p.sb([128, 14, 64], BF16, f"xr{i}") for i in range(2)]
        RT = [[p.sb([128, 6, 2, 16], F32, f"ropeT{b_}{i}") for i in range(4)] for b_ in range(2)]
        def qk_iter(t):
            b = t % 2
            lat = t >= 2
            bA, bB, bV = (0, 1, 2) if t % 2 == 0 else (3, 6, 7)
            yield
            for bank, c0, c1 in ((bA, 0, 512), (bB, 512, 1024), (bV, 1024, 1280)):
                for j in range(8):
                    p.op('pe', lambda e, bank=bank, c0=c0, c1=c1, j=j, t=t: e.matmul(
                        ps[bank][:, 0:c1 - c0], hT[:, j, t * 128:(t + 1) * 128], wAB[:, j, c0:c1],
                        start=(j == 0), stop=(j == 7)),
                        r=[('hT', t), ('wAB', j)], w=[PS(bank)])
            yield
            if lat:
                tl = t - 2
                p.dma(lambda e, b=b, tl=tl: e.dma_start(out=cs[b][:, 0, :], in_=I['cos'][tl * 128:(tl + 1) * 128, :]),
                      w=[('cs', b)])
                p.dma(lambda e, b=b, tl=tl: e.dma_start(out=cs[b][:, 1, :], in_=I['sin'][tl * 128:(tl + 1) * 128, :]),
                      w=[('cs', b)])
            yield
            p.op('act', lambda e, t=t, bA=bA: e.activation(out=Vaug[:, t, 0:2, 0:64],
                                                    in_=ps[bA][:, 384:512].rearrange("p (h d) -> p h d", h=2),
                                                    func=AF.Copy), r=[PS(bA)], w=[('V', t)])
            yield
            p.op('act', lambda e, t=t, bV=bV: e.activation(out=Vaug[:, t, 2:6, 0:64],
                                                    in_=ps[bV][:, 0:256].rearrange("p (h d) -> p h d", h=4),
                                                    func=AF.Copy), r=[PS(bV)], w=[('V', t)])
            yield
            p.op('act', lambda e, b=b, bA=bA: e.activation(out=xn[b][:, 0:6, :],
                                                    in_=ps[bA][:, 0:384].rearrange("p (h d) -> p h d", h=6),
                                                    func=AF.Copy), r=[PS(bA)], w=[('xn', b)])
            yield
            p.op('act', lambda e, b=b, bB=bB: e.activation(out=xn[b][:, 6:14, :],
                                                    in_=ps[bB][:, 0:512].rearrange("p (h d) -> p h d", h=8),
                                                    func=AF.Copy), r=[PS(bB)], w=[('xn', b)])
            yield
            p.op('dve', lambda e, b=b: e.tensor_tensor(out=sq[b][:], in0=xn[b][:], in1=xn[b][:], op=ALU.mult),
                 r=[('xn', b)], w=[('sq', b)])
            yield
            p.op('dve', lambda e, b=b: e.tensor_reduce(out=ssq[b][:], in_=sq[b][:], axis=AX.X, op=ALU.add),
                 r=[('sq', b)], w=[('ssq', b)])
            yield
            p.op('act', lambda e, b=b: e.activation(out=ssq[b][:], in_=ssq[b][:], func=AF.Sqrt, bias=eps_col[:],
                                                    scale=1.0 / 64), r=[('ssq', b)], w=[('ssq', b)])
            yield
            p.op('dve', lambda e, b=b: e.reciprocal(out=ssq[b][:], in_=ssq[b][:]), r=[('ssq', b)], w=[('ssq', b)])
            yield
            p.op('dve', lambda e, b=b: e.tensor_tensor(out=xn[b][:], in0=xn[b][:],
                                                       in1=ssq[b][:].unsqueeze(2).to_broadcast([128, 14, 64]),
                                                       op=ALU.mult), r=[('xn', b), ('ssq', b)], w=[('xn', b)])
            yield
            p.op('dve', lambda e, b=b: e.tensor_tensor(out=xr[b][:, 6:14, :], in0=xn[b][:, 6:14, :], in1=GB[:],
                                                       op=ALU.mult), r=[('xn', b), 'GB'], w=[('xr', b)])
            yield
            if lat:
                p.op('dve', lambda e, b=b: e.tensor_tensor(out=xn[b][:, 0:6, :], in0=xn[b][:, 0:6, :], in1=GA[:],
                                                           op=ALU.mult), r=[('xn', b), 'GA'], w=[('xn', b)])
                xv = xn[b][:, 0:6, :].rearrange("p h (a g f) -> p h a g f", a=2, g=2)
                x1 = xv[:, :, :, 0, :]
                x2 = xv[:, :, :, 1, :]
                ov = xr[b][:, 0:6, :].rearrange("p h (a g f) -> p h a g f", a=2, g=2)
                cosb = cs[b][:, 0, :].rearrange("p (a f) -> p a f", a=2).unsqueeze(1).to_broadcast([128, 6, 2, 16])
                sinb = cs[b][:, 1, :].rearrange("p (a f) -> p a f", a=2).unsqueeze(1).to_broadcast([128, 6, 2, 16])
                rk = [('xn', b), ('cs', b)]
                for i, (xa, tb) in enumerate(((x1, cosb), (x2, sinb), (x2, cosb), (x1, sinb))):
                    p.op('dve', lambda e, i=i, xa=xa, tb=tb, b=b: e.tensor_tensor(out=RT[b][i][:], in0=xa, in1=tb, op=ALU.mult),
                         r=rk, w=[('RT', b, i)])
                p.op('dve', lambda e, ov=ov, b=b: e.tensor_tensor(out=ov[:, :, :, 0, :], in0=RT[b][0][:], in1=RT[b][1][:],
                                                             op=ALU.subtract), r=[('RT', b, 0), ('RT', b, 1)], w=[('xr', b)])
                p.op('dve', lambda e, ov=ov, b=b: e.tensor_tensor(out=ov[:, :, :, 1, :], in0=RT[b][2][:], in1=RT[b][3][:],
                                                             op=ALU.add), r=[('RT', b, 2), ('RT', b, 3)], w=[('xr', b)])
            else:
                p.op('dve', lambda e, b=b: e.tensor_tensor(out=xr[b][:, 0:6, :], in0=xn[b][:, 0:6, :], in1=GA[:],
                                                           op=ALU.mult), r=[('xn', b), 'GA'], w=[('xr', b)])
            yield
            for half in range(2):
                bank = 4 + half
                pv = ps[bank][0:64, 0:448].bitcast(BF16)
                for hh in range(7):
                    h = half * 7 + hh
                    p.op('pe', lambda e, b=b, h=h, hh=hh, pv=pv: e.transpose(
                        out=pv[:, hh * 128:(hh + 1) * 128], in_=xr[b][:, h, :], identity=ident_b[:]),
                        r=[('xr', b), 'identb'], w=[PS(bank)])
                p.op('act', lambda e, half=half, pv=pv, t=t: e.activation(
                    out=qkT[:, half * 7:(half + 1) * 7, t * 128:(t + 1) * 128],
                    in_=pv.rearrange("p (h t) -> p h t", h=7), func=AF.Copy), r=[PS(bank)], w=[('qkT', t)])
        def rr3(gens):
            gens = list(gens)
            while gens:
                for g_ in list(gens):
                    try:
                        next(g_)
                    except StopIteration:
                        gens.remove(g_)

        for t in range(0, NT, 2):
            rr3([qk_iter(t), qk_iter(t + 1)])
        p.barrier()
        p.sb_reset(m_afterwc)
        cw = p.sb([128, 15, 3], F32, "cw")
        p.dma(lambda e: e.dma_start(out=cw[:], in_=I['r7_conv'][l]), w=['cw'])
        rawc = [p.sb([128, NCTX + 2], F32, f"rawc{i}") for i in range(2)]
        rawl = [p.sb([128, NLAT + 2], F32, f"rawl{i}") for i in range(2)]
        cvo = [p.sb([128, NTOK], F32, f"cvo{i}") for i in range(2)]
        for i in range(2):
            p.op('pool', lambda e, i=i: e.memset(rawc[i][:], 0.0), w=[('rawc', i)])
            p.op('pool', lambda e, i=i: e.memset(rawl[i][:], 0.0), w=[('rawl', i)])
        def cproj_iter(ch):
            b = ch % 2
            groups = [(rawc[b], ('rawc', b), 1, 0, 256)] + [(rawl[b], ('rawl', b), 1 + 512 * g, 256 + 512 * g, 512)
                                                             for g in range(4)]
            yield
            for gidx_, (raw, rkey, ro, tok0, n) in enumerate(groups):
                bank = 2 * (ch % 2) + gidx_ % 2
                for j in range(8):
                    p.op('pe', lambda e, bank=bank, j=j, ch=ch, tok0=tok0, n=n: e.matmul(
                        ps[bank][:, 0:n], wC[:, j, ch * 128:(ch + 1) * 128], hT[:, j, tok0:tok0 + n],
                        start=(j == 0), stop=(j == 7)), r=[('wC', j)], w=[PS(bank)])
                p.op('act', lambda e, raw=raw, ro=ro, n=n, bank=bank: e.activation(
                    out=raw[:, ro:ro + n], in_=ps[bank][:, 0:n], func=AF.Copy), r=[PS(bank)], w=[rkey])
            yield
            for (raw, rkey, n, o0) in ((rawc[b], ('rawc', b), NCTX, 0), (rawl[b], ('rawl', b), NLAT, NCTX)):
                dst = cvo[b][:, o0:o0 + n]
                p.op('dve', lambda e, raw=raw, n=n, dst=dst, ch=ch: e.tensor_scalar(
                    out=dst, in0=raw[:, 1:1 + n], scalar1=cw[:, ch, 1:2], scalar2=None, op0=ALU.mult),
                    r=[rkey, 'cw'], w=[('cvo', b)])
                p.op('dve', lambda e, raw=raw, n=n, dst=dst, ch=ch: e.scalar_tensor_tensor(
                    out=dst, in0=raw[:, 0:n], scalar=cw[:, ch, 0:1], in1=dst, op0=ALU.mult, op1=ALU.add),
                    r=[rkey, 'cw'], w=[('cvo', b)])
                p.op('dve', lambda e, raw=raw, n=n, dst=dst, ch=ch: e.scalar_tensor_tensor(
                    out=dst, in0=raw[:, 2:2 + n], scalar=cw[:, ch, 2:3], in1=dst, op0=ALU.mult, op1=ALU.add),
                    r=[rkey, 'cw'], w=[('cvo', b)])
            yield
            if ch == 12:
                p.op('act', lambda e, b=b: e.activation(out=cvo[b][:], in_=cvo[b][:], func=AF.Tanh),
                     r=[('cvo', b)], w=[('cvo', b)])
            yield
            if ch == 14:
                p.op('act', lambda e, b=b: e.activation(out=cvo[b][:], in_=cvo[b][:], func=AF.Sigmoid),
                     r=[('cvo', b)], w=[('cvo', b)])
            yield
            p.dma(lambda e, b=b, ch=ch: e.dma_start(out=S['pcT'][ch * 128:(ch + 1) * 128, :], in_=cvo[b][:]),
                  r=[('cvo', b)], w=[('pcT', ch)])
        def rr4(gens):
            gens = list(gens)
            while gens:
                for g_ in list(gens):
                    try:
                        next(g_)
                    except StopIteration:
                        gens.remove(g_)

        for ch0 in range(0, 15, 2):
            rr4([cproj_iter(ch) for ch in range(ch0, min(ch0 + 2, 15))])
        p.barrier()
        return qkT, Vaug, mark_persist

    def phase_attn(l, qkT, Vaug, mark_persist):
        with_ctx = l < DEPTH - 1
        p.sb_reset(mark_persist)
        btab = p.sb([128, NTAB, 4, 128], F32, "btab")
        bmask = p.sb([128, NTAB, 128], F32, "bmask")
        amask = p.sb([128, 2, 128], F32, "amask")
        esink = p.sb([128, 4], F32, "esink")
        o_all = [p.sb([128, 512], F32, f"oall{i}") for i in range(2)]
        ex = [p.sb([128, 8, 128], F32, f"ex{i}") for i in range(2)]
        pT = [p.sb([128, 8, 128], BF16, f"pT{i}") for i in range(2)]
        den = [p.sb([128, 1], F32, f"den{i}") for i in range(2)]
        for tb in range(NTAB):
            p.dma(lambda e, tb=tb: e.dma_start(out=btab[:, tb], in_=I['btab'][l, :, tb]), w=['btab'])
        p.dma(lambda e: e.dma_start(out=bmask[:], in_=I['bmask']), w=['bmask'])
        p.dma(lambda e: e.dma_start(out=amask[:], in_=I['amask']), w=['amask'])
        load_bc(esink[:], I['a_sink'][l], 'esink')
        p.op('act', lambda e: e.activation(out=esink[:], in_=esink[:], func=AF.Exp), r=['esink'], w=['esink'])
        p.op('act', lambda e: e.activation(out=btab[:], in_=btab[:], func=AF.Exp), r=['btab'], w=['btab'])
        for h in range(4):
            p.op('dve', lambda e, h=h: e.tensor_tensor(out=btab[:, :, h, :], in0=btab[:, :, h, :], in1=bmask[:],
                                                       op=ALU.mult), r=['btab', 'bmask'], w=['btab'])
        def attn_iter(t, grp, h, b, ob):
            if grp == 0:
                qs, ks, vs = h, 4 + h // 2, h // 2
            else:
                qs, ks, vs = 6 + h, 10 + h, 2 + h
            if t < 2:
                blocks = [(0, None), (1, None)]
            elif grp == 0:
                n = t - 2
                blocks = [(t, None), (0, None), (1, None)]
                if n > 0:
                    blocks.append((t - 1, amask[:, 0, :]))
                if n < 15:
                    blocks.append((t + 1, amask[:, 1, :]))
            else:
                pq = t - 2
                blocks = [(0, None), (1, None)]
                for kb in range(16):
                    if (pq, kb) in NA_CASES:
                        blocks.append((kb + 2, btab[:, NA_CASES[(pq, kb)], h, :]))
            nb = len(blocks)
            nn = sum(1 for _, tb in blocks if tb is None)
            sb0, sb1 = (0, 1) if b == 0 else (2, 3)
            ob_ps = 4 + b
            yield
            for i, (kt, tb) in enumerate(blocks):
                bank = sb0 if i < 4 else sb1
                p.op('pe', lambda e, bank=bank, i=i, kt=kt, ks=ks, qs=qs, t=t: e.matmul(
                    ps[bank][:, (i % 4) * 128:(i % 4 + 1) * 128], qkT[:, ks, kt * 128:(kt + 1) * 128],
                    qkT[:, qs, t * 128:(t + 1) * 128], start=True, stop=True), w=[PS(bank)])
            n0 = min(nb, 4)
            yield
            p.op('act', lambda e, b=b, n0=n0, sb0=sb0: e.activation(
                out=ex[b][:, 0:n0, :], in_=ps[sb0][:, 0:n0 * 128].rearrange("p (n k) -> p n k", n=n0),
                func=AF.Exp), r=[PS(sb0)], w=[('ex', b)])
            if nb > 4:
                n1 = nb - 4
                p.op('act', lambda e, b=b, n1=n1, sb1=sb1: e.activation(
                    out=ex[b][:, 4:4 + n1, :], in_=ps[sb1][:, 0:n1 * 128].rearrange("p (n k) -> p n k", n=n1),
                    func=AF.Exp), r=[PS(sb1)], w=[('ex', b)])
            yield
            p.op('pool', lambda e, b=b, nn=nn: e.tensor_copy(out=pT[b][:, 0:nn, :], in_=ex[b][:, 0:nn, :]),
                 r=[('ex', b)], w=[('pT', b)])
            yield
            for i, (kt, tb) in enumerate(blocks):
                if tb is None:
                    continue
                p.op('dve', lambda e, b=b, i=i, tb=tb: e.tensor_tensor(out=pT[b][:, i, :], in0=ex[b][:, i, :],
                                                                      in1=tb, op=ALU.mult),
                     r=[('ex', b), 'btab', 'amask'], w=[('pT', b)])
            yield
            for i, (kt, tb) in enumerate(blocks):
                p.op('pe', lambda e, b=b, i=i, kt=kt, vs=vs, ob_ps=ob_ps, nb=nb: e.matmul(
                    ps[ob_ps][:, 0:65], pT[b][:, i, :], Vaug[:, kt, vs, :], start=(i == 0), stop=(i == nb - 1)),
                    r=[('pT', b)], w=[PS(ob_ps)])
            if grp == 0:
                p.op('dve', lambda e, b=b, h=h, ob_ps=ob_ps: e.tensor_scalar(
                    out=den[b][:], in0=ps[ob_ps][:, 64:65], scalar1=esink[:, h:h + 1], scalar2=None,
                    op0=ALU.add), r=[PS(ob_ps), 'esink'], w=[('den', b)])
                p.op('dve', lambda e, b=b: e.reciprocal(out=den[b][:], in_=den[b][:]),
                     r=[('den', b)], w=[('den', b)])
            else:
                p.op('dve', lambda e, b=b, ob_ps=ob_ps: e.reciprocal(out=den[b][:], in_=ps[ob_ps][:, 64:65]),
                     r=[PS(ob_ps)], w=[('den', b)])
            col = grp * 256 + h * 64
            yield
            p.op('dve', lambda e, b=b, ob=ob, col=col, ob_ps=ob_ps: e.tensor_scalar(
                out=o_all[ob][:, col:col + 64], in0=ps[ob_ps][:, 0:64], scalar1=den[b][:, 0:1], scalar2=None,
                op0=ALU.mult), r=[PS(ob_ps), ('den', b)], w=[('oall', ob)])

        def rr2(gens):
            gens = list(gens)
            while gens:
                for g_ in list(gens):
                    try:
                        next(g_)
                    except StopIteration:
                        gens.remove(g_)

        it = 0
        for t in range(NT):
            if t < 2 and not with_ctx:
                continue
            ob = t % 2
            its = []
            for grp in range(2):
                for h in range(4):
                    its.append((t, grp, h, it % 2, ob))
                    it += 1
            for i in range(0, 8, 2):
                rr2([attn_iter(*its[i]), attn_iter(*its[i + 1])])
            p.dma(lambda e, ob=ob, t=t: e.dma_start(out=S['o'][t * 128:(t + 1) * 128, 0:512], in_=o_all[ob][:]),
                  r=[('oall', ob)], w=[('So', t)])
        p.barrier()


    def phase_rprep(l):
        p.sb_reset(base_mark)
        w2 = p.sb([128, 512], F32, "w2")
        a2 = p.sb([128, 512], F32, "a2")
        g2 = p.sb([128, 512], F32, "g2")
        w0 = p.sb([1, 2, 512], F32, "w0")
        a0 = p.sb([1, 2, 512], F32, "a0")
        ones = p.sb([1, 128], F32, "ones")
        KKW = p.sb([128, 512], F32, "KKW")
        KA = p.sb([128, 512], F32, "KA")
        RK = p.sb([128, 512], F32, "RK")
        p.dma(lambda e: e.dma_start(out=w2[:], in_=I['r7_w2'][l].rearrange("d r c -> (d r) c")), w=['w2'])
        p.dma(lambda e: e.dma_start(out=a2[:], in_=I['r7_a2'][l].rearrange("d r c -> (d r) c")), w=['a2'])
        p.dma(lambda e: e.dma_start(out=g2[:], in_=I['r7_g2'][l]), w=['g2'])
        p.dma(lambda e: e.dma_start(out=w0[:], in_=I['r7_w0'][l:l + 1]), w=['w0'])
        p.dma(lambda e: e.dma_start(out=a0[:], in_=I['r7_a0'][l:l + 1]), w=['a0'])
        p.op('dve', lambda e: e.memset(ones[:], 1.0), w=['ones'])
        load_bc(KKW[:], I['r7_kk'][l], 'KKW')
        load_bc(KA[:], I['r7_ka'][l], 'KA')
        load_bc(RK[:], I['r7_rk'][l].rearrange("h d -> (h d)"), 'RK')
        fm = [p.sb([128, 15, 128], F32, f"fm{i}") for i in range(2)]
        TM = [p.sb([128, 10, 512], F32, f"TM{i}") for i in range(2)]
        kt = p.sb([128, 512], F32, "kt")
        av = [p.sb([128, 512], F32, f"av{i}") for i in range(2)]
        tmp = p.sb([128, 512], F32, "tmp")
        tmp2 = p.sb([128, 512], F32, "tmp2")
        s8 = p.sb([128, 8], F32, "s8")
        bs = [p.sb([128, 8], F32, f"bs{i}") for i in range(2)]
        for t in range(NT):
            b = t % 2
            p.dma(lambda e, b=b, t=t: e.dma_start(
                out=fm[b][:], in_=S['pcT'][:, t * 128:(t + 1) * 128].rearrange("(c p) t -> p c t", p=128)),
                w=[('fm', b)])
            for q in range(3):
                for c4 in range(4):
                    p.op('pe', lambda e, b=b, q=q, c4=c4: e.transpose(
                        out=ps[q][:, c4 * 128:(c4 + 1) * 128], in_=fm[b][:, q * 4 + c4, :], identity=ident_f[:]),
                        r=[('fm', b), 'identf'], w=[PS(q)])
            for d in range(2):
                pr = slice(d * 64, d * 64 + 64)
                p.op('pe', lambda e, b=b, d=d, pr=pr: e.matmul(ps[3 + d][:, :], fm[b][pr, 12, :], w2[pr, :],
                                                              start=True, stop=False), r=[('fm', b), 'w2'], w=[PS(3 + d)])
                p.op('pe', lambda e, d=d: e.matmul(ps[3 + d][:, :], ones[0:1, :], w0[0:1, d, :], start=False, stop=True),
                     r=['ones', 'w0'], w=[PS(3 + d)])
                p.op('pe', lambda e, b=b, d=d, pr=pr: e.matmul(ps[5 + d][:, :], fm[b][pr, 13, :], a2[pr, :],
                                                              start=True, stop=False), r=[('fm', b), 'a2'], w=[PS(5 + d)])
                p.op('pe', lambda e, d=d: e.matmul(ps[5 + d][:, :], ones[0:1, :], a0[0:1, d, :], start=False, stop=True),
                     r=['ones', 'a0'], w=[PS(5 + d)])
            p.op('pe', lambda e, b=b: e.matmul(ps[7][:, :], fm[b][:, 14, :], g2[:], start=True, stop=True),
                 r=[('fm', b), 'g2'], w=[PS(7)])
            T = TM[b]
            wk = [('TM', b)]
            p.op('act', lambda e, T=T: e.activation(out=T[:, 0, :], in_=ps[0][:, :], func=AF.Copy), r=[PS(0)], w=wk)
            p.op('act', lambda e: e.activation(out=kt[:], in_=ps[1][:, :], func=AF.Copy), r=[PS(1)], w=['kt'])
            p.op('act', lambda e, T=T: e.activation(out=T[:, 1, :], in_=ps[2][:, :], func=AF.Copy), r=[PS(2)], w=wk)
            p.op('act', lambda e, T=T: e.activation(out=T[:, 2, :], in_=ps[7][:, :], func=AF.Copy), r=[PS(7)], w=wk)
            for d in range(2):
                p.op('act', lambda e, T=T, d=d: e.activation(out=T[:, 8 + d, :], in_=ps[3 + d][:, :], func=AF.Sigmoid),
                     r=[PS(3 + d)], w=wk)
                p.op('act', lambda e, d=d: e.activation(out=av[d][:], in_=ps[5 + d][:, :], func=AF.Sigmoid),
                     r=[PS(5 + d)], w=[('av', d)])
                p.op('dve', lambda e, T=T, d=d: e.tensor_scalar(out=T[:, 8 + d, :], in0=T[:, 8 + d, :],
                                                                scalar1=-0.6065306597126334, scalar2=None, op0=ALU.mult),
                     r=wk, w=wk)
            p.op('dve', lambda e: e.tensor_tensor(out=tmp[:], in0=kt[:], in1=KKW[:], op=ALU.mult), r=['kt', 'KKW'], w=['tmp'])
            p.op('dve', lambda e: e.tensor_tensor(out=tmp2[:], in0=tmp[:], in1=tmp[:], op=ALU.mult), r=['tmp'], w=['tmp2'])
            p.op('dve', lambda e: e.tensor_reduce(out=s8[:], in_=tmp2[:].rearrange("p (h d) -> p h d", h=8), axis=AX.X,
                                                  op=ALU.add), r=['tmp2'], w=['s8'])
            p.op('act', lambda e: e.activation(out=s8[:], in_=s8[:], func=AF.Sqrt), r=['s8'], w=['s8'])
            p.op('dve', lambda e: e.tensor_scalar(out=s8[:], in0=s8[:], scalar1=1e-12, scalar2=None, op0=ALU.max),
                 r=['s8'], w=['s8'])
            p.op('dve', lambda e: e.reciprocal(out=s8[:], in_=s8[:]), r=['s8'], w=['s8'])
            p.op('dve', lambda e, T=T: e.tensor_tensor(out=T[:, 3, :].rearrange("p (h d) -> p h d", h=8),
                                                       in0=tmp[:].rearrange("p (h d) -> p h d", h=8),
                                                       in1=s8[:].unsqueeze(2).to_broadcast([128, 8, 64]), op=ALU.mult),
                 r=['tmp', 's8'], w=wk)
            for d in range(2):
                p.op('dve', lambda e, d=d: e.scalar_tensor_tensor(out=tmp2[:], in0=av[d][:], scalar=-1.0, in1=KA[:],
                                                                  op0=ALU.add, op1=ALU.mult),
                     r=[('av', d), 'KA'], w=['tmp2'])
                p.op('dve', lambda e, T=T, d=d: e.scalar_tensor_tensor(out=T[:, 4 + d, :], in0=tmp2[:], scalar=1.0,
                                                                       in1=kt[:], op0=ALU.add, op1=ALU.mult),
                     r=['tmp2', 'kt'], w=wk)
                p.op('dve', lambda e, T=T, d=d: e.tensor_tensor(out=T[:, 6 + d, :], in0=T[:, 3, :], in1=av[d][:],
                                                                op=ALU.mult), r=wk + [('av', d)], w=wk)
            p.op('dve', lambda e, T=T: e.tensor_tensor(out=tmp[:], in0=T[:, 4, :], in1=T[:, 5, :], op=ALU.add),
                 r=wk, w=['tmp'])
            p.op('dve', lambda e: e.tensor_tensor(out=tmp[:], in0=tmp[:], in1=RK[:], op=ALU.mult), r=['tmp', 'RK'], w=['tmp'])
            p.op('dve', lambda e, T=T: e.tensor_tensor(out=tmp[:], in0=tmp[:], in1=T[:, 0, :], op=ALU.mult),
                 r=['tmp'] + wk, w=['tmp'])
            p.op('dve', lambda e, b=b: e.tensor_reduce(out=bs[b][:], in_=tmp[:].rearrange("p (h d) -> p h d", h=8),
                                                       axis=AX.X, op=ALU.add), r=['tmp'], w=[('bs', b)])
            p.dma(lambda e, T=T, t=t: e.dma_start(out=S['tm'][t * 128:(t + 1) * 128], in_=T[:]), r=wk, w=[('Stm', t)])
            p.dma(lambda e, b=b, t=t: e.dma_start(out=S['bon'][t * 128:(t + 1) * 128], in_=bs[b][:]),
                  r=[('bs', b)], w=[('Sbon', t)])
        p.barrier()

    def phase_scan(l):
        p.sb_reset(base_mark)
        PSB = ps
        C = 64
        NCH = NTOK // C
        tri = p.sb([64, 2, 64], F32, "tri")
        mg = p.sb([64, 2, 128], F32, "mg")
        mn = p.sb([64, 2, 64], F32, "mn")
        ones = p.sb([64, 1], F32, "ones1")
        p.dma(lambda e: e.dma_start(out=tri[:], in_=I['tri']), w=['tri'])
        p.dma(lambda e: e.dma_start(out=mg[:], in_=I['mg']), w=['mg'])
        p.dma(lambda e: e.dma_start(out=mn[:], in_=I['mn']), w=['mn'])
        p.op('dve', lambda e: e.memset(ones[:], 1.0), w=['ones1'])
        M = [p.sb([64, 8, 64], F32, f"M{d}") for d in range(2)]
        for d in range(2):
            M0_PLACEHOLDER = None
        X = [[p.sb([64, 6, 512], F32, f"X{d}{i}") for i in range(2)] for d in range(2)]
        def mk(shape, name):
            return [p.sb(shape, F32, f"{name}{d}") for d in range(2)]
        E0s, E1s, E2s = mk([64, 512], "E0"), mk([64, 512], "E1"), mk([64, 512], "E2")
        Ats, Rts, Bts, Kts = mk([64, 512], "At"), mk([64, 512], "Rt"), mk([64, 512], "Bt"), mk([64, 512], "Kt")
        FARs, FBs, FKs = mk([64, 8, 128], "FAR"), mk([64, 8, 64], "FB"), mk([64, 8, 64], "FK")
        G1s, G2s = mk([64, 8, 128], "G1"), mk([64, 8, 128], "G2")
        Tms = [mk([64, 8, 64], f"Tm{i}_") for i in range(2)]
        Nms = [mk([64, 8, 64], f"Nm{i}_") for i in range(2)]
        Zs, Wss, Uss, PCs = mk([64, 8, 64], "Z"), mk([64, 512], "Ws"), mk([64, 512], "Us"), mk([64, 8], "PC")
        Ys = [p.sb([64, 512], F32, f"Ys{d}") for d in range(2)]
        order = {0: list(range(0, 4)) + list(range(4, NCH)), 1: list(range(3, -1, -1)) + list(range(NCH - 1, 3, -1))}
        v3 = lambda ap: ap.rearrange("p (h d) -> p h d", h=8)
        F32R = mybir.dt.float32r
        use_r = cfg.get("fp32r", True)

        def RR(ap):
            return ap.bitcast(F32R) if use_r else ap

        Vrs = mk([64, 512], "Vr")
        Mts = mk([64, 8, 64], "Mt")
        for d in range(2):
            p.op('dve', lambda e, d=d: e.memset(Mts[d][:], 0.0), w=[('Mt', d)])
            p.op('dve', lambda e, d=d: e.tensor_copy(out=RR(M[d][:]), in_=Mts[d][:]), r=[('Mt', d)], w=[('M', d)])

        def mmr(e, out, lhsT, rhs, **kw):
            if use_r:
                return e.matmul(out, lhsT.bitcast(F32R), rhs.bitcast(F32R), **kw)
            return e.matmul(out, lhsT, rhs, **kw)

        def scan_unit(d, c):
            if True:
                tok0 = c * C
                Xd = X[d][c % 2]
                E0, E1, E2, At, Rt, Bt, Kt = E0s[d], E1s[d], E2s[d], Ats[d], Rts[d], Bts[d], Kts[d]
                FAR, FB, FK, G1, G2 = FARs[d], FBs[d], FKs[d], G1s[d], G2s[d]
                Tm = [Tms[0][d], Tms[1][d]]
                Nm = [Nms[0][d], Nms[1][d]]
                Z, Ws, Us, PC = Zs[d], Wss[d], Uss[d], PCs[d]
                ps = [PSB[4 * d + (i % 4)] for i in range(8)]
                PS = lambda i: ('ps', 4 * d + (i % 4))
                xk = [('X', d, c % 2)]
                srcs = [0, 1, 3, 4 + d, 6 + d, 8 + d]
                yield
                for i, s in enumerate(srcs):
                    p.dma(lambda e, Xd=Xd, i=i, s=s, tok0=tok0: e.dma_start(out=Xd[:, i, :],
                                                                           in_=S['tm'][tok0:tok0 + C, s, :]), w=xk)
                r_, v_, kk_, k_, b_, lw_ = [Xd[:, i, :] for i in range(6)]
                Vr = Vrs[d]
                yield
                p.op('act', lambda e, v_=v_: e.activation(out=RR(Vr[:]), in_=v_, func=AF.Copy), r=xk, w=[('Vr', d)])
                v_ = Vr[:]
                vk = [('Vr', d)]
                yield
                p.op('pe', lambda e, d=d, lw_=lw_: e.matmul(ps[0][0:64, :], tri[:, d, :], lw_, start=True, stop=True),
                     r=xk + ['tri'], w=[PS(0)])
                yield
                for h in range(8):
                    p.op('pe', lambda e, h=h, lw_=lw_: e.matmul(ps[1][0:64, h:h + 1], lw_[:, h * 64:(h + 1) * 64],
                                                               ones[:, 0:1], start=True, stop=True),
                         r=xk + ['ones1'], w=[PS(1)])
                yield
                p.op('act', lambda e: e.activation(out=PC[:], in_=ps[1][0:64, 0:8], func=AF.Exp), r=[PS(1)], w=[('PC', d)])
                yield
                p.op('act', lambda e: e.activation(out=E1[:], in_=ps[0][0:64, :], func=AF.Exp), r=[PS(0)], w=[('E1', d)])
                yield
                p.op('act', lambda e: e.activation(out=E2[:], in_=ps[0][0:64, :], func=AF.Exp, scale=-1.0),
                     r=[PS(0)], w=[('E2', d)])
                yield
                p.op('dve', lambda e, lw_=lw_: e.tensor_tensor(out=E0[:], in0=ps[0][0:64, :], in1=lw_, op=ALU.subtract),
                     r=[PS(0)] + xk, w=[('E0', d)])
                yield
                p.op('act', lambda e: e.activation(out=E0[:], in_=E0[:], func=AF.Exp), r=[('E0', d)], w=[('E0', d)])
                yield
                p.op('dve', lambda e, kk_=kk_: e.scalar_tensor_tensor(out=At[:], in0=kk_, scalar=-1.0, in1=E0[:],
                                                                      op0=ALU.mult, op1=ALU.mult),
                     r=xk + [('E0', d)], w=[('At', d)])
                yield
                p.op('dve', lambda e, r_=r_: e.tensor_tensor(out=Rt[:], in0=r_, in1=E1[:], op=ALU.mult),
                     r=xk + [('E1', d)], w=[('Rt', d)])
                yield
                p.op('dve', lambda e, b_=b_: e.tensor_tensor(out=RR(Bt[:]), in0=b_, in1=E2[:], op=ALU.mult),
                     r=xk + [('E2', d)], w=[('Bt', d)])
                yield
                p.op('dve', lambda e, k_=k_: e.tensor_tensor(out=RR(Kt[:]), in0=k_, in1=E2[:], op=ALU.mult),
                     r=xk + [('E2', d)], w=[('Kt', d)])
                yield
                for bank, src, key in ((2, At, ('At', d)), (3, Rt, ('Rt', d)), (4, Bt, ('Bt', d)), (5, Kt, ('Kt', d))):
                    for h in range(8):
                        p.op('pe', lambda e, bank=bank, src=src, h=h: e.transpose(
                            out=ps[bank][0:64, h * 64:(h + 1) * 64], in_=src[:, h * 64:(h + 1) * 64],
                            identity=ident_f[0:64, 0:64]), r=[key, 'identf'], w=[PS(bank)])
                yield
                p.op('act', lambda e: e.activation(out=RR(FAR[:, :, 0:64]), in_=v3(ps[2][0:64, :]), func=AF.Copy),
                     r=[PS(2)], w=[('FAR', d)])
                yield
                p.op('act', lambda e: e.activation(out=RR(FAR[:, :, 64:128]), in_=v3(ps[3][0:64, :]), func=AF.Copy),
                     r=[PS(3)], w=[('FAR', d)])
                yield
                p.op('dve', lambda e: e.tensor_copy(out=RR(FB[:]), in_=v3(ps[4][0:64, :])), r=[PS(4)], w=[('FB', d)])
                yield
                p.op('dve', lambda e: e.tensor_copy(out=RR(FK[:]), in_=v3(ps[5][0:64, :])), r=[PS(5)], w=[('FK', d)])
                yield
                for h in range(8):
                    bank = 6 + (h // 4)
                    p.op('pe', lambda e, h=h, bank=bank: mmr(e, ps[bank][0:64, (h % 4) * 128:(h % 4 + 1) * 128],
                                                                   FB[:, h, :], FAR[:, h, :], start=True, stop=True),
                         r=[('FB', d), ('FAR', d)], w=[PS(bank)])
                yield
                for hb in range(2):
                    p.op('dve', lambda e, hb=hb, d=d: e.tensor_tensor(
                        out=RR(G1[:, hb * 4:(hb + 1) * 4, :]), in0=ps[6 + hb][0:64, :].rearrange("p (h t) -> p h t", h=4),
                        in1=mg[:, d, :].unsqueeze(1).to_broadcast([64, 4, 128]), op=ALU.mult),
                        r=[PS(6 + hb), 'mg'], w=[('G1', d)])
                yield
                for h in range(8):
                    bank = 2 + (h // 4)
                    p.op('pe', lambda e, h=h, bank=bank: mmr(e, ps[bank][0:64, (h % 4) * 128:(h % 4 + 1) * 128],
                                                                   FK[:, h, :], FAR[:, h, :], start=True, stop=True),
                         r=[('FK', d), ('FAR', d)], w=[PS(bank)])
                yield
                for hb in range(2):
                    p.op('dve', lambda e, hb=hb, d=d: e.tensor_tensor(
                        out=RR(G2[:, hb * 4:(hb + 1) * 4, :]), in0=ps[2 + hb][0:64, :].rearrange("p (h t) -> p h t", h=4),
                        in1=mg[:, d, :].unsqueeze(1).to_broadcast([64, 4, 128]), op=ALU.mult),
                        r=[PS(2 + hb), 'mg'], w=[('G2', d)])
                yield
                for h in range(8):
                    p.op('pe', lambda e, h=h: mmr(e, ps[4][0:64, h * 64:(h + 1) * 64], FAR[:, h, 0:64], FB[:, h, :],
                                                       start=True, stop=True), r=[('FAR', d), ('FB', d)], w=[PS(4)])
                yield
                p.op('dve', lambda e, d=d: e.tensor_tensor(out=RR(Nm[0][:]), in0=v3(ps[4][0:64, :]),
                                                           in1=mn[:, d, :].unsqueeze(1).to_broadcast([64, 8, 64]),
                                                           op=ALU.mult), r=[PS(4), 'mn'], w=[('Nm', d, 0)])
                yield
                p.op('dve', lambda e: e.tensor_copy(out=RR(Tm[0][:]), in_=G1[:, :, 0:64]), r=[('G1', d)], w=[('Tm', d, 0)])
                yield
                p.op('dve', lambda e: e.tensor_tensor(out=RR(Z[:]), in0=G1[:, :, 0:64],
                                                      in1=ident_f[0:64, 0:64].unsqueeze(1).to_broadcast([64, 8, 64]),
                                                      op=ALU.add), r=[('G1', d), 'identf'], w=[('Z', d)])
                cur = 0
                yield
                for lev in range(5):
                    nxt = 1 - cur
                    last = lev == 4
                    for h in range(8):
                        p.op('pe', lambda e, h=h, cur=cur: mmr(e, ps[5][0:64, h * 64:(h + 1) * 64], Tm[cur][:, h, :],
                                                                    Nm[cur][:, h, :], start=True, stop=True),
                             r=[('Tm', d, cur), ('Nm', d, cur)], w=[PS(5)])
                    p.op('act', lambda e, nxt=nxt: e.activation(out=RR(Nm[nxt][:]), in_=v3(ps[5][0:64, :]), func=AF.Copy),
                         r=[PS(5)], w=[('Nm', d, nxt)])
                    if not last:
                        for h in range(8):
                            p.op('pe', lambda e, h=h, cur=cur: mmr(e, ps[6][0:64, h * 64:(h + 1) * 64],
                                                                        Nm[cur][:, h, :], Tm[cur][:, h, :],
                                                                        start=True, stop=True),
                                 r=[('Tm', d, cur), ('Nm', d, cur)], w=[PS(6)])
                        p.op('dve', lambda e, nxt=nxt: e.tensor_copy(out=RR(Tm[nxt][:]), in_=v3(ps[6][0:64, :])),
                             r=[PS(6)], w=[('Tm', d, nxt)])
                    for h in range(8):
                        p.op('pe', lambda e, h=h, nxt=nxt: mmr(e, ps[7][0:64, h * 64:(h + 1) * 64], Nm[nxt][:, h, :],
                                                                    Z[:, h, :], start=True, stop=True),
                             r=[('Nm', d, nxt), ('Z', d)], w=[PS(7)])
                    p.op('dve', lambda e: e.tensor_tensor(out=RR(Z[:]), in0=Z[:], in1=v3(ps[7][0:64, :]), op=ALU.add),
                         r=[PS(7), ('Z', d)], w=[('Z', d)])
                    cur = nxt
                Md = M[d]
                yield
                for h in range(8):
                    o = ps[0][0:64, h * 64:(h + 1) * 64]
                    p.op('pe', lambda e, h=h, o=o, Md=Md: mmr(e, o, FAR[:, h, 0:64], Md[:, h, :], start=True, stop=False),
                         r=[('FAR', d), ('M', d)], w=[PS(0)])
                    p.op('pe', lambda e, h=h, o=o, v_=v_: mmr(e, o, G2[:, h, 0:64], v_[:, h * 64:(h + 1) * 64],
                                                                   start=False, stop=True), r=[('G2', d)] + vk, w=[PS(0)])
                yield
                p.op('act', lambda e: e.activation(out=RR(Ws[:]), in_=ps[0][0:64, :], func=AF.Copy), r=[PS(0)], w=[('Ws', d)])
                yield
                for h in range(8):
                    p.op('pe', lambda e, h=h: mmr(e, ps[1][0:64, h * 64:(h + 1) * 64], Z[:, h, :],
                                                       Ws[:, h * 64:(h + 1) * 64], start=True, stop=True),
                         r=[('Z', d), ('Ws', d)], w=[PS(1)])
                yield
                p.op('act', lambda e: e.activation(out=RR(Us[:]), in_=ps[1][0:64, :], func=AF.Copy), r=[PS(1)], w=[('Us', d)])
                yield
                for h in range(8):
                    o = ps[2][0:64, h * 64:(h + 1) * 64]
                    hs = slice(h * 64, (h + 1) * 64)
                    p.op('pe', lambda e, h=h, o=o, Md=Md: mmr(e, o, FAR[:, h, 64:128], Md[:, h, :], start=True, stop=False),
                         r=[('FAR', d), ('M', d)], w=[PS(2)])
                    p.op('pe', lambda e, h=h, o=o, hs=hs: mmr(e, o, G1[:, h, 64:128], Us[:, hs], start=False, stop=False),
                         r=[('G1', d), ('Us', d)], w=[PS(2)])
                    p.op('pe', lambda e, h=h, o=o, hs=hs, v_=v_: mmr(e, o, G2[:, h, 64:128], v_[:, hs], start=False, stop=True),
                         r=[('G2', d)] + vk, w=[PS(2)])
                yield
                p.op('act', lambda e, d=d: e.activation(out=Ys[d][:], in_=ps[2][0:64, :], func=AF.Copy),
                     r=[PS(2)], w=[('Ys', d)])
                yield
                p.dma(lambda e, d=d, tok0=tok0: e.dma_start(out=S['y'][d, tok0:tok0 + C, :], in_=Ys[d][:]),
                      r=[('Ys', d)], w=[('Sy', d, c)])
                yield
                for h in range(8):
                    o = ps[3][0:64, h * 64:(h + 1) * 64]
                    hs = slice(h * 64, (h + 1) * 64)
                    p.op('pe', lambda e, o=o, hs=hs: mmr(e, o, Bt[:, hs], Us[:, hs], start=True, stop=False),
                         r=[('Bt', d), ('Us', d)], w=[PS(3)])
                    p.op('pe', lambda e, o=o, hs=hs, v_=v_: mmr(e, o, Kt[:, hs], v_[:, hs], start=False, stop=True),
                         r=[('Kt', d)] + vk, w=[PS(3)])
                Mt = Mts[d]
                yield
                p.op('dve', lambda e, Md=Md, Mt=Mt: e.tensor_tensor(out=Mt[:], in0=Md[:], in1=v3(ps[3][0:64, :]), op=ALU.add),
                     r=[PS(3), ('M', d)], w=[('Mt', d)])
                yield
                p.op('dve', lambda e, Md=Md, Mt=Mt: e.tensor_tensor(out=RR(Md[:]), in0=Mt[:],
                                                             in1=PC[:].unsqueeze(2).to_broadcast([64, 8, 64]),
                                                             op=ALU.mult), r=[('PC', d), ('Mt', d)], w=[('M', d)])
        cin = [[p.sb([128, 1, D], F32, f"cin{i}{q}") for q in range(2)] for i in range(2)]
        cout = [p.sb([128, 1, 2 * D], BF16, f"cout{i}") for i in range(2)]

        def conv_block(blk):
            b = blk % 2
            rows = slice(blk * 128, (blk + 1) * 128)
            for q, tabn in enumerate(('peer_u', 'peer_v')):
                p.dma(lambda e, b=b, q=q, tabn=tabn, rows=rows: e.dma_start(
                    out=cin[b][q][:], in_=I[tabn][l][rows, :].rearrange("(j p) d -> p j d", p=128)), w=[('cin', b, q)],
                    eng="pool")
                p.op('pool', lambda e, b=b, q=q: e.tensor_copy(out=cout[b][:, :, q * D:(q + 1) * D], in_=cin[b][q][:]),
                     r=[('cin', b, q)], w=[('cout', b, q)])
            p.dma(lambda e, b=b, rows=rows: e.dma_start(
                out=S['T'][l][rows, :].rearrange("(j p) d -> p j d", p=128), in_=cout[b][:]),
                r=[('cout', b, 0), ('cout', b, 1)], w=[('cout', b, 0), ('cout', b, 1)], eng="pool")

        nblk = 0
        for step in range(NCH):
            gens = [scan_unit(d, order[d][step]) for d in range(2)]
            while gens:
                for g_ in list(gens):
                    try:
                        next(g_)
                    except StopIteration:
                        gens.remove(g_)
            for _ in range(4):
                if nblk < 128:
                    conv_block(nblk)
                    nblk += 1
        while nblk < 128:
            conv_block(nblk)
            nblk += 1
        p.barrier()


    def phase_rout(l):
        p.sb_reset(base_mark)
        with_ctx = l < DEPTH - 1
        wo = p.sb([128, 8, D], BF16, "wo")
        for j in range(8):
            p.dma(lambda e, j=j: e.dma_start(out=wo[:, j, :], in_=I['w_out'][l, j * 128:(j + 1) * 128, :]),
                  w=[('wo', j)], eng="pool")
        LNW = p.sb([128, 512], F32, "LNW")
        LNB = p.sb([128, 512], F32, "LNB")
        G1b = [p.sb([128, D], F32, f"G1b{s}") for s in range(2)]
        load_bc(LNW[:], I['r7_lnw'][l], 'LNW')
        load_bc(LNB[:], I['r7_lnb'][l], 'LNB')
        gn_eps = p.sb([128, 1], F32, "gneps")
        p.op('dve', lambda e: e.memset(gn_eps[:], 64e-5), w=['gneps'])
        for s in range(2):
            load_bc(G1b[s][:], S['mod'][l, s, 2 * D:3 * D], ('G1b', s))
        yb = [[p.sb([128, 512], F32, f"y{d}{i}") for d in range(2)] for i in range(2)]
        vg = [p.sb([128, 2, 512], F32, f"vg{i}") for i in range(2)]
        bon = [p.sb([128, 8], F32, f"bon{i}") for i in range(2)]
        O = [p.sb([128, D], F32, f"O{i}") for i in range(2)]
        Ob = [p.sb([128, D], BF16, f"Ob{i}") for i in range(2)]
        oT = [p.sb([128, 8, 128], BF16, f"oT{i}") for i in range(2)]
        xt = [p.sb([128, D], F32, f"xr{i}") for i in range(2)]
        yc = [p.sb([128, 512], F32, f"yc{i}") for i in range(2)]
        sq = [p.sb([128, 512], F32, f"sq2{i}") for i in range(2)]
        m8 = [p.sb([128, 8], F32, f"m8{i}") for i in range(2)]
        v8 = [p.sb([128, 8], F32, f"v8{i}") for i in range(2)]
        src = I['x'] if l == 0 else S['xs']
        v3 = lambda ap: ap.rearrange("p (h d) -> p h d", h=8)
        bc8 = lambda ap: ap.unsqueeze(2).to_broadcast([128, 8, 64])
        def rout_iter(t):
            b = t % 2
            s = 1 if t < 2 else 0
            rows = slice(t * 128, (t + 1) * 128)
            yield
            for d in range(2):
                p.dma(lambda e, b=b, d=d, rows=rows: e.dma_start(out=yb[b][d][:], in_=S['y'][d, rows, :]), w=[('y', b, d)])
            yield
            p.dma(lambda e, b=b, rows=rows: e.dma_start(out=vg[b][:], in_=S['tm'][rows, 1:3, :]), w=[('vg', b)])
            yield
            p.dma(lambda e, b=b, rows=rows: e.dma_start(out=bon[b][:], in_=S['bon'][rows, :]), w=[('bon', b)])
            yield
            p.dma(lambda e, b=b, rows=rows: e.dma_start(out=O[b][:, 0:512], in_=S['o'][rows, 0:512]), w=[('O', b)])
            yield
            p.dma(lambda e, b=b, rows=rows: e.dma_start(out=xt[b][:], in_=src[rows, :]), w=[('xr', b)])
            yield
            p.op('dve', lambda e, b=b: e.tensor_tensor(out=yc[b][:], in0=yb[b][0][:], in1=yb[b][1][:], op=ALU.add),
                 r=[('y', b, 0), ('y', b, 1)], w=[('yc', b)])
            yield
            p.op('dve', lambda e, b=b: e.tensor_reduce(out=m8[b][:], in_=v3(yc[b][:]), axis=AX.X, op=ALU.add), r=[('yc', b)], w=[('m8', b)])
            yield
            p.op('dve', lambda e, b=b: e.tensor_scalar(out=m8[b][:], in0=m8[b][:], scalar1=1.0 / 64, scalar2=None, op0=ALU.mult),
                 r=[('m8', b)], w=[('m8', b)])
            yield
            p.op('dve', lambda e, b=b: e.tensor_tensor(out=v3(yc[b][:]), in0=v3(yc[b][:]), in1=bc8(m8[b][:]), op=ALU.subtract),
                 r=[('yc', b), ('m8', b)], w=[('yc', b)])
            yield
            p.op('dve', lambda e, b=b: e.tensor_tensor(out=sq[b][:], in0=yc[b][:], in1=yc[b][:], op=ALU.mult), r=[('yc', b)], w=[('sq2', b)])
            yield
            p.op('dve', lambda e, b=b: e.tensor_reduce(out=v8[b][:], in_=v3(sq[b][:]), axis=AX.X, op=ALU.add), r=[('sq2', b)], w=[('v8', b)])
            yield
            p.op('act', lambda e, b=b: e.activation(out=v8[b][:], in_=v8[b][:], func=AF.Sqrt, bias=gn_eps[:], scale=1.0 / 64),
                 r=[('v8', b), 'gneps'], w=[('v8', b)])
            yield
            p.op('dve', lambda e, b=b: e.reciprocal(out=v8[b][:], in_=v8[b][:]), r=[('v8', b)], w=[('v8', b)])
            yield
            p.op('dve', lambda e, b=b: e.tensor_tensor(out=v3(yc[b][:]), in0=v3(yc[b][:]), in1=bc8(v8[b][:]), op=ALU.mult),
                 r=[('yc', b), ('v8', b)], w=[('yc', b)])
            yield
            p.op('dve', lambda e, b=b: e.tensor_tensor(out=yc[b][:], in0=yc[b][:], in1=LNW[:], op=ALU.mult), r=[('yc', b), 'LNW'], w=[('yc', b)])
            yield
            p.op('dve', lambda e, b=b: e.tensor_tensor(out=yc[b][:], in0=yc[b][:], in1=LNB[:], op=ALU.add), r=[('yc', b), 'LNB'], w=[('yc', b)])
            yield
            p.op('dve', lambda e, b=b: e.tensor_tensor(out=v3(sq[b][:]), in0=v3(vg[b][:, 0, :]), in1=bc8(bon[b][:]),
                                                       op=ALU.mult), r=[('vg', b), ('bon', b)], w=[('sq2', b)])
            yield
            p.op('dve', lambda e, b=b: e.tensor_tensor(out=yc[b][:], in0=yc[b][:], in1=sq[b][:], op=ALU.add), r=[('yc', b), ('sq2', b)], w=[('yc', b)])
            yield
            p.op('dve', lambda e, b=b: e.tensor_tensor(out=O[b][:, 512:1024], in0=yc[b][:], in1=vg[b][:, 1, :], op=ALU.mult),
                 r=[('yc', b), ('vg', b)], w=[('O2', b)])
            yield
            p.dma(lambda e, b=b, rows=rows: e.dma_start(out=S['o'][rows, 512:1024], in_=O[b][:, 512:1024]),
                  r=[('O2', b)], w=[('So2', t)])
            yield
            p.op('act', lambda e, b=b: e.activation(out=Ob[b][:], in_=O[b][:], func=AF.Copy),
                 r=[('O', b), ('O2', b)], w=[('Ob', b)])
            bank = 6 + b
            pv = ps[bank][:, 0:512].bitcast(BF16)
            yield
            for j in range(8):
                p.op('pe', lambda e, b=b, j=j, pv=pv: e.transpose(out=pv[:, j * 128:(j + 1) * 128],
                                                                 in_=Ob[b][:, j * 128:(j + 1) * 128], identity=ident_b[:]),
                     r=[('Ob', b), 'identb'], w=[PS(bank)])
            yield
            p.op('act', lambda e, b=b, pv=pv: e.activation(out=oT[b][:], in_=pv.rearrange("p (j t) -> p j t", j=8),
                                                            func=AF.Copy), r=[PS(bank)], w=[('oT', b)])
            yield
            for half in range(2):
                ybank = 2 * b + half
                for j in range(8):
                    p.op('pe', lambda e, b=b, j=j, half=half, ybank=ybank: e.matmul(
                        ps[ybank][:, :], oT[b][:, j, :], wo[:, j, half * 512:(half + 1) * 512],
                        start=(j == 0), stop=(j == 7)), r=[('oT', b), ('wo', j)], w=[PS(ybank)])
                cs_ = slice(half * 512, (half + 1) * 512)
                p.op('dve', lambda e, b=b, s=s, cs_=cs_, ybank=ybank: e.tensor_tensor(
                    out=O[b][:, cs_], in0=ps[ybank][:, :], in1=G1b[s][:, cs_], op=ALU.mult),
                    r=[PS(ybank), ('G1b', s), ('Ob', b), ('So2', t)], w=[('O', b), ('O2', b)])
                p.op('dve', lambda e, b=b, cs_=cs_: e.tensor_tensor(out=xt[b][:, cs_], in0=xt[b][:, cs_], in1=O[b][:, cs_],
                                                                   op=ALU.add), r=[('O', b), ('xr', b)], w=[('xr', b)])
            yield
            p.dma(lambda e, b=b, rows=rows: e.dma_start(out=S['xs'][rows, :], in_=xt[b][:]), r=[('xr', b)], w=[('Sxs', t)])
        def rr5(gens):
            gens = list(gens)
            while gens:
                for g_ in list(gens):
                    try:
                        next(g_)
                    except StopIteration:
                        gens.remove(g_)

        tl2 = [t for t in range(NT) if not (t < 2 and not with_ctx)]
        for i in range(0, len(tl2), 2):
            rr5([rout_iter(t) for t in tl2[i:i + 2]])
        p.barrier()

    def phase_peer(l):
        p.sb_reset(base_mark)
        last = l == DEPTH - 1
        eu_all = p.sb([128, NT, 128], U32, "eu_all")
        gate_all = p.sb([128, NT, 128], F32, "gate_all")
        G2b = [p.sb([128, D], F32, f"G2b{s}") for s in range(2)]
        m1 = p.sb_mark()
        hT = p.sb([128, 8, NTOK], BF16, "hT2")
        wq = p.sb([128, 8, 2048], BF16, "wq")
        for j in range(8):
            p.dma(lambda e, j=j: e.dma_start(out=wq[:, j, :], in_=I['peer_wq'][l, j * 128:(j + 1) * 128, :]),
                  w=[('wq', j)], eng="pool")
        keysT = p.sb([128, 16, 128], F32, "keysT")
        m0 = p.sb_mark()
        kraw = p.sb([128, 16, 128], F32, "kraw")
        p.dma(lambda e: e.dma_start(out=kraw[:], in_=I['peer_keys'][l].rearrange("h q n d -> n (h q) d")), w=['kraw'])
        for g in range(4):
            for i in range(4):
                hp = g * 4 + i
                p.op('pe', lambda e, g=g, i=i, hp=hp: e.transpose(out=ps[g][:, i * 128:(i + 1) * 128], in_=kraw[:, hp, :],
                                                                 identity=ident_f[:]), r=['kraw', 'identf'], w=[PS(g)])
            p.op('act', lambda e, g=g: e.activation(out=keysT[:, g * 4:(g + 1) * 4, :],
                                                    in_=ps[g][:, :].rearrange("p (i n) -> p i n", i=4), func=AF.Copy),
                 r=[PS(g)], w=['keysT'])
        p.barrier()
        p.sb_reset(m0)
        norm_tiles(l, 1, S['xs'], hT, lambda t: t * 128, tm_dram=S['h2'])
        p.barrier()
        p.sb_reset(m0)
        for s in range(2):
            load_bc(G2b[s][:], S['mod'][l, s, 5 * D:6 * D], ('G2b', s))
        qT = p.sb([128, 16, 128], F32, "qT")
        sc = p.sb([128, 16, 128], F32, "sc")
        sc2 = p.sb([128, 16, 128], F32, "sc2")
        sv = p.sb([128, 16, 16], F32, "sv")
        si = p.sb([128, 16, 16], U32, "si")
        sif = p.sb([128, 16, 16], F32, "sif")
        cand = p.sb([128, 8, 16, 16], F32, "cand")
        cand2 = p.sb([128, 8, 16, 16], F32, "cand2")
        eidx = p.sb([128, 8, 16, 16], F32, "eidx")
        best = p.sb([128, 8, 16], F32, "best")
        ci = p.sb([128, 8, 16], U32, "ci")
        cif = p.sb([128, 8, 16], F32, "cif")
        iota = p.sb([128, 256], F32, "iota")
        p.dma(lambda e: e.dma_start(out=iota[:], in_=I['iota']), w=['iota'])
        eq4 = p.sb([128, 8, 16, 16], F32, "eq4")
        cu = p.sb([128, 2, 8, 16], U32, "cu")
        cf = p.sb([128, 2, 8, 16], F32, "cf")
        e12 = p.sb([128, 2, 8, 16], F32, "e12")
        esel = p.sb([128, 128], F32, "esel")
        g8 = p.sb([128, 8], F32, "g8")
        tiles = [t for t in range(NT) if not (t < 2 and last)]
        for t in tiles:
            for g in range(4):
                for i in range(4):
                    hp = g * 4 + i
                    for j in range(8):
                        p.op('pe', lambda e, g=g, i=i, hp=hp, j=j, t=t: e.matmul(
                            ps[g][:, i * 128:(i + 1) * 128], wq[:, j, hp * 128:(hp + 1) * 128],
                            hT[:, j, t * 128:(t + 1) * 128], start=(j == 0), stop=(j == 7)),
                            r=[('hT', t), ('wq', j)], w=[PS(g)])
                p.op('act', lambda e, g=g: e.activation(out=qT[:, g * 4:(g + 1) * 4, :],
                                                        in_=ps[g][:, :].rearrange("p (i n) -> p i n", i=4), func=AF.Copy),
                     r=[PS(g)], w=['qT'])
            for g in range(4):
                for i in range(4):
                    hp = g * 4 + i
                    p.op('pe', lambda e, g=g, i=i, hp=hp: e.matmul(ps[4 + g][:, i * 128:(i + 1) * 128], qT[:, hp, :],
                                                                   keysT[:, hp, :], start=True, stop=True),
                         r=['qT', 'keysT'], w=[PS(4 + g)])
                p.op('act', lambda e, g=g: e.activation(out=sc[:, g * 4:(g + 1) * 4, :],
                                                        in_=ps[4 + g][:, :].rearrange("p (i n) -> p i n", i=4), func=AF.Copy),
                     r=[PS(4 + g)], w=['sc'])
            SVK = [('sv', hp) for hp in range(16)]
            SIK = [('si', hp) for hp in range(16)]
            for hp in range(16):
                p.op('dve', lambda e, hp=hp: e.max(out=sv[:, hp, 0:8], in_=sc[:, hp, :]), r=['sc'], w=[('sv', hp)])
            for hp in range(16):
                p.op('dve', lambda e, hp=hp: e.max_index(out=si[:, hp, 0:8], in_max=sv[:, hp, 0:8], in_values=sc[:, hp, :]),
                     r=['sc', ('sv', hp)], w=[('si', hp)])
            for hp in range(16):
                p.op('dve', lambda e, hp=hp: e.match_replace(out=sc2[:, hp, :], in_to_replace=sv[:, hp, 0:8],
                                                             in_values=sc[:, hp, :], imm_value=-1e30),
                     r=['sc', ('sv', hp)], w=[('sc2', hp)])
            for hp in range(16):
                p.op('dve', lambda e, hp=hp: e.max(out=sv[:, hp, 8:16], in_=sc2[:, hp, :]), r=[('sc2', hp)], w=[('sv8', hp)])
            for hp in range(16):
                p.op('dve', lambda e, hp=hp: e.max_index(out=si[:, hp, 8:16], in_max=sv[:, hp, 8:16], in_values=sc2[:, hp, :]),
                     r=[('sc2', hp), ('sv8', hp)], w=[('si8', hp)])
            SVK = SVK + [('sv8', hp) for hp in range(16)]
            SIK = SIK + [('si8', hp) for hp in range(16)]
            p.op('dve', lambda e: e.tensor_copy(out=sif[:], in_=si[:]), r=SIK, w=['sif'])
            svv = sv[:].rearrange("p (h q) k -> p h q k", q=2)
            sfv = sif[:].rearrange("p (h q) k -> p h q k", q=2)
            p.op('dve', lambda e, svv=svv: e.tensor_tensor(
                out=cand[:], in0=svv[:, :, 0, :].unsqueeze(3).to_broadcast([128, 8, 16, 16]),
                in1=svv[:, :, 1, :].unsqueeze(2).to_broadcast([128, 8, 16, 16]), op=ALU.add), r=SVK, w=['cand'])
            p.op('dve', lambda e, sfv=sfv: e.tensor_scalar(out=sfv[:, :, 0, :], in0=sfv[:, :, 0, :], scalar1=128.0,
                                                           scalar2=None, op0=ALU.mult), r=['sif'], w=['sif'])
            chs = [cand[:, h].rearrange("p a b -> p (a b)") for h in range(8)]
            ch2s = [cand2[:, h].rearrange("p a b -> p (a b)") for h in range(8)]
            for h in range(8):
                p.op('dve', lambda e, h=h: e.max(out=best[:, h, 0:8], in_=chs[h]), r=['cand'], w=[('best', h)])
            for h in range(8):
                p.op('dve', lambda e, h=h: e.max_index(out=ci[:, h, 0:8], in_max=best[:, h, 0:8], in_values=chs[h]),
                     r=['cand', ('best', h)], w=[('ci', h)])
            for h in range(8):
                p.op('dve', lambda e, h=h: e.match_replace(out=ch2s[h], in_to_replace=best[:, h, 0:8],
                                                           in_values=chs[h], imm_value=-1e30),
                     r=['cand', ('best', h)], w=[('cand2', h)])
            for h in range(8):
                p.op('dve', lambda e, h=h: e.max(out=best[:, h, 8:16], in_=ch2s[h]), r=[('cand2', h)], w=[('best8', h)])
            for h in range(8):
                p.op('dve', lambda e, h=h: e.max_index(out=ci[:, h, 8:16], in_max=best[:, h, 8:16], in_values=ch2s[h]),
                     r=[('cand2', h), ('best8', h)], w=[('ci8', h)])
            BK = [('best', h) for h in range(8)] + [('best8', h) for h in range(8)]
            CIK = [('ci', h) for h in range(8)] + [('ci8', h) for h in range(8)]
            p.op('dve', lambda e: e.tensor_scalar(out=cu[:, 0], in0=ci[:], scalar1=4, scalar2=None,
                                                  op0=ALU.logical_shift_right), r=CIK, w=['cu'])
            p.op('dve', lambda e: e.tensor_scalar(out=cu[:, 1], in0=ci[:], scalar1=15, scalar2=None,
                                                  op0=ALU.bitwise_and), r=CIK, w=['cu'])
            p.op('dve', lambda e: e.tensor_copy(out=cf[:], in_=cu[:]), r=['cu'], w=['cf'])
            io16 = iota[:, 0:16].unsqueeze(1).unsqueeze(1).to_broadcast([128, 8, 16, 16])
            for q in range(2):
                p.op('dve', lambda e, q=q, io16=io16: e.tensor_tensor(
                    out=eq4[:], in0=io16, in1=cf[:, q].unsqueeze(3).to_broadcast([128, 8, 16, 16]), op=ALU.is_equal),
                    r=['iota', 'cf'], w=['eq4'])
                p.op('dve', lambda e, q=q, sfv=sfv: e.tensor_tensor(
                    out=eq4[:], in0=eq4[:], in1=sfv[:, :, q, :].unsqueeze(2).to_broadcast([128, 8, 16, 16]), op=ALU.mult),
                    r=['eq4', 'sif'], w=['eq4'])
                p.op('dve', lambda e, q=q: e.tensor_reduce(out=e12[:, q], in_=eq4[:], axis=AX.X, op=ALU.add),
                     r=['eq4'], w=['e12'])
            p.op('dve', lambda e: e.tensor_tensor(out=esel[:], in0=e12[:, 0].rearrange("p h k -> p (h k)"),
                                                  in1=e12[:, 1].rearrange("p h k -> p (h k)"), op=ALU.add),
                 r=['e12'], w=['esel'])
            p.op('dve', lambda e, t=t: e.tensor_copy(out=eu_all[:, t, :], in_=esel[:]), r=['esel'], w=[('eu', t)])
            gv = gate_all[:, t, :].rearrange("p (h k) -> p h k", h=8)
            p.op('dve', lambda e, gv=gv: e.tensor_tensor(out=gv, in0=best[:],
                                                         in1=best[:, :, 0:1].to_broadcast([128, 8, 16]), op=ALU.subtract),
                 r=BK, w=[('gate', t)])
            p.op('act', lambda e, t=t: e.activation(out=gate_all[:, t, :], in_=gate_all[:, t, :], func=AF.Exp),
                 r=[('gate', t)], w=[('gate', t)])
            p.op('dve', lambda e, gv=gv: e.tensor_reduce(out=g8[:], in_=gv, axis=AX.X, op=ALU.add), r=[('gate', t)], w=['g8'])
            p.op('dve', lambda e: e.reciprocal(out=g8[:], in_=g8[:]), r=['g8'], w=['g8'])
            p.op('dve', lambda e, gv=gv: e.tensor_tensor(out=gv, in0=gv, in1=g8[:].unsqueeze(2).to_broadcast([128, 8, 16]),
                                                         op=ALU.mult), r=[('gate', t), 'g8'], w=[('gate', t)])
        p.barrier()
        p.sb_reset(m1)
        h2 = [p.sb([128, D], F32, f"h2{i}") for i in range(2)]
        xt = [p.sb([128, D], F32, f"xp{i}") for i in range(2)]
        act = [p.sb([128, 128], F32, f"actv{i}") for i in range(2)]
        wg = [p.sb([128, 128], F32, f"wg{i}") for i in range(2)]
        NACC = 1
        acc = [[p.sb([128, D], F32, f"acc{i}{k}") for k in range(NACC)] for i in range(2)]
        junk = p.sb([128, D], BF16, "pjunk")
        NG = 32
        GS = 8
        gbuf = [p.sb([128, 2 * D], BF16, f"gb{i}") for i in range(NG)]
        NDG = 8
        dg = [p.sb([128, 128], BF16, f"dg{i}") for i in range(NDG)]
        gi = 0
        di = 0
        for t in tiles:
            b = t % 2
            s = 1 if t < 2 else 0
            rows = slice(t * 128, (t + 1) * 128)
            p.dma(lambda e, b=b, rows=rows: e.dma_start(out=h2[b][:], in_=S['h2'][rows, :]), w=[('h2', b)])
            p.dma(lambda e, b=b, rows=rows: e.dma_start(out=xt[b][:], in_=S['xs'][rows, :]), w=[('xp', b)])
            p.op('dve', lambda e, b=b: e.memset(act[b][:], 0.0), w=[('actv', b)])
            for g in range(128 // GS):
                ks = []
                for sidx in range(g * GS, (g + 1) * GS):
                    k = gi % NG
                    gi += 1
                    ks.append(k)
                    p.dma(lambda e, k=k, t=t, sidx=sidx: e.indirect_dma_start(
                        out=gbuf[k][:], out_offset=None, in_=S['T'][l],
                        in_offset=bass.IndirectOffsetOnAxis(ap=eu_all[:, t, sidx:sidx + 1], axis=0)),
                        r=[], w=[('gb', k)], eng="pool")
                    p.op('dve', lambda e, k=k, b=b, sidx=sidx: e.scalar_tensor_tensor(
                        out=junk[:], in0=gbuf[k][:, 0:D], scalar=1.0, in1=h2[b][:], op0=ALU.mult, op1=ALU.mult,
                        accum_out=act[b][:, sidx:sidx + 1]), r=[('gb', k), ('h2', b), ('actv', b)], w=[('actc', b, sidx)])
                gs = slice(g * GS, (g + 1) * GS)
                p.op('act', lambda e, b=b, gs=gs: e.activation(out=wg[b][:, gs], in_=act[b][:, gs], func=AF.Gelu),
                     r=[('actc', b, sidx) for sidx in range(g * GS, (g + 1) * GS)], w=[('wg', b, g)])
                p.op('dve', lambda e, b=b, gs=gs, t=t: e.tensor_tensor(out=wg[b][:, gs], in0=wg[b][:, gs],
                                                                      in1=gate_all[:, t, gs], op=ALU.mult),
                     r=[('wg', b, g)], w=[('wg', b, g)])
                for j, sidx in enumerate(range(g * GS, (g + 1) * GS)):
                    k = ks[j]
                    dj = di % NDG
                    di += 1
                    p.op('act', lambda e, dj=dj, b=b, sidx=sidx: e.activation(
                        out=dg[dj][:], in_=ident_f[:], func=AF.Copy, scale=wg[b][:, sidx:sidx + 1]),
                        r=[('wg', b, g), 'identf'], w=[('dg', dj)])
                    for half in range(2):
                        bank = 2 * b + half
                        p.op('pe', lambda e, dj=dj, k=k, half=half, bank=bank, sidx=sidx: e.matmul(
                            ps[bank][:, :], dg[dj][:], gbuf[k][:, D + half * 512:D + (half + 1) * 512],
                            start=(sidx == 0), stop=(sidx == 127)), r=[('dg', dj), ('gb', k)], w=[PS(bank), ('gbr', k, half)])
            a0 = acc[b][0]
            for half in range(2):
                hs_ = slice(half * 512, (half + 1) * 512)
                p.op('dve', lambda e, a0=a0, s=s, b=b, half=half, hs_=hs_: e.tensor_tensor(
                    out=a0[:, hs_], in0=ps[2 * b + half][:, :], in1=G2b[s][:, hs_], op=ALU.mult),
                    r=[PS(2 * b + half), ('G2b', s)], w=[('acc', b, 0)])
            p.op('dve', lambda e, a0=a0, b=b: e.tensor_tensor(out=xt[b][:], in0=xt[b][:], in1=a0[:], op=ALU.add),
                 r=[('acc', b, 0), ('xp', b)], w=[('xp', b)])
            if last:
                p.dma(lambda e, b=b, t=t: e.dma_start(out=out_d[(t - 2) * 128:(t - 1) * 128, :], in_=xt[b][:]),
                      r=[('xp', b)], w=[('outd', t)])
            else:
                p.dma(lambda e, b=b, rows=rows: e.dma_start(out=S['xs'][rows, :], in_=xt[b][:]),
                      r=[('xp', b)], w=[('Sxs', t)])
        p.barrier()

    PHASES = cfg.get("phases", ["proj", "rprep", "scan", "rout", "peer"])

    phase_mod()
    for l in range(cfg.get("layers", DEPTH)):
        if 'proj' in PHASES:
            qkT, Vaug, mp = phase_proj(l)
            phase_attn(l, qkT, Vaug, mp)
        if 'rprep' in PHASES:
            phase_rprep(l)
        if 'scan' in PHASES:
            phase_scan(l)
        if 'rout' in PHASES:
            phase_rout(l)
        if 'peer' in PHASES:
            phase_peer(l)
    p.barrier()
    p.emit()
    return nc


def prep_inputs(inputs):
    f = lambda a: np.ascontiguousarray(np.asarray(a, dtype=np.float32))
    x, c, ctx, c_ctx = f(inputs['x']), f(inputs['c']), f(inputs['ctx']), f(inputs['c_ctx'])
    shared = {}
    for n in ['norm_mix', 'norm_ffn', 'w_mod', 'b_mod', 'w_in', 'w_out', 'a_qnorm', 'a_knorm', 'b_qnorm', 'b_knorm',
              'a_sink']:
        shared[n] = f(inputs[n])
    rpb = f(inputs['b_rpb'])
    btab = np.zeros((DEPTH, 128, NTAB, 4, 128), np.float32)
    bmask = np.zeros((128, NTAB, 128), np.float32)
    for i, (dr, dc, valid) in enumerate(NA_TABS):
        g = rpb[:, :, dr, dc]
        btab[:, :, i, :, :] = np.where(valid[None, None], g, 0.0).transpose(0, 2, 1, 3)
        bmask[:, i, :] = valid
    shared['btab'] = btab
    shared['bmask'] = bmask
    ar = np.arange(128)
    am = np.zeros((128, 2, 128), np.float32)
    am[:, 0, :] = (ar[:, None] >= ar[None, :])
    am[:, 1, :] = (ar[:, None] <= ar[None, :])
    shared['amask'] = am
    shared['ident'] = np.eye(128, dtype=np.float32)
    cos, sin = rope_tables()
    shared['cos'], shared['sin'] = cos, sin
    rc = f(inputs['r7_conv'])
    for n in ['r7_w0', 'r7_a0', 'r7_w2', 'r7_a2', 'r7_g2', 'r7_kk', 'r7_ka', 'r7_lnw', 'r7_lnb', 'r7_rk', 'peer_wq', 'peer_keys']:
        shared[n] = f(inputs[n])
    for l in range(DEPTH):
        shared[f'peer_u{l}'] = f(inputs['peer_u'][l])
        shared[f'peer_v{l}'] = f(inputs['peer_v'][l])
    a64 = np.arange(64)
    tri = np.zeros((64, 2, 64), np.float32)
    tri[:, 0, :] = a64[:, None] <= a64[None, :]
    tri[:, 1, :] = a64[:, None] >= a64[None, :]
    mg = np.zeros((64, 2, 128), np.float32)
    mg[:, 0, 0:64] = a64[:, None] < a64[None, :]
    mg[:, 0, 64:128] = a64[:, None] <= a64[None, :]
    mg[:, 1, 0:64] = a64[:, None] > a64[None, :]
    mg[:, 1, 64:128] = a64[:, None] >= a64[None, :]
    mn = np.zeros((64, 2, 64), np.float32)
    mn[:, 0, :] = a64[None, :] < a64[:, None]
    mn[:, 1, :] = a64[None, :] > a64[:, None]
    shared['tri'], shared['mg'], shared['mn'] = tri, mg, mn
    shared['iota'] = np.ascontiguousarray(np.broadcast_to(np.arange(256, dtype=np.float32), (128, 256)))
    shared['r7_conv'] = np.ascontiguousarray(rc.reshape(DEPTH, 3, 15, 128).transpose(0, 3, 2, 1))
    maps = []
    for b in range(8):
        m = dict(shared)
        m['x'] = np.ascontiguousarray(np.concatenate([ctx[b], x[b]], axis=0))
        cc = np.stack([c[b], c_ctx], axis=-1)
        m['cc'] = np.ascontiguousarray(cc.reshape(8, 128, 2).transpose(1, 0, 2))
        maps.append(m)
    return maps


_NC_CACHE = {}


def kernel(**inputs):
    if 'nc' not in _NC_CACHE:
        _NC_CACHE['nc'] = build({})
    nc = _NC_CACHE['nc']
    maps = prep_inputs(inputs)
    res = run_bass_kernel_spmd(nc, maps, core_ids=list(range(8)))
    return np.stack([np.asarray(r['out'], dtype=np.float32) for r in res.results], axis=0)
```

```python
import numpy as np
import ml_dtypes
import concourse.bass as bass
import concourse.mybir as mybir
from concourse.bass_utils import run_bass_kernel_spmd

F32 = mybir.dt.float32
BF16 = mybir.dt.bfloat16
U32 = mybir.dt.uint32
I32 = mybir.dt.int32
AF = mybir.ActivationFunctionType
ALU = mybir.AluOpType
AX = mybir.AxisListType

ENGS = ["pe", "act", "dve", "pool", "sp"]
DT_SIZE = {F32: 4, BF16: 2, U32: 4, I32: 4}

D = 1024
NCTX = 256
NLAT = 2048
NTOK = NCTX + NLAT
NT = NTOK // 128
DEPTH = 2
EPS = 1e-6


class Prog:
    def __init__(self, nc, n_dma_sems=32):
        self.nc = nc
        self.ops = {e: [] for e in ENGS}
        self.cnt = {e: 0 for e in ENGS}
        self.waited = {e: {} for e in ENGS}
        self.res = {}
        self.n_dma_sems = n_dma_sems
        self.dma_use = [0] * n_dma_sems
        self.dma_last = [None] * n_dma_sems
        self.dma_rr = 0
        self.sb_off = 16 * 1024
        self.sb_id = 0
        self.SB_CAP = 216 * 1024

    def sb_mark(self):
        return self.sb_off

    def sb_reset(self, off=0):
        self.sb_off = off

    def sb(self, shape, dtype, name=""):
        nbytes = int(np.prod(shape[1:])) * DT_SIZE[dtype]
        off = (self.sb_off + 63) // 64 * 64
        assert off + nbytes <= self.SB_CAP, f"SBUF overflow {off}+{nbytes} ({name})"
        self.sb_off = off + nbytes
        self.sb_id += 1
        return self.nc.alloc_sbuf_tensor_at(f"sb{self.sb_id}_{name}", list(shape), dtype, offset=off)

    def _deps(self, r, w):
        deps = []
        for k in r:
            st = self.res.get(k)
            if st and st[0] is not None:
                deps.append(st[0])
        for k in w:
            st = self.res.get(k)
            if st:
                if st[0] is not None:
                    deps.append(st[0])
                deps.extend(st[1])
        return deps

    def _commit(self, tok, r, w):
        for k in r:
            st = self.res.setdefault(k, [None, []])
            st[1].append(tok)
        for k in w:
            self.res[k] = [tok, []]

    def _waits_for(self, eng, deps):
        wd = self.waited[eng]
        best = {}
        for t in deps:
            if t[0] == 'c':
                if t[1] == eng and eng == 'pe':
                    continue
                key = ('c', t[1])
            else:
                key = ('d', t[1])
            if wd.get(key, 0) >= t[2]:
                continue
            best[key] = max(best.get(key, 0), t[2])
        for k, v in best.items():
            wd[k] = v
        return list(best.items())

    def op(self, eng, fn, r=(), w=()):
        deps = self._deps(r, w)
        waits = self._waits_for(eng, deps)
        self.cnt[eng] += 1
        tok = ('c', eng, self.cnt[eng])
        self.ops[eng].append((waits, fn, ('c', eng), 1))
        self._commit(tok, r, w)
        return tok

    def dma(self, fn, r=(), w=(), eng="sp"):
        deps = list(self._deps(r, w))
        i = self.dma_rr
        self.dma_rr = (self.dma_rr + 1) % self.n_dma_sems
        if self.dma_last[i] is not None:
            deps.append(self.dma_last[i])
        waits = self._waits_for(eng, deps)
        self.dma_use[i] += 1
        tok = ('d', i, 16 * self.dma_use[i])
        self.dma_last[i] = tok
        self.ops[eng].append((waits, fn, ('d', i), 16))
        self._commit(tok, r, w)
        return tok

    def barrier(self):
        toks = [('c', e, self.cnt[e]) for e in ENGS if self.cnt[e] > 0]
        toks += [t for t in self.dma_last if t is not None]
        for e in ENGS:
            waits = self._waits_for(e, toks)
            if waits:
                self.ops[e].append((waits, None, None, 0))
        self.res = {}

    def emit(self):
        nc = self.nc
        from contextlib import ExitStack
        with ExitStack() as es:
            csem = {e: es.enter_context(nc.semaphore(f"c_{e}")) for e in ENGS}
            dsem = [es.enter_context(nc.semaphore(f"d_{i}")) for i in range(self.n_dma_sems)]
            block = es.enter_context(nc.Block())

            def sem_of(key):
                return csem[key[1]] if key[0] == 'c' else dsem[key[1]]

            def run(engname, e):
                for waits, fn, inc_key, inc in self.ops[engname]:
                    for k, v in waits:
                        e.wait_ge(sem_of(k), v)
                    if fn is None:
                        continue
                    ins = fn(e)
                    ins.then_inc(sem_of(inc_key), inc)

            @block.tensor
            def _(e):
                run("pe", e)

            @block.scalar
            def _(e):
                run("act", e)

            @block.vector
            def _(e):
                run("dve", e)

            @block.gpsimd
            def _(e):
                run("pool", e)

            @block.sync
            def _(e):
                run("sp", e)


def na_tables():
    cases = {}
    tabs = []
    keys = {}
    ar = np.arange(128)
    for p in range(16):
        for kb in range(16):
            krow = 2 * kb + ar // 64
            kcol = ar % 64
            qrow = 2 * p + ar // 64
            qcol = ar % 64
            rs = np.clip(qrow - 4, 0, 24)
            vr = (krow[:, None] >= rs[None, :]) & (krow[:, None] < rs[None, :] + 8)
            ws = np.clip(qcol - 8, 0, 48)
            vc = (kcol[:, None] >= ws[None, :]) & (kcol[:, None] < ws[None, :] + 16)
            valid = vr & vc
            if not valid.any():
                continue
            dr = krow[:, None] - qrow[None, :] + 7
            dc = np.clip(kcol[:, None] - qcol[None, :] + 15, 0, 30)
            dr = np.where(valid, dr, 0)
            dc = np.where(valid, dc, 0)
            key = (dr.tobytes(), dc.tobytes(), valid.tobytes())
            if key not in keys:
                keys[key] = len(tabs)
                tabs.append((dr, dc, valid))
            cases[(p, kb)] = keys[key]
    return cases, tabs


NA_CASES, NA_TABS = na_tables()
NTAB = len(NA_TABS)


def rope_tables():
    t = np.arange(NLAT)
    inv_freq = 10000.0 ** (-np.arange(0, 32, 2) / 32)
    ang = np.stack([(t // 64)[:, None] * inv_freq[None], (t % 64)[:, None] * inv_freq[None]], axis=1)
    return np.cos(ang).astype(np.float32).reshape(NLAT, 32), np.sin(ang).astype(np.float32).reshape(NLAT, 32)


def build(cfg=None):
    cfg = cfg or {}
    dbg = cfg.get("dbg", [])
    nc = bass.Bass("TRN2", target_bir_lowering=False)
    p = Prog(nc)

    def din(name, shape, dt=F32):
        return nc.dram_tensor(name, list(shape), dt, kind="ExternalInput").ap()

    def dscr(name, shape, dt=F32):
        kind = "Internal"
        if name in cfg.get("dump", []):
            kind = "ExternalOutput"
        if name in cfg.get("feed", []):
            kind = "ExternalInput"
        return nc.dram_tensor(name, list(shape), dt, kind=kind).ap()

    I = {}
    I['x'] = din('x', [NTOK, D])
    I['cc'] = din('cc', [128, 8, 2])
    I['norm_mix'] = din('norm_mix', [DEPTH, D])
    I['norm_ffn'] = din('norm_ffn', [DEPTH, D])
    I['w_mod'] = din('w_mod', [DEPTH, D, 6 * D])
    I['b_mod'] = din('b_mod', [DEPTH, 6 * D])
    I['w_in'] = din('w_in', [DEPTH, D, 3200])
    I['w_out'] = din('w_out', [DEPTH, D, D])
    for n in ['a_qnorm', 'a_knorm', 'b_qnorm', 'b_knorm']:
        I[n] = din(n, [DEPTH, 64])
    I['a_sink'] = din('a_sink', [DEPTH, 4])
    I['btab'] = din('btab', [DEPTH, 128, NTAB, 4, 128])
    I['bmask'] = din('bmask', [128, NTAB, 128])
    I['amask'] = din('amask', [128, 2, 128])
    I['ident'] = din('ident', [128, 128])
    I['cos'] = din('cos', [NLAT, 32])
    I['sin'] = din('sin', [NLAT, 32])
    I['r7_conv'] = din('r7_conv', [DEPTH, 128, 15, 3])
    I['r7_w0'] = din('r7_w0', [DEPTH, 2, 512])
    I['r7_a0'] = din('r7_a0', [DEPTH, 2, 512])
    I['r7_w2'] = din('r7_w2', [DEPTH, 2, 64, 512])
    I['r7_a2'] = din('r7_a2', [DEPTH, 2, 64, 512])
    I['r7_g2'] = din('r7_g2', [DEPTH, 128, 512])
    for n in ['r7_kk', 'r7_ka', 'r7_lnw', 'r7_lnb']:
        I[n] = din(n, [DEPTH, 512])
    I['r7_rk'] = din('r7_rk', [DEPTH, 8, 64])
    I['peer_wq'] = din('peer_wq', [DEPTH, D, 2048])
    I['peer_keys'] = din('peer_keys', [DEPTH, 8, 2, 128, 128])
    I['peer_u'] = [din(f'peer_u{l}', [16384, D]) for l in range(DEPTH)]
    I['peer_v'] = [din(f'peer_v{l}', [16384, D]) for l in range(DEPTH)]
    I['iota'] = din('iota', [128, 256])
    I['tri'] = din('tri', [64, 2, 64])
    I['mg'] = din('mg', [64, 2, 128])
    I['mn'] = din('mn', [64, 2, 64])
    out_d = nc.dram_tensor('out', [NLAT, D], F32, kind="ExternalOutput").ap()

    S = {}
    S['mod'] = dscr('s_mod', [DEPTH, 2, 6 * D])
    S['xs'] = dscr('s_xs', [NTOK, D])
    S['o'] = dscr('s_o', [NTOK, D])
    S['pcT'] = dscr('s_pcT', [1920, NTOK])
    S['tm'] = dscr('s_tm', [NTOK, 10, 512])
    S['bon'] = dscr('s_bon', [NTOK, 8])
    S['y'] = dscr('s_y', [2, NTOK, 512])
    S['h2'] = dscr('s_h2', [NTOK, D])
    S['T'] = [dscr(f's_T{l}', [16384, 2 * D], BF16) for l in range(DEPTH)]
    DBG = {}
    for name, shape in cfg.get("dbg_out", {}).items():
        DBG[name] = nc.dram_tensor(name, list(shape), F32, kind="ExternalOutput").ap()

    ps = [nc.alloc_psum_tensor(f"ps{i}", [128, 512], F32) for i in range(8)]

    def PS(i):
        return ('ps', i)

    def rolling(makers, stagger=6, window=2):
        active = []
        i = 0
        since = stagger
        while i < len(makers) or active:
            if len(active) < window and i < len(makers) and (since >= stagger or not active):
                active.append(makers[i]())
                i += 1
                since = 0
            for g_ in list(active):
                try:
                    next(g_)
                except StopIteration:
                    active.remove(g_)
            since += 1

    ident_f = p.sb([128, 128], F32, "identf")
    ident_b = p.sb([128, 128], BF16, "identb")
    eps_col = p.sb([128, 1], F32, "eps")
    p.dma(lambda e: e.dma_start(out=ident_f[:], in_=I['ident']), w=['identf'])
    p.op('dve', lambda e: e.tensor_copy(out=ident_b[:], in_=ident_f[:]), r=['identf'], w=['identb'])
    p.op('dve', lambda e: e.memset(eps_col[:], EPS), w=['eps'])
    p.barrier()
    base_mark = p.sb_mark()

    def phase_mod():
        p.sb_reset(base_mark)
        cc = p.sb([128, 8, 2], F32, "cc")
        scc = p.sb([128, 8, 2], F32, "scc")
        p.dma(lambda e: e.dma_start(out=cc[:], in_=I['cc']), w=['cc'])
        p.op('act', lambda e: e.activation(out=scc[:], in_=cc[:], func=AF.Silu), r=['cc'], w=['scc'])
        wt = [p.sb([128, 8, 512], F32, f"wmod{i}") for i in range(2)]
        bm = p.sb([2, 6 * D], F32, "bm")
        mo = p.sb([2, 6 * D], F32, "mo")
        k = 0
        for l in range(DEPTH):
            p.dma(lambda e, l=l: e.dma_start(out=bm[:], in_=I['b_mod'][l].partition_broadcast(2)),
                  w=['bm'])
            for cch in range(12):
                b = k % 2
                k += 1
                src = I['w_mod'][l, :, cch * 512:(cch + 1) * 512].rearrange("(j p) n -> p j n", p=128)
                p.dma(lambda e, b=b, src=src: e.dma_start(out=wt[b][:], in_=src), w=[('wmod', b)])
                pb = cch % 2
                for j in range(8):
                    p.op('pe', lambda e, b=b, j=j, pb=pb: e.matmul(ps[pb][0:2, :], scc[:, j, :], wt[b][:, j, :],
                                                                    start=(j == 0), stop=(j == 7)),
                         r=['scc', ('wmod', b)], w=[PS(pb)])
                p.op('dve', lambda e, pb=pb, cch=cch: e.tensor_tensor(
                    out=mo[:, cch * 512:(cch + 1) * 512], in0=ps[pb][0:2, :], in1=bm[:, cch * 512:(cch + 1) * 512],
                    op=ALU.add), r=[PS(pb), 'bm'], w=['mo'])
            p.dma(lambda e, l=l: e.dma_start(out=S['mod'][l], in_=mo[:]), r=['mo'], w=['S_mod'])
        p.barrier()

    def load_bc(dst, src_1d, key):
        P = dst.shape[0]
        p.dma(lambda e: e.dma_start(out=dst, in_=src_1d.partition_broadcast(P)), w=[key])

    def norm_tiles(l, which, src, hT, hT_off, tm_dram=None):
        nv = I['norm_mix'] if which == 0 else I['norm_ffn']
        so = 0 if which == 0 else 3
        G = [p.sb([128, D], F32, f"G{s}") for s in range(2)]
        SH = [p.sb([128, D], F32, f"SH{s}") for s in range(2)]
        tmp = p.sb([128, D], F32, "gtmp")
        for s in range(2):
            load_bc(tmp[:], nv[l], 'gtmp')
            load_bc(G[s][:], S['mod'][l, s, (so + 1) * D:(so + 2) * D], ('G', s))
            load_bc(SH[s][:], S['mod'][l, s, so * D:(so + 1) * D], ('SH', s))
            p.op('dve', lambda e, s=s: e.scalar_tensor_tensor(out=G[s][:], in0=G[s][:], scalar=1.0, in1=tmp[:],
                                                             op0=ALU.add, op1=ALU.mult),
                 r=['gtmp', ('G', s)], w=[('G', s)])
        NBUF = 4
        xt = [p.sb([128, D], F32, f"xt{i}") for i in range(NBUF)]
        junk = p.sb([128, D], F32, "junk")
        hb = [p.sb([128, D], BF16, f"hb{i}") for i in range(NBUF)]
        ss = [p.sb([128, 1], F32, f"ss{i}") for i in range(NBUF)]
        def stageA(t):
            b = t % NBUF
            s = 1 if t < 2 else 0
            yield
            p.dma(lambda e, b=b, t=t: e.dma_start(out=xt[b][:], in_=src[t * 128:(t + 1) * 128, :]), w=[('xt', b)])
            yield
            p.op('act', lambda e, b=b: e.activation(out=junk[:], in_=xt[b][:], func=AF.Square, accum_out=ss[b][:]),
                 r=[('xt', b)], w=[('ss', b)])
            yield
            p.op('act', lambda e, b=b: e.activation(out=ss[b][:], in_=ss[b][:], func=AF.Sqrt, bias=eps_col[:],
                                                    scale=1.0 / D), r=[('ss', b)], w=[('ss', b)])
            yield
            p.op('dve', lambda e, b=b: e.reciprocal(out=ss[b][:], in_=ss[b][:]), r=[('ss', b)], w=[('ss', b)])
            yield
            p.op('dve', lambda e, b=b, s=s: e.scalar_tensor_tensor(out=xt[b][:], in0=xt[b][:], scalar=ss[b][:, 0:1],
                                                                 in1=G[s][:], op0=ALU.mult, op1=ALU.mult),
                 r=[('xt', b), ('ss', b), ('G', s)], w=[('xt', b)])
            yield
            if tm_dram is not None:
                p.op('dve', lambda e, b=b, s=s: e.tensor_tensor(out=xt[b][:], in0=xt[b][:], in1=SH[s][:], op=ALU.add),
                     r=[('xt', b), ('SH', s)], w=[('xt', b)])
                p.dma(lambda e, b=b, t=t: e.dma_start(out=tm_dram[t * 128:(t + 1) * 128, :], in_=xt[b][:]),
                      r=[('xt', b)], w=[('tmd', t)], eng='pool')
                p.op('act', lambda e, b=b: e.activation(out=hb[b][:], in_=xt[b][:], func=AF.Copy),
                     r=[('xt', b)], w=[('hb', b)])
            else:
                p.op('dve', lambda e, b=b, s=s: e.tensor_tensor(out=hb[b][:], in0=xt[b][:], in1=SH[s][:], op=ALU.add),
                     r=[('xt', b), ('SH', s)], w=[('hb', b)])

        def stageB(t):
            b = t % NBUF
            pbank = 4 + b
            pv = ps[pbank][:, 0:512].bitcast(BF16)
            yield
            for j in range(8):
                p.op('pe', lambda e, b=b, j=j, pv=pv: e.transpose(out=pv[:, j * 128:(j + 1) * 128],
                                                                 in_=hb[b][:, j * 128:(j + 1) * 128],
                                                                 identity=ident_b[:]),
                     r=[('hb', b), 'identb'], w=[PS(pbank)])
            o = hT_off(t)
            yield
            p.op('act', lambda e, pv=pv, o=o: e.activation(
                out=hT[:, :, o:o + 128], in_=pv.rearrange("p (j t) -> p j t", j=8), func=AF.Copy),
                r=[PS(pbank)], w=[('hT', t)])


        tl_ = [t for t in range(NT) if not (t < 2 and l == DEPTH - 1 and which == 1)]
        def rr(gens):
            gens = list(gens)
            while gens:
                for g_ in list(gens):
                    try:
                        next(g_)
                    except StopIteration:
                        gens.remove(g_)

        groups = [tl_[i:i + NBUF] for i in range(0, len(tl_), NBUF)]
        for gi_ in range(len(groups) + 1):
            gl = []
            if gi_ < len(groups):
                gl += [stageA(t) for t in groups[gi_]]
            if gi_ > 0:
                gl += [stageB(t) for t in groups[gi_ - 1]]
            rr(gl)

    def phase_proj(l):
        p.sb_reset(base_mark)
        qkT = p.sb([64, 14, NTOK], BF16, "qkT")
        Vaug = p.sb([128, NT, 6, 65], BF16, "Vaug")
        mark_persist = p.sb_mark()
        hT = p.sb([128, 8, NTOK], BF16, "hT")
        m_afterh = p.sb_mark()
        wAB = p.sb([128, 8, 1280], BF16, "wAB")
        for j in range(8):
            p.dma(lambda e, j=j: e.dma_start(out=wAB[:, j, :], in_=I['w_in'][l, j * 128:(j + 1) * 128, 0:1280]),
                  w=[('wAB', j)], eng="pool")
        p.op('pool', lambda e: e.memset(Vaug[:, :, :, 64:65], 1.0), w=['Vones'])
        m0 = p.sb_mark()
        norm_tiles(l, 0, I['x'] if l == 0 else S['xs'], hT, lambda t: t * 128)
        p.barrier()
        p.sb_reset(m0)
        wC = p.sb([128, 8, 1920], BF16, "wC")
        for j in range(8):
            p.dma(lambda e, j=j: e.dma_start(out=wC[:, j, :], in_=I['w_in'][l, j * 128:(j + 1) * 128, 1280:3200]),
                  w=[('wC', j)], eng="pool")
        m_afterwc = p.sb_mark()
        GA = p.sb([128, 6, 64], F32, "GA")
        GB = p.sb([128, 8, 64], F32, "GB")
        for h in range(6):
            load_bc(GA[:, h, :], I['a_qnorm'][l] if h < 4 else I['a_knorm'][l], 'GA')
        for h in range(8):
            load_bc(GB[:, h, :], I['b_qnorm'][l] if h < 4 else I['b_knorm'][l], 'GB')
        p.op('act', lambda e: e.mul(out=GA[:, 0:4, :], in_=GA[:, 0:4, :], mul=0.125), r=['GA'], w=['GA'])
        p.op('act', lambda e: e.mul(out=GB[:, 0:4, :], in_=GB[:, 0:4, :], mul=0.125), r=['GB'], w=['GB'])
        cs = [p.sb([128, 2, 32], F32, f"cs{i}") for i in range(2)]
        xn = [p.sb([128, 14, 64], F32, f"xn{i}") for i in range(2)]
        sq = [p.sb([128, 14, 64], F32, f"sq{i}") for i in range(2)]
        ssq = [p.sb([128, 14], F32, f"ssq{i}") for i in range(2)]
        xr = [p.sb([128, 14, 64], BF16, f"xr{i}") for i in range(2)]
        RT = [[p.sb([128, 6, 2, 16], F32, f"ropeT{b_}{i}") for i in range(4)] for b_ in range(2)]
        def qk_iter(t):
            b = t % 2
            lat = t >= 2
            bA, bB, bV = (0, 1, 2) if t % 2 == 0 else (3, 6, 7)
            yield
            for bank, c0, c1 in ((bA, 0, 512), (bB, 512, 1024), (bV, 1024, 1280)):
                for j in range(8):
                    p.op('pe', lambda e, bank=bank, c0=c0, c1=c1, j=j, t=t: e.matmul(
                        ps[bank][:, 0:c1 - c0], hT[:, j, t * 128:(t + 1) * 128], wAB[:, j, c0:c1],
                        start=(j == 0), stop=(j == 7)),
                        r=[('hT', t), ('wAB', j)], w=[PS(bank)])
            yield
            if lat:
                tl = t - 2
                p.dma(lambda e, b=b, tl=tl: e.dma_start(out=cs[b][:, 0, :], in_=I['cos'][tl * 128:(tl + 1) * 128, :]),
                      w=[('cs', b)])
                p.dma(lambda e, b=b, tl=tl: e.dma_start(out=cs[b][:, 1, :], in_=I['sin'][tl * 128:(tl + 1) * 128, :]),
                      w=[('cs', b)])
            yield
            p.op('act', lambda e, t=t, bA=bA: e.activation(out=Vaug[:, t, 0:2, 0:64],
                                                    in_=ps[bA][:, 384:512].rearrange("p (h d) -> p h d", h=2),
                                                    func=AF.Copy), r=[PS(bA)], w=[('V', t)])
            yield
            p.op('act', lambda e, t=t, bV=bV: e.activation(out=Vaug[:, t, 2:6, 0:64],
                                                    in_=ps[bV][:, 0:256].rearrange("p (h d) -> p h d", h=4),
                                                    func=AF.Copy), r=[PS(bV)], w=[('V', t)])
            yield
            p.op('act', lambda e, b=b, bA=bA: e.activation(out=xn[b][:, 0:6, :],
                                                    in_=ps[bA][:, 0:384].rearrange("p (h d) -> p h d", h=6),
                                                    func=AF.Copy), r=[PS(bA)], w=[('xn', b)])
            yield
            p.op('act', lambda e, b=b, bB=bB: e.activation(out=xn[b][:, 6:14, :],
                                                    in_=ps[bB][:, 0:512].rearrange("p (h d) -> p h d", h=8),
                                                    func=AF.Copy), r=[PS(bB)], w=[('xn', b)])
            yield
            p.op('dve', lambda e, b=b: e.tensor_tensor(out=sq[b][:], in0=xn[b][:], in1=xn[b][:], op=ALU.mult),
                 r=[('xn', b)], w=[('sq', b)])
            yield
            p.op('dve', lambda e, b=b: e.tensor_reduce(out=ssq[b][:], in_=sq[b][:], axis=AX.X, op=ALU.add),
                 r=[('sq', b)], w=[('ssq', b)])
            yield
            p.op('act', lambda e, b=b: e.activation(out=ssq[b][:], in_=ssq[b][:], func=AF.Sqrt, bias=eps_col[:],
                                                    scale=1.0 / 64), r=[('ssq', b)], w=[('ssq', b)])
            yield
            p.op('dve', lambda e, b=b: e.reciprocal(out=ssq[b][:], in_=ssq[b][:]), r=[('ssq', b)], w=[('ssq', b)])
            yield
            p.op('dve', lambda e, b=b: e.tensor_tensor(out=xn[b][:], in0=xn[b][:],
                                                       in1=ssq[b][:].unsqueeze(2).to_broadcast([128, 14, 64]),
                                                       op=ALU.mult), r=[('xn', b), ('ssq', b)], w=[('xn', b)])
            yield
            p.op('dve', lambda e, b=b: e.tensor_tensor(out=xr[b][:, 6:14, :], in0=xn[b][:, 6:14, :], in1=GB[:],
                                                       op=ALU.mult), r=[('xn', b), 'GB'], w=[('xr', b)])
            yield
            if lat:
                p.op('dve', lambda e, b=b: e.tensor_tensor(out=xn[b][:, 0:6, :], in0=xn[b][:, 0:6, :], in1=GA[:],
                                                           op=ALU.mult), r=[('xn', b), 'GA'], w=[('xn', b)])
                xv = xn[b][:, 0:6, :].rearrange("p h (a g f) -> p h a g f", a=2, g=2)
                x1 = xv[:, :, :, 0, :]
                x2 = xv[:, :, :, 1, :]
                ov = xr[b][:, 0:6, :].rearrange("p h (a g f) -> p h a g f", a=2, g=2)
                cosb = cs[b][:, 0, :].rearrange("p (a f) -> p a f", a=2).unsqueeze(1).to_broadcast([128, 6, 2, 16])
                sinb = cs[b][:, 1, :].rearrange("p (a f) -> p a f", a=2).unsqueeze(1).to_broadcast([128, 6, 2, 16])
                rk = [('xn', b), ('cs', b)]
                for i, (xa, tb) in enumerate(((x1, cosb), (x2, sinb), (x2, cosb), (x1, sinb))):
                    p.op('dve', lambda e, i=i, xa=xa, tb=tb, b=b: e.tensor_tensor(out=RT[b][i][:], in0=xa, in1=tb, op=ALU.mult),
                         r=rk, w=[('RT', b, i)])
                p.op('dve', lambda e, ov=ov, b=b: e.tensor_tensor(out=ov[:, :, :, 0, :], in0=RT[b][0][:], in1=RT[b][1][:],
                                                             op=ALU.subtract), r=[('RT', b, 0), ('RT', b, 1)], w=[('xr', b)])
                p.op('dve', lambda e, ov=ov, b=b: e.tensor_tensor(out=ov[:, :, :, 1, :], in0=RT[b][2][:], in1=RT[b][3][:],
                                                             op=ALU.add), r=[('RT', b, 2), ('RT', b, 3)], w=[('xr', b)])
            else:
                p.op('dve', lambda e, b=b: e.tensor_tensor(out=xr[b][:, 0:6, :], in0=xn[b][:, 0:6, :], in1=GA[:],
                                                           op=ALU.mult), r=[('xn', b), 'GA'], w=[('xr', b)])
            yield
            for half in range(2):
                bank = 4 + half
                pv = ps[bank][0:64, 0:448].bitcast(BF16)
                for hh in range(7):
                    h = half * 7 + hh
                    p.op('pe', lambda e, b=b, h=h, hh=hh, pv=pv: e.transpose(
                        out=pv[:, hh * 128:(hh + 1) * 128], in_=xr[b][:, h, :], identity=ident_b[:]),
                        r=[('xr', b), 'identb'], w=[PS(bank)])
                p.op('act', lambda e, half=half, pv=pv, t=t: e.activation(
                    out=qkT[:, half * 7:(half + 1) * 7, t * 128:(t + 1) * 128],
                    in_=pv.rearrange("p (h t) -> p h t", h=7), func=AF.Copy), r=[PS(bank)], w=[('qkT', t)])
        def rr3(gens):
            gens = list(gens)
            while gens:
                for g_ in list(gens):
                    try:
                        next(g_)
                    except StopIteration:
                        gens.remove(g_)

        rolling([(lambda t=t: qk_iter(t)) for t in range(NT)], stagger=8)
        p.barrier()
        p.sb_reset(m_afterwc)
        cw = p.sb([128, 15, 3], F32, "cw")
        p.dma(lambda e: e.dma_start(out=cw[:], in_=I['r7_conv'][l]), w=['cw'])
        rawc = [p.sb([128, NCTX + 2], F32, f"rawc{i}") for i in range(2)]
        rawl = [p.sb([128, NLAT + 2], F32, f"rawl{i}") for i in range(2)]
        cvo = [p.sb([128, NTOK], F32, f"cvo{i}") for i in range(2)]
        for i in range(2):
            p.op('pool', lambda e, i=i: e.memset(rawc[i][:], 0.0), w=[('rawc', i)])
            p.op('pool', lambda e, i=i: e.memset(rawl[i][:], 0.0), w=[('rawl', i)])
        def cproj_iter(ch):
            b = ch % 2
            groups = [(rawc[b], ('rawc', b), 1, 0, 256)] + [(rawl[b], ('rawl', b), 1 + 512 * g, 256 + 512 * g, 512)
                                                             for g in range(4)]
            yield
            for gidx_, (raw, rkey, ro, tok0, n) in enumerate(groups):
                bank = 2 * (ch % 2) + gidx_ % 2
                for j in range(8):
                    p.op('pe', lambda e, bank=bank, j=j, ch=ch, tok0=tok0, n=n: e.matmul(
                        ps[bank][:, 0:n], wC[:, j, ch * 128:(ch + 1) * 128], hT[:, j, tok0:tok0 + n],
                        start=(j == 0), stop=(j == 7)), r=[('wC', j)], w=[PS(bank)])
                p.op('act', lambda e, raw=raw, ro=ro, n=n, bank=bank: e.activation(
                    out=raw[:, ro:ro + n], in_=ps[bank][:, 0:n], func=AF.Copy), r=[PS(bank)], w=[rkey])
            yield
            for (raw, rkey, n, o0) in ((rawc[b], ('rawc', b), NCTX, 0), (rawl[b], ('rawl', b), NLAT, NCTX)):
                dst = cvo[b][:, o0:o0 + n]
                p.op('dve', lambda e, raw=raw, n=n, dst=dst, ch=ch: e.tensor_scalar(
                    out=dst, in0=raw[:, 1:1 + n], scalar1=cw[:, ch, 1:2], scalar2=None, op0=ALU.mult),
                    r=[rkey, 'cw'], w=[('cvo', b)])
                p.op('dve', lambda e, raw=raw, n=n, dst=dst, ch=ch: e.scalar_tensor_tensor(
                    out=dst, in0=raw[:, 0:n], scalar=cw[:, ch, 0:1], in1=dst, op0=ALU.mult, op1=ALU.add),
                    r=[rkey, 'cw'], w=[('cvo', b)])
                p.op('dve', lambda e, raw=raw, n=n, dst=dst, ch=ch: e.scalar_tensor_tensor(
                    out=dst, in0=raw[:, 2:2 + n], scalar=cw[:, ch, 2:3], in1=dst, op0=ALU.mult, op1=ALU.add),
                    r=[rkey, 'cw'], w=[('cvo', b)])
            yield
            if ch == 12:
                p.op('act', lambda e, b=b: e.activation(out=cvo[b][:], in_=cvo[b][:], func=AF.Tanh),
                     r=[('cvo', b)], w=[('cvo', b)])
            yield
            if ch == 14:
                p.op('act', lambda e, b=b: e.activation(out=cvo[b][:], in_=cvo[b][:], func=AF.Sigmoid),
                     r=[('cvo', b)], w=[('cvo', b)])
            yield
            p.dma(lambda e, b=b, ch=ch: e.dma_start(out=S['pcT'][ch * 128:(ch + 1) * 128, :], in_=cvo[b][:]),
                  r=[('cvo', b)], w=[('pcT', ch)])
        def rr4(gens):
            gens = list(gens)
            while gens:
                for g_ in list(gens):
                    try:
                        next(g_)
                    except StopIteration:
                        gens.remove(g_)

        rolling([(lambda ch=ch: cproj_iter(ch)) for ch in range(15)], stagger=5)
        p.barrier()
        return qkT, Vaug, mark_persist

    def phase_attn(l, qkT, Vaug, mark_persist):
        with_ctx = l < DEPTH - 1
        p.sb_reset(mark_persist)
        btab = p.sb([128, NTAB, 4, 128], F32, "btab")
        bmask = p.sb([128, NTAB, 128], F32, "bmask")
        amask = p.sb([128, 2, 128], F32, "amask")
        esink = p.sb([128, 4], F32, "esink")
        o_all = [p.sb([128, 512], F32, f"oall{i}") for i in range(2)]
        ex = [p.sb([128, 8, 128], F32, f"ex{i}") for i in range(2)]
        pT = [p.sb([128, 8, 128], BF16, f"pT{i}") for i in range(2)]
        den = [p.sb([128, 1], F32, f"den{i}") for i in range(2)]
        for tb in range(NTAB):
            p.dma(lambda e, tb=tb: e.dma_start(out=btab[:, tb], in_=I['btab'][l, :, tb]), w=['btab'])
        p.dma(lambda e: e.dma_start(out=bmask[:], in_=I['bmask']), w=['bmask'])
        p.dma(lambda e: e.dma_start(out=amask[:], in_=I['amask']), w=['amask'])
        load_bc(esink[:], I['a_sink'][l], 'esink')
        p.op('act', lambda e: e.activation(out=esink[:], in_=esink[:], func=AF.Exp), r=['esink'], w=['esink'])
        p.op('act', lambda e: e.activation(out=btab[:], in_=btab[:], func=AF.Exp), r=['btab'], w=['btab'])
        for h in range(4):
            p.op('dve', lambda e, h=h: e.tensor_tensor(out=btab[:, :, h, :], in0=btab[:, :, h, :], in1=bmask[:],
                                                       op=ALU.mult), r=['btab', 'bmask'], w=['btab'])
        def attn_iter(t, grp, h, b, ob):
            if grp == 0:
                qs, ks, vs = h, 4 + h // 2, h // 2
            else:
                qs, ks, vs = 6 + h, 10 + h, 2 + h
            if t < 2:
                blocks = [(0, None), (1, None)]
            elif grp == 0:
                n = t - 2
                blocks = [(t, None), (0, None), (1, None)]
                if n > 0:
                    blocks.append((t - 1, amask[:, 0, :]))
                if n < 15:
                    blocks.append((t + 1, amask[:, 1, :]))
            else:
                pq = t - 2
                blocks = [(0, None), (1, None)]
                for kb in range(16):
                    if (pq, kb) in NA_CASES:
                        blocks.append((kb + 2, btab[:, NA_CASES[(pq, kb)], h, :]))
            nb = len(blocks)
            nn = sum(1 for _, tb in blocks if tb is None)
            sb0, sb1 = (0, 1) if b == 0 else (2, 3)
            ob_ps = 4 + b
            yield
            for i, (kt, tb) in enumerate(blocks):
                bank = sb0 if i < 4 else sb1
                p.op('pe', lambda e, bank=bank, i=i, kt=kt, ks=ks, qs=qs, t=t: e.matmul(
                    ps[bank][:, (i % 4) * 128:(i % 4 + 1) * 128], qkT[:, ks, kt * 128:(kt + 1) * 128],
                    qkT[:, qs, t * 128:(t + 1) * 128], start=True, stop=True), w=[PS(bank)])
            n0 = min(nb, 4)
            yield
            p.op('act', lambda e, b=b, n0=n0, sb0=sb0: e.activation(
                out=ex[b][:, 0:n0, :], in_=ps[sb0][:, 0:n0 * 128].rearrange("p (n k) -> p n k", n=n0),
                func=AF.Exp), r=[PS(sb0)], w=[('ex', b)])
            if nb > 4:
                n1 = nb - 4
                p.op('act', lambda e, b=b, n1=n1, sb1=sb1: e.activation(
                    out=ex[b][:, 4:4 + n1, :], in_=ps[sb1][:, 0:n1 * 128].rearrange("p (n k) -> p n k", n=n1),
                    func=AF.Exp), r=[PS(sb1)], w=[('ex', b)])
            yield
            p.op('pool', lambda e, b=b, nn=nn: e.tensor_copy(out=pT[b][:, 0:nn, :], in_=ex[b][:, 0:nn, :]),
                 r=[('ex', b)], w=[('pT', b)])
            yield
            for i, (kt, tb) in enumerate(blocks):
                if tb is None:
                    continue
                p.op('dve', lambda e, b=b, i=i, tb=tb: e.tensor_tensor(out=pT[b][:, i, :], in0=ex[b][:, i, :],
                                                                      in1=tb, op=ALU.mult),
                     r=[('ex', b), 'btab', 'amask'], w=[('pT', b)])
            yield
            for i, (kt, tb) in enumerate(blocks):
                p.op('pe', lambda e, b=b, i=i, kt=kt, vs=vs, ob_ps=ob_ps, nb=nb: e.matmul(
                    ps[ob_ps][:, 0:65], pT[b][:, i, :], Vaug[:, kt, vs, :], start=(i == 0), stop=(i == nb - 1)),
                    r=[('pT', b)], w=[PS(ob_ps)])
            if grp == 0:
                p.op('dve', lambda e, b=b, h=h, ob_ps=ob_ps: e.tensor_scalar(
                    out=den[b][:], in0=ps[ob_ps][:, 64:65], scalar1=esink[:, h:h + 1], scalar2=None,
                    op0=ALU.add), r=[PS(ob_ps), 'esink'], w=[('den', b)])
                p.op('dve', lambda e, b=b: e.reciprocal(out=den[b][:], in_=den[b][:]),
                     r=[('den', b)], w=[('den', b)])
            else:
                p.op('dve', lambda e, b=b, ob_ps=ob_ps: e.reciprocal(out=den[b][:], in_=ps[ob_ps][:, 64:65]),
                     r=[PS(ob_ps)], w=[('den', b)])
            col = grp * 256 + h * 64
            yield
            p.op('dve', lambda e, b=b, ob=ob, col=col, ob_ps=ob_ps: e.tensor_scalar(
                out=o_all[ob][:, col:col + 64], in0=ps[ob_ps][:, 0:64], scalar1=den[b][:, 0:1], scalar2=None,
                op0=ALU.mult), r=[PS(ob_ps), ('den', b)], w=[('oall', ob)])

        def rr2(gens):
            gens = list(gens)
            while gens:
                for g_ in list(gens):
                    try:
                        next(g_)
                    except StopIteration:
                        gens.remove(g_)

        it = 0
        all_its = []
        for t in range(NT):
            if t < 2 and not with_ctx:
                continue
            ob = t % 2
            for grp in range(2):
                for h in range(4):
                    all_its.append((t, grp, h, it % 2, ob, grp == 1 and h == 3))
                    it += 1

        def attn_wrap(t, grp, h, b, ob, is_last):
            yield from attn_iter(t, grp, h, b, ob)
            if is_last:
                p.dma(lambda e, ob=ob, t=t: e.dma_start(out=S['o'][t * 128:(t + 1) * 128, 0:512], in_=o_all[ob][:]),
                      r=[('oall', ob)], w=[('So', t)])

        rolling([(lambda a=a: attn_wrap(*a)) for a in all_its], stagger=3)
        p.barrier()


    def phase_rprep(l):
        p.sb_reset(base_mark)
        w2 = p.sb([128, 512], F32, "w2")
        a2 = p.sb([128, 512], F32, "a2")
        g2 = p.sb([128, 512], F32, "g2")
        w0 = p.sb([1, 2, 512], F32, "w0")
        a0 = p.sb([1, 2, 512], F32, "a0")
        ones = p.sb([1, 128], F32, "ones")
        KKW = p.sb([128, 512], F32, "KKW")
        KA = p.sb([128, 512], F32, "KA")
        RK = p.sb([128, 512], F32, "RK")
        p.dma(lambda e: e.dma_start(out=w2[:], in_=I['r7_w2'][l].rearrange("d r c -> (d r) c")), w=['w2'])
        p.dma(lambda e: e.dma_start(out=a2[:], in_=I['r7_a2'][l].rearrange("d r c -> (d r) c")), w=['a2'])
        p.dma(lambda e: e.dma_start(out=g2[:], in_=I['r7_g2'][l]), w=['g2'])
        p.dma(lambda e: e.dma_start(out=w0[:], in_=I['r7_w0'][l:l + 1]), w=['w0'])
        p.dma(lambda e: e.dma_start(out=a0[:], in_=I['r7_a0'][l:l + 1]), w=['a0'])
        p.op('dve', lambda e: e.memset(ones[:], 1.0), w=['ones'])
        load_bc(KKW[:], I['r7_kk'][l], 'KKW')
        load_bc(KA[:], I['r7_ka'][l], 'KA')
        load_bc(RK[:], I['r7_rk'][l].rearrange("h d -> (h d)"), 'RK')
        fm = [p.sb([128, 15, 128], F32, f"fm{i}") for i in range(2)]
        TM = [p.sb([128, 10, 512], F32, f"TM{i}") for i in range(2)]
        kt = p.sb([128, 512], F32, "kt")
        av = [p.sb([128, 512], F32, f"av{i}") for i in range(2)]
        tmp = p.sb([128, 512], F32, "tmp")
        tmp2 = p.sb([128, 512], F32, "tmp2")
        s8 = p.sb([128, 8], F32, "s8")
        bs = [p.sb([128, 8], F32, f"bs{i}") for i in range(2)]
        for t in range(NT):
            b = t % 2
            p.dma(lambda e, b=b, t=t: e.dma_start(
                out=fm[b][:], in_=S['pcT'][:, t * 128:(t + 1) * 128].rearrange("(c p) t -> p c t", p=128)),
                w=[('fm', b)])
            for q in range(3):
                for c4 in range(4):
                    p.op('pe', lambda e, b=b, q=q, c4=c4: e.transpose(
                        out=ps[q][:, c4 * 128:(c4 + 1) * 128], in_=fm[b][:, q * 4 + c4, :], identity=ident_f[:]),
                        r=[('fm', b), 'identf'], w=[PS(q)])
            for d in range(2):
                pr = slice(d * 64, d * 64 + 64)
                p.op('pe', lambda e, b=b, d=d, pr=pr: e.matmul(ps[3 + d][:, :], fm[b][pr, 12, :], w2[pr, :],
                                                              start=True, stop=False), r=[('fm', b), 'w2'], w=[PS(3 + d)])
                p.op('pe', lambda e, d=d: e.matmul(ps[3 + d][:, :], ones[0:1, :], w0[0:1, d, :], start=False, stop=True),
                     r=['ones', 'w0'], w=[PS(3 + d)])
                p.op('pe', lambda e, b=b, d=d, pr=pr: e.matmul(ps[5 + d][:, :], fm[b][pr, 13, :], a2[pr, :],
                                                              start=True, stop=False), r=[('fm', b), 'a2'], w=[PS(5 + d)])
                p.op('pe', lambda e, d=d: e.matmul(ps[5 + d][:, :], ones[0:1, :], a0[0:1, d, :], start=False, stop=True),
                     r=['ones', 'a0'], w=[PS(5 + d)])
            p.op('pe', lambda e, b=b: e.matmul(ps[7][:, :], fm[b][:, 14, :], g2[:], start=True, stop=True),
                 r=[('fm', b), 'g2'], w=[PS(7)])
            T = TM[b]
            wk = [('TM', b)]
            p.op('act', lambda e, T=T: e.activation(out=T[:, 0, :], in_=ps[0][:, :], func=AF.Copy), r=[PS(0)], w=wk)
            p.op('act', lambda e: e.activation(out=kt[:], in_=ps[1][:, :], func=AF.Copy), r=[PS(1)], w=['kt'])
            p.op('act', lambda e, T=T: e.activation(out=T[:, 1, :], in_=ps[2][:, :], func=AF.Copy), r=[PS(2)], w=wk)
            p.op('act', lambda e, T=T: e.activation(out=T[:, 2, :], in_=ps[7][:, :], func=AF.Copy), r=[PS(7)], w=wk)
            for d in range(2):
                p.op('act', lambda e, T=T, d=d: e.activation(out=T[:, 8 + d, :], in_=ps[3 + d][:, :], func=AF.Sigmoid),
                     r=[PS(3 + d)], w=wk)
                p.op('act', lambda e, d=d: e.activation(out=av[d][:], in_=ps[5 + d][:, :], func=AF.Sigmoid),
                     r=[PS(5 + d)], w=[('av', d)])
                p.op('dve', lambda e, T=T, d=d: e.tensor_scalar(out=T[:, 8 + d, :], in0=T[:, 8 + d, :],
                                                                scalar1=-0.6065306597126334, scalar2=None, op0=ALU.mult),
                     r=wk, w=wk)
            p.op('dve', lambda e: e.tensor_tensor(out=tmp[:], in0=kt[:], in1=KKW[:], op=ALU.mult), r=['kt', 'KKW'], w=['tmp'])
            p.op('dve', lambda e: e.tensor_tensor(out=tmp2[:], in0=tmp[:], in1=tmp[:], op=ALU.mult), r=['tmp'], w=['tmp2'])
            p.op('dve', lambda e: e.tensor_reduce(out=s8[:], in_=tmp2[:].rearrange("p (h d) -> p h d", h=8), axis=AX.X,
                                                  op=ALU.add), r=['tmp2'], w=['s8'])
            p.op('act', lambda e: e.activation(out=s8[:], in_=s8[:], func=AF.Sqrt), r=['s8'], w=['s8'])
            p.op('dve', lambda e: e.tensor_scalar(out=s8[:], in0=s8[:], scalar1=1e-12, scalar2=None, op0=ALU.max),
                 r=['s8'], w=['s8'])
            p.op('dve', lambda e: e.reciprocal(out=s8[:], in_=s8[:]), r=['s8'], w=['s8'])
            p.op('dve', lambda e, T=T: e.tensor_tensor(out=T[:, 3, :].rearrange("p (h d) -> p h d", h=8),
                                                       in0=tmp[:].rearrange("p (h d) -> p h d", h=8),
                                                       in1=s8[:].unsqueeze(2).to_broadcast([128, 8, 64]), op=ALU.mult),
                 r=['tmp', 's8'], w=wk)
            for d in range(2):
                p.op('dve', lambda e, d=d: e.scalar_tensor_tensor(out=tmp2[:], in0=av[d][:], scalar=-1.0, in1=KA[:],
                                                                  op0=ALU.add, op1=ALU.mult),
                     r=[('av', d), 'KA'], w=['tmp2'])
                p.op('dve', lambda e, T=T, d=d: e.scalar_tensor_tensor(out=T[:, 4 + d, :], in0=tmp2[:], scalar=1.0,
                                                                       in1=kt[:], op0=ALU.add, op1=ALU.mult),
                     r=['tmp2', 'kt'], w=wk)
                p.op('dve', lambda e, T=T, d=d: e.tensor_tensor(out=T[:, 6 + d, :], in0=T[:, 3, :], in1=av[d][:],
                                                                op=ALU.mult), r=wk + [('av', d)], w=wk)
            p.op('dve', lambda e, T=T: e.tensor_tensor(out=tmp[:], in0=T[:, 4, :], in1=T[:, 5, :], op=ALU.add),
                 r=wk, w=['tmp'])
            p.op('dve', lambda e: e.tensor_tensor(out=tmp[:], in0=tmp[:], in1=RK[:], op=ALU.mult), r=['tmp', 'RK'], w=['tmp'])
            p.op('dve', lambda e, T=T: e.tensor_tensor(out=tmp[:], in0=tmp[:], in1=T[:, 0, :], op=ALU.mult),
                 r=['tmp'] + wk, w=['tmp'])
            p.op('dve', lambda e, b=b: e.tensor_reduce(out=bs[b][:], in_=tmp[:].rearrange("p (h d) -> p h d", h=8),
                                                       axis=AX.X, op=ALU.add), r=['tmp'], w=[('bs', b)])
            p.dma(lambda e, T=T, t=t: e.dma_start(out=S['tm'][t * 128:(t + 1) * 128], in_=T[:]), r=wk, w=[('Stm', t)],
                  eng='pool')
            p.dma(lambda e, b=b, t=t: e.dma_start(out=S['bon'][t * 128:(t + 1) * 128], in_=bs[b][:]),
                  r=[('bs', b)], w=[('Sbon', t)], eng='pool')
        p.barrier()

    def phase_scan(l):
        p.sb_reset(base_mark)
        PSB = ps
        C = 64
        NCH = NTOK // C
        tri = p.sb([64, 2, 64], F32, "tri")
        mg = p.sb([64, 2, 128], F32, "mg")
        mn = p.sb([64, 2, 64], F32, "mn")
        ones = p.sb([64, 1], F32, "ones1")
        p.dma(lambda e: e.dma_start(out=tri[:], in_=I['tri']), w=['tri'])
        p.dma(lambda e: e.dma_start(out=mg[:], in_=I['mg']), w=['mg'])
        p.dma(lambda e: e.dma_start(out=mn[:], in_=I['mn']), w=['mn'])
        p.op('dve', lambda e: e.memset(ones[:], 1.0), w=['ones1'])
        M = [p.sb([64, 8, 64], F32, f"M{d}") for d in range(2)]
        for d in range(2):
            M0_PLACEHOLDER = None
        X = [[p.sb([64, 6, 512], F32, f"X{d}{i}") for i in range(2)] for d in range(2)]
        def mk(shape, name):
            return [p.sb(shape, F32, f"{name}{d}") for d in range(2)]
        E0s, E1s, E2s = mk([64, 512], "E0"), mk([64, 512], "E1"), mk([64, 512], "E2")
        Ats, Rts, Bts, Kts = mk([64, 512], "At"), mk([64, 512], "Rt"), mk([64, 512], "Bt"), mk([64, 512], "Kt")
        FARs, FBs, FKs = mk([64, 8, 128], "FAR"), mk([64, 8, 64], "FB"), mk([64, 8, 64], "FK")
        G1s, G2s = mk([64, 8, 128], "G1"), mk([64, 8, 128], "G2")
        Tms = [mk([64, 8, 64], f"Tm{i}_") for i in range(2)]
        Nms = [mk([64, 8, 64], f"Nm{i}_") for i in range(2)]
        Zs, Wss, Uss, PCs = mk([64, 8, 64], "Z"), mk([64, 512], "Ws"), mk([64, 512], "Us"), mk([64, 8], "PC")
        Ys = [p.sb([64, 512], F32, f"Ys{d}") for d in range(2)]
        order = {0: list(range(0, 4)) + list(range(4, NCH)), 1: list(range(3, -1, -1)) + list(range(NCH - 1, 3, -1))}
        v3 = lambda ap: ap.rearrange("p (h d) -> p h d", h=8)
        F32R = mybir.dt.float32r
        use_r = cfg.get("fp32r", True)

        def RR(ap):
            return ap.bitcast(F32R) if use_r else ap

        Vrs = mk([64, 512], "Vr")
        Mts = mk([64, 8, 64], "Mt")
        for d in range(2):
            p.op('dve', lambda e, d=d: e.memset(Mts[d][:], 0.0), w=[('Mt', d)])
            p.op('dve', lambda e, d=d: e.tensor_copy(out=RR(M[d][:]), in_=Mts[d][:]), r=[('Mt', d)], w=[('M', d)])

        def mmr(e, out, lhsT, rhs, **kw):
            if use_r:
                return e.matmul(out, lhsT.bitcast(F32R), rhs.bitcast(F32R), **kw)
            return e.matmul(out, lhsT, rhs, **kw)

        def scan_unit(d, c):
            if True:
                tok0 = c * C
                Xd = X[d][c % 2]
                E0, E1, E2, At, Rt, Bt, Kt = E0s[d], E1s[d], E2s[d], Ats[d], Rts[d], Bts[d], Kts[d]
                FAR, FB, FK, G1, G2 = FARs[d], FBs[d], FKs[d], G1s[d], G2s[d]
                Tm = [Tms[0][d], Tms[1][d]]
                Nm = [Nms[0][d], Nms[1][d]]
                Z, Ws, Us, PC = Zs[d], Wss[d], Uss[d], PCs[d]
                ps = [PSB[4 * d + (i % 4)] for i in range(8)]
                PS = lambda i: ('ps', 4 * d + (i % 4))
                xk = [('X', d, c % 2)]
                srcs = [0, 1, 3, 4 + d, 6 + d, 8 + d]
                yield
                for i, s in enumerate(srcs):
                    p.dma(lambda e, Xd=Xd, i=i, s=s, tok0=tok0: e.dma_start(out=Xd[:, i, :],
                                                                           in_=S['tm'][tok0:tok0 + C, s, :]), w=xk)
                r_, v_, kk_, k_, b_, lw_ = [Xd[:, i, :] for i in range(6)]
                Vr = Vrs[d]
                yield
                p.op('act', lambda e, v_=v_: e.activation(out=RR(Vr[:]), in_=v_, func=AF.Copy), r=xk, w=[('Vr', d)])
                v_ = Vr[:]
                vk = [('Vr', d)]
                yield
                p.op('pe', lambda e, d=d, lw_=lw_: e.matmul(ps[0][0:64, :], tri[:, d, :], lw_, start=True, stop=True),
                     r=xk + ['tri'], w=[PS(0)])
                yield
                for h in range(8):
                    p.op('pe', lambda e, h=h, lw_=lw_: e.matmul(ps[1][0:64, h:h + 1], lw_[:, h * 64:(h + 1) * 64],
                                                               ones[:, 0:1], start=True, stop=True),
                         r=xk + ['ones1'], w=[PS(1)])
                yield
                p.op('act', lambda e: e.activation(out=PC[:], in_=ps[1][0:64, 0:8], func=AF.Exp), r=[PS(1)], w=[('PC', d)])
                yield
                p.op('act', lambda e: e.activation(out=E1[:], in_=ps[0][0:64, :], func=AF.Exp), r=[PS(0)], w=[('E1', d)])
                yield
                p.op('act', lambda e: e.activation(out=E2[:], in_=ps[0][0:64, :], func=AF.Exp, scale=-1.0),
                     r=[PS(0)], w=[('E2', d)])
                yield
                p.op('dve', lambda e, lw_=lw_: e.tensor_tensor(out=E0[:], in0=ps[0][0:64, :], in1=lw_, op=ALU.subtract),
                     r=[PS(0)] + xk, w=[('E0', d)])
                yield
                p.op('act', lambda e: e.activation(out=E0[:], in_=E0[:], func=AF.Exp), r=[('E0', d)], w=[('E0', d)])
                yield
                p.op('dve', lambda e, kk_=kk_: e.scalar_tensor_tensor(out=At[:], in0=kk_, scalar=-1.0, in1=E0[:],
                                                                      op0=ALU.mult, op1=ALU.mult),
                     r=xk + [('E0', d)], w=[('At', d)])
                yield
                p.op('dve', lambda e, r_=r_: e.tensor_tensor(out=Rt[:], in0=r_, in1=E1[:], op=ALU.mult),
                     r=xk + [('E1', d)], w=[('Rt', d)])
                yield
                p.op('dve', lambda e, b_=b_: e.tensor_tensor(out=RR(Bt[:]), in0=b_, in1=E2[:], op=ALU.mult),
                     r=xk + [('E2', d)], w=[('Bt', d)])
                yield
                p.op('dve', lambda e, k_=k_: e.tensor_tensor(out=RR(Kt[:]), in0=k_, in1=E2[:], op=ALU.mult),
                     r=xk + [('E2', d)], w=[('Kt', d)])
                yield
                for bank, src, key in ((2, At, ('At', d)), (3, Rt, ('Rt', d)), (4, Bt, ('Bt', d)), (5, Kt, ('Kt', d))):
                    for h in range(8):
                        p.op('pe', lambda e, bank=bank, src=src, h=h: e.transpose(
                            out=ps[bank][0:64, h * 64:(h + 1) * 64], in_=src[:, h * 64:(h + 1) * 64],
                            identity=ident_f[0:64, 0:64]), r=[key, 'identf'], w=[PS(bank)])
                yield
                p.op('act', lambda e: e.activation(out=RR(FAR[:, :, 0:64]), in_=v3(ps[2][0:64, :]), func=AF.Copy),
                     r=[PS(2)], w=[('FAR', d)])
                yield
                p.op('act', lambda e: e.activation(out=RR(FAR[:, :, 64:128]), in_=v3(ps[3][0:64, :]), func=AF.Copy),
                     r=[PS(3)], w=[('FAR', d)])
                yield
                p.op('dve', lambda e: e.tensor_copy(out=RR(FB[:]), in_=v3(ps[4][0:64, :])), r=[PS(4)], w=[('FB', d)])
                yield
                p.op('dve', lambda e: e.tensor_copy(out=RR(FK[:]), in_=v3(ps[5][0:64, :])), r=[PS(5)], w=[('FK', d)])
                yield
                for h in range(8):
                    bank = 6 + (h // 4)
                    p.op('pe', lambda e, h=h, bank=bank: mmr(e, ps[bank][0:64, (h % 4) * 128:(h % 4 + 1) * 128],
                                                                   FB[:, h, :], FAR[:, h, :], start=True, stop=True),
                         r=[('FB', d), ('FAR', d)], w=[PS(bank)])
                yield
                for hb in range(2):
                    p.op('dve', lambda e, hb=hb, d=d: e.tensor_tensor(
                        out=RR(G1[:, hb * 4:(hb + 1) * 4, :]), in0=ps[6 + hb][0:64, :].rearrange("p (h t) -> p h t", h=4),
                        in1=mg[:, d, :].unsqueeze(1).to_broadcast([64, 4, 128]), op=ALU.mult),
                        r=[PS(6 + hb), 'mg'], w=[('G1', d)])
                yield
                for h in range(8):
                    bank = 2 + (h // 4)
                    p.op('pe', lambda e, h=h, bank=bank: mmr(e, ps[bank][0:64, (h % 4) * 128:(h % 4 + 1) * 128],
                                                                   FK[:, h, :], FAR[:, h, :], start=True, stop=True),
                         r=[('FK', d), ('FAR', d)], w=[PS(bank)])
                yield
                for hb in range(2):
                    p.op('dve', lambda e, hb=hb, d=d: e.tensor_tensor(
                        out=RR(G2[:, hb * 4:(hb + 1) * 4, :]), in0=ps[2 + hb][0:64, :].rearrange("p (h t) -> p h t", h=4),
                        in1=mg[:, d, :].unsqueeze(1).to_broadcast([64, 4, 128]), op=ALU.mult),
                        r=[PS(2 + hb), 'mg'], w=[('G2', d)])
                yield
                for h in range(8):
                    p.op('pe', lambda e, h=h: mmr(e, ps[4][0:64, h * 64:(h + 1) * 64], FAR[:, h, 0:64], FB[:, h, :],
                                                       start=True, stop=True), r=[('FAR', d), ('FB', d)], w=[PS(4)])
                yield
                p.op('dve', lambda e, d=d: e.tensor_tensor(out=RR(Nm[0][:]), in0=v3(ps[4][0:64, :]),
                                                           in1=mn[:, d, :].unsqueeze(1).to_broadcast([64, 8, 64]),
                                                           op=ALU.mult), r=[PS(4), 'mn'], w=[('Nm', d, 0)])
                yield
                p.op('dve', lambda e: e.tensor_copy(out=RR(Tm[0][:]), in_=G1[:, :, 0:64]), r=[('G1', d)], w=[('Tm', d, 0)])
                yield
                p.op('dve', lambda e: e.tensor_tensor(out=RR(Z[:]), in0=G1[:, :, 0:64],
                                                      in1=ident_f[0:64, 0:64].unsqueeze(1).to_broadcast([64, 8, 64]),
                                                      op=ALU.add), r=[('G1', d), 'identf'], w=[('Z', d)])
                cur = 0
                yield
                for lev in range(5):
                    nxt = 1 - cur
                    last = lev == 4
                    for h in range(8):
                        p.op('pe', lambda e, h=h, cur=cur: mmr(e, ps[5][0:64, h * 64:(h + 1) * 64], Tm[cur][:, h, :],
                                                                    Nm[cur][:, h, :], start=True, stop=True),
                             r=[('Tm', d, cur), ('Nm', d, cur)], w=[PS(5)])
                    p.op('act', lambda e, nxt=nxt: e.activation(out=RR(Nm[nxt][:]), in_=v3(ps[5][0:64, :]), func=AF.Copy),
                         r=[PS(5)], w=[('Nm', d, nxt)])
                    if not last:
                        for h in range(8):
                            p.op('pe', lambda e, h=h, cur=cur: mmr(e, ps[6][0:64, h * 64:(h + 1) * 64],
                                                                        Nm[cur][:, h, :], Tm[cur][:, h, :],
                                                                        start=True, stop=True),
                                 r=[('Tm', d, cur), ('Nm', d, cur)], w=[PS(6)])
                        p.op('dve', lambda e, nxt=nxt: e.tensor_copy(out=RR(Tm[nxt][:]), in_=v3(ps[6][0:64, :])),
                             r=[PS(6)], w=[('Tm', d, nxt)])
                    for h in range(8):
                        p.op('pe', lambda e, h=h, nxt=nxt: mmr(e, ps[7][0:64, h * 64:(h + 1) * 64], Nm[nxt][:, h, :],
                                                                    Z[:, h, :], start=True, stop=True),
                             r=[('Nm', d, nxt), ('Z', d)], w=[PS(7)])
                    p.op('dve', lambda e: e.tensor_tensor(out=RR(Z[:]), in0=Z[:], in1=v3(ps[7][0:64, :]), op=ALU.add),
                         r=[PS(7), ('Z', d)], w=[('Z', d)])
                    cur = nxt
                Md = M[d]
                yield
                for h in range(8):
                    o = ps[0][0:64, h * 64:(h + 1) * 64]
                    p.op('pe', lambda e, h=h, o=o, Md=Md: mmr(e, o, FAR[:, h, 0:64], Md[:, h, :], start=True, stop=False),
                         r=[('FAR', d), ('M', d)], w=[PS(0)])
                    p.op('pe', lambda e, h=h, o=o, v_=v_: mmr(e, o, G2[:, h, 0:64], v_[:, h * 64:(h + 1) * 64],
                                                                   start=False, stop=True), r=[('G2', d)] + vk, w=[PS(0)])
                yield
                p.op('act', lambda e: e.activation(out=RR(Ws[:]), in_=ps[0][0:64, :], func=AF.Copy), r=[PS(0)], w=[('Ws', d)])
                yield
                for h in range(8):
                    p.op('pe', lambda e, h=h: mmr(e, ps[1][0:64, h * 64:(h + 1) * 64], Z[:, h, :],
                                                       Ws[:, h * 64:(h + 1) * 64], start=True, stop=True),
                         r=[('Z', d), ('Ws', d)], w=[PS(1)])
                yield
                p.op('act', lambda e: e.activation(out=RR(Us[:]), in_=ps[1][0:64, :], func=AF.Copy), r=[PS(1)], w=[('Us', d)])
                yield
                for h in range(8):
                    o = ps[2][0:64, h * 64:(h + 1) * 64]
                    hs = slice(h * 64, (h + 1) * 64)
                    p.op('pe', lambda e, h=h, o=o, Md=Md: mmr(e, o, FAR[:, h, 64:128], Md[:, h, :], start=True, stop=False),
                         r=[('FAR', d), ('M', d)], w=[PS(2)])
                    p.op('pe', lambda e, h=h, o=o, hs=hs: mmr(e, o, G1[:, h, 64:128], Us[:, hs], start=False, stop=False),
                         r=[('G1', d), ('Us', d)], w=[PS(2)])
                    p.op('pe', lambda e, h=h, o=o, hs=hs, v_=v_: mmr(e, o, G2[:, h, 64:128], v_[:, hs], start=False, stop=True),
                         r=[('G2', d)] + vk, w=[PS(2)])
                yield
                p.op('act', lambda e, d=d: e.activation(out=Ys[d][:], in_=ps[2][0:64, :], func=AF.Copy),
                     r=[PS(2)], w=[('Ys', d)])
                yield
                p.dma(lambda e, d=d, tok0=tok0: e.dma_start(out=S['y'][d, tok0:tok0 + C, :], in_=Ys[d][:]),
                      r=[('Ys', d)], w=[('Sy', d, c)], eng='act')
                yield
                for h in range(8):
                    o = ps[3][0:64, h * 64:(h + 1) * 64]
                    hs = slice(h * 64, (h + 1) * 64)
                    p.op('pe', lambda e, o=o, hs=hs: mmr(e, o, Bt[:, hs], Us[:, hs], start=True, stop=False),
                         r=[('Bt', d), ('Us', d)], w=[PS(3)])
                    p.op('pe', lambda e, o=o, hs=hs, v_=v_: mmr(e, o, Kt[:, hs], v_[:, hs], start=False, stop=True),
                         r=[('Kt', d)] + vk, w=[PS(3)])
                Mt = Mts[d]
                yield
                p.op('dve', lambda e, Md=Md, Mt=Mt: e.tensor_tensor(out=Mt[:], in0=Md[:], in1=v3(ps[3][0:64, :]), op=ALU.add),
                     r=[PS(3), ('M', d)], w=[('Mt', d)])
                yield
                p.op('dve', lambda e, Md=Md, Mt=Mt: e.tensor_tensor(out=RR(Md[:]), in0=Mt[:],
                                                             in1=PC[:].unsqueeze(2).to_broadcast([64, 8, 64]),
                                                             op=ALU.mult), r=[('PC', d), ('Mt', d)], w=[('M', d)])
        cin = [[p.sb([128, 1, D], F32, f"cin{i}{q}") for q in range(2)] for i in range(2)]
        cout = [p.sb([128, 1, 2 * D], BF16, f"cout{i}") for i in range(2)]

        def conv_block(blk):
            b = blk % 2
            rows = slice(blk * 128, (blk + 1) * 128)
            for q, tabn in enumerate(('peer_u', 'peer_v')):
                p.dma(lambda e, b=b, q=q, tabn=tabn, rows=rows: e.dma_start(
                    out=cin[b][q][:], in_=I[tabn][l][rows, :].rearrange("(j p) d -> p j d", p=128)), w=[('cin', b, q)],
                    eng="pool")
                p.op('pool', lambda e, b=b, q=q: e.tensor_copy(out=cout[b][:, :, q * D:(q + 1) * D], in_=cin[b][q][:]),
                     r=[('cin', b, q)], w=[('cout', b, q)])
            p.dma(lambda e, b=b, rows=rows: e.dma_start(
                out=S['T'][l][rows, :].rearrange("(j p) d -> p j d", p=128), in_=cout[b][:]),
                r=[('cout', b, 0), ('cout', b, 1)], w=[('cout', b, 0), ('cout', b, 1)], eng="pool")

        nblk = 0
        for step in range(NCH):
            gens = [scan_unit(d, order[d][step]) for d in range(2)]
            while gens:
                for g_ in list(gens):
                    try:
                        next(g_)
                    except StopIteration:
                        gens.remove(g_)
            for _ in range(4):
                if nblk < 128:
                    conv_block(nblk)
                    nblk += 1
        while nblk < 128:
            conv_block(nblk)
            nblk += 1
        p.barrier()


    def phase_rout(l):
        p.sb_reset(base_mark)
        with_ctx = l < DEPTH - 1
        wo = p.sb([128, 8, D], BF16, "wo")
        for j in range(8):
            p.dma(lambda e, j=j: e.dma_start(out=wo[:, j, :], in_=I['w_out'][l, j * 128:(j + 1) * 128, :]),
                  w=[('wo', j)], eng="pool")
        LNW = p.sb([128, 512], F32, "LNW")
        LNB = p.sb([128, 512], F32, "LNB")
        G1b = [p.sb([128, D], F32, f"G1b{s}") for s in range(2)]
        load_bc(LNW[:], I['r7_lnw'][l], 'LNW')
        load_bc(LNB[:], I['r7_lnb'][l], 'LNB')
        gn_eps = p.sb([128, 1], F32, "gneps")
        p.op('dve', lambda e: e.memset(gn_eps[:], 64e-5), w=['gneps'])
        for s in range(2):
            load_bc(G1b[s][:], S['mod'][l, s, 2 * D:3 * D], ('G1b', s))
        yb = [[p.sb([128, 512], F32, f"y{d}{i}") for d in range(2)] for i in range(2)]
        vg = [p.sb([128, 2, 512], F32, f"vg{i}") for i in range(2)]
        bon = [p.sb([128, 8], F32, f"bon{i}") for i in range(2)]
        O = [p.sb([128, D], F32, f"O{i}") for i in range(2)]
        Ob = [p.sb([128, D], BF16, f"Ob{i}") for i in range(2)]
        oT = [p.sb([128, 8, 128], BF16, f"oT{i}") for i in range(2)]
        xt = [p.sb([128, D], F32, f"xr{i}") for i in range(2)]
        yc = [p.sb([128, 512], F32, f"yc{i}") for i in range(2)]
        sq = [p.sb([128, 512], F32, f"sq2{i}") for i in range(2)]
        m8 = [p.sb([128, 8], F32, f"m8{i}") for i in range(2)]
        v8 = [p.sb([128, 8], F32, f"v8{i}") for i in range(2)]
        src = I['x'] if l == 0 else S['xs']
        v3 = lambda ap: ap.rearrange("p (h d) -> p h d", h=8)
        bc8 = lambda ap: ap.unsqueeze(2).to_broadcast([128, 8, 64])
        def rout_iter(t):
            b = t % 2
            s = 1 if t < 2 else 0
            rows = slice(t * 128, (t + 1) * 128)
            yield
            for d in range(2):
                p.dma(lambda e, b=b, d=d, rows=rows: e.dma_start(out=yb[b][d][:], in_=S['y'][d, rows, :]), w=[('y', b, d)])
            yield
            p.dma(lambda e, b=b, rows=rows: e.dma_start(out=vg[b][:], in_=S['tm'][rows, 1:3, :]), w=[('vg', b)])
            yield
            p.dma(lambda e, b=b, rows=rows: e.dma_start(out=bon[b][:], in_=S['bon'][rows, :]), w=[('bon', b)])
            yield
            p.dma(lambda e, b=b, rows=rows: e.dma_start(out=O[b][:, 0:512], in_=S['o'][rows, 0:512]), w=[('O', b)])
            yield
            p.dma(lambda e, b=b, rows=rows: e.dma_start(out=xt[b][:], in_=src[rows, :]), w=[('xr', b)])
            yield
            p.op('dve', lambda e, b=b: e.tensor_tensor(out=yc[b][:], in0=yb[b][0][:], in1=yb[b][1][:], op=ALU.add),
                 r=[('y', b, 0), ('y', b, 1)], w=[('yc', b)])
            yield
            p.op('dve', lambda e, b=b: e.tensor_reduce(out=m8[b][:], in_=v3(yc[b][:]), axis=AX.X, op=ALU.add), r=[('yc', b)], w=[('m8', b)])
            yield
            p.op('dve', lambda e, b=b: e.tensor_scalar(out=m8[b][:], in0=m8[b][:], scalar1=1.0 / 64, scalar2=None, op0=ALU.mult),
                 r=[('m8', b)], w=[('m8', b)])
            yield
            p.op('dve', lambda e, b=b: e.tensor_tensor(out=v3(yc[b][:]), in0=v3(yc[b][:]), in1=bc8(m8[b][:]), op=ALU.subtract),
                 r=[('yc', b), ('m8', b)], w=[('yc', b)])
            yield
            p.op('dve', lambda e, b=b: e.tensor_tensor(out=sq[b][:], in0=yc[b][:], in1=yc[b][:], op=ALU.mult), r=[('yc', b)], w=[('sq2', b)])
            yield
            p.op('dve', lambda e, b=b: e.tensor_reduce(out=v8[b][:], in_=v3(sq[b][:]), axis=AX.X, op=ALU.add), r=[('sq2', b)], w=[('v8', b)])
            yield
            p.op('act', lambda e, b=b: e.activation(out=v8[b][:], in_=v8[b][:], func=AF.Sqrt, bias=gn_eps[:], scale=1.0 / 64),
                 r=[('v8', b), 'gneps'], w=[('v8', b)])
            yield
            p.op('dve', lambda e, b=b: e.reciprocal(out=v8[b][:], in_=v8[b][:]), r=[('v8', b)], w=[('v8', b)])
            yield
            p.op('dve', lambda e, b=b: e.tensor_tensor(out=v3(yc[b][:]), in0=v3(yc[b][:]), in1=bc8(v8[b][:]), op=ALU.mult),
                 r=[('yc', b), ('v8', b)], w=[('yc', b)])
            yield
            p.op('dve', lambda e, b=b: e.tensor_tensor(out=yc[b][:], in0=yc[b][:], in1=LNW[:], op=ALU.mult), r=[('yc', b), 'LNW'], w=[('yc', b)])
            yield
            p.op('dve', lambda e, b=b: e.tensor_tensor(out=yc[b][:], in0=yc[b][:], in1=LNB[:], op=ALU.add), r=[('yc', b), 'LNB'], w=[('yc', b)])
            yield
            p.op('dve', lambda e, b=b: e.tensor_tensor(out=v3(sq[b][:]), in0=v3(vg[b][:, 0, :]), in1=bc8(bon[b][:]),
                                                       op=ALU.mult), r=[('vg', b), ('bon', b)], w=[('sq2', b)])
            yield
            p.op('dve', lambda e, b=b: e.tensor_tensor(out=yc[b][:], in0=yc[b][:], in1=sq[b][:], op=ALU.add), r=[('yc', b), ('sq2', b)], w=[('yc', b)])
            yield
            p.op('dve', lambda e, b=b: e.tensor_tensor(out=O[b][:, 512:1024], in0=yc[b][:], in1=vg[b][:, 1, :], op=ALU.mult),
                 r=[('yc', b), ('vg', b)], w=[('O2', b)])
            yield
            p.dma(lambda e, b=b, rows=rows: e.dma_start(out=S['o'][rows, 512:1024], in_=O[b][:, 512:1024]),
                  r=[('O2', b)], w=[('So2', t)], eng='pool')
            yield
            p.op('act', lambda e, b=b: e.activation(out=Ob[b][:], in_=O[b][:], func=AF.Copy),
                 r=[('O', b), ('O2', b)], w=[('Ob', b)])
            bank = 6 + b
            pv = ps[bank][:, 0:512].bitcast(BF16)
            yield
            for j in range(8):
                p.op('pe', lambda e, b=b, j=j, pv=pv: e.transpose(out=pv[:, j * 128:(j + 1) * 128],
                                                                 in_=Ob[b][:, j * 128:(j + 1) * 128], identity=ident_b[:]),
                     r=[('Ob', b), 'identb'], w=[PS(bank)])
            yield
            p.op('act', lambda e, b=b, pv=pv: e.activation(out=oT[b][:], in_=pv.rearrange("p (j t) -> p j t", j=8),
                                                            func=AF.Copy), r=[PS(bank)], w=[('oT', b)])
            yield
            for half in range(2):
                ybank = 2 * b + half
                for j in range(8):
                    p.op('pe', lambda e, b=b, j=j, half=half, ybank=ybank: e.matmul(
                        ps[ybank][:, :], oT[b][:, j, :], wo[:, j, half * 512:(half + 1) * 512],
                        start=(j == 0), stop=(j == 7)), r=[('oT', b), ('wo', j)], w=[PS(ybank)])
                cs_ = slice(half * 512, (half + 1) * 512)
                p.op('dve', lambda e, b=b, s=s, cs_=cs_, ybank=ybank: e.tensor_tensor(
                    out=O[b][:, cs_], in0=ps[ybank][:, :], in1=G1b[s][:, cs_], op=ALU.mult),
                    r=[PS(ybank), ('G1b', s), ('Ob', b), ('So2', t)], w=[('O', b), ('O2', b)])
                p.op('dve', lambda e, b=b, cs_=cs_: e.tensor_tensor(out=xt[b][:, cs_], in0=xt[b][:, cs_], in1=O[b][:, cs_],
                                                                   op=ALU.add), r=[('O', b), ('xr', b)], w=[('xr', b)])
            yield
            p.dma(lambda e, b=b, rows=rows: e.dma_start(out=S['xs'][rows, :], in_=xt[b][:]), r=[('xr', b)], w=[('Sxs', t)], eng='pool')
        def rr5(gens):
            gens = list(gens)
            while gens:
                for g_ in list(gens):
                    try:
                        next(g_)
                    except StopIteration:
                        gens.remove(g_)

        tl2 = [t for t in range(NT) if not (t < 2 and not with_ctx)]
        rolling([(lambda t=t: rout_iter(t)) for t in tl2], stagger=12)
        p.barrier()

    def phase_peer(l):
        p.sb_reset(base_mark)
        last = l == DEPTH - 1
        eu_all = p.sb([128, NT, 128], U32, "eu_all")
        gate_all = p.sb([128, NT, 128], F32, "gate_all")
        G2b = [p.sb([128, D], F32, f"G2b{s}") for s in range(2)]
        m1 = p.sb_mark()
        hT = p.sb([128, 8, NTOK], BF16, "hT2")
        wq = p.sb([128, 8, 2048], BF16, "wq")
        for j in range(8):
            p.dma(lambda e, j=j: e.dma_start(out=wq[:, j, :], in_=I['peer_wq'][l, j * 128:(j + 1) * 128, :]),
                  w=[('wq', j)], eng="pool")
        keysT = p.sb([128, 16, 128], F32, "keysT")
        m0 = p.sb_mark()
        kraw = p.sb([128, 16, 128], F32, "kraw")
        p.dma(lambda e: e.dma_start(out=kraw[:], in_=I['peer_keys'][l].rearrange("h q n d -> n (h q) d")), w=['kraw'])
        for g in range(4):
            for i in range(4):
                hp = g * 4 + i
                p.op('pe', lambda e, g=g, i=i, hp=hp: e.transpose(out=ps[g][:, i * 128:(i + 1) * 128], in_=kraw[:, hp, :],
                                                                 identity=ident_f[:]), r=['kraw', 'identf'], w=[PS(g)])
            p.op('act', lambda e, g=g: e.activation(out=keysT[:, g * 4:(g + 1) * 4, :],
                                                    in_=ps[g][:, :].rearrange("p (i n) -> p i n", i=4), func=AF.Copy),
                 r=[PS(g)], w=['keysT'])
        p.barrier()
        p.sb_reset(m0)
        norm_tiles(l, 1, S['xs'], hT, lambda t: t * 128, tm_dram=S['h2'])
        p.barrier()
        p.sb_reset(m0)
        for s in range(2):
            load_bc(G2b[s][:], S['mod'][l, s, 5 * D:6 * D], ('G2b', s))
        qT = [p.sb([128, 16, 128], F32, f"qT{i}") for i in range(2)]
        sc = [p.sb([128, 16, 128], F32, f"sc{i}") for i in range(2)]
        sc2 = p.sb([128, 16, 128], F32, "sc2")
        sv = p.sb([128, 16, 16], F32, "sv")
        si = p.sb([128, 16, 16], U32, "si")
        sif = p.sb([128, 16, 16], F32, "sif")
        cand = p.sb([128, 8, 16, 16], F32, "cand")
        cand2 = p.sb([128, 8, 16, 16], F32, "cand2")
        eidx = p.sb([128, 8, 16, 16], F32, "eidx")
        best = p.sb([128, 8, 16], F32, "best")
        ci = p.sb([128, 8, 16], U32, "ci")
        cif = p.sb([128, 8, 16], F32, "cif")
        iota = p.sb([128, 256], F32, "iota")
        p.dma(lambda e: e.dma_start(out=iota[:], in_=I['iota']), w=['iota'])
        eq4 = p.sb([128, 8, 16, 16], F32, "eq4")
        cu = p.sb([128, 2, 8, 16], U32, "cu")
        cf = p.sb([128, 2, 8, 16], F32, "cf")
        e12 = p.sb([128, 2, 8, 16], F32, "e12")
        esel = p.sb([128, 128], F32, "esel")
        g8 = p.sb([128, 8], F32, "g8")
        tiles = [t for t in range(NT) if not (t < 2 and last)]
        def h1_stage1(t, bq):
            for g in range(4):
                for i in range(4):
                    hp = g * 4 + i
                    for j in range(8):
                        p.op('pe', lambda e, g=g, i=i, hp=hp, j=j, t=t: e.matmul(
                            ps[g][:, i * 128:(i + 1) * 128], wq[:, j, hp * 128:(hp + 1) * 128],
                            hT[:, j, t * 128:(t + 1) * 128], start=(j == 0), stop=(j == 7)),
                            r=[('hT', t), ('wq', j)], w=[PS(g)])
                p.op('act', lambda e, g=g, bq=bq: e.activation(out=qT[bq][:, g * 4:(g + 1) * 4, :],
                                                        in_=ps[g][:, :].rearrange("p (i n) -> p i n", i=4), func=AF.Copy),
                     r=[PS(g)], w=[('qT', bq)])
            for g in range(4):
                for i in range(4):
                    hp = g * 4 + i
                    p.op('pe', lambda e, g=g, i=i, hp=hp, bq=bq: e.matmul(ps[4 + g][:, i * 128:(i + 1) * 128], qT[bq][:, hp, :],
                                                                   keysT[:, hp, :], start=True, stop=True),
                         r=[('qT', bq), 'keysT'], w=[PS(4 + g)])
                p.op('act', lambda e, g=g, bq=bq: e.activation(out=sc[bq][:, g * 4:(g + 1) * 4, :],
                                                        in_=ps[4 + g][:, :].rearrange("p (i n) -> p i n", i=4), func=AF.Copy),
                     r=[PS(4 + g)], w=[('sc', bq)])

        def h1_stage2(t, bq):
            SVK = [('sv', hp) for hp in range(16)]
            SIK = [('si', hp) for hp in range(16)]
            for hp in range(16):
                p.op('dve', lambda e, hp=hp, bq=bq: e.max(out=sv[:, hp, 0:8], in_=sc[bq][:, hp, :]), r=[('sc', bq)], w=[('sv', hp)])
            for hp in range(16):
                p.op('dve', lambda e, hp=hp, bq=bq: e.max_index(out=si[:, hp, 0:8], in_max=sv[:, hp, 0:8], in_values=sc[bq][:, hp, :]),
                     r=[('sc', bq), ('sv', hp)], w=[('si', hp)])
            for hp in range(16):
                p.op('dve', lambda e, hp=hp, bq=bq: e.match_replace(out=sc2[:, hp, :], in_to_replace=sv[:, hp, 0:8],
                                                             in_values=sc[bq][:, hp, :], imm_value=-1e30),
                     r=[('sc', bq), ('sv', hp)], w=[('sc2', hp)])
            for hp in range(16):
                p.op('dve', lambda e, hp=hp, bq=bq: e.max(out=sv[:, hp, 8:16], in_=sc2[:, hp, :]), r=[('sc2', hp)], w=[('sv8', hp)])
            for hp in range(16):
                p.op('dve', lambda e, hp=hp, bq=bq: e.max_index(out=si[:, hp, 8:16], in_max=sv[:, hp, 8:16], in_values=sc2[:, hp, :]),
                     r=[('sc2', hp), ('sv8', hp)], w=[('si8', hp)])
            SVK = SVK + [('sv8', hp) for hp in range(16)]
            SIK = SIK + [('si8', hp) for hp in range(16)]
            p.op('dve', lambda e: e.tensor_copy(out=sif[:], in_=si[:]), r=SIK, w=['sif'])
            svv = sv[:].rearrange("p (h q) k -> p h q k", q=2)
            sfv = sif[:].rearrange("p (h q) k -> p h q k", q=2)
            p.op('dve', lambda e, svv=svv: e.tensor_tensor(
                out=cand[:], in0=svv[:, :, 0, :].unsqueeze(3).to_broadcast([128, 8, 16, 16]),
                in1=svv[:, :, 1, :].unsqueeze(2).to_broadcast([128, 8, 16, 16]), op=ALU.add), r=SVK, w=['cand'])
            p.op('dve', lambda e, sfv=sfv: e.tensor_scalar(out=sfv[:, :, 0, :], in0=sfv[:, :, 0, :], scalar1=128.0,
                                                           scalar2=None, op0=ALU.mult), r=['sif'], w=['sif'])
            chs = [cand[:, h].rearrange("p a b -> p (a b)") for h in range(8)]
            ch2s = [cand2[:, h].rearrange("p a b -> p (a b)") for h in range(8)]
            for h in range(8):
                p.op('dve', lambda e, h=h: e.max(out=best[:, h, 0:8], in_=chs[h]), r=['cand'], w=[('best', h)])
            for h in range(8):
                p.op('dve', lambda e, h=h: e.max_index(out=ci[:, h, 0:8], in_max=best[:, h, 0:8], in_values=chs[h]),
                     r=['cand', ('best', h)], w=[('ci', h)])
            for h in range(8):
                p.op('dve', lambda e, h=h: e.match_replace(out=ch2s[h], in_to_replace=best[:, h, 0:8],
                                                           in_values=chs[h], imm_value=-1e30),
                     r=['cand', ('best', h)], w=[('cand2', h)])
            for h in range(8):
                p.op('dve', lambda e, h=h: e.max(out=best[:, h, 8:16], in_=ch2s[h]), r=[('cand2', h)], w=[('best8', h)])
            for h in range(8):
                p.op('dve', lambda e, h=h: e.max_index(out=ci[:, h, 8:16], in_max=best[:, h, 8:16], in_values=ch2s[h]),
                     r=[('cand2', h), ('best8', h)], w=[('ci8', h)])
            BK = [('best', h) for h in range(8)] + [('best8', h) for h in range(8)]
            CIK = [('ci', h) for h in range(8)] + [('ci8', h) for h in range(8)]
            p.op('dve', lambda e: e.tensor_scalar(out=cu[:, 0], in0=ci[:], scalar1=4, scalar2=None,
                                                  op0=ALU.logical_shift_right), r=CIK, w=['cu'])
            p.op('dve', lambda e: e.tensor_scalar(out=cu[:, 1], in0=ci[:], scalar1=15, scalar2=None,
                                                  op0=ALU.bitwise_and), r=CIK, w=['cu'])
            p.op('dve', lambda e: e.tensor_copy(out=cf[:], in_=cu[:]), r=['cu'], w=['cf'])
            io16 = iota[:, 0:16].unsqueeze(1).unsqueeze(1).to_broadcast([128, 8, 16, 16])
            for q in range(2):
                p.op('dve', lambda e, q=q, io16=io16: e.tensor_tensor(
                    out=eq4[:], in0=io16, in1=cf[:, q].unsqueeze(3).to_broadcast([128, 8, 16, 16]), op=ALU.is_equal),
                    r=['iota', 'cf'], w=['eq4'])
                p.op('dve', lambda e, q=q, sfv=sfv: e.tensor_tensor(
                    out=eq4[:], in0=eq4[:], in1=sfv[:, :, q, :].unsqueeze(2).to_broadcast([128, 8, 16, 16]), op=ALU.mult),
                    r=['eq4', 'sif'], w=['eq4'])
                p.op('dve', lambda e, q=q: e.tensor_reduce(out=e12[:, q], in_=eq4[:], axis=AX.X, op=ALU.add),
                     r=['eq4'], w=['e12'])
            p.op('dve', lambda e: e.tensor_tensor(out=esel[:], in0=e12[:, 0].rearrange("p h k -> p (h k)"),
                                                  in1=e12[:, 1].rearrange("p h k -> p (h k)"), op=ALU.add),
                 r=['e12'], w=['esel'])
            p.op('dve', lambda e, t=t: e.tensor_copy(out=eu_all[:, t, :], in_=esel[:]), r=['esel'], w=[('eu', t)])
            gv = gate_all[:, t, :].rearrange("p (h k) -> p h k", h=8)
            p.op('dve', lambda e, gv=gv: e.tensor_tensor(out=gv, in0=best[:],
                                                         in1=best[:, :, 0:1].to_broadcast([128, 8, 16]), op=ALU.subtract),
                 r=BK, w=[('gate', t)])
            p.op('act', lambda e, t=t: e.activation(out=gate_all[:, t, :], in_=gate_all[:, t, :], func=AF.Exp),
                 r=[('gate', t)], w=[('gate', t)])
            p.op('dve', lambda e, gv=gv: e.tensor_reduce(out=g8[:], in_=gv, axis=AX.X, op=ALU.add), r=[('gate', t)], w=['g8'])
            p.op('dve', lambda e: e.reciprocal(out=g8[:], in_=g8[:]), r=['g8'], w=['g8'])
            p.op('dve', lambda e, gv=gv: e.tensor_tensor(out=gv, in0=gv, in1=g8[:].unsqueeze(2).to_broadcast([128, 8, 16]),
                                                         op=ALU.mult), r=[('gate', t), 'g8'], w=[('gate', t)])

        for i_, t_ in enumerate(tiles):
            if i_ == 0:
                h1_stage1(t_, 0)
            if i_ + 1 < len(tiles):
                h1_stage1(tiles[i_ + 1], (i_ + 1) % 2)
            h1_stage2(t_, i_ % 2)
        p.barrier()
        p.sb_reset(m1)
        h2 = [p.sb([128, D], F32, f"h2{i}") for i in range(2)]
        xt = [p.sb([128, D], F32, f"xp{i}") for i in range(2)]
        act = [p.sb([128, 128], F32, f"actv{i}") for i in range(2)]
        wg = [p.sb([128, 128], F32, f"wg{i}") for i in range(2)]
        NACC = 1
        acc = [[p.sb([128, D], F32, f"acc{i}{k}") for k in range(NACC)] for i in range(2)]
        junk = p.sb([128, D], BF16, "pjunk")
        NG = 32
        GS = 8
        gbuf = [p.sb([128, 2 * D], BF16, f"gb{i}") for i in range(NG)]
        NDG = 8
        dg = [p.sb([128, 128], BF16, f"dg{i}") for i in range(NDG)]
        gi = 0
        di = 0
        for t in tiles:
            b = t % 2
            s = 1 if t < 2 else 0
            rows = slice(t * 128, (t + 1) * 128)
            p.dma(lambda e, b=b, rows=rows: e.dma_start(out=h2[b][:], in_=S['h2'][rows, :]), w=[('h2', b)])
            p.dma(lambda e, b=b, rows=rows: e.dma_start(out=xt[b][:], in_=S['xs'][rows, :]), w=[('xp', b)])
            p.op('dve', lambda e, b=b: e.memset(act[b][:], 0.0), w=[('actv', b)])
            for g in range(128 // GS):
                ks = []
                for sidx in range(g * GS, (g + 1) * GS):
                    k = gi % NG
                    gi += 1
                    ks.append(k)
                    p.dma(lambda e, k=k, t=t, sidx=sidx: e.indirect_dma_start(
                        out=gbuf[k][:], out_offset=None, in_=S['T'][l],
                        in_offset=bass.IndirectOffsetOnAxis(ap=eu_all[:, t, sidx:sidx + 1], axis=0)),
                        r=[], w=[('gb', k)], eng="pool")
                    p.op('dve', lambda e, k=k, b=b, sidx=sidx: e.scalar_tensor_tensor(
                        out=junk[:], in0=gbuf[k][:, 0:D], scalar=1.0, in1=h2[b][:], op0=ALU.mult, op1=ALU.mult,
                        accum_out=act[b][:, sidx:sidx + 1]), r=[('gb', k), ('h2', b), ('actv', b)], w=[('actc', b, sidx)])
                gs = slice(g * GS, (g + 1) * GS)
                p.op('act', lambda e, b=b, gs=gs: e.activation(out=wg[b][:, gs], in_=act[b][:, gs], func=AF.Gelu),
                     r=[('actc', b, sidx) for sidx in range(g * GS, (g + 1) * GS)], w=[('wg', b, g)])
                p.op('dve', lambda e, b=b, gs=gs, t=t: e.tensor_tensor(out=wg[b][:, gs], in0=wg[b][:, gs],
                                                                      in1=gate_all[:, t, gs], op=ALU.mult),
                     r=[('wg', b, g)], w=[('wg', b, g)])
                for j, sidx in enumerate(range(g * GS, (g + 1) * GS)):
                    k = ks[j]
                    dj = di % NDG
                    di += 1
                    p.op('act', lambda e, dj=dj, b=b, sidx=sidx: e.activation(
                        out=dg[dj][:], in_=ident_f[:], func=AF.Copy, scale=wg[b][:, sidx:sidx + 1]),
                        r=[('wg', b, g), 'identf'], w=[('dg', dj)])
                    for half in range(2):
                        bank = 2 * b + half
                        p.op('pe', lambda e, dj=dj, k=k, half=half, bank=bank, sidx=sidx: e.matmul(
                            ps[bank][:, :], dg[dj][:], gbuf[k][:, D + half * 512:D + (half + 1) * 512],
                            start=(sidx == 0), stop=(sidx == 127)), r=[('dg', dj), ('gb', k)], w=[PS(bank), ('gbr', k, half)])
            a0 = acc[b][0]
            for half in range(2):
                hs_ = slice(half * 512, (half + 1) * 512)
                p.op('dve', lambda e, a0=a0, s=s, b=b, half=half, hs_=hs_: e.tensor_tensor(
                    out=a0[:, hs_], in0=ps[2 * b + half][:, :], in1=G2b[s][:, hs_], op=ALU.mult),
                    r=[PS(2 * b + half), ('G2b', s)], w=[('acc', b, 0)])
            p.op('dve', lambda e, a0=a0, b=b: e.tensor_tensor(out=xt[b][:], in0=xt[b][:], in1=a0[:], op=ALU.add),
                 r=[('acc', b, 0), ('xp', b)], w=[('xp', b)])
            if last:
                p.dma(lambda e, b=b, t=t: e.dma_start(out=out_d[(t - 2) * 128:(t - 1) * 128, :], in_=xt[b][:]),
                      r=[('xp', b)], w=[('outd', t)], eng='act')
            else:
                p.dma(lambda e, b=b, rows=rows: e.dma_start(out=S['xs'][rows, :], in_=xt[b][:]),
                      r=[('xp', b)], w=[('Sxs', t)], eng='act')
        p.barrier()

    PHASES = cfg.get("phases", ["proj", "rprep", "scan", "rout", "peer"])

    phase_mod()
    for l in range(cfg.get("layers", DEPTH)):
        if 'proj' in PHASES:
            qkT, Vaug, mp = phase_proj(l)
            phase_attn(l, qkT, Vaug, mp)
        if 'rprep' in PHASES:
            phase_rprep(l)
        if 'scan' in PHASES:
            phase_scan(l)
        if 'rout' in PHASES:
            phase_rout(l)
        if 'peer' in PHASES:
            phase_peer(l)
    p.barrier()
    p.emit()
    return nc


def prep_inputs(inputs):
    f = lambda a: np.ascontiguousarray(np.asarray(a, dtype=np.float32))
    x, c, ctx, c_ctx = f(inputs['x']), f(inputs['c']), f(inputs['ctx']), f(inputs['c_ctx'])
    shared = {}
    for n in ['norm_mix', 'norm_ffn', 'w_mod', 'b_mod', 'w_in', 'w_out', 'a_qnorm', 'a_knorm', 'b_qnorm', 'b_knorm',
              'a_sink']:
        shared[n] = f(inputs[n])
    rpb = f(inputs['b_rpb'])
    btab = np.zeros((DEPTH, 128, NTAB, 4, 128), np.float32)
    bmask = np.zeros((128, NTAB, 128), np.float32)
    for i, (dr, dc, valid) in enumerate(NA_TABS):
        g = rpb[:, :, dr, dc]
        btab[:, :, i, :, :] = np.where(valid[None, None], g, 0.0).transpose(0, 2, 1, 3)
        bmask[:, i, :] = valid
    shared['btab'] = btab
    shared['bmask'] = bmask
    ar = np.arange(128)
    am = np.zeros((128, 2, 128), np.float32)
    am[:, 0, :] = (ar[:, None] >= ar[None, :])
    am[:, 1, :] = (ar[:, None] <= ar[None, :])
    shared['amask'] = am
    shared['ident'] = np.eye(128, dtype=np.float32)
    cos, sin = rope_tables()
    shared['cos'], shared['sin'] = cos, sin
    rc = f(inputs['r7_conv'])
    for n in ['r7_w0', 'r7_a0', 'r7_w2', 'r7_a2', 'r7_g2', 'r7_kk', 'r7_ka', 'r7_lnw', 'r7_lnb', 'r7_rk', 'peer_wq', 'peer_keys']:
        shared[n] = f(inputs[n])
    for l in range(DEPTH):
        shared[f'peer_u{l}'] = f(inputs['peer_u'][l])
        shared[f'peer_v{l}'] = f(inputs['peer_v'][l])
    a64 = np.arange(64)
    tri = np.zeros((64, 2, 64), np.float32)
    tri[:, 0, :] = a64[:, None] <= a64[None, :]
    tri[:, 1, :] = a64[:, None] >= a64[None, :]
    mg = np.zeros((64, 2, 128), np.float32)
    mg[:, 0, 0:64] = a64[:, None] < a64[None, :]
    mg[:, 0, 64:128] = a64[:, None] <= a64[None, :]
    mg[:, 1, 0:64] = a64[:, None] > a64[None, :]
    mg[:, 1, 64:128] = a64[:, None] >= a64[None, :]
    mn = np.zeros((64, 2, 64), np.float32)
    mn[:, 0, :] = a64[None, :] < a64[:, None]
    mn[:, 1, :] = a64[None, :] > a64[:, None]
    shared['tri'], shared['mg'], shared['mn'] = tri, mg, mn
    shared['iota'] = np.ascontiguousarray(np.broadcast_to(np.arange(256, dtype=np.float32), (128, 256)))
    shared['r7_conv'] = np.ascontiguousarray(rc.reshape(DEPTH, 3, 15, 128).transpose(0, 3, 2, 1))
    maps = []
    for b in range(8):
        m = dict(shared)
        m['x'] = np.ascontiguousarray(np.concatenate([ctx[b], x[b]], axis=0))
        cc = np.stack([c[b], c_ctx], axis=-1)
        m['cc'] = np.ascontiguousarray(cc.reshape(8, 128, 2).transpose(1, 0, 2))
        maps.append(m)
    return maps


_NC_CACHE = {}


def kernel(**inputs):
    if 'nc' not in _NC_CACHE:
        _NC_CACHE['nc'] = build({})
    nc = _NC_CACHE['nc']
    maps = prep_inputs(inputs)
    res = run_bass_kernel_spmd(nc, maps, core_ids=list(range(8)))
    return np.stack([np.asarray(r['out'], dtype=np.float32) for r in res.results], axis=0)
```

```python
import numpy as np
import ml_dtypes
import concourse.bass as bass
import concourse.mybir as mybir
from concourse.bass_utils import run_bass_kernel_spmd

F32 = mybir.dt.float32
BF16 = mybir.dt.bfloat16
U32 = mybir.dt.uint32
I32 = mybir.dt.int32
AF = mybir.ActivationFunctionType
ALU = mybir.AluOpType
AX = mybir.AxisListType

ENGS = ["pe", "act", "dve", "pool", "sp"]
DT_SIZE = {F32: 4, BF16: 2, U32: 4, I32: 4}

D = 1024
NCTX = 256
NLAT = 2048
NTOK = NCTX + NLAT
NT = NTOK // 128
DEPTH = 2
EPS = 1e-6


class Prog:
    def __init__(self, nc, n_dma_sems=32):
        self.nc = nc
        self.ops = {e: [] for e in ENGS}
        self.cnt = {e: 0 for e in ENGS}
        self.waited = {e: {} for e in ENGS}
        self.res = {}
        self.n_dma_sems = n_dma_sems
        self.dma_use = [0] * n_dma_sems
        self.dma_last = [None] * n_dma_sems
        self.dma_rr = 0
        self.sb_off = 16 * 1024
        self.sb_id = 0
        self.SB_CAP = 216 * 1024

    def sb_mark(self):
        return self.sb_off

    def sb_reset(self, off=0):
        self.sb_off = off

    def sb(self, shape, dtype, name=""):
        nbytes = int(np.prod(shape[1:])) * DT_SIZE[dtype]
        off = (self.sb_off + 63) // 64 * 64
        assert off + nbytes <= self.SB_CAP, f"SBUF overflow {off}+{nbytes} ({name})"
        self.sb_off = off + nbytes
        self.sb_id += 1
        return self.nc.alloc_sbuf_tensor_at(f"sb{self.sb_id}_{name}", list(shape), dtype, offset=off)

    def _deps(self, r, w):
        deps = []
        for k in r:
            st = self.res.get(k)
            if st and st[0] is not None:
                deps.append(st[0])
        for k in w:
            st = self.res.get(k)
            if st:
                if st[0] is not None:
                    deps.append(st[0])
                deps.extend(st[1])
        return deps

    def _commit(self, tok, r, w):
        for k in r:
            st = self.res.setdefault(k, [None, []])
            st[1].append(tok)
        for k in w:
            self.res[k] = [tok, []]

    def _waits_for(self, eng, deps):
        wd = self.waited[eng]
        best = {}
        for t in deps:
            if t[0] == 'c':
                if t[1] == eng and eng == 'pe':
                    continue
                key = ('c', t[1])
            else:
                key = ('d', t[1])
            if wd.get(key, 0) >= t[2]:
                continue
            best[key] = max(best.get(key, 0), t[2])
        for k, v in best.items():
            wd[k] = v
        return list(best.items())

    def op(self, eng, fn, r=(), w=()):
        deps = self._deps(r, w)
        waits = self._waits_for(eng, deps)
        self.cnt[eng] += 1
        tok = ('c', eng, self.cnt[eng])
        self.ops[eng].append((waits, fn, ('c', eng), 1))
        self._commit(tok, r, w)
        return tok

    def dma(self, fn, r=(), w=(), eng="sp"):
        deps = list(self._deps(r, w))
        i = self.dma_rr
        self.dma_rr = (self.dma_rr + 1) % self.n_dma_sems
        if self.dma_last[i] is not None:
            deps.append(self.dma_last[i])
        waits = self._waits_for(eng, deps)
        self.dma_use[i] += 1
        tok = ('d', i, 16 * self.dma_use[i])
        self.dma_last[i] = tok
        self.ops[eng].append((waits, fn, ('d', i), 16))
        self._commit(tok, r, w)
        return tok

    def barrier(self):
        toks = [('c', e, self.cnt[e]) for e in ENGS if self.cnt[e] > 0]
        toks += [t for t in self.dma_last if t is not None]
        for e in ENGS:
            waits = self._waits_for(e, toks)
            if waits:
                self.ops[e].append((waits, None, None, 0))
        self.res = {}

    def emit(self):
        nc = self.nc
        from contextlib import ExitStack
        with ExitStack() as es:
            csem = {e: es.enter_context(nc.semaphore(f"c_{e}")) for e in ENGS}
            dsem = [es.enter_context(nc.semaphore(f"d_{i}")) for i in range(self.n_dma_sems)]
            block = es.enter_context(nc.Block())

            def sem_of(key):
                return csem[key[1]] if key[0] == 'c' else dsem[key[1]]

            def run(engname, e):
                for waits, fn, inc_key, inc in self.ops[engname]:
                    for k, v in waits:
                        e.wait_ge(sem_of(k), v)
                    if fn is None:
                        continue
                    ins = fn(e)
                    ins.then_inc(sem_of(inc_key), inc)

            @block.tensor
            def _(e):
                run("pe", e)

            @block.scalar
            def _(e):
                run("act", e)

            @block.vector
            def _(e):
                run("dve", e)

            @block.gpsimd
            def _(e):
                run("pool", e)

            @block.sync
            def _(e):
                run("sp", e)


def na_tables():
    cases = {}
    tabs = []
    keys = {}
    ar = np.arange(128)
    for p in range(16):
        for kb in range(16):
            krow = 2 * kb + ar // 64
            kcol = ar % 64
            qrow = 2 * p + ar // 64
            qcol = ar % 64
            rs = np.clip(qrow - 4, 0, 24)
            vr = (krow[:, None] >= rs[None, :]) & (krow[:, None] < rs[None, :] + 8)
            ws = np.clip(qcol - 8, 0, 48)
            vc = (kcol[:, None] >= ws[None, :]) & (kcol[:, None] < ws[None, :] + 16)
            valid = vr & vc
            if not valid.any():
                continue
            dr = krow[:, None] - qrow[None, :] + 7
            dc = np.clip(kcol[:, None] - qcol[None, :] + 15, 0, 30)
            dr = np.where(valid, dr, 0)
            dc = np.where(valid, dc, 0)
            key = (dr.tobytes(), dc.tobytes(), valid.tobytes())
            if key not in keys:
                keys[key] = len(tabs)
                tabs.append((dr, dc, valid))
            cases[(p, kb)] = keys[key]
    return cases, tabs


NA_CASES, NA_TABS = na_tables()
NTAB = len(NA_TABS)


def rope_tables():
    t = np.arange(NLAT)
    inv_freq = 10000.0 ** (-np.arange(0, 32, 2) / 32)
    ang = np.stack([(t // 64)[:, None] * inv_freq[None], (t % 64)[:, None] * inv_freq[None]], axis=1)
    return np.cos(ang).astype(np.float32).reshape(NLAT, 32), np.sin(ang).astype(np.float32).reshape(NLAT, 32)


def build(cfg=None):
    cfg = cfg or {}
    dbg = cfg.get("dbg", [])
    nc = bass.Bass("TRN2", target_bir_lowering=False)
    p = Prog(nc)

    def din(name, shape, dt=F32):
        return nc.dram_tensor(name, list(shape), dt, kind="ExternalInput").ap()

    def dscr(name, shape, dt=F32):
        kind = "Internal"
        if name in cfg.get("dump", []):
            kind = "ExternalOutput"
        if name in cfg.get("feed", []):
            kind = "ExternalInput"
        return nc.dram_tensor(name, list(shape), dt, kind=kind).ap()

    I = {}
    I['x'] = din('x', [NTOK, D])
    I['cc'] = din('cc', [128, 8, 2])
    I['norm_mix'] = din('norm_mix', [DEPTH, D])
    I['norm_ffn'] = din('norm_ffn', [DEPTH, D])
    I['w_mod'] = din('w_mod', [DEPTH, D, 6 * D])
    I['b_mod'] = din('b_mod', [DEPTH, 6 * D])
    I['w_in'] = din('w_in', [DEPTH, D, 3200])
    I['w_out'] = din('w_out', [DEPTH, D, D])
    for n in ['a_qnorm', 'a_knorm', 'b_qnorm', 'b_knorm']:
        I[n] = din(n, [DEPTH, 64])
    I['a_sink'] = din('a_sink', [DEPTH, 4])
    I['btab'] = din('btab', [DEPTH, 128, NTAB, 4, 128])
    I['bmask'] = din('bmask', [128, NTAB, 128])
    I['amask'] = din('amask', [128, 2, 128])
    I['ident'] = din('ident', [128, 128])
    I['cos'] = din('cos', [NLAT, 32])
    I['sin'] = din('sin', [NLAT, 32])
    I['r7_conv'] = din('r7_conv', [DEPTH, 128, 15, 3])
    I['r7_w0'] = din('r7_w0', [DEPTH, 2, 512])
    I['r7_a0'] = din('r7_a0', [DEPTH, 2, 512])
    I['r7_w2'] = din('r7_w2', [DEPTH, 2, 64, 512])
    I['r7_a2'] = din('r7_a2', [DEPTH, 2, 64, 512])
    I['r7_g2'] = din('r7_g2', [DEPTH, 128, 512])
    for n in ['r7_kk', 'r7_ka', 'r7_lnw', 'r7_lnb']:
        I[n] = din(n, [DEPTH, 512])
    I['r7_rk'] = din('r7_rk', [DEPTH, 8, 64])
    I['peer_wq'] = din('peer_wq', [DEPTH, D, 2048])
    I['peer_keys'] = din('peer_keys', [DEPTH, 8, 2, 128, 128])
    I['peer_u'] = [din(f'peer_u{l}', [16384, D]) for l in range(DEPTH)]
    I['peer_v'] = [din(f'peer_v{l}', [16384, D]) for l in range(DEPTH)]
    I['iota'] = din('iota', [128, 256])
    I['tri'] = din('tri', [64, 2, 64])
    I['mg'] = din('mg', [64, 2, 128])
    I['mn'] = din('mn', [64, 2, 64])
    out_d = nc.dram_tensor('out', [NLAT, D], F32, kind="ExternalOutput").ap()

    S = {}
    S['mod'] = dscr('s_mod', [DEPTH, 2, 6 * D])
    S['xs'] = dscr('s_xs', [NTOK, D])
    S['o'] = dscr('s_o', [NTOK, D])
    S['pcT'] = dscr('s_pcT', [1920, NTOK])
    S['tm'] = dscr('s_tm', [NTOK, 10, 512])
    S['bon'] = dscr('s_bon', [NTOK, 8])
    S['y'] = dscr('s_y', [2, NTOK, 512])
    S['h2'] = dscr('s_h2', [NTOK, D])
    S['T'] = [dscr(f's_T{l}', [16384, 2 * D], BF16) for l in range(DEPTH)]
    DBG = {}
    for name, shape in cfg.get("dbg_out", {}).items():
        DBG[name] = nc.dram_tensor(name, list(shape), F32, kind="ExternalOutput").ap()

    ps = [nc.alloc_psum_tensor(f"ps{i}", [128, 512], F32) for i in range(8)]

    def PS(i):
        return ('ps', i)

    def rolling(makers, stagger=6, window=2):
        active = []
        i = 0
        since = stagger
        while i < len(makers) or active:
            if len(active) < window and i < len(makers) and (since >= stagger or not active):
                active.append(makers[i]())
                i += 1
                since = 0
            for g_ in list(active):
                try:
                    next(g_)
                except StopIteration:
                    active.remove(g_)
            since += 1

    ident_f = p.sb([128, 128], F32, "identf")
    ident_b = p.sb([128, 128], BF16, "identb")
    eps_col = p.sb([128, 1], F32, "eps")
    p.dma(lambda e: e.dma_start(out=ident_f[:], in_=I['ident']), w=['identf'])
    p.op('dve', lambda e: e.tensor_copy(out=ident_b[:], in_=ident_f[:]), r=['identf'], w=['identb'])
    p.op('dve', lambda e: e.memset(eps_col[:], EPS), w=['eps'])
    p.barrier()
    base_mark = p.sb_mark()

    def phase_mod():
        p.sb_reset(base_mark)
        cc = p.sb([128, 8, 2], F32, "cc")
        scc = p.sb([128, 8, 2], F32, "scc")
        p.dma(lambda e: e.dma_start(out=cc[:], in_=I['cc']), w=['cc'])
        p.op('act', lambda e: e.activation(out=scc[:], in_=cc[:], func=AF.Silu), r=['cc'], w=['scc'])
        wt = [p.sb([128, 8, 512], F32, f"wmod{i}") for i in range(2)]
        bm = p.sb([2, 6 * D], F32, "bm")
        mo = p.sb([2, 6 * D], F32, "mo")
        k = 0
        for l in range(DEPTH):
            p.dma(lambda e, l=l: e.dma_start(out=bm[:], in_=I['b_mod'][l].partition_broadcast(2)),
                  w=['bm'])
            for cch in range(12):
                b = k % 2
                k += 1
                src = I['w_mod'][l, :, cch * 512:(cch + 1) * 512].rearrange("(j p) n -> p j n", p=128)
                p.dma(lambda e, b=b, src=src: e.dma_start(out=wt[b][:], in_=src), w=[('wmod', b)])
                pb = cch % 2
                for j in range(8):
                    p.op('pe', lambda e, b=b, j=j, pb=pb: e.matmul(ps[pb][0:2, :], scc[:, j, :], wt[b][:, j, :],
                                                                    start=(j == 0), stop=(j == 7)),
                         r=['scc', ('wmod', b)], w=[PS(pb)])
                p.op('dve', lambda e, pb=pb, cch=cch: e.tensor_tensor(
                    out=mo[:, cch * 512:(cch + 1) * 512], in0=ps[pb][0:2, :], in1=bm[:, cch * 512:(cch + 1) * 512],
                    op=ALU.add), r=[PS(pb), 'bm'], w=['mo'])
            p.dma(lambda e, l=l: e.dma_start(out=S['mod'][l], in_=mo[:]), r=['mo'], w=['S_mod'])
        p.barrier()

    def load_bc(dst, src_1d, key):
        P = dst.shape[0]
        p.dma(lambda e: e.dma_start(out=dst, in_=src_1d.partition_broadcast(P)), w=[key])

    def norm_tiles(l, which, src, hT, hT_off, tm_dram=None):
        nv = I['norm_mix'] if which == 0 else I['norm_ffn']
        so = 0 if which == 0 else 3
        G = [p.sb([128, D], F32, f"G{s}") for s in range(2)]
        SH = [p.sb([128, D], F32, f"SH{s}") for s in range(2)]
        tmp = p.sb([128, D], F32, "gtmp")
        for s in range(2):
            load_bc(tmp[:], nv[l], 'gtmp')
            load_bc(G[s][:], S['mod'][l, s, (so + 1) * D:(so + 2) * D], ('G', s))
            load_bc(SH[s][:], S['mod'][l, s, so * D:(so + 1) * D], ('SH', s))
            p.op('dve', lambda e, s=s: e.scalar_tensor_tensor(out=G[s][:], in0=G[s][:], scalar=1.0, in1=tmp[:],
                                                             op0=ALU.add, op1=ALU.mult),
                 r=['gtmp', ('G', s)], w=[('G', s)])
        NBUF = 4
        xt = [p.sb([128, D], F32, f"xt{i}") for i in range(NBUF)]
        junk = p.sb([128, D], F32, "junk")
        hb = [p.sb([128, D], BF16, f"hb{i}") for i in range(NBUF)]
        ss = [p.sb([128, 1], F32, f"ss{i}") for i in range(NBUF)]
        def stageA(t):
            b = t % NBUF
            s = 1 if t < 2 else 0
            yield
            p.dma(lambda e, b=b, t=t: e.dma_start(out=xt[b][:], in_=src[t * 128:(t + 1) * 128, :]), w=[('xt', b)])
            yield
            p.op('act', lambda e, b=b: e.activation(out=junk[:], in_=xt[b][:], func=AF.Square, accum_out=ss[b][:]),
                 r=[('xt', b)], w=[('ss', b)])
            yield
            p.op('act', lambda e, b=b: e.activation(out=ss[b][:], in_=ss[b][:], func=AF.Sqrt, bias=eps_col[:],
                                                    scale=1.0 / D), r=[('ss', b)], w=[('ss', b)])
            yield
            p.op('dve', lambda e, b=b: e.reciprocal(out=ss[b][:], in_=ss[b][:]), r=[('ss', b)], w=[('ss', b)])
            yield
            p.op('dve', lambda e, b=b, s=s: e.scalar_tensor_tensor(out=xt[b][:], in0=xt[b][:], scalar=ss[b][:, 0:1],
                                                                 in1=G[s][:], op0=ALU.mult, op1=ALU.mult),
                 r=[('xt', b), ('ss', b), ('G', s)], w=[('xt', b)])
            yield
            if tm_dram is not None:
                p.op('dve', lambda e, b=b, s=s: e.tensor_tensor(out=xt[b][:], in0=xt[b][:], in1=SH[s][:], op=ALU.add),
                     r=[('xt', b), ('SH', s)], w=[('xt', b)])
                p.dma(lambda e, b=b, t=t: e.dma_start(out=tm_dram[t * 128:(t + 1) * 128, :], in_=xt[b][:]),
                      r=[('xt', b)], w=[('tmd', t)], eng='pool')
                p.op('act', lambda e, b=b: e.activation(out=hb[b][:], in_=xt[b][:], func=AF.Copy),
                     r=[('xt', b)], w=[('hb', b)])
            else:
                p.op('dve', lambda e, b=b, s=s: e.tensor_tensor(out=hb[b][:], in0=xt[b][:], in1=SH[s][:], op=ALU.add),
                     r=[('xt', b), ('SH', s)], w=[('hb', b)])

        def stageB(t):
            b = t % NBUF
            pbank = 4 + b
            pv = ps[pbank][:, 0:512].bitcast(BF16)
            yield
            for j in range(8):
                p.op('pe', lambda e, b=b, j=j, pv=pv: e.transpose(out=pv[:, j * 128:(j + 1) * 128],
                                                                 in_=hb[b][:, j * 128:(j + 1) * 128],
                                                                 identity=ident_b[:]),
                     r=[('hb', b), 'identb'], w=[PS(pbank)])
            o = hT_off(t)
            yield
            p.op('act', lambda e, pv=pv, o=o: e.activation(
                out=hT[:, :, o:o + 128], in_=pv.rearrange("p (j t) -> p j t", j=8), func=AF.Copy),
                r=[PS(pbank)], w=[('hT', t)])


        tl_ = [t for t in range(NT) if not (t < 2 and l == DEPTH - 1 and which == 1)]
        def rr(gens):
            gens = list(gens)
            while gens:
                for g_ in list(gens):
                    try:
                        next(g_)
                    except StopIteration:
                        gens.remove(g_)

        groups = [tl_[i:i + NBUF] for i in range(0, len(tl_), NBUF)]
        for gi_ in range(len(groups) + 1):
            gl = []
            if gi_ < len(groups):
                gl += [stageA(t) for t in groups[gi_]]
            if gi_ > 0:
                gl += [stageB(t) for t in groups[gi_ - 1]]
            rr(gl)

    def phase_proj(l):
        p.sb_reset(base_mark)
        qkT = p.sb([64, 14, NTOK], BF16, "qkT")
        Vaug = p.sb([128, NT, 6, 65], BF16, "Vaug")
        mark_persist = p.sb_mark()
        hT = p.sb([128, 8, NTOK], BF16, "hT")
        m_afterh = p.sb_mark()
        wAB = p.sb([128, 8, 1280], BF16, "wAB")
        for j in range(8):
            p.dma(lambda e, j=j: e.dma_start(out=wAB[:, j, :], in_=I['w_in'][l, j * 128:(j + 1) * 128, 0:1280]),
                  w=[('wAB', j)], eng="pool")
        p.op('pool', lambda e: e.memset(Vaug[:, :, :, 64:65], 1.0), w=['Vones'])
        m0 = p.sb_mark()
        norm_tiles(l, 0, I['x'] if l == 0 else S['xs'], hT, lambda t: t * 128)
        p.barrier()
        p.sb_reset(m0)
        wC = p.sb([128, 8, 1920], BF16, "wC")
        for j in range(8):
            p.dma(lambda e, j=j: e.dma_start(out=wC[:, j, :], in_=I['w_in'][l, j * 128:(j + 1) * 128, 1280:3200]),
                  w=[('wC', j)], eng="pool")
        m_afterwc = p.sb_mark()
        GA = p.sb([128, 6, 64], F32, "GA")
        GB = p.sb([128, 8, 64], F32, "GB")
        for h in range(6):
            load_bc(GA[:, h, :], I['a_qnorm'][l] if h < 4 else I['a_knorm'][l], 'GA')
        for h in range(8):
            load_bc(GB[:, h, :], I['b_qnorm'][l] if h < 4 else I['b_knorm'][l], 'GB')
        p.op('act', lambda e: e.mul(out=GA[:, 0:4, :], in_=GA[:, 0:4, :], mul=0.125), r=['GA'], w=['GA'])
        p.op('act', lambda e: e.mul(out=GB[:, 0:4, :], in_=GB[:, 0:4, :], mul=0.125), r=['GB'], w=['GB'])
        cs = [p.sb([128, 2, 32], F32, f"cs{i}") for i in range(2)]
        xn = [p.sb([128, 14, 64], F32, f"xn{i}") for i in range(2)]
        sq = [p.sb([128, 14, 64], F32, f"sq{i}") for i in range(2)]
        ssq = [p.sb([128, 14], F32, f"ssq{i}") for i in range(2)]
        xr = [p.sb([128, 14, 64], BF16, f"xr{i}") for i in range(2)]
        RT = [[p.sb([128, 6, 2, 16], F32, f"ropeT{b_}{i}") for i in range(4)] for b_ in range(2)]
        def qk_iter(t):
            b = t % 2
            lat = t >= 2
            bA, bB, bV = (0, 1, 2) if t % 2 == 0 else (3, 6, 7)
            yield
            for bank, c0, c1 in ((bA, 0, 512), (bB, 512, 1024), (bV, 1024, 1280)):
                for j in range(8):
                    p.op('pe', lambda e, bank=bank, c0=c0, c1=c1, j=j, t=t: e.matmul(
                        ps[bank][:, 0:c1 - c0], hT[:, j, t * 128:(t + 1) * 128], wAB[:, j, c0:c1],
                        start=(j == 0), stop=(j == 7)),
                        r=[('hT', t), ('wAB', j)], w=[PS(bank)])
            yield
            if lat:
                tl = t - 2
                p.dma(lambda e, b=b, tl=tl: e.dma_start(out=cs[b][:, 0, :], in_=I['cos'][tl * 128:(tl + 1) * 128, :]),
                      w=[('cs', b)])
                p.dma(lambda e, b=b, tl=tl: e.dma_start(out=cs[b][:, 1, :], in_=I['sin'][tl * 128:(tl + 1) * 128, :]),
                      w=[('cs', b)])
            yield
            p.op('act', lambda e, t=t, bA=bA: e.activation(out=Vaug[:, t, 0:2, 0:64],
                                                    in_=ps[bA][:, 384:512].rearrange("p (h d) -> p h d", h=2),
                                                    func=AF.Copy), r=[PS(bA)], w=[('V', t)])
            yield
            p.op('act', lambda e, t=t, bV=bV: e.activation(out=Vaug[:, t, 2:6, 0:64],
                                                    in_=ps[bV][:, 0:256].rearrange("p (h d) -> p h d", h=4),
                                                    func=AF.Copy), r=[PS(bV)], w=[('V', t)])
            yield
            p.op('act', lambda e, b=b, bA=bA: e.activation(out=xn[b][:, 0:6, :],
                                                    in_=ps[bA][:, 0:384].rearrange("p (h d) -> p h d", h=6),
                                                    func=AF.Copy), r=[PS(bA)], w=[('xn', b)])
            yield
            p.op('act', lambda e, b=b, bB=bB: e.activation(out=xn[b][:, 6:14, :],
                                                    in_=ps[bB][:, 0:512].rearrange("p (h d) -> p h d", h=8),
                                                    func=AF.Copy), r=[PS(bB)], w=[('xn', b)])
            yield
            p.op('dve', lambda e, b=b: e.tensor_tensor(out=sq[b][:], in0=xn[b][:], in1=xn[b][:], op=ALU.mult),
                 r=[('xn', b)], w=[('sq', b)])
            yield
            p.op('dve', lambda e, b=b: e.tensor_reduce(out=ssq[b][:], in_=sq[b][:], axis=AX.X, op=ALU.add),
                 r=[('sq', b)], w=[('ssq', b)])
            yield
            p.op('act', lambda e, b=b: e.activation(out=ssq[b][:], in_=ssq[b][:], func=AF.Sqrt, bias=eps_col[:],
                                                    scale=1.0 / 64), r=[('ssq', b)], w=[('ssq', b)])
            yield
            p.op('dve', lambda e, b=b: e.reciprocal(out=ssq[b][:], in_=ssq[b][:]), r=[('ssq', b)], w=[('ssq', b)])
            yield
            p.op('dve', lambda e, b=b: e.tensor_tensor(out=xn[b][:], in0=xn[b][:],
                                                       in1=ssq[b][:].unsqueeze(2).to_broadcast([128, 14, 64]),
                                                       op=ALU.mult), r=[('xn', b), ('ssq', b)], w=[('xn', b)])
            yield
            p.op('dve', lambda e, b=b: e.tensor_tensor(out=xr[b][:, 6:14, :], in0=xn[b][:, 6:14, :], in1=GB[:],
                                                       op=ALU.mult), r=[('xn', b), 'GB'], w=[('xr', b)])
            yield
            if lat:
                p.op('dve', lambda e, b=b: e.tensor_tensor(out=xn[b][:, 0:6, :], in0=xn[b][:, 0:6, :], in1=GA[:],
                                                           op=ALU.mult), r=[('xn', b), 'GA'], w=[('xn', b)])
                xv = xn[b][:, 0:6, :].rearrange("p h (a g f) -> p h a g f", a=2, g=2)
                x1 = xv[:, :, :, 0, :]
                x2 = xv[:, :, :, 1, :]
                ov = xr[b][:, 0:6, :].rearrange("p h (a g f) -> p h a g f", a=2, g=2)
                cosb = cs[b][:, 0, :].rearrange("p (a f) -> p a f", a=2).unsqueeze(1).to_broadcast([128, 6, 2, 16])
                sinb = cs[b][:, 1, :].rearrange("p (a f) -> p a f", a=2).unsqueeze(1).to_broadcast([128, 6, 2, 16])
                rk = [('xn', b), ('cs', b)]
                for i, (xa, tb) in enumerate(((x1, cosb), (x2, sinb), (x2, cosb), (x1, sinb))):
                    p.op('dve', lambda e, i=i, xa=xa, tb=tb, b=b: e.tensor_tensor(out=RT[b][i][:], in0=xa, in1=tb, op=ALU.mult),
                         r=rk, w=[('RT', b, i)])
                p.op('dve', lambda e, ov=ov, b=b: e.tensor_tensor(out=ov[:, :, :, 0, :], in0=RT[b][0][:], in1=RT[b][1][:],
                                                             op=ALU.subtract), r=[('RT', b, 0), ('RT', b, 1)], w=[('xr', b)])
                p.op('dve', lambda e, ov=ov, b=b: e.tensor_tensor(out=ov[:, :, :, 1, :], in0=RT[b][2][:], in1=RT[b][3][:],
                                                             op=ALU.add), r=[('RT', b, 2), ('RT', b, 3)], w=[('xr', b)])
            else:
                p.op('dve', lambda e, b=b: e.tensor_tensor(out=xr[b][:, 0:6, :], in0=xn[b][:, 0:6, :], in1=GA[:],
                                                           op=ALU.mult), r=[('xn', b), 'GA'], w=[('xr', b)])
            yield
            for half in range(2):
                bank = 4 + half
                pv = ps[bank][0:64, 0:448].bitcast(BF16)
                for hh in range(7):
                    h = half * 7 + hh
                    p.op('pe', lambda e, b=b, h=h, hh=hh, pv=pv: e.transpose(
                        out=pv[:, hh * 128:(hh + 1) * 128], in_=xr[b][:, h, :], identity=ident_b[:]),
                        r=[('xr', b), 'identb'], w=[PS(bank)])
                p.op('act', lambda e, half=half, pv=pv, t=t: e.activation(
                    out=qkT[:, half * 7:(half + 1) * 7, t * 128:(t + 1) * 128],
                    in_=pv.rearrange("p (h t) -> p h t", h=7), func=AF.Copy), r=[PS(bank)], w=[('qkT', t)])
        def rr3(gens):
            gens = list(gens)
            while gens:
                for g_ in list(gens):
                    try:
                        next(g_)
                    except StopIteration:
                        gens.remove(g_)

        rolling([(lambda t=t: qk_iter(t)) for t in range(NT)], stagger=8)
        p.barrier()
        p.sb_reset(m_afterwc)
        cw = p.sb([128, 15, 3], F32, "cw")
        p.dma(lambda e: e.dma_start(out=cw[:], in_=I['r7_conv'][l]), w=['cw'])
        rawc = [p.sb([128, NCTX + 2], F32, f"rawc{i}") for i in range(2)]
        rawl = [p.sb([128, NLAT + 2], F32, f"rawl{i}") for i in range(2)]
        cvo = [p.sb([128, NTOK], F32, f"cvo{i}") for i in range(2)]
        for i in range(2):
            p.op('pool', lambda e, i=i: e.memset(rawc[i][:], 0.0), w=[('rawc', i)])
            p.op('pool', lambda e, i=i: e.memset(rawl[i][:], 0.0), w=[('rawl', i)])
        def cproj_iter(ch):
            b = ch % 2
            groups = [(rawc[b], ('rawc', b), 1, 0, 256)] + [(rawl[b], ('rawl', b), 1 + 512 * g, 256 + 512 * g, 512)
                                                             for g in range(4)]
            yield
            for gidx_, (raw, rkey, ro, tok0, n) in enumerate(groups):
                bank = 2 * (ch % 2) + gidx_ % 2
                for j in range(8):
                    p.op('pe', lambda e, bank=bank, j=j, ch=ch, tok0=tok0, n=n: e.matmul(
                        ps[bank][:, 0:n], wC[:, j, ch * 128:(ch + 1) * 128], hT[:, j, tok0:tok0 + n],
                        start=(j == 0), stop=(j == 7)), r=[('wC', j)], w=[PS(bank)])
                p.op('act', lambda e, raw=raw, ro=ro, n=n, bank=bank: e.activation(
                    out=raw[:, ro:ro + n], in_=ps[bank][:, 0:n], func=AF.Copy), r=[PS(bank)], w=[rkey])
            yield
            for (raw, rkey, n, o0) in ((rawc[b], ('rawc', b), NCTX, 0), (rawl[b], ('rawl', b), NLAT, NCTX)):
                dst = cvo[b][:, o0:o0 + n]
                p.op('dve', lambda e, raw=raw, n=n, dst=dst, ch=ch: e.tensor_scalar(
                    out=dst, in0=raw[:, 1:1 + n], scalar1=cw[:, ch, 1:2], scalar2=None, op0=ALU.mult),
                    r=[rkey, 'cw'], w=[('cvo', b)])
                p.op('dve', lambda e, raw=raw, n=n, dst=dst, ch=ch: e.scalar_tensor_tensor(
                    out=dst, in0=raw[:, 0:n], scalar=cw[:, ch, 0:1], in1=dst, op0=ALU.mult, op1=ALU.add),
                    r=[rkey, 'cw'], w=[('cvo', b)])
                p.op('dve', lambda e, raw=raw, n=n, dst=dst, ch=ch: e.scalar_tensor_tensor(
                    out=dst, in0=raw[:, 2:2 + n], scalar=cw[:, ch, 2:3], in1=dst, op0=ALU.mult, op1=ALU.add),
                    r=[rkey, 'cw'], w=[('cvo', b)])
            yield
            if ch == 12:
                p.op('act', lambda e, b=b: e.activation(out=cvo[b][:], in_=cvo[b][:], func=AF.Tanh),
                     r=[('cvo', b)], w=[('cvo', b)])
            yield
            if ch == 14:
                p.op('act', lambda e, b=b: e.activation(out=cvo[b][:], in_=cvo[b][:], func=AF.Sigmoid),
                     r=[('cvo', b)], w=[('cvo', b)])
            yield
            p.dma(lambda e, b=b, ch=ch: e.dma_start(out=S['pcT'][ch * 128:(ch + 1) * 128, :], in_=cvo[b][:]),
                  r=[('cvo', b)], w=[('pcT', ch)])
        def rr4(gens):
            gens = list(gens)
            while gens:
                for g_ in list(gens):
                    try:
                        next(g_)
                    except StopIteration:
                        gens.remove(g_)

        rolling([(lambda ch=ch: cproj_iter(ch)) for ch in range(15)], stagger=5)
        p.barrier()
        return qkT, Vaug, mark_persist

    def phase_attn(l, qkT, Vaug, mark_persist):
        with_ctx = l < DEPTH - 1
        p.sb_reset(mark_persist)
        btab = p.sb([128, NTAB, 4, 128], F32, "btab")
        bmask = p.sb([128, NTAB, 128], F32, "bmask")
        amask = p.sb([128, 2, 128], F32, "amask")
        esink = p.sb([128, 4], F32, "esink")
        o_all = [p.sb([128, 512], F32, f"oall{i}") for i in range(2)]
        ex = [p.sb([128, 8, 128], F32, f"ex{i}") for i in range(2)]
        pT = [p.sb([128, 8, 128], BF16, f"pT{i}") for i in range(2)]
        den = [p.sb([128, 1], F32, f"den{i}") for i in range(2)]
        for tb in range(NTAB):
            p.dma(lambda e, tb=tb: e.dma_start(out=btab[:, tb], in_=I['btab'][l, :, tb]), w=['btab'])
        p.dma(lambda e: e.dma_start(out=bmask[:], in_=I['bmask']), w=['bmask'])
        p.dma(lambda e: e.dma_start(out=amask[:], in_=I['amask']), w=['amask'])
        load_bc(esink[:], I['a_sink'][l], 'esink')
        p.op('act', lambda e: e.activation(out=esink[:], in_=esink[:], func=AF.Exp), r=['esink'], w=['esink'])
        p.op('act', lambda e: e.activation(out=btab[:], in_=btab[:], func=AF.Exp), r=['btab'], w=['btab'])
        for h in range(4):
            p.op('dve', lambda e, h=h: e.tensor_tensor(out=btab[:, :, h, :], in0=btab[:, :, h, :], in1=bmask[:],
                                                       op=ALU.mult), r=['btab', 'bmask'], w=['btab'])
        def attn_iter(t, grp, h, b, ob):
            if grp == 0:
                qs, ks, vs = h, 4 + h // 2, h // 2
            else:
                qs, ks, vs = 6 + h, 10 + h, 2 + h
            if t < 2:
                blocks = [(0, None), (1, None)]
            elif grp == 0:
                n = t - 2
                blocks = [(t, None), (0, None), (1, None)]
                if n > 0:
                    blocks.append((t - 1, amask[:, 0, :]))
                if n < 15:
                    blocks.append((t + 1, amask[:, 1, :]))
            else:
                pq = t - 2
                blocks = [(0, None), (1, None)]
                for kb in range(16):
                    if (pq, kb) in NA_CASES:
                        blocks.append((kb + 2, btab[:, NA_CASES[(pq, kb)], h, :]))
            nb = len(blocks)
            nn = sum(1 for _, tb in blocks if tb is None)
            sb0, sb1 = (0, 1) if b == 0 else (2, 3)
            ob_ps = 4 + b
            yield
            for i, (kt, tb) in enumerate(blocks):
                bank = sb0 if i < 4 else sb1
                p.op('pe', lambda e, bank=bank, i=i, kt=kt, ks=ks, qs=qs, t=t: e.matmul(
                    ps[bank][:, (i % 4) * 128:(i % 4 + 1) * 128], qkT[:, ks, kt * 128:(kt + 1) * 128],
                    qkT[:, qs, t * 128:(t + 1) * 128], start=True, stop=True), w=[PS(bank)])
            n0 = min(nb, 4)
            yield
            p.op('act', lambda e, b=b, n0=n0, sb0=sb0: e.activation(
                out=ex[b][:, 0:n0, :], in_=ps[sb0][:, 0:n0 * 128].rearrange("p (n k) -> p n k", n=n0),
                func=AF.Exp), r=[PS(sb0)], w=[('ex', b)])
            if nb > 4:
                n1 = nb - 4
                p.op('act', lambda e, b=b, n1=n1, sb1=sb1: e.activation(
                    out=ex[b][:, 4:4 + n1, :], in_=ps[sb1][:, 0:n1 * 128].rearrange("p (n k) -> p n k", n=n1),
                    func=AF.Exp), r=[PS(sb1)], w=[('ex', b)])
            yield
            p.op('pool', lambda e, b=b, nn=nn: e.tensor_copy(out=pT[b][:, 0:nn, :], in_=ex[b][:, 0:nn, :]),
                 r=[('ex', b)], w=[('pT', b)])
            yield
            for i, (kt, tb) in enumerate(blocks):
                if tb is None:
                    continue
                p.op('dve', lambda e, b=b, i=i, tb=tb: e.tensor_tensor(out=pT[b][:, i, :], in0=ex[b][:, i, :],
                                                                      in1=tb, op=ALU.mult),
                     r=[('ex', b), 'btab', 'amask'], w=[('pT', b)])
            yield
            for i, (kt, tb) in enumerate(blocks):
                p.op('pe', lambda e, b=b, i=i, kt=kt, vs=vs, ob_ps=ob_ps, nb=nb: e.matmul(
                    ps[ob_ps][:, 0:65], pT[b][:, i, :], Vaug[:, kt, vs, :], start=(i == 0), stop=(i == nb - 1)),
                    r=[('pT', b)], w=[PS(ob_ps)])
            if grp == 0:
                p.op('dve', lambda e, b=b, h=h, ob_ps=ob_ps: e.tensor_scalar(
                    out=den[b][:], in0=ps[ob_ps][:, 64:65], scalar1=esink[:, h:h + 1], scalar2=None,
                    op0=ALU.add), r=[PS(ob_ps), 'esink'], w=[('den', b)])
                p.op('dve', lambda e, b=b: e.reciprocal(out=den[b][:], in_=den[b][:]),
                     r=[('den', b)], w=[('den', b)])
            else:
                p.op('dve', lambda e, b=b, ob_ps=ob_ps: e.reciprocal(out=den[b][:], in_=ps[ob_ps][:, 64:65]),
                     r=[PS(ob_ps)], w=[('den', b)])
            col = grp * 256 + h * 64
            yield
            p.op('dve', lambda e, b=b, ob=ob, col=col, ob_ps=ob_ps: e.tensor_scalar(
                out=o_all[ob][:, col:col + 64], in0=ps[ob_ps][:, 0:64], scalar1=den[b][:, 0:1], scalar2=None,
                op0=ALU.mult), r=[PS(ob_ps), ('den', b)], w=[('oall', ob)])

        def rr2(gens):
            gens = list(gens)
            while gens:
                for g_ in list(gens):
                    try:
                        next(g_)
                    except StopIteration:
                        gens.remove(g_)

        it = 0
        all_its = []
        for t in range(NT):
            if t < 2 and not with_ctx:
                continue
            ob = t % 2
            for grp in range(2):
                for h in range(4):
                    all_its.append((t, grp, h, it % 2, ob, grp == 1 and h == 3))
                    it += 1

        def attn_wrap(t, grp, h, b, ob, is_last):
            yield from attn_iter(t, grp, h, b, ob)
            if is_last:
                p.dma(lambda e, ob=ob, t=t: e.dma_start(out=S['o'][t * 128:(t + 1) * 128, 0:512], in_=o_all[ob][:]),
                      r=[('oall', ob)], w=[('So', t)])

        rolling([(lambda a=a: attn_wrap(*a)) for a in all_its], stagger=3)
        p.barrier()


    def phase_rprep(l):
        p.sb_reset(base_mark)
        w2 = p.sb([128, 512], F32, "w2")
        a2 = p.sb([128, 512], F32, "a2")
        g2 = p.sb([128, 512], F32, "g2")
        w0 = p.sb([1, 2, 512], F32, "w0")
        a0 = p.sb([1, 2, 512], F32, "a0")
        ones = p.sb([1, 128], F32, "ones")
        KKW = p.sb([128, 512], F32, "KKW")
        KA = p.sb([128, 512], F32, "KA")
        RK = p.sb([128, 512], F32, "RK")
        p.dma(lambda e: e.dma_start(out=w2[:], in_=I['r7_w2'][l].rearrange("d r c -> (d r) c")), w=['w2'])
        p.dma(lambda e: e.dma_start(out=a2[:], in_=I['r7_a2'][l].rearrange("d r c -> (d r) c")), w=['a2'])
        p.dma(lambda e: e.dma_start(out=g2[:], in_=I['r7_g2'][l]), w=['g2'])
        p.dma(lambda e: e.dma_start(out=w0[:], in_=I['r7_w0'][l:l + 1]), w=['w0'])
        p.dma(lambda e: e.dma_start(out=a0[:], in_=I['r7_a0'][l:l + 1]), w=['a0'])
        p.op('dve', lambda e: e.memset(ones[:], 1.0), w=['ones'])
        load_bc(KKW[:], I['r7_kk'][l], 'KKW')
        load_bc(KA[:], I['r7_ka'][l], 'KA')
        load_bc(RK[:], I['r7_rk'][l].rearrange("h d -> (h d)"), 'RK')
        fm = [p.sb([128, 15, 128], F32, f"fm{i}") for i in range(2)]
        TM = [p.sb([128, 10, 512], F32, f"TM{i}") for i in range(2)]
        kt = p.sb([128, 512], F32, "kt")
        av = [p.sb([128, 512], F32, f"av{i}") for i in range(2)]
        tmp = p.sb([128, 512], F32, "tmp")
        tmp2 = p.sb([128, 512], F32, "tmp2")
        s8 = p.sb([128, 8], F32, "s8")
        bs = [p.sb([128, 8], F32, f"bs{i}") for i in range(2)]
        for t in range(NT):
            b = t % 2
            p.dma(lambda e, b=b, t=t: e.dma_start(
                out=fm[b][:], in_=S['pcT'][:, t * 128:(t + 1) * 128].rearrange("(c p) t -> p c t", p=128)),
                w=[('fm', b)])
            for q in range(3):
                for c4 in range(4):
                    p.op('pe', lambda e, b=b, q=q, c4=c4: e.transpose(
                        out=ps[q][:, c4 * 128:(c4 + 1) * 128], in_=fm[b][:, q * 4 + c4, :], identity=ident_f[:]),
                        r=[('fm', b), 'identf'], w=[PS(q)])
            for d in range(2):
                pr = slice(d * 64, d * 64 + 64)
                p.op('pe', lambda e, b=b, d=d, pr=pr: e.matmul(ps[3 + d][:, :], fm[b][pr, 12, :], w2[pr, :],
                                                              start=True, stop=False), r=[('fm', b), 'w2'], w=[PS(3 + d)])
                p.op('pe', lambda e, d=d: e.matmul(ps[3 + d][:, :], ones[0:1, :], w0[0:1, d, :], start=False, stop=True),
                     r=['ones', 'w0'], w=[PS(3 + d)])
                p.op('pe', lambda e, b=b, d=d, pr=pr: e.matmul(ps[5 + d][:, :], fm[b][pr, 13, :], a2[pr, :],
                                                              start=True, stop=False), r=[('fm', b), 'a2'], w=[PS(5 + d)])
                p.op('pe', lambda e, d=d: e.matmul(ps[5 + d][:, :], ones[0:1, :], a0[0:1, d, :], start=False, stop=True),
                     r=['ones', 'a0'], w=[PS(5 + d)])
            p.op('pe', lambda e, b=b: e.matmul(ps[7][:, :], fm[b][:, 14, :], g2[:], start=True, stop=True),
                 r=[('fm', b), 'g2'], w=[PS(7)])
            T = TM[b]
            wk = [('TM', b)]
            p.op('act', lambda e, T=T: e.activation(out=T[:, 0, :], in_=ps[0][:, :], func=AF.Copy), r=[PS(0)], w=wk)
            p.op('act', lambda e: e.activation(out=kt[:], in_=ps[1][:, :], func=AF.Copy), r=[PS(1)], w=['kt'])
            p.op('act', lambda e, T=T: e.activation(out=T[:, 1, :], in_=ps[2][:, :], func=AF.Copy), r=[PS(2)], w=wk)
            p.op('act', lambda e, T=T: e.activation(out=T[:, 2, :], in_=ps[7][:, :], func=AF.Copy), r=[PS(7)], w=wk)
            for d in range(2):
                p.op('act', lambda e, T=T, d=d: e.activation(out=T[:, 8 + d, :], in_=ps[3 + d][:, :], func=AF.Sigmoid),
                     r=[PS(3 + d)], w=wk)
                p.op('act', lambda e, d=d: e.activation(out=av[d][:], in_=ps[5 + d][:, :], func=AF.Sigmoid),
                     r=[PS(5 + d)], w=[('av', d)])
                p.op('dve', lambda e, T=T, d=d: e.tensor_scalar(out=T[:, 8 + d, :], in0=T[:, 8 + d, :],
                                                                scalar1=-0.6065306597126334, scalar2=None, op0=ALU.mult),
                     r=wk, w=wk)
            p.op('dve', lambda e: e.tensor_tensor(out=tmp[:], in0=kt[:], in1=KKW[:], op=ALU.mult), r=['kt', 'KKW'], w=['tmp'])
            p.op('dve', lambda e: e.tensor_tensor(out=tmp2[:], in0=tmp[:], in1=tmp[:], op=ALU.mult), r=['tmp'], w=['tmp2'])
            p.op('dve', lambda e: e.tensor_reduce(out=s8[:], in_=tmp2[:].rearrange("p (h d) -> p h d", h=8), axis=AX.X,
                                                  op=ALU.add), r=['tmp2'], w=['s8'])
            p.op('act', lambda e: e.activation(out=s8[:], in_=s8[:], func=AF.Sqrt), r=['s8'], w=['s8'])
            p.op('dve', lambda e: e.tensor_scalar(out=s8[:], in0=s8[:], scalar1=1e-12, scalar2=None, op0=ALU.max),
                 r=['s8'], w=['s8'])
            p.op('dve', lambda e: e.reciprocal(out=s8[:], in_=s8[:]), r=['s8'], w=['s8'])
            p.op('dve', lambda e, T=T: e.tensor_tensor(out=T[:, 3, :].rearrange("p (h d) -> p h d", h=8),
                                                       in0=tmp[:].rearrange("p (h d) -> p h d", h=8),
                                                       in1=s8[:].unsqueeze(2).to_broadcast([128, 8, 64]), op=ALU.mult),
                 r=['tmp', 's8'], w=wk)
            for d in range(2):
                p.op('dve', lambda e, d=d: e.scalar_tensor_tensor(out=tmp2[:], in0=av[d][:], scalar=-1.0, in1=KA[:],
                                                                  op0=ALU.add, op1=ALU.mult),
                     r=[('av', d), 'KA'], w=['tmp2'])
                p.op('dve', lambda e, T=T, d=d: e.scalar_tensor_tensor(out=T[:, 4 + d, :], in0=tmp2[:], scalar=1.0,
                                                                       in1=kt[:], op0=ALU.add, op1=ALU.mult),
                     r=['tmp2', 'kt'], w=wk)
                p.op('dve', lambda e, T=T, d=d: e.tensor_tensor(out=T[:, 6 + d, :], in0=T[:, 3, :], in1=av[d][:],
                                                                op=ALU.mult), r=wk + [('av', d)], w=wk)
            p.op('dve', lambda e, T=T: e.tensor_tensor(out=tmp[:], in0=T[:, 4, :], in1=T[:, 5, :], op=ALU.add),
                 r=wk, w=['tmp'])
            p.op('dve', lambda e: e.tensor_tensor(out=tmp[:], in0=tmp[:], in1=RK[:], op=ALU.mult), r=['tmp', 'RK'], w=['tmp'])
            p.op('dve', lambda e, T=T: e.tensor_tensor(out=tmp[:], in0=tmp[:], in1=T[:, 0, :], op=ALU.mult),
                 r=['tmp'] + wk, w=['tmp'])
            p.op('dve', lambda e, b=b: e.tensor_reduce(out=bs[b][:], in_=tmp[:].rearrange("p (h d) -> p h d", h=8),
                                                       axis=AX.X, op=ALU.add), r=['tmp'], w=[('bs', b)])
            p.dma(lambda e, T=T, t=t: e.dma_start(out=S['tm'][t * 128:(t + 1) * 128], in_=T[:]), r=wk, w=[('Stm', t)],
                  eng='pool')
            p.dma(lambda e, b=b, t=t: e.dma_start(out=S['bon'][t * 128:(t + 1) * 128], in_=bs[b][:]),
                  r=[('bs', b)], w=[('Sbon', t)], eng='pool')
        p.barrier()

    def phase_scan(l):
        p.sb_reset(base_mark)
        PSB = ps
        C = 64
        NCH = NTOK // C
        tri = p.sb([64, 2, 64], F32, "tri")
        mg = p.sb([64, 2, 128], F32, "mg")
        mn = p.sb([64, 2, 64], F32, "mn")
        ones = p.sb([64, 1], F32, "ones1")
        p.dma(lambda e: e.dma_start(out=tri[:], in_=I['tri']), w=['tri'])
        p.dma(lambda e: e.dma_start(out=mg[:], in_=I['mg']), w=['mg'])
        p.dma(lambda e: e.dma_start(out=mn[:], in_=I['mn']), w=['mn'])
        p.op('dve', lambda e: e.memset(ones[:], 1.0), w=['ones1'])
        M = [p.sb([64, 8, 64], F32, f"M{d}") for d in range(2)]
        for d in range(2):
            M0_PLACEHOLDER = None
        X = [[p.sb([64, 6, 512], F32, f"X{d}{i}") for i in range(2)] for d in range(2)]
        def mk(shape, name):
            return [p.sb(shape, F32, f"{name}{d}") for d in range(2)]
        E0s, E1s, E2s = mk([64, 512], "E0"), mk([64, 512], "E1"), mk([64, 512], "E2")
        Ats, Rts, Bts, Kts = mk([64, 512], "At"), mk([64, 512], "Rt"), mk([64, 512], "Bt"), mk([64, 512], "Kt")
        FARs, FBs, FKs = mk([64, 8, 128], "FAR"), mk([64, 8, 64], "FB"), mk([64, 8, 64], "FK")
        G1s, G2s = mk([64, 8, 128], "G1"), mk([64, 8, 128], "G2")
        Tms = [mk([64, 8, 64], f"Tm{i}_") for i in range(2)]
        Nms = [mk([64, 8, 64], f"Nm{i}_") for i in range(2)]
        Zs, Wss, Uss, PCs = mk([64, 8, 64], "Z"), mk([64, 512], "Ws"), mk([64, 512], "Us"), mk([64, 8], "PC")
        Ys = [p.sb([64, 512], F32, f"Ys{d}") for d in range(2)]
        order = {0: list(range(0, 4)) + list(range(4, NCH)), 1: list(range(3, -1, -1)) + list(range(NCH - 1, 3, -1))}
        v3 = lambda ap: ap.rearrange("p (h d) -> p h d", h=8)
        F32R = mybir.dt.float32r
        use_r = cfg.get("fp32r", True)

        def RR(ap):
            return ap.bitcast(F32R) if use_r else ap

        Vrs = mk([64, 512], "Vr")
        Mts = mk([64, 8, 64], "Mt")
        for d in range(2):
            p.op('dve', lambda e, d=d: e.memset(Mts[d][:], 0.0), w=[('Mt', d)])
            p.op('dve', lambda e, d=d: e.tensor_copy(out=RR(M[d][:]), in_=Mts[d][:]), r=[('Mt', d)], w=[('M', d)])

        def mmr(e, out, lhsT, rhs, **kw):
            if use_r:
                return e.matmul(out, lhsT.bitcast(F32R), rhs.bitcast(F32R), **kw)
            return e.matmul(out, lhsT, rhs, **kw)

        def scan_unit(d, c):
            if True:
                tok0 = c * C
                Xd = X[d][c % 2]
                E0, E1, E2, At, Rt, Bt, Kt = E0s[d], E1s[d], E2s[d], Ats[d], Rts[d], Bts[d], Kts[d]
                FAR, FB, FK, G1, G2 = FARs[d], FBs[d], FKs[d], G1s[d], G2s[d]
                Tm = [Tms[0][d], Tms[1][d]]
                Nm = [Nms[0][d], Nms[1][d]]
                Z, Ws, Us, PC = Zs[d], Wss[d], Uss[d], PCs[d]
                ps = [PSB[4 * d + (i % 4)] for i in range(8)]
                PS = lambda i: ('ps', 4 * d + (i % 4))
                xk = [('X', d, c % 2)]
                srcs = [0, 1, 3, 4 + d, 6 + d, 8 + d]
                yield
                for i, s in enumerate(srcs):
                    p.dma(lambda e, Xd=Xd, i=i, s=s, tok0=tok0: e.dma_start(out=Xd[:, i, :],
                                                                           in_=S['tm'][tok0:tok0 + C, s, :]), w=xk)
                r_, v_, kk_, k_, b_, lw_ = [Xd[:, i, :] for i in range(6)]
                Vr = Vrs[d]
                yield
                p.op('act', lambda e, v_=v_: e.activation(out=RR(Vr[:]), in_=v_, func=AF.Copy), r=xk, w=[('Vr', d)])
                v_ = Vr[:]
                vk = [('Vr', d)]
                yield
                p.op('pe', lambda e, d=d, lw_=lw_: e.matmul(ps[0][0:64, :], tri[:, d, :], lw_, start=True, stop=True),
                     r=xk + ['tri'], w=[PS(0)])
                yield
                for h in range(8):
                    p.op('pe', lambda e, h=h, lw_=lw_: e.matmul(ps[1][0:64, h:h + 1], lw_[:, h * 64:(h + 1) * 64],
                                                               ones[:, 0:1], start=True, stop=True),
                         r=xk + ['ones1'], w=[PS(1)])
                yield
                p.op('act', lambda e: e.activation(out=PC[:], in_=ps[1][0:64, 0:8], func=AF.Exp), r=[PS(1)], w=[('PC', d)])
                yield
                p.op('act', lambda e: e.activation(out=E1[:], in_=ps[0][0:64, :], func=AF.Exp), r=[PS(0)], w=[('E1', d)])
                yield
                p.op('act', lambda e: e.activation(out=E2[:], in_=ps[0][0:64, :], func=AF.Exp, scale=-1.0),
                     r=[PS(0)], w=[('E2', d)])
                yield
                p.op('dve', lambda e, lw_=lw_: e.tensor_tensor(out=E0[:], in0=ps[0][0:64, :], in1=lw_, op=ALU.subtract),
                     r=[PS(0)] + xk, w=[('E0', d)])
                yield
                p.op('act', lambda e: e.activation(out=E0[:], in_=E0[:], func=AF.Exp), r=[('E0', d)], w=[('E0', d)])
                yield
                p.op('dve', lambda e, kk_=kk_: e.scalar_tensor_tensor(out=At[:], in0=kk_, scalar=-1.0, in1=E0[:],
                                                                      op0=ALU.mult, op1=ALU.mult),
                     r=xk + [('E0', d)], w=[('At', d)])
                yield
                p.op('dve', lambda e, r_=r_: e.tensor_tensor(out=Rt[:], in0=r_, in1=E1[:], op=ALU.mult),
                     r=xk + [('E1', d)], w=[('Rt', d)])
                yield
                p.op('dve', lambda e, b_=b_: e.tensor_tensor(out=RR(Bt[:]), in0=b_, in1=E2[:], op=ALU.mult),
                     r=xk + [('E2', d)], w=[('Bt', d)])
                yield
                p.op('dve', lambda e, k_=k_: e.tensor_tensor(out=RR(Kt[:]), in0=k_, in1=E2[:], op=ALU.mult),
                     r=xk + [('E2', d)], w=[('Kt', d)])
                yield
                for bank, src, key in ((2, At, ('At', d)), (3, Rt, ('Rt', d)), (4, Bt, ('Bt', d)), (5, Kt, ('Kt', d))):
                    for h in range(8):
                        p.op('pe', lambda e, bank=bank, src=src, h=h: e.transpose(
                            out=ps[bank][0:64, h * 64:(h + 1) * 64], in_=src[:, h * 64:(h + 1) * 64],
                            identity=ident_f[0:64, 0:64]), r=[key, 'identf'], w=[PS(bank)])
                yield
                p.op('act', lambda e: e.activation(out=RR(FAR[:, :, 0:64]), in_=v3(ps[2][0:64, :]), func=AF.Copy),
                     r=[PS(2)], w=[('FAR', d)])
                yield
                p.op('act', lambda e: e.activation(out=RR(FAR[:, :, 64:128]), in_=v3(ps[3][0:64, :]), func=AF.Copy),
                     r=[PS(3)], w=[('FAR', d)])
                yield
                p.op('dve', lambda e: e.tensor_copy(out=RR(FB[:]), in_=v3(ps[4][0:64, :])), r=[PS(4)], w=[('FB', d)])
                yield
                p.op('dve', lambda e: e.tensor_copy(out=RR(FK[:]), in_=v3(ps[5][0:64, :])), r=[PS(5)], w=[('FK', d)])
                yield
                for h in range(8):
                    bank = 6 + (h // 4)
                    p.op('pe', lambda e, h=h, bank=bank: mmr(e, ps[bank][0:64, (h % 4) * 128:(h % 4 + 1) * 128],
                                                                   FB[:, h, :], FAR[:, h, :], start=True, stop=True),
                         r=[('FB', d), ('FAR', d)], w=[PS(bank)])
                yield
                for hb in range(2):
                    p.op('dve', lambda e, hb=hb, d=d: e.tensor_tensor(
                        out=RR(G1[:, hb * 4:(hb + 1) * 4, :]), in0=ps[6 + hb][0:64, :].rearrange("p (h t) -> p h t", h=4),
                        in1=mg[:, d, :].unsqueeze(1).to_broadcast([64, 4, 128]), op=ALU.mult),
                        r=[PS(6 + hb), 'mg'], w=[('G1', d)])
                yield
                for h in range(8):
                    bank = 2 + (h // 4)
                    p.op('pe', lambda e, h=h, bank=bank: mmr(e, ps[bank][0:64, (h % 4) * 128:(h % 4 + 1) * 128],
                                                                   FK[:, h, :], FAR[:, h, :], start=True, stop=True),
                         r=[('FK', d), ('FAR', d)], w=[PS(bank)])
                yield
                for hb in range(2):
                    p.op('dve', lambda e, hb=hb, d=d: e.tensor_tensor(
                        out=RR(G2[:, hb * 4:(hb + 1) * 4, :]), in0=ps[2 + hb][0:64, :].rearrange("p (h t) -> p h t", h=4),
                        in1=mg[:, d, :].unsqueeze(1).to_broadcast([64, 4, 128]), op=ALU.mult),
                        r=[PS(2 + hb), 'mg'], w=[('G2', d)])
                yield
                for h in range(8):
                    p.op('pe', lambda e, h=h: mmr(e, ps[4][0:64, h * 64:(h + 1) * 64], FAR[:, h, 0:64], FB[:, h, :],
                                                       start=True, stop=True), r=[('FAR', d), ('FB', d)], w=[PS(4)])
                yield
                p.op('dve', lambda e, d=d: e.tensor_tensor(out=RR(Nm[0][:]), in0=v3(ps[4][0:64, :]),
                                                           in1=mn[:, d, :].unsqueeze(1).to_broadcast([64, 8, 64]),
                                                           op=ALU.mult), r=[PS(4), 'mn'], w=[('Nm', d, 0)])
                yield
                p.op('dve', lambda e: e.tensor_copy(out=RR(Tm[0][:]), in_=G1[:, :, 0:64]), r=[('G1', d)], w=[('Tm', d, 0)])
                yield
                p.op('dve', lambda e: e.tensor_tensor(out=RR(Z[:]), in0=G1[:, :, 0:64],
                                                      in1=ident_f[0:64, 0:64].unsqueeze(1).to_broadcast([64, 8, 64]),
                                                      op=ALU.add), r=[('G1', d), 'identf'], w=[('Z', d)])
                cur = 0
                yield
                for lev in range(5):
                    nxt = 1 - cur
                    last = lev == 4
                    for h in range(8):
                        p.op('pe', lambda e, h=h, cur=cur: mmr(e, ps[5][0:64, h * 64:(h + 1) * 64], Tm[cur][:, h, :],
                                                                    Nm[cur][:, h, :], start=True, stop=True),
                             r=[('Tm', d, cur), ('Nm', d, cur)], w=[PS(5)])
                    p.op('act', lambda e, nxt=nxt: e.activation(out=RR(Nm[nxt][:]), in_=v3(ps[5][0:64, :]), func=AF.Copy),
                         r=[PS(5)], w=[('Nm', d, nxt)])
                    if not last:
                        for h in range(8):
                            p.op('pe', lambda e, h=h, cur=cur: mmr(e, ps[6][0:64, h * 64:(h + 1) * 64],
                                                                        Nm[cur][:, h, :], Tm[cur][:, h, :],
                                                                        start=True, stop=True),
                                 r=[('Tm', d, cur), ('Nm', d, cur)], w=[PS(6)])
                        p.op('dve', lambda e, nxt=nxt: e.tensor_copy(out=RR(Tm[nxt][:]), in_=v3(ps[6][0:64, :])),
                             r=[PS(6)], w=[('Tm', d, nxt)])
                    for h in range(8):
                        p.op('pe', lambda e, h=h, nxt=nxt: mmr(e, ps[7][0:64, h * 64:(h + 1) * 64], Nm[nxt][:, h, :],
                                                                    Z[:, h, :], start=True, stop=True),
                             r=[('Nm', d, nxt), ('Z', d)], w=[PS(7)])
                    p.op('dve', lambda e: e.tensor_tensor(out=RR(Z[:]), in0=Z[:], in1=v3(ps[7][0:64, :]), op=ALU.add),
                         r=[PS(7), ('Z', d)], w=[('Z', d)])
                    cur = nxt
                Md = M[d]
                yield
                for h in range(8):
                    o = ps[0][0:64, h * 64:(h + 1) * 64]
                    p.op('pe', lambda e, h=h, o=o, Md=Md: mmr(e, o, FAR[:, h, 0:64], Md[:, h, :], start=True, stop=False),
                         r=[('FAR', d), ('M', d)], w=[PS(0)])
                    p.op('pe', lambda e, h=h, o=o, v_=v_: mmr(e, o, G2[:, h, 0:64], v_[:, h * 64:(h + 1) * 64],
                                                                   start=False, stop=True), r=[('G2', d)] + vk, w=[PS(0)])
                yield
                p.op('act', lambda e: e.activation(out=RR(Ws[:]), in_=ps[0][0:64, :], func=AF.Copy), r=[PS(0)], w=[('Ws', d)])
                yield
                for h in range(8):
                    p.op('pe', lambda e, h=h: mmr(e, ps[1][0:64, h * 64:(h + 1) * 64], Z[:, h, :],
                                                       Ws[:, h * 64:(h + 1) * 64], start=True, stop=True),
                         r=[('Z', d), ('Ws', d)], w=[PS(1)])
                yield
                p.op('act', lambda e: e.activation(out=RR(Us[:]), in_=ps[1][0:64, :], func=AF.Copy), r=[PS(1)], w=[('Us', d)])
                yield
                for h in range(8):
                    o = ps[2][0:64, h * 64:(h + 1) * 64]
                    hs = slice(h * 64, (h + 1) * 64)
                    p.op('pe', lambda e, h=h, o=o, Md=Md: mmr(e, o, FAR[:, h, 64:128], Md[:, h, :], start=True, stop=False),
                         r=[('FAR', d), ('M', d)], w=[PS(2)])
                    p.op('pe', lambda e, h=h, o=o, hs=hs: mmr(e, o, G1[:, h, 64:128], Us[:, hs], start=False, stop=False),
                         r=[('G1', d), ('Us', d)], w=[PS(2)])
                    p.op('pe', lambda e, h=h, o=o, hs=hs, v_=v_: mmr(e, o, G2[:, h, 64:128], v_[:, hs], start=False, stop=True),
                         r=[('G2', d)] + vk, w=[PS(2)])
                yield
                p.op('act', lambda e, d=d: e.activation(out=Ys[d][:], in_=ps[2][0:64, :], func=AF.Copy),
                     r=[PS(2)], w=[('Ys', d)])
                yield
                p.dma(lambda e, d=d, tok0=tok0: e.dma_start(out=S['y'][d, tok0:tok0 + C, :], in_=Ys[d][:]),
                      r=[('Ys', d)], w=[('Sy', d, c)], eng='act')
                yield
                for h in range(8):
                    o = ps[3][0:64, h * 64:(h + 1) * 64]
                    hs = slice(h * 64, (h + 1) * 64)
                    p.op('pe', lambda e, o=o, hs=hs: mmr(e, o, Bt[:, hs], Us[:, hs], start=True, stop=False),
                         r=[('Bt', d), ('Us', d)], w=[PS(3)])
                    p.op('pe', lambda e, o=o, hs=hs, v_=v_: mmr(e, o, Kt[:, hs], v_[:, hs], start=False, stop=True),
                         r=[('Kt', d)] + vk, w=[PS(3)])
                Mt = Mts[d]
                yield
                p.op('dve', lambda e, Md=Md, Mt=Mt: e.tensor_tensor(out=Mt[:], in0=Md[:], in1=v3(ps[3][0:64, :]), op=ALU.add),
                     r=[PS(3), ('M', d)], w=[('Mt', d)])
                yield
                p.op('dve', lambda e, Md=Md, Mt=Mt: e.tensor_tensor(out=RR(Md[:]), in0=Mt[:],
                                                             in1=PC[:].unsqueeze(2).to_broadcast([64, 8, 64]),
                                                             op=ALU.mult), r=[('PC', d), ('Mt', d)], w=[('M', d)])
        cin = [[p.sb([128, 1, D], F32, f"cin{i}{q}") for q in range(2)] for i in range(2)]
        cout = [p.sb([128, 1, 2 * D], BF16, f"cout{i}") for i in range(2)]

        def conv_block(blk):
            b = blk % 2
            rows = slice(blk * 128, (blk + 1) * 128)
            for q, tabn in enumerate(('peer_u', 'peer_v')):
                p.dma(lambda e, b=b, q=q, tabn=tabn, rows=rows: e.dma_start(
                    out=cin[b][q][:], in_=I[tabn][l][rows, :].rearrange("(j p) d -> p j d", p=128)), w=[('cin', b, q)],
                    eng="pool")
                p.op('pool', lambda e, b=b, q=q: e.tensor_copy(out=cout[b][:, :, q * D:(q + 1) * D], in_=cin[b][q][:]),
                     r=[('cin', b, q)], w=[('cout', b, q)])
            p.dma(lambda e, b=b, rows=rows: e.dma_start(
                out=S['T'][l][rows, :].rearrange("(j p) d -> p j d", p=128), in_=cout[b][:]),
                r=[('cout', b, 0), ('cout', b, 1)], w=[('cout', b, 0), ('cout', b, 1)], eng="pool")

        nblk = 0
        for step in range(NCH):
            gens = [scan_unit(d, order[d][step]) for d in range(2)]
            while gens:
                for g_ in list(gens):
                    try:
                        next(g_)
                    except StopIteration:
                        gens.remove(g_)
            for _ in range(4):
                if nblk < 128:
                    conv_block(nblk)
                    nblk += 1
        while nblk < 128:
            conv_block(nblk)
            nblk += 1
        p.barrier()


    def phase_rout(l):
        p.sb_reset(base_mark)
        with_ctx = l < DEPTH - 1
        wo = p.sb([128, 8, D], BF16, "wo")
        for j in range(8):
            p.dma(lambda e, j=j: e.dma_start(out=wo[:, j, :], in_=I['w_out'][l, j * 128:(j + 1) * 128, :]),
                  w=[('wo', j)], eng="pool")
        LNW = p.sb([128, 512], F32, "LNW")
        LNB = p.sb([128, 512], F32, "LNB")
        G1b = [p.sb([128, D], F32, f"G1b{s}") for s in range(2)]
        load_bc(LNW[:], I['r7_lnw'][l], 'LNW')
        load_bc(LNB[:], I['r7_lnb'][l], 'LNB')
        gn_eps = p.sb([128, 1], F32, "gneps")
        p.op('dve', lambda e: e.memset(gn_eps[:], 64e-5), w=['gneps'])
        for s in range(2):
            load_bc(G1b[s][:], S['mod'][l, s, 2 * D:3 * D], ('G1b', s))
        yb = [[p.sb([128, 512], F32, f"y{d}{i}") for d in range(2)] for i in range(2)]
        vg = [p.sb([128, 2, 512], F32, f"vg{i}") for i in range(2)]
        bon = [p.sb([128, 8], F32, f"bon{i}") for i in range(2)]
        O = [p.sb([128, D], F32, f"O{i}") for i in range(2)]
        Ob = [p.sb([128, D], BF16, f"Ob{i}") for i in range(2)]
        oT = [p.sb([128, 8, 128], BF16, f"oT{i}") for i in range(2)]
        xt = [p.sb([128, D], F32, f"xr{i}") for i in range(2)]
        yc = [p.sb([128, 512], F32, f"yc{i}") for i in range(2)]
        sq = [p.sb([128, 512], F32, f"sq2{i}") for i in range(2)]
        m8 = [p.sb([128, 8], F32, f"m8{i}") for i in range(2)]
        v8 = [p.sb([128, 8], F32, f"v8{i}") for i in range(2)]
        src = I['x'] if l == 0 else S['xs']
        v3 = lambda ap: ap.rearrange("p (h d) -> p h d", h=8)
        bc8 = lambda ap: ap.unsqueeze(2).to_broadcast([128, 8, 64])
        def rout_iter(t):
            b = t % 2
            s = 1 if t < 2 else 0
            rows = slice(t * 128, (t + 1) * 128)
            yield
            for d in range(2):
                p.dma(lambda e, b=b, d=d, rows=rows: e.dma_start(out=yb[b][d][:], in_=S['y'][d, rows, :]), w=[('y', b, d)])
            yield
            p.dma(lambda e, b=b, rows=rows: e.dma_start(out=vg[b][:], in_=S['tm'][rows, 1:3, :]), w=[('vg', b)])
            yield
            p.dma(lambda e, b=b, rows=rows: e.dma_start(out=bon[b][:], in_=S['bon'][rows, :]), w=[('bon', b)])
            yield
            p.dma(lambda e, b=b, rows=rows: e.dma_start(out=O[b][:, 0:512], in_=S['o'][rows, 0:512]), w=[('O', b)])
            yield
            p.dma(lambda e, b=b, rows=rows: e.dma_start(out=xt[b][:], in_=src[rows, :]), w=[('xr', b)])
            yield
            p.op('dve', lambda e, b=b: e.tensor_tensor(out=yc[b][:], in0=yb[b][0][:], in1=yb[b][1][:], op=ALU.add),
                 r=[('y', b, 0), ('y', b, 1)], w=[('yc', b)])
            yield
            p.op('dve', lambda e, b=b: e.tensor_reduce(out=m8[b][:], in_=v3(yc[b][:]), axis=AX.X, op=ALU.add), r=[('yc', b)], w=[('m8', b)])
            yield
            p.op('dve', lambda e, b=b: e.tensor_scalar(out=m8[b][:], in0=m8[b][:], scalar1=1.0 / 64, scalar2=None, op0=ALU.mult),
                 r=[('m8', b)], w=[('m8', b)])
            yield
            p.op('dve', lambda e, b=b: e.tensor_tensor(out=v3(yc[b][:]), in0=v3(yc[b][:]), in1=bc8(m8[b][:]), op=ALU.subtract),
                 r=[('yc', b), ('m8', b)], w=[('yc', b)])
            yield
            p.op('dve', lambda e, b=b: e.tensor_tensor(out=sq[b][:], in0=yc[b][:], in1=yc[b][:], op=ALU.mult), r=[('yc', b)], w=[('sq2', b)])
            yield
            p.op('dve', lambda e, b=b: e.tensor_reduce(out=v8[b][:], in_=v3(sq[b][:]), axis=AX.X, op=ALU.add), r=[('sq2', b)], w=[('v8', b)])
            yield
            p.op('act', lambda e, b=b: e.activation(out=v8[b][:], in_=v8[b][:], func=AF.Sqrt, bias=gn_eps[:], scale=1.0 / 64),
                 r=[('v8', b), 'gneps'], w=[('v8', b)])
            yield
            p.op('dve', lambda e, b=b: e.reciprocal(out=v8[b][:], in_=v8[b][:]), r=[('v8', b)], w=[('v8', b)])
            yield
            p.op('dve', lambda e, b=b: e.tensor_tensor(out=v3(yc[b][:]), in0=v3(yc[b][:]), in1=bc8(v8[b][:]), op=ALU.mult),
                 r=[('yc', b), ('v8', b)], w=[('yc', b)])
            yield
            p.op('dve', lambda e, b=b: e.tensor_tensor(out=yc[b][:], in0=yc[b][:], in1=LNW[:], op=ALU.mult), r=[('yc', b), 'LNW'], w=[('yc', b)])
            yield
            p.op('dve', lambda e, b=b: e.tensor_tensor(out=yc[b][:], in0=yc[b][:], in1=LNB[:], op=ALU.add), r=[('yc', b), 'LNB'], w=[('yc', b)])
            yield
            p.op('dve', lambda e, b=b: e.tensor_tensor(out=v3(sq[b][:]), in0=v3(vg[b][:, 0, :]), in1=bc8(bon[b][:]),
                                                       op=ALU.mult), r=[('vg', b), ('bon', b)], w=[('sq2', b)])
            yield
            p.op('dve', lambda e, b=b: e.tensor_tensor(out=yc[b][:], in0=yc[b][:], in1=sq[b][:], op=ALU.add), r=[('yc', b), ('sq2', b)], w=[('yc', b)])
            yield
            p.op('dve', lambda e, b=b: e.tensor_tensor(out=O[b][:, 512:1024], in0=yc[b][:], in1=vg[b][:, 1, :], op=ALU.mult),
                 r=[('yc', b), ('vg', b)], w=[('O2', b)])
            yield
            p.dma(lambda e, b=b, rows=rows: e.dma_start(out=S['o'][rows, 512:1024], in_=O[b][:, 512:1024]),
                  r=[('O2', b)], w=[('So2', t)], eng='pool')
            yield
            p.op('act', lambda e, b=b: e.activation(out=Ob[b][:], in_=O[b][:], func=AF.Copy),
                 r=[('O', b), ('O2', b)], w=[('Ob', b)])
            bank = 6 + b
            pv = ps[bank][:, 0:512].bitcast(BF16)
            yield
            for j in range(8):
                p.op('pe', lambda e, b=b, j=j, pv=pv: e.transpose(out=pv[:, j * 128:(j + 1) * 128],
                                                                 in_=Ob[b][:, j * 128:(j + 1) * 128], identity=ident_b[:]),
                     r=[('Ob', b), 'identb'], w=[PS(bank)])
            yield
            p.op('act', lambda e, b=b, pv=pv: e.activation(out=oT[b][:], in_=pv.rearrange("p (j t) -> p j t", j=8),
                                                            func=AF.Copy), r=[PS(bank)], w=[('oT', b)])
            yield
            for half in range(2):
                ybank = 2 * b + half
                for j in range(8):
                    p.op('pe', lambda e, b=b, j=j, half=half, ybank=ybank: e.matmul(
                        ps[ybank][:, :], oT[b][:, j, :], wo[:, j, half * 512:(half + 1) * 512],
                        start=(j == 0), stop=(j == 7)), r=[('oT', b), ('wo', j)], w=[PS(ybank)])
                cs_ = slice(half * 512, (half + 1) * 512)
                p.op('dve', lambda e, b=b, s=s, cs_=cs_, ybank=ybank: e.tensor_tensor(
                    out=O[b][:, cs_], in0=ps[ybank][:, :], in1=G1b[s][:, cs_], op=ALU.mult),
                    r=[PS(ybank), ('G1b', s), ('Ob', b), ('So2', t)], w=[('O', b), ('O2', b)])
                p.op('dve', lambda e, b=b, cs_=cs_: e.tensor_tensor(out=xt[b][:, cs_], in0=xt[b][:, cs_], in1=O[b][:, cs_],
                                                                   op=ALU.add), r=[('O', b), ('xr', b)], w=[('xr', b)])
            yield
            p.dma(lambda e, b=b, rows=rows: e.dma_start(out=S['xs'][rows, :], in_=xt[b][:]), r=[('xr', b)], w=[('Sxs', t)], eng='pool')
        def rr5(gens):
            gens = list(gens)
            while gens:
                for g_ in list(gens):
                    try:
                        next(g_)
                    except StopIteration:
                        gens.remove(g_)

        tl2 = [t for t in range(NT) if not (t < 2 and not with_ctx)]
        rolling([(lambda t=t: rout_iter(t)) for t in tl2], stagger=12)
        p.barrier()

    def phase_peer(l):
        p.sb_reset(base_mark)
        last = l == DEPTH - 1
        eu_all = p.sb([128, NT, 128], U32, "eu_all")
        gate_all = p.sb([128, NT, 128], F32, "gate_all")
        G2b = [p.sb([128, D], F32, f"G2b{s}") for s in range(2)]
        m1 = p.sb_mark()
        hT = p.sb([128, 8, NTOK], BF16, "hT2")
        wq = p.sb([128, 8, 2048], BF16, "wq")
        for j in range(8):
            p.dma(lambda e, j=j: e.dma_start(out=wq[:, j, :], in_=I['peer_wq'][l, j * 128:(j + 1) * 128, :]),
                  w=[('wq', j)], eng="pool")
        keysT = p.sb([128, 16, 128], F32, "keysT")
        m0 = p.sb_mark()
        kraw = p.sb([128, 16, 128], F32, "kraw")
        p.dma(lambda e: e.dma_start(out=kraw[:], in_=I['peer_keys'][l].rearrange("h q n d -> n (h q) d")), w=['kraw'])
        for g in range(4):
            for i in range(4):
                hp = g * 4 + i
                p.op('pe', lambda e, g=g, i=i, hp=hp: e.transpose(out=ps[g][:, i * 128:(i + 1) * 128], in_=kraw[:, hp, :],
                                                                 identity=ident_f[:]), r=['kraw', 'identf'], w=[PS(g)])
            p.op('act', lambda e, g=g: e.activation(out=keysT[:, g * 4:(g + 1) * 4, :],
                                                    in_=ps[g][:, :].rearrange("p (i n) -> p i n", i=4), func=AF.Copy),
                 r=[PS(g)], w=['keysT'])
        p.barrier()
        p.sb_reset(m0)
        norm_tiles(l, 1, S['xs'], hT, lambda t: t * 128, tm_dram=S['h2'])
        p.barrier()
        p.sb_reset(m0)
        for s in range(2):
            load_bc(G2b[s][:], S['mod'][l, s, 5 * D:6 * D], ('G2b', s))
        qT = [p.sb([128, 16, 128], F32, f"qT{i}") for i in range(2)]
        sc = [p.sb([128, 16, 128], F32, f"sc{i}") for i in range(2)]
        sc2 = p.sb([128, 16, 128], F32, "sc2")
        sv = p.sb([128, 16, 16], F32, "sv")
        si = p.sb([128, 16, 16], U32, "si")
        sif = p.sb([128, 16, 16], F32, "sif")
        cand = p.sb([128, 8, 16, 16], F32, "cand")
        cand2 = p.sb([128, 8, 16, 16], F32, "cand2")
        eidx = p.sb([128, 8, 16, 16], F32, "eidx")
        best = p.sb([128, 8, 16], F32, "best")
        ci = p.sb([128, 8, 16], U32, "ci")
        cif = p.sb([128, 8, 16], F32, "cif")
        iota = p.sb([128, 256], F32, "iota")
        p.dma(lambda e: e.dma_start(out=iota[:], in_=I['iota']), w=['iota'])
        eq4 = p.sb([128, 8, 16, 16], F32, "eq4")
        cu = p.sb([128, 2, 8, 16], U32, "cu")
        cf = p.sb([128, 2, 8, 16], F32, "cf")
        e12 = p.sb([128, 2, 8, 16], F32, "e12")
        esel = p.sb([128, 128], F32, "esel")
        g8 = p.sb([128, 8], F32, "g8")
        tiles = [t for t in range(NT) if not (t < 2 and last)]
        def h1_stage1(t, bq):
            for g in range(4):
                for i in range(4):
                    hp = g * 4 + i
                    for j in range(8):
                        p.op('pe', lambda e, g=g, i=i, hp=hp, j=j, t=t: e.matmul(
                            ps[g][:, i * 128:(i + 1) * 128], wq[:, j, hp * 128:(hp + 1) * 128],
                            hT[:, j, t * 128:(t + 1) * 128], start=(j == 0), stop=(j == 7)),
                            r=[('hT', t), ('wq', j)], w=[PS(g)])
                p.op('act', lambda e, g=g, bq=bq: e.activation(out=qT[bq][:, g * 4:(g + 1) * 4, :],
                                                        in_=ps[g][:, :].rearrange("p (i n) -> p i n", i=4), func=AF.Copy),
                     r=[PS(g)], w=[('qT', bq)])
            for g in range(4):
                for i in range(4):
                    hp = g * 4 + i
                    p.op('pe', lambda e, g=g, i=i, hp=hp, bq=bq: e.matmul(ps[4 + g][:, i * 128:(i + 1) * 128], qT[bq][:, hp, :],
                                                                   keysT[:, hp, :], start=True, stop=True),
                         r=[('qT', bq), 'keysT'], w=[PS(4 + g)])
                p.op('act', lambda e, g=g, bq=bq: e.activation(out=sc[bq][:, g * 4:(g + 1) * 4, :],
                                                        in_=ps[4 + g][:, :].rearrange("p (i n) -> p i n", i=4), func=AF.Copy),
                     r=[PS(4 + g)], w=[('sc', bq)])

        def h1_stage2(t, bq):
            SVK = [('sv', hp) for hp in range(16)]
            SIK = [('si', hp) for hp in range(16)]
            for hp in range(16):
                p.op('dve', lambda e, hp=hp, bq=bq: e.max(out=sv[:, hp, 0:8], in_=sc[bq][:, hp, :]), r=[('sc', bq)], w=[('sv', hp)])
            for hp in range(16):
                p.op('dve', lambda e, hp=hp, bq=bq: e.max_index(out=si[:, hp, 0:8], in_max=sv[:, hp, 0:8], in_values=sc[bq][:, hp, :]),
                     r=[('sc', bq), ('sv', hp)], w=[('si', hp)])
            for hp in range(16):
                p.op('dve', lambda e, hp=hp, bq=bq: e.match_replace(out=sc2[:, hp, :], in_to_replace=sv[:, hp, 0:8],
                                                             in_values=sc[bq][:, hp, :], imm_value=-1e30),
                     r=[('sc', bq), ('sv', hp)], w=[('sc2', hp)])
            for hp in range(16):
                p.op('dve', lambda e, hp=hp, bq=bq: e.max(out=sv[:, hp, 8:16], in_=sc2[:, hp, :]), r=[('sc2', hp)], w=[('sv8', hp)])
            for hp in range(16):
                p.op('dve', lambda e, hp=hp, bq=bq: e.max_index(out=si[:, hp, 8:16], in_max=sv[:, hp, 8:16], in_values=sc2[:, hp, :]),
                     r=[('sc2', hp), ('sv8', hp)], w=[('si8', hp)])
            SVK = SVK + [('sv8', hp) for hp in range(16)]
            SIK = SIK + [('si8', hp) for hp in range(16)]
            p.op('dve', lambda e: e.tensor_copy(out=sif[:], in_=si[:]), r=SIK, w=['sif'])
            svv = sv[:].rearrange("p (h q) k -> p h q k", q=2)
            sfv = sif[:].rearrange("p (h q) k -> p h q k", q=2)
            p.op('dve', lambda e, svv=svv: e.tensor_tensor(
                out=cand[:], in0=svv[:, :, 0, :].unsqueeze(3).to_broadcast([128, 8, 16, 16]),
                in1=svv[:, :, 1, :].unsqueeze(2).to_broadcast([128, 8, 16, 16]), op=ALU.add), r=SVK, w=['cand'])
            p.op('dve', lambda e, sfv=sfv: e.tensor_scalar(out=sfv[:, :, 0, :], in0=sfv[:, :, 0, :], scalar1=128.0,
                                                           scalar2=None, op0=ALU.mult), r=['sif'], w=['sif'])
            chs = [cand[:, h].rearrange("p a b -> p (a b)") for h in range(8)]
            ch2s = [cand2[:, h].rearrange("p a b -> p (a b)") for h in range(8)]
            for h in range(8):
                p.op('dve', lambda e, h=h: e.max(out=best[:, h, 0:8], in_=chs[h]), r=['cand'], w=[('best', h)])
            for h in range(8):
                p.op('dve', lambda e, h=h: e.max_index(out=ci[:, h, 0:8], in_max=best[:, h, 0:8], in_values=chs[h]),
                     r=['cand', ('best', h)], w=[('ci', h)])
            for h in range(8):
                p.op('dve', lambda e, h=h: e.match_replace(out=ch2s[h], in_to_replace=best[:, h, 0:8],
                                                           in_values=chs[h], imm_value=-1e30),
                     r=['cand', ('best', h)], w=[('cand2', h)])
            for h in range(8):
                p.op('dve', lambda e, h=h: e.max(out=best[:, h, 8:16], in_=ch2s[h]), r=[('cand2', h)], w=[('best8', h)])
            for h in range(8):
                p.op('dve', lambda e, h=h: e.max_index(out=ci[:, h, 8:16], in_max=best[:, h, 8:16], in_values=ch2s[h]),
                     r=[('cand2', h), ('best8', h)], w=[('ci8', h)])
            BK = [('best', h) for h in range(8)] + [('best8', h) for h in range(8)]
            CIK = [('ci', h) for h in range(8)] + [('ci8', h) for h in range(8)]
            p.op('dve', lambda e: e.tensor_scalar(out=cu[:, 0], in0=ci[:], scalar1=4, scalar2=None,
                                                  op0=ALU.logical_shift_right), r=CIK, w=['cu'])
            p.op('dve', lambda e: e.tensor_scalar(out=cu[:, 1], in0=ci[:], scalar1=15, scalar2=None,
                                                  op0=ALU.bitwise_and), r=CIK, w=['cu'])
            p.op('dve', lambda e: e.tensor_copy(out=cf[:], in_=cu[:]), r=['cu'], w=['cf'])
            io16 = iota[:, 0:16].unsqueeze(1).unsqueeze(1).to_broadcast([128, 8, 16, 16])
            for q in range(2):
                p.op('dve', lambda e, q=q, io16=io16: e.tensor_tensor(
                    out=eq4[:], in0=io16, in1=cf[:, q].unsqueeze(3).to_broadcast([128, 8, 16, 16]), op=ALU.is_equal),
                    r=['iota', 'cf'], w=['eq4'])
                p.op('dve', lambda e, q=q, sfv=sfv: e.tensor_tensor(
                    out=eq4[:], in0=eq4[:], in1=sfv[:, :, q, :].unsqueeze(2).to_broadcast([128, 8, 16, 16]), op=ALU.mult),
                    r=['eq4', 'sif'], w=['eq4'])
                p.op('dve', lambda e, q=q: e.tensor_reduce(out=e12[:, q], in_=eq4[:], axis=AX.X, op=ALU.add),
                     r=['eq4'], w=['e12'])
            p.op('dve', lambda e: e.tensor_tensor(out=esel[:], in0=e12[:, 0].rearrange("p h k -> p (h k)"),
                                                  in1=e12[:, 1].rearrange("p h k -> p (h k)"), op=ALU.add),
                 r=['e12'], w=['esel'])
            p.op('dve', lambda e, t=t: e.tensor_copy(out=eu_all[:, t, :], in_=esel[:]), r=['esel'], w=[('eu', t)])
            gv = gate_all[:, t, :].rearrange("p (h k) -> p h k", h=8)
            p.op('dve', lambda e, gv=gv: e.tensor_tensor(out=gv, in0=best[:],
                                                         in1=best[:, :, 0:1].to_broadcast([128, 8, 16]), op=ALU.subtract),
                 r=BK, w=[('gate', t)])
            p.op('act', lambda e, t=t: e.activation(out=gate_all[:, t, :], in_=gate_all[:, t, :], func=AF.Exp),
                 r=[('gate', t)], w=[('gate', t)])
            p.op('dve', lambda e, gv=gv: e.tensor_reduce(out=g8[:], in_=gv, axis=AX.X, op=ALU.add), r=[('gate', t)], w=['g8'])
            p.op('dve', lambda e: e.reciprocal(out=g8[:], in_=g8[:]), r=['g8'], w=['g8'])
            p.op('dve', lambda e, gv=gv: e.tensor_tensor(out=gv, in0=gv, in1=g8[:].unsqueeze(2).to_broadcast([128, 8, 16]),
                                                         op=ALU.mult), r=[('gate', t), 'g8'], w=[('gate', t)])

        for i_, t_ in enumerate(tiles):
            if i_ == 0:
                h1_stage1(t_, 0)
            if i_ + 1 < len(tiles):
                h1_stage1(tiles[i_ + 1], (i_ + 1) % 2)
            h1_stage2(t_, i_ % 2)
        p.barrier()
        p.sb_reset(m1)
        h2 = [p.sb([128, D], F32, f"h2{i}") for i in range(2)]
        xt = [p.sb([128, D], F32, f"xp{i}") for i in range(2)]
        act = [p.sb([128, 128], F32, f"actv{i}") for i in range(2)]
        wg = [p.sb([128, 128], F32, f"wg{i}") for i in range(2)]
        NACC = 1
        acc = [[p.sb([128, D], F32, f"acc{i}{k}") for k in range(NACC)] for i in range(2)]
        junk = p.sb([128, D], BF16, "pjunk")
        NG = 32
        GS = 4
        gbuf = [p.sb([128, 2 * D], BF16, f"gb{i}") for i in range(NG)]
        NDG = 8
        dg = [p.sb([128, 128], BF16, f"dg{i}") for i in range(NDG)]
        gi = 0
        di = 0
        for t in tiles:
            b = t % 2
            s = 1 if t < 2 else 0
            rows = slice(t * 128, (t + 1) * 128)
            p.dma(lambda e, b=b, rows=rows: e.dma_start(out=h2[b][:], in_=S['h2'][rows, :]), w=[('h2', b)])
            p.dma(lambda e, b=b, rows=rows: e.dma_start(out=xt[b][:], in_=S['xs'][rows, :]), w=[('xp', b)])
            p.op('dve', lambda e, b=b: e.memset(act[b][:], 0.0), w=[('actv', b)])
            for g in range(128 // GS):
                ks = []
                for sidx in range(g * GS, (g + 1) * GS):
                    k = gi % NG
                    gi += 1
                    ks.append(k)
                    p.dma(lambda e, k=k, t=t, sidx=sidx: e.indirect_dma_start(
                        out=gbuf[k][:], out_offset=None, in_=S['T'][l],
                        in_offset=bass.IndirectOffsetOnAxis(ap=eu_all[:, t, sidx:sidx + 1], axis=0)),
                        r=[], w=[('gb', k)], eng="pool")
                    p.op('dve', lambda e, k=k, b=b, sidx=sidx: e.scalar_tensor_tensor(
                        out=junk[:], in0=gbuf[k][:, 0:D], scalar=1.0, in1=h2[b][:], op0=ALU.mult, op1=ALU.mult,
                        accum_out=act[b][:, sidx:sidx + 1]), r=[('gb', k), ('h2', b), ('actv', b)], w=[('actc', b, sidx)])
                gs = slice(g * GS, (g + 1) * GS)
                p.op('act', lambda e, b=b, gs=gs: e.activation(out=wg[b][:, gs], in_=act[b][:, gs], func=AF.Gelu),
                     r=[('actc', b, sidx) for sidx in range(g * GS, (g + 1) * GS)], w=[('wg', b, g)])
                p.op('dve', lambda e, b=b, gs=gs, t=t: e.tensor_tensor(out=wg[b][:, gs], in0=wg[b][:, gs],
                                                                      in1=gate_all[:, t, gs], op=ALU.mult),
                     r=[('wg', b, g)], w=[('wg', b, g)])
                for j, sidx in enumerate(range(g * GS, (g + 1) * GS)):
                    k = ks[j]
                    dj = di % NDG
                    di += 1
                    p.op('act', lambda e, dj=dj, b=b, sidx=sidx: e.activation(
                        out=dg[dj][:], in_=ident_f[:], func=AF.Copy, scale=wg[b][:, sidx:sidx + 1]),
                        r=[('wg', b, g), 'identf'], w=[('dg', dj)])
                    for half in range(2):
                        bank = 2 * b + half
                        p.op('pe', lambda e, dj=dj, k=k, half=half, bank=bank, sidx=sidx: e.matmul(
                            ps[bank][:, :], dg[dj][:], gbuf[k][:, D + half * 512:D + (half + 1) * 512],
                            start=(sidx == 0), stop=(sidx == 127)), r=[('dg', dj), ('gb', k)], w=[PS(bank), ('gbr', k, half)])
            a0 = acc[b][0]
            for half in range(2):
                hs_ = slice(half * 512, (half + 1) * 512)
                p.op('dve', lambda e, a0=a0, s=s, b=b, half=half, hs_=hs_: e.tensor_tensor(
                    out=a0[:, hs_], in0=ps[2 * b + half][:, :], in1=G2b[s][:, hs_], op=ALU.mult),
                    r=[PS(2 * b + half), ('G2b', s)], w=[('acc', b, 0)])
            p.op('dve', lambda e, a0=a0, b=b: e.tensor_tensor(out=xt[b][:], in0=xt[b][:], in1=a0[:], op=ALU.add),
                 r=[('acc', b, 0), ('xp', b)], w=[('xp', b)])
            if last:
                p.dma(lambda e, b=b, t=t: e.dma_start(out=out_d[(t - 2) * 128:(t - 1) * 128, :], in_=xt[b][:]),
                      r=[('xp', b)], w=[('outd', t)], eng='act')
            else:
                p.dma(lambda e, b=b, rows=rows: e.dma_start(out=S['xs'][rows, :], in_=xt[b][:]),
                      r=[('xp', b)], w=[('Sxs', t)], eng='act')
        p.barrier()

    PHASES = cfg.get("phases", ["proj", "rprep", "scan", "rout", "peer"])

    phase_mod()
    for l in range(cfg.get("layers", DEPTH)):
        if 'proj' in PHASES:
            qkT, Vaug, mp = phase_proj(l)
            phase_attn(l, qkT, Vaug, mp)
        if 'rprep' in PHASES:
            phase_rprep(l)
        if 'scan' in PHASES:
            phase_scan(l)
        if 'rout' in PHASES:
            phase_rout(l)
        if 'peer' in PHASES:
            phase_peer(l)
    p.barrier()
    p.emit()
    return nc


def prep_inputs(inputs):
    f = lambda a: np.ascontiguousarray(np.asarray(a, dtype=np.float32))
    x, c, ctx, c_ctx = f(inputs['x']), f(inputs['c']), f(inputs['ctx']), f(inputs['c_ctx'])
    shared = {}
    for n in ['norm_mix', 'norm_ffn', 'w_mod', 'b_mod', 'w_in', 'w_out', 'a_qnorm', 'a_knorm', 'b_qnorm', 'b_knorm',
              'a_sink']:
        shared[n] = f(inputs[n])
    rpb = f(inputs['b_rpb'])
    btab = np.zeros((DEPTH, 128, NTAB, 4, 128), np.float32)
    bmask = np.zeros((128, NTAB, 128), np.float32)
    for i, (dr, dc, valid) in enumerate(NA_TABS):
        g = rpb[:, :, dr, dc]
        btab[:, :, i, :, :] = np.where(valid[None, None], g, 0.0).transpose(0, 2, 1, 3)
        bmask[:, i, :] = valid
    shared['btab'] = btab
    shared['bmask'] = bmask
    ar = np.arange(128)
    am = np.zeros((128, 2, 128), np.float32)
    am[:, 0, :] = (ar[:, None] >= ar[None, :])
    am[:, 1, :] = (ar[:, None] <= ar[None, :])
    shared['amask'] = am
    shared['ident'] = np.eye(128, dtype=np.float32)
    cos, sin = rope_tables()
    shared['cos'], shared['sin'] = cos, sin
    rc = f(inputs['r7_conv'])
    for n in ['r7_w0', 'r7_a0', 'r7_w2', 'r7_a2', 'r7_g2', 'r7_kk', 'r7_ka', 'r7_lnw', 'r7_lnb', 'r7_rk', 'peer_wq', 'peer_keys']:
        shared[n] = f(inputs[n])
    for l in range(DEPTH):
        shared[f'peer_u{l}'] = f(inputs['peer_u'][l])
        shared[f'peer_v{l}'] = f(inputs['peer_v'][l])
    a64 = np.arange(64)
    tri = np.zeros((64, 2, 64), np.float32)
    tri[:, 0, :] = a64[:, None] <= a64[None, :]
    tri[:, 1, :] = a64[:, None] >= a64[None, :]
    mg = np.zeros((64, 2, 128), np.float32)
    mg[:, 0, 0:64] = a64[:, None] < a64[None, :]
    mg[:, 0, 64:128] = a64[:, None] <= a64[None, :]
    mg[:, 1, 0:64] = a64[:, None] > a64[None, :]
    mg[:, 1, 64:128] = a64[:, None] >= a64[None, :]
    mn = np.zeros((64, 2, 64), np.float32)
    mn[:, 0, :] = a64[None, :] < a64[:, None]
    mn[:, 1, :] = a64[None, :] > a64[:, None]
    shared['tri'], shared['mg'], shared['mn'] = tri, mg, mn
    shared['iota'] = np.ascontiguousarray(np.broadcast_to(np.arange(256, dtype=np.float32), (128, 256)))
    shared['r7_conv'] = np.ascontiguousarray(rc.reshape(DEPTH, 3, 15, 128).transpose(0, 3, 2, 1))
    maps = []
    for b in range(8):
        m = dict(shared)
        m['x'] = np.ascontiguousarray(np.concatenate([ctx[b], x[b]], axis=0))
        cc = np.stack([c[b], c_ctx], axis=-1)
        m['cc'] = np.ascontiguousarray(cc.reshape(8, 128, 2).transpose(1, 0, 2))
        maps.append(m)
    return maps


_NC_CACHE = {}


def kernel(**inputs):
    if 'nc' not in _NC_CACHE:
        _NC_CACHE['nc'] = build({})
    nc = _NC_CACHE['nc']
    maps = prep_inputs(inputs)
    res = run_bass_kernel_spmd(nc, maps, core_ids=list(range(8)))
    return np.stack([np.asarray(r['out'], dtype=np.float32) for r in res.results], axis=0)
```
